# Optimizing a Trainium2 kernel written in Bass

```python
import jax
import jax.numpy as jnp
from jax import lax
import numpy as np

D_MODEL = 1024
BATCH = 2
SEQ = 8192
DEPTH = 4

GRID_W = 64
CTX_LEN = 256
HEAD_DIM = 64
N_Q_HEADS = 12
N_KV_HEADS = 4
GQA_GROUP = N_Q_HEADS // N_KV_HEADS
N_FOURIER_GROUPS = 4
FOURIER_GROUP_DIM = 64
D_Q = N_Q_HEADS * HEAD_DIM
D_KV = N_KV_HEADS * HEAD_DIM
D_FOURIER = N_FOURIER_GROUPS * FOURIER_GROUP_DIM
D_IN_EVEN = D_Q + 2 * D_KV + D_FOURIER
D_CAT_EVEN = D_Q + D_FOURIER
ROPE_PAIRS_PER_AXIS = HEAD_DIM // 4
ROPE_THETA = 10000.0
Q_BLOCK = 128
CONV_WIDTH = 31
D_CONV = D_MODEL
D_FF = 4 * D_MODEL
N_MOD = 6
EPS = 1e-6
ATTN_SCALE = HEAD_DIM ** -0.5

kernel_name = "hybrid_gqa_fourier_conformer_dit_prefix"


def rms_norm(x, g):
    xf = x.astype(jnp.float32)
    y = xf * lax.rsqrt(jnp.mean(xf * xf, axis=-1, keepdims=True) + EPS)
    return (y * g.astype(jnp.float32)).astype(x.dtype)


def layer_norm(x, g, b):
    xf = x.astype(jnp.float32)
    mu = jnp.mean(xf, axis=-1, keepdims=True)
    var = jnp.mean(jnp.square(xf - mu), axis=-1, keepdims=True)
    y = (xf - mu) * lax.rsqrt(var + EPS)
    return (y * g.astype(jnp.float32) + b.astype(jnp.float32)).astype(x.dtype)


def adaln_params(s, w, b):
    m = (s @ w + b).reshape(s.shape[0], 1, N_MOD, D_MODEL)
    return tuple(m[:, :, j] for j in range(N_MOD))


def modulate(h, shift, scale):
    return h * (1 + scale) + shift


def axial_rope_tables(row_idx, col_idx):
    freqs = ROPE_THETA ** (-jnp.arange(ROPE_PAIRS_PER_AXIS, dtype=jnp.float32) / ROPE_PAIRS_PER_AXIS)
    ang = jnp.concatenate([row_idx.astype(jnp.float32)[:, None] * freqs,
                           col_idx.astype(jnp.float32)[:, None] * freqs], axis=-1)
    return jnp.cos(ang), jnp.sin(ang)


def apply_rope(x, cos, sin):
    xp = x.reshape(*x.shape[:-1], HEAD_DIM // 2, 2)
    x1, x2 = xp[..., 0], xp[..., 1]
    cs = cos[None, :, None, :].astype(x.dtype)
    sn = sin[None, :, None, :].astype(x.dtype)
    out = jnp.stack([x1 * cs - x2 * sn, x1 * sn + x2 * cs], axis=-1)
    return out.reshape(x.shape)


def attend(q, k, v):
    b, lq = q.shape[:2]
    qg = q.reshape(b, lq, N_KV_HEADS, GQA_GROUP, HEAD_DIM)
    s = jnp.einsum("bqkgd,bskd->bkgqs", qg, k, preferred_element_type=jnp.float32) * ATTN_SCALE
    p = jax.nn.softmax(s, axis=-1).astype(v.dtype)
    return jnp.einsum("bkgqs,bskd->bqkgd", p, v).reshape(b, lq, D_Q)


def attend_blocks(q, k, v):
    b, s = q.shape[:2]
    nb = s // Q_BLOCK
    qb = q.reshape(b, nb, Q_BLOCK, N_Q_HEADS, HEAD_DIM).swapaxes(0, 1)
    out = lax.map(lambda q_blk: attend(q_blk, k, v), qb)
    return out.swapaxes(0, 1).reshape(b, s, D_Q)


def fourier_mix(f):
    b, l, _ = f.shape
    fg = f.reshape(b, l, N_FOURIER_GROUPS, FOURIER_GROUP_DIM).astype(jnp.float32)
    out = jnp.fft.fftn(fg, axes=(1, 3), norm="ortho").real
    return out.reshape(b, l, D_FOURIER).astype(f.dtype)


def even_mixer(h_x, h_c, w_in, q_g, k_g, w_out, cos, sin, need_ctx):
    def project(h):
        p = h @ w_in
        bsz, l = h.shape[:2]
        q = rms_norm(p[..., :D_Q].reshape(bsz, l, N_Q_HEADS, HEAD_DIM), q_g)
        k = rms_norm(p[..., D_Q:D_Q + D_KV].reshape(bsz, l, N_KV_HEADS, HEAD_DIM), k_g)
        v = p[..., D_Q + D_KV:D_Q + 2 * D_KV].reshape(bsz, l, N_KV_HEADS, HEAD_DIM)
        f = p[..., D_Q + 2 * D_KV:]
        return q, k, v, f

    qx, kx, vx, fx = project(h_x)
    qc, kc, vc, fc = project(h_c)
    qx = apply_rope(qx, cos, sin)
    kx = apply_rope(kx, cos, sin)
    k_all = jnp.concatenate([kx, kc], axis=1)
    v_all = jnp.concatenate([vx, vc], axis=1)
    y_x = jnp.concatenate([attend_blocks(qx, k_all, v_all), fourier_mix(fx)], axis=-1) @ w_out
    y_c = None
    if need_ctx:
        y_c = jnp.concatenate([attend(qc, kc, vc), fourier_mix(fc)], axis=-1) @ w_out
    return y_x, y_c


def conformer_conv(h, w_pw1, b_pw1, w_dw, b_dw, ln_g, ln_b, w_pw2, b_pw2):
    a = h @ w_pw1 + b_pw1
    u = a[..., :D_CONV] * jax.nn.sigmoid(a[..., D_CONV:])
    pad = CONV_WIDTH // 2
    u = lax.conv_general_dilated(u, w_dw[:, None, :].astype(u.dtype), window_strides=(1,),
                                 padding=[(pad, pad)], dimension_numbers=("NWC", "WIO", "NWC"),
                                 feature_group_count=D_CONV) + b_dw
    u = jax.nn.silu(layer_norm(u, ln_g, ln_b))
    return u @ w_pw2 + b_pw2


def sq_relu_mlp(h, w1, w2):
    return jnp.square(jax.nn.relu(h @ w1)) @ w2


def setup_inputs(seed: int = 0) -> dict:
    key = jax.random.key(seed)
    ks = jax.random.split(key, 24)
    n_even = (DEPTH + 1) // 2
    n_odd = DEPTH // 2

    def nrm(k, shape, scale):
        return jax.random.normal(k, shape, jnp.float32) * scale

    def gain(k, shape):
        return 1.0 + nrm(k, shape, 0.02)

    return {
        "x": nrm(ks[0], (BATCH, SEQ, D_MODEL), 1.0),
        "c": nrm(ks[1], (BATCH, D_MODEL), 1.0),
        "ctx": nrm(ks[2], (BATCH, CTX_LEN, D_MODEL), 1.0),
        "c_ctx": nrm(ks[3], (D_MODEL,), 1.0),
        "ada_w": nrm(ks[4], (DEPTH, D_MODEL, N_MOD * D_MODEL), 0.5 * D_MODEL ** -0.5),
        "ada_b": nrm(ks[5], (DEPTH, N_MOD * D_MODEL), 0.02),
        "norm1_g": gain(ks[6], (DEPTH, D_MODEL)),
        "norm2_g": gain(ks[7], (DEPTH, D_MODEL)),
        "mlp_w1": nrm(ks[8], (DEPTH, D_MODEL, D_FF), D_MODEL ** -0.5),
        "mlp_w2": nrm(ks[9], (DEPTH, D_FF, D_MODEL), D_FF ** -0.5),
        "attn_w_in": nrm(ks[10], (n_even, D_MODEL, D_IN_EVEN), D_MODEL ** -0.5),
        "q_norm_g": gain(ks[11], (n_even, HEAD_DIM)),
        "k_norm_g": gain(ks[12], (n_even, HEAD_DIM)),
        "attn_w_out": nrm(ks[13], (n_even, D_CAT_EVEN, D_MODEL), D_CAT_EVEN ** -0.5),
        "conv_w_pw1": nrm(ks[14], (n_odd, D_MODEL, 2 * D_CONV), D_MODEL ** -0.5),
        "conv_b_pw1": nrm(ks[15], (n_odd, 2 * D_CONV), 0.02),
        "conv_w_dw": nrm(ks[16], (n_odd, CONV_WIDTH, D_CONV), CONV_WIDTH ** -0.5),
        "conv_b_dw": nrm(ks[17], (n_odd, D_CONV), 0.02),
        "conv_ln_g": gain(ks[18], (n_odd, D_CONV)),
        "conv_ln_b": nrm(ks[19], (n_odd, D_CONV), 0.02),
        "conv_w_pw2": nrm(ks[20], (n_odd, D_CONV, D_MODEL), D_CONV ** -0.5),
        "conv_b_pw2": nrm(ks[21], (n_odd, D_MODEL), 0.02),
    }


def reference(x, c, ctx, c_ctx, ada_w, ada_b, norm1_g, norm2_g, mlp_w1, mlp_w2,
              attn_w_in, q_norm_g, k_norm_g, attn_w_out,
              conv_w_pw1, conv_b_pw1, conv_w_dw, conv_b_dw, conv_ln_g, conv_ln_b,
              conv_w_pw2, conv_b_pw2):
    seq = x.shape[1]
    rows = seq // GRID_W
    row_idx = jnp.repeat(jnp.arange(rows, dtype=jnp.int32), GRID_W)
    col_idx = jnp.tile(jnp.arange(GRID_W, dtype=jnp.int32), rows)
    cos, sin = axial_rope_tables(row_idx, col_idx)
    s_x = jax.nn.silu(c)
    s_c = jax.nn.silu(c_ctx)[None, :]

    for i in range(DEPTH):
        even = i % 2 == 0
        j = i // 2
        ctx_update = i != DEPTH - 1
        sh1, sc1, g1, sh2, sc2, g2 = adaln_params(s_x, ada_w[i], ada_b[i])
        h_x = modulate(rms_norm(x, norm1_g[i]), sh1, sc1)
        if even or ctx_update:
            csh1, csc1, cg1, csh2, csc2, cg2 = adaln_params(s_c, ada_w[i], ada_b[i])
            h_c = modulate(rms_norm(ctx, norm1_g[i]), csh1, csc1)
        if even:
            y_x, y_c = even_mixer(h_x, h_c, attn_w_in[j], q_norm_g[j], k_norm_g[j], attn_w_out[j],
                                  cos, sin, ctx_update)
        else:
            conv_args = (conv_w_pw1[j], conv_b_pw1[j], conv_w_dw[j], conv_b_dw[j],
                         conv_ln_g[j], conv_ln_b[j], conv_w_pw2[j], conv_b_pw2[j])
            y_x = conformer_conv(h_x, *conv_args)
            y_c = conformer_conv(h_c, *conv_args) if ctx_update else None
        x = x + g1 * y_x
        x = x + g2 * sq_relu_mlp(modulate(rms_norm(x, norm2_g[i]), sh2, sc2), mlp_w1[i], mlp_w2[i])
        if ctx_update:
            ctx = ctx + cg1 * y_c
            ctx = ctx + cg2 * sq_relu_mlp(modulate(rms_norm(ctx, norm2_g[i]), csh2, csc2),
                                          mlp_w1[i], mlp_w2[i])
    return x
```

```python
import math
from contextlib import ExitStack

import numpy as np
import ml_dtypes
import concourse.bass as bass
import concourse.mybir as mybir
from concourse.bass_utils import run_bass_kernel_spmd

F32 = mybir.dt.float32
BF16 = mybir.dt.bfloat16
AF = mybir.ActivationFunctionType
ALU = mybir.AluOpType
NPBF = ml_dtypes.bfloat16

D = 1024
NLAT = 2048
NCTX = 256
NT = NLAT + NCTX
TCS = [(0, 512), (512, 512), (1024, 512), (1536, 512), (2048, 256)]
EPS = 1e-6
NCORES = 8


class _Op:
    __slots__ = ("eng", "fn", "deps", "dma", "sem", "semval", "signal", "count", "idx", "final", "inc")


class Prog:
    ENGS = ("pe", "act", "dve", "pool", "sp")

    def __init__(self, nc):
        self.nc = nc
        self.ops = []
        self.state = {}
        self.dma_sem_of = {}
        self.dma_sem_cnt = []
        self.finals = []

    def _add(self, eng, fn, r, w, dma=False, final=False, inc=16):
        op = _Op()
        op.inc = inc
        op.eng, op.fn, op.dma, op.final = eng, fn, dma, final
        op.signal = False
        op.count = None
        op.idx = len(self.ops)
        deps = {}
        for k in r:
            st = self.state.setdefault(k, [None, []])
            if st[0] is not None:
                deps[st[0]] = "raw"
        for k in w:
            st = self.state.setdefault(k, [None, []])
            if st[0] is not None:
                deps[st[0]] = "waw"
            for ri in st[1]:
                if ri not in deps:
                    deps[ri] = "war"
        for k in r:
            rl = self.state[k][1]
            if not dma:
                rl[:] = [ri for ri in rl if self.ops[ri].dma or self.ops[ri].eng != eng]
            rl.append(op.idx)
        for k in w:
            self.state[k] = [op.idx, []]
        op.deps = []
        latest = {}
        for di, kind in deps.items():
            dop = self.ops[di]
            if dop.dma:
                op.deps.append(di)
            elif dop.eng == eng:
                if eng == "pe":
                    continue
                latest[dop.eng] = max(latest.get(dop.eng, -1), di)
            else:
                latest[dop.eng] = max(latest.get(dop.eng, -1), di)
        for di in latest.values():
            self.ops[di].signal = True
            op.deps.append(di)
        if dma:
            key = w[0]
            if key not in self.dma_sem_of:
                self.dma_sem_of[key] = len(self.dma_sem_cnt)
                self.dma_sem_cnt.append(0)
            si = self.dma_sem_of[key]
            self.dma_sem_cnt[si] += inc
            op.sem = si
            op.semval = self.dma_sem_cnt[si]
            if final:
                self.finals.append(op.idx)
        self.ops.append(op)
        return op

    def barrier(self, keep=()):
        last = {}
        lastdma = {}
        kept = {}
        for k in keep:
            st = self.state.get(k)
            if st is not None and st[0] is not None and self.ops[st[0]].dma:
                kept[k] = st[0]
        skip_sems = set(self.ops[i].sem for i in kept.values())
        seen = getattr(self, "_bar_seen", {})
        for op in self.ops:
            if op.dma:
                if op.sem not in skip_sems and op.semval > seen.get(op.sem, 0):
                    lastdma[op.sem] = op.idx
            elif op.fn is not None:
                last[op.eng] = op.idx
        for si, li in lastdma.items():
            seen[si] = self.ops[li].semval
        self._bar_seen = seen
        for e in self.ENGS:
            op = _Op()
            op.eng, op.fn, op.dma, op.final = e, None, False, False
            op.signal = False
            op.count = None
            op.idx = len(self.ops)
            op.deps = []
            for e2, li in last.items():
                if e2 != e:
                    self.ops[li].signal = True
                    op.deps.append(li)
            for si, li in lastdma.items():
                op.deps.append(li)
            self.ops.append(op)
        self.state = {k: [i, []] for k, i in kept.items()}

    def op(self, eng, fn, r=(), w=()):
        return self._add(eng, fn, tuple(r), tuple(w))

    def dma(self, q, out, in_, r=(), w=(), final=False):
        assert q in ("sp", "pool")
        return self._add(q, lambda e: e.dma_start(out=out, in_=in_), tuple(r), tuple(w),
                         dma=True, final=final)

    def cc(self, kind, groups, in_ap, out_ap, r=(), w=()):
        return self._add("pool", lambda e: e.collective_compute(kind, ALU.bypass, replica_groups=groups,
                                                                ins=[in_ap], outs=[out_ap]),
                         tuple(r), tuple(w), dma=True, inc=1)

    def emit(self):
        nc = self.nc
        cnt = {e: 0 for e in self.ENGS}
        for op in self.ops:
            if op.dma or op.fn is None:
                continue
            if op.signal:
                cnt[op.eng] += 1
                op.count = cnt[op.eng]
        for e in self.ENGS:
            assert cnt[e] < 60000, (e, cnt[e])
        for v in self.dma_sem_cnt:
            assert v < 60000, v
        with ExitStack() as es:
            esem = {e: es.enter_context(nc.semaphore("s_" + e)) for e in ("pe", "act", "dve", "pool")}
            dsem = [es.enter_context(nc.semaphore("d%d" % i)) for i in range(len(self.dma_sem_cnt))]
            block = es.enter_context(nc.Block())
            ops = self.ops
            finals = self.finals

            def run(ename, e):
                waited = {}
                for op in ops:
                    if op.eng != ename:
                        continue
                    for di in op.deps:
                        dop = ops[di]
                        if dop.dma:
                            sem, val, key = dsem[dop.sem], dop.semval, ("d", dop.sem)
                        else:
                            sem, val, key = esem[dop.eng], dop.count, ("e", dop.eng)
                        if waited.get(key, 0) >= val:
                            continue
                        waited[key] = val
                        e.wait_ge(sem, val)
                    if op.fn is None:
                        continue
                    ins = op.fn(e)
                    if op.dma:
                        ins.then_inc(dsem[op.sem], op.inc)
                    elif op.signal:
                        ins.then_inc(esem[ename], 1)
                if ename == "sp":
                    for fi in finals:
                        fop = ops[fi]
                        e.wait_ge(dsem[fop.sem], fop.semval)

            @block.tensor
            def _(e):
                run("pe", e)

            @block.scalar
            def _(e):
                run("act", e)

            @block.vector
            def _(e):
                run("dve", e)

            @block.gpsimd
            def _(e):
                run("pool", e)

            @block.sync
            def _(e):
                run("sp", e)


class Ctx:
    def __init__(self, nc, es):
        self.nc = nc
        self.es = es
        self.P = Prog(nc)
        self.din = {}
        self.dout = {}

    def dram_in(self, name, shape, dt):
        t = self.nc.dram_tensor(name, list(shape), dt, kind="ExternalInput").ap()
        self.din[name] = t
        return t

    def dram_out(self, name, shape, dt):
        t = self.nc.dram_tensor(name, list(shape), dt, kind="ExternalOutput").ap()
        self.dout[name] = t
        return t

    def sb(self, name, shape, dt):
        self.nsb = getattr(self, "nsb", 0) + 1
        return self.es.enter_context(self.nc.sbuf_tensor("%s_%d" % (name, self.nsb), list(shape), dt))

    def mm(self, out, lhsT, rhs, start, stop, r, w):
        self.P.op("pe", lambda e: e.matmul(out, lhsT, rhs, start=start, stop=stop), r, w)

    def act(self, out, in_, func, r, w, bias=None, scale=None):
        kw = {}
        if bias is not None:
            kw["bias"] = bias
        if scale is not None:
            kw["scale"] = scale
        self.P.op("act", lambda e: e.activation(out, in_, func, **kw), r, w)

    def tt(self, eng, out, in0, in1, op, r, w):
        self.P.op(eng, lambda e: e.tensor_tensor(out, in0, in1, op), r, w)

    def ts(self, eng, out, in0, s1, s2, op0, op1, r, w):
        if op1 is None:
            self.P.op(eng, lambda e: e.tensor_scalar(out, in0, s1, None, op0), r, w)
        else:
            self.P.op(eng, lambda e: e.tensor_scalar(out, in0, s1, s2, op0, op1), r, w)

    def stt(self, out, in0, scalar, in1, op0, op1, r, w):
        self.P.op("dve", lambda e: e.scalar_tensor_tensor(out, in0, scalar, in1, op0, op1), r, w)

    def cp(self, eng, out, in_, r, w):
        self.P.op(eng, lambda e: e.tensor_copy(out, in_), r, w)

    def recip(self, out, in_, r, w):
        self.P.op("dve", lambda e: e.reciprocal(out, in_), r, w)

    def memset(self, eng, ap, val, w):
        self.P.op(eng, lambda e: e.memset(ap, val), (), w)


def _bf(a):
    return np.ascontiguousarray(a).astype(NPBF)


def build_mod():
    nc = bass.Bass("TRN2", target_bir_lowering=False)
    with ExitStack() as es:
        K = Ctx(nc, es)
        P = K.P
        adaw = K.dram_in("adaw", [D, 6 * D], F32)
        adab = K.dram_in("adab", [128, 48, 2], F32)
        cvec = K.dram_in("cvec", [128, 8, 2], F32)
        ng = K.dram_in("ng", [128, 2, 8], F32)
        modo = K.dram_out("modo", [128, 48, 2], F32)
        CV = K.sb("CV", [128, 8, 2], F32)
        ABs = K.sb("ABs", [128, 48, 2], F32)
        NG = K.sb("NG", [128, 2, 8], F32)
        E1 = K.sb("E1", [128, 8, 2], F32)
        S = K.sb("S", [128, 8, 2], BF16)
        MOD = K.sb("MOD", [128, 48, 2], F32)
        WM = [K.sb("WM%d" % i, [128, 8, 1024], BF16) for i in range(2)]
        PS = es.enter_context(nc.psum_tensor("PS", [128, 8, 512], F32))
        P.dma("sp", CV[:], cvec, w=["CV"])
        P.dma("sp", ABs[:], adab, w=["ABs"])
        P.dma("sp", NG[:], ng, w=["NG"])
        K.act(E1[:], CV[:], AF.Exp, ["CV"], ["E1"], scale=-1.0)
        K.ts("dve", E1[:], E1[:], 1.0, None, ALU.add, None, ["E1"], ["E1"])
        K.recip(E1[:], E1[:], ["E1"], ["E1"])
        K.tt("dve", S[:], CV[:], E1[:], ALU.mult, ["CV", "E1"], ["S"])
        adaw_v = adaw.rearrange("(kc p) n -> p kc n", p=128)
        for j in range(6):
            wm = WM[j % 2]
            wk = "WM%d" % (j % 2)
            P.dma("pool", wm[:], adaw_v[:, :, j * 1024:(j + 1) * 1024], w=[wk])
            bank = j % 2
            for c in range(8):
                for kc in range(8):
                    K.mm(PS[:, bank, c * 2:c * 2 + 2], wm[:, kc, c * 128:(c + 1) * 128], S[:, kc, :],
                         kc == 0, kc == 7, [wk, "S"], ["ps%d" % bank])
            K.tt("dve", MOD[:, j * 8:(j + 1) * 8, :],
                 PS[:, bank, 0:16].rearrange("p (c v) -> p c v", v=2),
                 ABs[:, j * 8:(j + 1) * 8, :], ALU.add, ["ps%d" % bank, "ABs"], ["MOD%d" % j])
        for j, gi in ((1, 0), (4, 1)):
            for v in range(2):
                K.stt(MOD[:, j * 8:(j + 1) * 8, v], MOD[:, j * 8:(j + 1) * 8, v], 1.0, NG[:, gi, :],
                      ALU.add, ALU.mult, ["MOD%d" % j, "NG"], ["MOD%d" % j])
        P.dma("sp", modo, MOD[:], r=["MOD%d" % j for j in range(6)], w=["modo"], final=True)
        P.emit()
    return nc


def alloc_common(K):
    nc, es = K.nc, K.es
    K.X = K.sb("X", [128, 8, NT], F32)
    K.PS = es.enter_context(nc.psum_tensor("PS", [128, 8, 512], F32))
    K.ONES = K.sb("ONES", [128, 128], BF16)
    K.MOD = {}
    K.memset("pool", K.ONES[:], 1.0, ["ONES"])


def alloc_norm(K):
    K.SQ = K.sb("SQ", [128, 8, 512], BF16)
    K.LNV = K.sb("LNV", [128, 512], F32)
    K.RSTD = K.sb("RSTD", [128, 512], F32)
    K.TMP = [K.sb("TMP%d" % i, [128, 512], F32) for i in range(2)]


def load_mod(K, L, name=None):
    t = K.dram_in(name or ("mod%d" % L), [128, 48, 2], F32)
    K.MOD[L] = K.sb("MODL%d" % L, [128, 48, 2], F32)
    K.P.dma("sp", K.MOD[L][:], t, w=["MOD%d" % L])


def xkeys(tc):
    return ["X%d.%d" % (tc, c) for c in range(8)]


def norm_sq(K, tc):
    t0, W = TCS[tc]
    X = K.X
    for c in range(8):
        eng = "dve" if c % 2 == 0 else "pool"
        K.tt(eng, K.SQ[:, c, :W], X[:, c, t0:t0 + W], X[:, c, t0:t0 + W], ALU.mult,
             ["X%d.%d" % (tc, c)], ["SQ%d" % c])


def norm_rest(K, L, which, tc, out_fn, out_keys):
    t0, W = TCS[tc]
    v = 1 if tc == 4 else 0
    MOD = K.MOD[L]
    ja, jb = (1, 0) if which == 0 else (4, 3)
    mk = "MOD%d" % L
    X, PS = K.X, K.PS
    for c in range(8):
        K.mm(PS[:, 7, :W], K.ONES[:], K.SQ[:, c, :W], c == 0, c == 7, ["ONES", "SQ%d" % c], ["ps7"])
    K.act(K.LNV[:, :W], PS[:, 7, :W], AF.Ln, ["ps7"], ["LNV"], bias=K.EPSB[:], scale=1.0 / D)
    K.act(K.RSTD[:, :W], K.LNV[:, :W], AF.Exp, ["LNV"], ["RSTD"], scale=-0.5)
    for c in range(8):
        tb = c % 2
        K.stt(K.TMP[tb][:, :W], X[:, c, t0:t0 + W], MOD[:, ja * 8 + c, v:v + 1], K.RSTD[:, :W],
              ALU.mult, ALU.mult, ["X%d.%d" % (tc, c), mk, "RSTD"], ["TMP%d" % tb])
        K.act(out_fn(c), K.TMP[tb][:, :W], AF.Identity, ["TMP%d" % tb, mk], out_keys(c),
              bias=MOD[:, jb * 8 + c, v:v + 1], scale=1.0)


def norm_mod(K, L, which, tc, out_fn, out_keys):
    norm_sq(K, tc)
    norm_rest(K, L, which, tc, out_fn, out_keys)


def alloc_eps(K):
    K.EPSB = K.sb("EPSB", [128, 1], F32)
    K.memset("pool", K.EPSB[:], EPS, ["EPSB"])
    K.ONEB = K.sb("ONEB", [128, 1], F32)
    K.memset("pool", K.ONEB[:], 1.0, ["ONEB"])


def mlp_segment(K, L, w1, w2, ntc=5, after_last=None):
    X, PS = K.X, K.PS
    H = K.H
    MOD = K.MOD[L]
    mk = "MOD%d" % L
    def norm2(tc):
        t0, W = TCS[tc]
        norm_mod(K, L, 1, tc, lambda c, t0=t0, W=W: H[:, c, t0:t0 + W], lambda c, tc=tc: ["H%d" % tc])

    norm2(0)
    w1v = w1.rearrange("(kc p) n -> p kc n", p=128)
    w2v = w2.rearrange("(fc p) n -> p fc n", p=128)
    items = [(e, tc) for e in range(8) for tc in range(ntc)]

    def load(e):
        K.P.dma("pool", K.W1E[e % 2][:], w1v[:, :, e * 512:(e + 1) * 512], w=["W1E%d" % (e % 2)])
        K.P.dma("pool", K.W2E[e % 2][:], w2v[:, e * 4:(e + 1) * 4, :], w=["W2E%d" % (e % 2)])

    def part1(i):
        e, tc = items[i]
        t0, W = TCS[tc]
        ab = i % 2
        for f in range(4):
            bank = f % 2
            for kc in range(8):
                K.mm(PS[:, bank, :W], K.W1E[e % 2][:, kc, f * 128:(f + 1) * 128], H[:, kc, t0:t0 + W],
                     kc == 0, kc == 7, ["W1E%d" % (e % 2), "H%d" % tc], ["ps%d" % bank])
            K.act(K.RL[f % 2][:, :W], PS[:, bank, :W], AF.Relu, ["ps%d" % bank], ["RL%d" % (f % 2)])
            eng = "pool" if f % 2 == 0 else "dve"
            K.tt(eng, K.AH[ab][:, f, :W], K.RL[f % 2][:, :W], K.RL[f % 2][:, :W], ALU.mult,
                 ["RL%d" % (f % 2)], ["AH%d.%d" % (ab, f)])

    def part2(i):
        e, tc = items[i]
        t0, W = TCS[tc]
        v = 1 if tc == 4 else 0
        ab = i % 2
        for d in range(8):
            bank = 2 + d % 2
            for f in range(4):
                K.mm(PS[:, bank, :W], K.W2E[e % 2][:, f, d * 128:(d + 1) * 128], K.AH[ab][:, f, :W],
                     f == 0, f == 3, ["W2E%d" % (e % 2), "AH%d.%d" % (ab, f)], ["ps%d" % bank])
            K.stt(X[:, d, t0:t0 + W], PS[:, bank, :W], MOD[:, 40 + d, v:v + 1], X[:, d, t0:t0 + W],
                  ALU.mult, ALU.add, ["ps%d" % bank, mk, "X%d.%d" % (tc, d)], ["X%d.%d" % (tc, d)])

    load(0)
    part1(0)
    for i in range(len(items)):
        e0, tc0 = items[i]
        if tc0 == 1 and e0 + 1 < 8:
            load(e0 + 1)
        if i + 1 < len(items):
            if items[i + 1][0] == 0:
                norm2(items[i + 1][1])
            part1(i + 1)
        part2(i)
        if e0 == 7 and after_last is not None:
            after_last(tc0)


def alloc_mlp(K):
    K.H = K.sb("H", [128, 8, NT], BF16)
    K.W1E = [K.sb("W1E%d" % i, [128, 8, 512], BF16) for i in range(2)]
    K.W2E = [K.sb("W2E%d" % i, [128, 4, 1024], BF16) for i in range(2)]
    K.RL = [K.sb("RL%d" % i, [128, 512], F32) for i in range(2)]
    K.AH = [K.sb("AH%d" % i, [128, 4, 512], BF16) for i in range(2)]


def alloc_epre_noqt(K):
    alloc_epre(K, with_qt=False)


def alloc_epre(K, with_qt=True):
    K.WIN = K.sb("WIN", [128, 8, 1536], BF16)
    K.HC = [K.sb("HC%d" % i, [128, 8, 512], BF16) for i in range(2)]
    if with_qt:
        K.QT = K.sb("QT", [128, 6, NT], BF16)
    K.KTL = K.sb("KTL", [128, 2, NT], BF16)
    K.VTC = [K.sb("VTC%d" % i, [128, 4, 2, 2, 65], BF16) for i in range(2)]
    K.ABC = [K.sb("ABC%d" % i, [128, 4, 512], BF16) for i in range(2)]
    K.FTC = K.sb("FTC", [128, 2, 512], BF16)
    K.ROPE = [K.sb("ROPE0", [128, 2, 512], F32)] * 2
    K.PSB = [K.sb("PSB%d" % i, [128, 512], F32) for i in range(2)]
    K.SQ1 = [K.sb("SQ1%d" % i, [128, 512], BF16) for i in range(2)]
    K.QN = [K.sb("QN%d" % i, [128, 512], F32) for i in range(2)]
    K.T1 = [K.sb("T10", [128, 512], F32)] * 2
    K.T2 = [K.sb("T20", [128, 512], F32)] * 2
    K.LN2 = [K.sb("LN20", [128, 512], F32)] * 2
    K.RS2 = [K.sb("RS20", [128, 512], F32)] * 2
    if not hasattr(K, "BLK"):
        K.BLK = K.sb("BLK", [128, 128], BF16)
        K.PERM = K.sb("PERM", [128, 128], F32)
        K.CS64 = K.sb("CS64", [128, 256], BF16)
    K.QKG = K.sb("QKG", [128, 8], F32)


def epre_segment(K, L, win, rope, qkg, blk, perm, cs64, kt_o, v_o, ab_o, final=True, load_consts=True, after_tc=None):
    X, PS, P = K.X, K.PS, K.P
    P.dma("pool", K.WIN[:], win.rearrange("(kc p) n -> p kc n", p=128), w=["WIN"])
    if load_consts:
        P.dma("sp", K.BLK[:], blk, w=["BLK"])
        P.dma("sp", K.PERM[:], perm, w=["PERM"])
        P.dma("sp", K.CS64[:], cs64, w=["CS64"])
    P.dma("sp", K.QKG[:], qkg, w=["QKG"])
    K.memset("pool", K.VTC[0][:], 1.0, ["VTC0"])
    K.memset("pool", K.VTC[1][:], 1.0, ["VTC1"])
    def nrm(tc_, part):
        W_ = TCS[tc_][1]
        hcb = K.HC[tc_ % 2]
        hkb = "HC%d" % (tc_ % 2)
        if part == 0:
            norm_sq(K, tc_)
        else:
            norm_rest(K, L, 0, tc_, lambda c, hcb=hcb, W_=W_: hcb[:, c, :W_], lambda c, hkb=hkb: [hkb])

    nrm(0, 0)
    nrm(0, 1)
    for tc in range(5):
        t0, W = TCS[tc]
        hb = tc % 2
        hc = K.HC[hb]
        hk = "HC%d" % hb
        if tc < 4:
            P.dma("sp", K.ROPE[0][:], rope[:, :, t0:t0 + W], w=["ROPE0"])
        def stA(fc):
            pb = fc % 2
            b0 = fc % 2
            for kc in range(8):
                K.mm(PS[:, b0, :W], K.WIN[:, kc, fc * 128:(fc + 1) * 128], hc[:, kc, :W],
                     kc == 0, kc == 7, ["WIN", hk], ["ps%d" % b0])
            K.act(K.PSB[pb][:, :W], PS[:, b0, :W], AF.Identity, ["ps%d" % b0], ["PSB%d" % pb])
            K.tt("pool", K.SQ1[pb][:, :W], K.PSB[pb][:, :W], K.PSB[pb][:, :W], ALU.mult,
                 ["PSB%d" % pb], ["SQ1%d" % pb])

        def stB(fc):
            pb = fc % 2
            b1 = 2 + fc % 2
            K.mm(PS[:, b1, :W], K.BLK[:], K.SQ1[pb][:, :W], True, True, ["BLK", "SQ1%d" % pb], ["ps%d" % b1])
            K.act(K.LN2[pb][:, :W], PS[:, b1, :W], AF.Ln, ["ps%d" % b1], ["LN20"],
                  bias=K.EPSB[:], scale=1.0 / 64)
            K.act(K.RS2[pb][:, :W], K.LN2[pb][:, :W], AF.Exp, ["LN20"], ["RS20"], scale=-0.5)
            K.stt(K.QN[pb][:, :W], K.PSB[pb][:, :W], K.QKG[:, fc:fc + 1], K.RS2[pb][:, :W],
                  ALU.mult, ALU.mult, ["PSB%d" % pb, "QKG", "RS20"], ["QN%d" % pb])

        def stC(fc):
            pb = fc % 2
            b2 = 4 + fc % 2
            if fc < 6:
                dest, dk = K.QT[:, fc, t0:t0 + W], "QT%d.%d" % (tc, fc)
            else:
                dest, dk = K.KTL[:, fc - 6, t0:t0 + W], "KTL%d" % (fc - 6)
            if tc < 4:
                rp = K.ROPE[0]
                rk = "ROPE0"
                K.mm(PS[:, b2, :W], K.PERM[:], K.QN[pb][:, :W], True, True, ["PERM", "QN%d" % pb], ["ps%d" % b2])
                K.tt("dve", K.T1[pb][:, :W], K.QN[pb][:, :W], rp[:, 0, :W], ALU.mult,
                     ["QN%d" % pb, rk], ["T10"])
                K.tt("dve", K.T2[pb][:, :W], PS[:, b2, :W], rp[:, 1, :W], ALU.mult,
                     ["ps%d" % b2, rk], ["T20"])
                K.tt("pool", dest, K.T1[pb][:, :W], K.T2[pb][:, :W], ALU.add,
                     ["T10", "T20"], [dk])
            else:
                K.cp("pool", dest, K.QN[pb][:, :W], ["QN%d" % pb], [dk])

        stA(0)
        if tc + 1 < 5:
            nrm(tc + 1, 0)
        for fc in range(8):
            if fc + 1 < 8:
                stA(fc + 1)
            stB(fc)
            if fc >= 1:
                stC(fc - 1)
            if fc == 1 and tc + 1 < 5:
                nrm(tc + 1, 1)
        stC(7)
        rb = [6, 4, 5]
        ri = [0]

        def nb():
            b = rb[ri[0] % 3]
            ri[0] += 1
            return b

        for tt_ in range(W // 128):
            gt = t0 // 128 + tt_
            b = nb()
            for kc in range(8):
                K.mm(PS[:, b, 0:256], hc[:, kc, tt_ * 128:(tt_ + 1) * 128], K.WIN[:, kc, 1024:1280],
                     kc == 0, kc == 7, ["WIN", hk], ["ps%d" % b])
            K.act(K.VTC[tc % 2][:, tt_, :, :, 0:64], PS[:, b, 0:256].rearrange("p (a s e) -> p a s e", a=2, s=2),
                  AF.Identity, ["ps%d" % b], ["VTC%d" % (tc % 2)])
        for half in range(2):
            b = nb()
            for kc in range(8):
                K.mm(PS[:, b, :W], K.WIN[:, kc, 1280 + half * 128:1280 + (half + 1) * 128], hc[:, kc, :W],
                     kc == 0, kc == 7, ["WIN", hk], ["ps%d" % b])
            K.act(K.FTC[:, half, :W], PS[:, b, :W], AF.Identity, ["ps%d" % b], ["FTC%d" % half])
        for tt_ in range(W // 128):
            gt = t0 // 128 + tt_
            for half in range(2):
                b = nb()
                K.mm(PS[:, b, 0:256], K.FTC[:, half, tt_ * 128:(tt_ + 1) * 128], K.CS64[:],
                     True, True, ["FTC%d" % half, "CS64"], ["ps%d" % b])
                K.cp("dve", K.ABC[tc % 2][:, tt_, half * 256:(half + 1) * 256], PS[:, b, 0:256],
                     ["ps%d" % b], ["ABC%d" % (tc % 2)])
        nt_ = W // 128
        g0 = t0 // 128
        for a_ in range(2):
            P.dma("sp", v_o[a_][:, g0:g0 + nt_], K.VTC[tc % 2][:, :nt_, a_], r=["VTC%d" % (tc % 2)],
                  w=["v_o%d.%d" % (tc, a_)], final=final)
        for t_ in range(0, nt_, 2):
            ch, off = (g0 + t_) // 6, (g0 + t_) % 6
            P.dma("sp", ab_o[ch][:, off:off + 2, :], K.ABC[tc % 2][:, t_:t_ + 2, :], r=["ABC%d" % (tc % 2)],
                  w=["ab_o%d.%d" % (tc, t_)], final=final)
        if after_tc is not None:
            after_tc(tc)
    for k_ in range(2):
        P.dma("sp", kt_o[k_], K.KTL[:, k_, :], r=["KTL%d" % k_], w=["kt_o%d" % k_], final=final)


def fm_vec(v):
    v = np.asarray(v, np.float32)
    return np.ascontiguousarray(v.reshape(-1, 128).T)


def fm_tokens(a):
    T = a.shape[0]
    return np.ascontiguousarray(a.reshape(T, 8, 128).transpose(2, 1, 0))


_CONST = {}


def consts():
    if _CONST:
        return _CONST
    blk = np.zeros((128, 128), np.float32)
    blk[:64, :64] = 1.0
    blk[64:, 64:] = 1.0
    perm = np.zeros((128, 128), np.float32)
    for j in range(64):
        perm[2 * j + 1, 2 * j] = -1.0
        perm[2 * j, 2 * j + 1] = 1.0
    n = np.arange(64)
    ang = 2 * np.pi * np.outer(n, n) / 64.0
    c64, s64 = np.cos(ang), np.sin(ang)
    cs = np.zeros((128, 256), np.float64)
    for g in range(2):
        cs[g * 64:(g + 1) * 64, g * 64:(g + 1) * 64] = c64
        cs[g * 64:(g + 1) * 64, 128 + g * 64:128 + (g + 1) * 64] = s64
    _CONST["blk"] = _bf(blk)
    _CONST["perm"] = perm
    _CONST["cs64"] = _bf(cs)
    freqs = 10000.0 ** (-np.arange(16, dtype=np.float32) / 16)
    ropes = []
    for r in range(4):
        t = np.arange(r * NLAT, (r + 1) * NLAT)
        row = (t // 64).astype(np.float32)
        col = (t % 64).astype(np.float32)
        ang = np.concatenate([row[:, None] * freqs, col[:, None] * freqs], axis=-1).astype(np.float32)
        cos, sin = np.cos(ang), np.sin(ang)
        tab = np.zeros((128, 2, NLAT), np.float32)
        for p in range(128):
            jj = (p % 64) // 2
            tab[p, 0] = cos[:, jj]
            tab[p, 1] = sin[:, jj]
        ropes.append(tab)
    _CONST["rope"] = ropes
    tabs = []
    l = np.arange(8192, dtype=np.int64)
    sc = 1.0 / math.sqrt(8192 * 64)
    for r in range(4):
        k = np.arange(r * NLAT, (r + 1) * NLAT, dtype=np.int64)
        m = (l[:, None] * k[None, :]) % 8192
        a = 2 * np.pi * m / 8192.0
        tab = np.stack([np.cos(a) * sc, -np.sin(a) * sc], axis=1)
        tabs.append(_bf(tab.reshape(64, 128, 2, NLAT)))
    _CONST["dft"] = tabs
    l2 = np.arange(256, dtype=np.int64)
    m = (l2[:, None] * l2[None, :]) % 256
    a = 2 * np.pi * m / 256.0
    sc2 = 1.0 / math.sqrt(256 * 64)
    _CONST["dftc"] = _bf(np.stack([np.cos(a) * sc2, -np.sin(a) * sc2], axis=1).reshape(2, 128, 2, 256))
    return _CONST


def perm_win(w):
    cols = []
    for a in range(6):
        cols += list(range(a * 64, a * 64 + 64)) + list(range((a + 6) * 64, (a + 6) * 64 + 64))
    for kv in (0, 2, 1, 3):
        cols += list(range(768 + kv * 64, 768 + kv * 64 + 64))
    for kv in (0, 2, 1, 3):
        cols += list(range(1024 + kv * 64, 1024 + kv * 64 + 64))
    cols += list(range(1280, 1536))
    return np.ascontiguousarray(w[:, cols])


def build_p0():
    nc = bass.Bass("TRN2", target_bir_lowering=False)
    with ExitStack() as es:
        K = Ctx(nc, es)
        xin = K.dram_in("x_in", [128, 8, NT], F32)
        win = K.dram_in("win", [D, 1536], F32)
        rope = K.dram_in("rope", [128, 2, NLAT], F32)
        qkg = K.dram_in("qkg", [128, 8], F32)
        blk = K.dram_in("blk", [128, 128], BF16)
        perm = K.dram_in("perm", [128, 128], F32)
        cs64 = K.dram_in("cs64", [128, 256], BF16)
        kt_o = K.dram_out("kt_o", [128, 2, NT], BF16)
        v_o = K.dram_out("v_o", [128, 18, 2, 2, 65], BF16)
        ab_o = K.dram_out("ab_o", [128, 18, 512], BF16)
        qt_o = K.dram_out("qt_o", [128, 6, NT], BF16)
        alloc_common(K)
        alloc_eps(K)
        alloc_norm(K)
        alloc_epre(K)
        load_mod(K, 0)
        for c in range(8):
            K.P.dma("sp", K.X[:, c, :], xin[:, c, :], w=["X%d.%d" % (tc, c) for tc in range(5)])
        epre_segment(K, 0, win, rope, qkg, blk, perm, cs64, kt_o, v_o, ab_o)
        K.P.dma("sp", qt_o, K.QT[:], r=["QT%d.%d" % (tc, fc) for tc in range(5) for fc in range(6)],
                w=["qt_o"], final=True)
        K.P.emit()
    return nc


def alloc_epost(K, with_qt=True):
    if with_qt:
        K.QT = K.sb("QT", [128, 6, NT], BF16)
    K.KT = K.sb("KT", [128, 8448], BF16)
    K.V = K.sb("V", [128, 66, 2, 65], BF16)
    K.WO = K.sb("WO", [64, 12, 1024], BF16)
    K.WOF = K.sb("WOF", [128, 2, 1024], BF16)
    K.CAT = K.sb("CAT", [64, 6, 512], BF16)
    K.PB = [K.sb("PB%d" % i, [128, 2, 512], BF16) for i in range(3)]
    K.FT = K.sb("FT", [128, 2, NT], BF16)
    K.ABG = [K.sb("ABG%d" % i, [128, 2, 512], BF16) for i in range(2)]
    K.TABC = K.sb("TABC", [128, 2, 2, 256], BF16)
    K.ABGC = K.sb("ABGC", [128, 2, 512], BF16)
    K.OSB = K.sb("OSB", [65, 2, 512], F32)
    K.RDL = K.sb("RDL", [65, 2, 512], F32)
    K.RD = K.sb("RD", [65, 2, 512], BF16)
    K.ONESR = K.sb("ONESR", [128, 64], BF16)


def epost_segment(K, L, kt_all, v_all, ab_all, dft, dftc, wout, gk=()):
    X, PS, P = K.X, K.PS, K.P
    MOD = K.MOD[L]
    mk = "MOD%d" % L
    K.memset("pool", K.ONESR[:], 1.0, ["ONESR"])
    P.dma("pool", K.WO[:], wout[0:768, :].rearrange("(h d) n -> d h n", d=64), w=["WO"])
    P.dma("pool", K.WOF[:], wout[768:1024, :].rearrange("(c p) n -> p c n", p=128), w=["WOF"])
    TAB = [K.KT[:, 0:8192].rearrange("p (a b c) -> p a b c", a=2, b=2),
           K.V[:].rearrange("p a b c -> p (a b c)")[:, 0:8192].rearrange("p (a b c) -> p a b c", a=2, b=2)]
    TK = ["KTa", "Va"]
    for g in range(32):
        r = g // 8
        tl = (2 * g) % 16
        tb = g % 2
        P.dma("sp", TAB[tb], dft[2 * g:2 * g + 2].rearrange("l p s k -> p l s k"), w=[TK[tb]])
        P.dma("sp", K.ABG[tb][:], ab_all[tl // 6][r, :, tl % 6:tl % 6 + 2, :], r=gk, w=["ABG%d" % tb])
        for li in range(2):
            for half in range(2):
                for s in range(2):
                    for kc in range(4):
                        bank = half * 4 + kc
                        K.mm(PS[:, bank, :], K.ABG[tb][:, li, half * 256 + s * 128:half * 256 + (s + 1) * 128],
                             TAB[tb][:, li, s, kc * 512:(kc + 1) * 512],
                             g == 0 and li == 0 and s == 0, g == 31 and li == 1 and s == 1,
                             ["ABG%d" % tb, TK[tb]], ["ps%d" % bank])
    for half in range(2):
        for kc in range(4):
            bank = half * 4 + kc
            if bank % 2 == 0:
                K.act(K.FT[:, half, kc * 512:(kc + 1) * 512], PS[:, bank, :], AF.Identity, ["ps%d" % bank], ["FT%d" % kc])
            else:
                K.cp("dve", K.FT[:, half, kc * 512:(kc + 1) * 512], PS[:, bank, :], ["ps%d" % bank], ["FT%d" % kc])
    P.dma("sp", K.TABC[:], dftc.rearrange("l p s k -> p l s k"), w=["TABC"])
    P.dma("sp", K.ABGC[:], ab_all[2][0, :, 4:6, :], r=gk, w=["ABGC"])
    for half in range(2):
        for li in range(2):
            for s in range(2):
                K.mm(PS[:, half, 0:256], K.ABGC[:, li, half * 256 + s * 128:half * 256 + (s + 1) * 128],
                     K.TABC[:, li, s, :], li == 0 and s == 0, li == 1 and s == 1, ["ABGC", "TABC"], ["ps%d" % half])
        K.act(K.FT[:, half, 2048:2304], PS[:, half, 0:256], AF.Identity, ["ps%d" % half], ["FT4"])
    P.barrier()
    jobs = [(kp, tc, a3) for kp in range(2) for tc in range(5) for a3 in range(3)]

    def load_kv(kp):
        for r in range(4):
            P.dma("sp", K.KT[:, r * 2048:(r + 1) * 2048], kt_all[kp][r, :, 0:2048], r=["kt_gat%d" % kp], w=["KT%d" % r])
            P.dma("sp", K.V[:, r * 16:(r + 1) * 16, :, :], v_all[kp][r, :, 0:16, :, :], r=["v_gat%d" % kp], w=["V%d" % r])
        P.dma("sp", K.KT[:, 8192:8448], kt_all[kp][0, :, 2048:2304], r=["kt_gat%d" % kp], w=["KT4"])
        P.dma("sp", K.V[:, 64:66, :, :], v_all[kp][0, :, 16:18, :, :], r=["v_gat%d" % kp], w=["V4"])

    def kts_of(tc):
        return list(range(66)) if tc < 4 else [64, 65]

    def S(job, i, g0):
        kp, tc, a3 = job
        t0, W = TCS[tc]
        a = 3 * kp + a3
        qk = "QT%d.%d" % (tc, a)
        kt = kts_of(tc)[i]
        sb = (g0 + i) % 3
        kk = "KT%d" % min(kt // 16, 4)
        K.mm(PS[:, 2 * sb, :W], K.KT[0:64, kt * 128:(kt + 1) * 128], K.QT[0:64, a, t0:t0 + W],
             True, True, [kk, qk], ["ps%d" % (2 * sb)])
        K.mm(PS[:, 2 * sb + 1, :W], K.KT[64:128, kt * 128:(kt + 1) * 128], K.QT[64:128, a, t0:t0 + W],
             True, True, [kk, qk], ["ps%d" % (2 * sb + 1)])

    def prologue(job, g0):
        n = len(kts_of(job[1]))
        S(job, 0, g0)
        if n > 1:
            S(job, 1, g0)

    def body(job, g0, pending=None):
        kp, tc, a3 = job
        t0, W = TCS[tc]
        kts = kts_of(tc)
        n = len(kts)
        had_pending = pending is not None
        for i in range(n):
            if i == 2 and pending is not None:
                pending()
                pending = None
                S(job, 2, g0)
                if n > 3:
                    S(job, 3, g0)
            sb = (g0 + i) % 3
            kt = kts[i]
            vk = "V%d" % min(kt // 16, 4)
            if i + 2 < n and (not had_pending or i >= 2):
                S(job, i + 2, g0)
            K.act(K.PB[sb][:, :, :W], PS[:, 2 * sb:2 * sb + 2, :W], AF.Exp,
                  ["ps%d" % (2 * sb), "ps%d" % (2 * sb + 1)], ["PB%d" % sb], scale=0.125)
            for s_ in range(2):
                K.mm(PS[0:65, 6 + s_, :W], K.V[:, kt, s_, :], K.PB[sb][:, s_, :W],
                     i == 0, i == n - 1, [vk, "PB%d" % sb], ["ps%d" % (6 + s_)])
        if pending is not None:
            pending()
        K.cp("dve", K.OSB[0:65, :, :W], PS[0:65, 6:8, :W], ["ps6", "ps7"], ["OSB"])

    def finish(job, g0):
        kp, tc, a3 = job
        t0, W = TCS[tc]
        n = len(kts_of(tc))
        bs = (g0 + n) % 3
        v = 1 if tc == 4 else 0
        K.act(K.RDL[64:65, :, :W], K.OSB[64:65, :, :W], AF.Ln, ["OSB"], ["RDL"])
        K.act(K.RD[64:65, :, :W], K.RDL[64:65, :, :W], AF.Exp, ["RDL"], ["RD"], scale=-1.0)
        for s_ in range(2):
            K.mm(PS[0:64, 2 * bs + s_, :W], K.ONESR[64:65, 0:64], K.RD[64:65, s_, :W], True, True,
                 ["ONESR", "RD"], ["ps%d" % (2 * bs + s_)])
        K.tt("dve", K.CAT[0:64, 2 * a3:2 * a3 + 2, :W], K.OSB[0:64, :, :W], PS[0:64, 2 * bs:2 * bs + 2, :W],
             ALU.mult, ["OSB", "ps%d" % (2 * bs), "ps%d" % (2 * bs + 1)], ["CAT%d" % (2 * a3), "CAT%d" % (2 * a3 + 1)])
        if a3 == 2:
            for d in range(8):
                bank = 2 * bs + d % 2
                for slot in range(6):
                    head = 3 * kp + slot // 2 + 6 * (slot % 2)
                    K.mm(PS[:, bank, :W], K.WO[0:64, head, d * 128:(d + 1) * 128], K.CAT[0:64, slot, :W],
                         slot == 0, slot == 5 and kp == 1, ["WO", "CAT%d" % slot], ["ps%d" % bank])
                if kp == 0:
                    for half in range(2):
                        K.mm(PS[:, bank, :W], K.WOF[:, half, d * 128:(d + 1) * 128], K.FT[:, half, t0:t0 + W],
                             False, half == 1, ["WOF", "FT%d" % tc], ["ps%d" % bank])
                K.stt(X[:, d, t0:t0 + W], PS[:, bank, :W], MOD[:, 16 + d, v:v + 1], X[:, d, t0:t0 + W],
                      ALU.mult, ALU.add, ["ps%d" % bank, mk, "X%d.%d" % (tc, d)], ["X%d.%d" % (tc, d)])

    g0 = 0
    load_kv(0)
    prologue(jobs[0], g0)
    pending = None
    for ji, job in enumerate(jobs):
        n = len(kts_of(job[1]))
        body(job, g0, pending)
        pending = (lambda job=job, g0=g0: finish(job, g0))
        g0n = g0 + n + 1
        if ji + 1 < len(jobs):
            nj = jobs[ji + 1]
            if nj[0] != job[0]:
                pending()
                pending = None
                load_kv(nj[0])
            prologue(nj, g0n)
        g0 = g0n
    if pending is not None:
        pending()


def load_x(K, xin):
    for c in range(8):
        K.P.dma("sp", K.X[:, c, :], xin[:, c, :], w=["X%d.%d" % (tc, c) for tc in range(5)])


def store_x(K, xo, n=NT):
    for c in range(8):
        K.P.dma("sp", xo[:, c, :], K.X[:, c, 0:n], r=["X%d.%d" % (tc, c) for tc in range(5)],
                w=["xo%d" % c], final=True)


UW = NLAT + 30 + NCTX + 30


def ucol(tc):
    return 15 + TCS[tc][0] if tc < 4 else NLAT + 30 + 15


def alloc_opre(K, with_u=True):
    K.WPW1 = K.sb("WPW1", [128, 8, 2048], BF16)
    K.HC = [K.sb("HC%d" % i, [128, 8, 512], BF16) for i in range(2)]
    if with_u:
        K.U = K.sb("U", [128, 8, UW], BF16)
    K.BPW1 = K.sb("BPW1", [128, 16], F32)
    K.NEGB = K.sb("NEGB", [128, 16], F32)
    K.EG = [K.sb("EG%d" % i, [128, 512], F32) for i in range(2)]
    K.SG = [K.sb("SG%d" % i, [128, 512], F32) for i in range(2)]


def opre_segment(K, L, wpw1, bpw1, ntc=5):
    X, PS, P = K.X, K.PS, K.P
    P.dma("pool", K.WPW1[:], wpw1.rearrange("(kc p) n -> p kc n", p=128), w=["WPW1"])
    P.dma("sp", K.BPW1[:], bpw1, w=["BPW1"])
    K.ts("dve", K.NEGB[:], K.BPW1[:], -1.0, None, ALU.mult, None, ["BPW1"], ["NEGB"])
    K.memset("pool", K.U[:], 0.0, ["U%d" % tc for tc in range(5)])
    def nrm(tc, part):
        W = TCS[tc][1]
        hcb = K.HC[tc % 2]
        hkb = "HC%d" % (tc % 2)
        if part == 0:
            norm_sq(K, tc)
        else:
            norm_rest(K, L, 0, tc, lambda c, hcb=hcb, W=W: hcb[:, c, :W], lambda c, hkb=hkb: [hkb])

    nrm(0, 0)
    nrm(0, 1)
    for tc in range(ntc):
        t0, W = TCS[tc]
        hc = K.HC[tc % 2]
        hk = "HC%d" % (tc % 2)
        u0 = ucol(tc)
        for c in range(8):
            if tc + 1 < ntc and c == 0:
                nrm(tc + 1, 0)
            if tc + 1 < ntc and c == 2:
                nrm(tc + 1, 1)
            pb = c % 2
            bv, bg = c % 2, 2 + c % 2
            for kc in range(8):
                K.mm(PS[:, bv, :W], K.WPW1[:, kc, c * 128:(c + 1) * 128], hc[:, kc, :W],
                     kc == 0, kc == 7, ["WPW1", hk], ["ps%d" % bv])
            for kc in range(8):
                K.mm(PS[:, bg, :W], K.WPW1[:, kc, 1024 + c * 128:1024 + (c + 1) * 128], hc[:, kc, :W],
                     kc == 0, kc == 7, ["WPW1", hk], ["ps%d" % bg])
            K.act(K.EG[pb][:, :W], PS[:, bg, :W], AF.Exp, ["ps%d" % bg, "NEGB"], ["EG%d" % pb],
                  bias=K.NEGB[:, 8 + c:9 + c], scale=-1.0)
            K.act(K.EG[pb][:, :W], K.EG[pb][:, :W], AF.Ln, ["EG%d" % pb, "ONEB"], ["EG%d" % pb], bias=K.ONEB[:], scale=1.0)
            K.act(K.SG[pb][:, :W], K.EG[pb][:, :W], AF.Exp, ["EG%d" % pb], ["SG%d" % pb], scale=-1.0)
            K.stt(K.U[:, c, u0:u0 + W], PS[:, bv, :W], K.BPW1[:, c:c + 1], K.SG[pb][:, :W],
                  ALU.add, ALU.mult, ["ps%d" % bv, "BPW1", "SG%d" % pb], ["U%d" % tc])


def alloc_opost(K, with_u=True):
    if with_u:
        K.U = K.sb("U", [128, 8, UW], BF16)
    K.WPW2 = K.sb("WPW2", [128, 8, 1024], BF16)
    K.ACC = K.sb("ACC", [128, 8, 512], F32)
    K.SQC = K.sb("SQC", [128, 8, 512], BF16)
    K.Z = K.sb("Z", [128, 8, 512], BF16)
    K.COB = K.Z
    K.DIAG = [K.sb("DIAG%d" % i, [128, 31, 128], BF16) for i in range(2)]
    if not hasattr(K, "IDENT"):
        K.IDENT = K.sb("IDENT", [128, 128], BF16)
    K.WDW = K.sb("WDW", [128, 8, 31], F32)
    K.PV5 = K.sb("PV5", [128, 5, 8], F32)
    K.GB = K.sb("GB", [128, 8, 2], F32)
    K.MEAN = K.sb("MEAN", [128, 512], F32)
    K.VAR = K.sb("VAR", [128, 512], F32)
    K.LNV2 = K.sb("LNV2", [128, 512], F32)
    K.RSTD2 = K.sb("RSTD2", [128, 512], F32)
    K.TA = [K.sb("TA0", [128, 512], F32)] * 2
    K.TB = [K.sb("TB%d" % i, [128, 512], F32) for i in range(2)]
    K.TE = [K.sb("TE%d" % i, [128, 512], F32) for i in range(2)]
    K.T3 = [K.sb("T3%d" % i, [128, 512], F32) for i in range(2)]


def opost_segment(K, L, wdw, pv4, wpw2, ntc=5, ukeys=False, ident=None):
    X, PS, P = K.X, K.PS, K.P
    if ident is not None:
        P.dma("sp", K.IDENT[:], ident, w=["IDENT"])
    MOD = K.MOD[L]
    mk = "MOD%d" % L
    P.dma("pool", K.WPW2[:], wpw2.rearrange("(kc p) n -> p kc n", p=128), w=["WPW2"])
    P.dma("sp", K.WDW[:], wdw, w=["WDW"])
    P.dma("sp", K.PV5[:, 0:4, :], pv4, w=["PV5"])
    K.ts("dve", K.PV5[:, 4, :], K.PV5[:, 2, :], -1.0, None, ALU.mult, None, ["PV5"], ["PV5n"])
    for v in range(2):
        K.tt("dve", K.GB[:, :, v], MOD[:, 16:24, v], K.PV5[:, 3, :], ALU.mult, [mk, "PV5"], ["GB"])
    for tc in range(ntc):
        t0, W = TCS[tc]
        v = 1 if tc == 4 else 0
        s0 = ucol(tc) - 15
        for c in range(8):
            db = c % 2
            dk = "DIAG%d" % db
            idv = K.IDENT[:]
            wv = K.WDW[:, c, :]
            in0 = bass.AP(idv.tensor, idv.offset, [list(idv.ap[0]), [0, 31], [1, 128]])
            in1 = bass.AP(wv.tensor, wv.offset, [list(wv.ap[0]), [1, 31], [0, 128]])
            K.tt("dve", K.DIAG[db][:], in0, in1, ALU.mult, ["IDENT", "WDW"], [dk])
            bank = 2 + c % 2
            for j in range(31):
                K.mm(PS[:, bank, :W], K.DIAG[db][:, j, :], K.U[:, c, s0 + j:s0 + j + W], j == 0, j == 30,
                     [dk, "U"], ["ps%d" % bank])
            K.act(K.ACC[:, c, :W], PS[:, bank, :W], AF.Identity, ["ps%d" % bank, "PV5"], ["ACC%d" % c],
                  bias=K.PV5[:, 0, c:c + 1], scale=1.0)
        for c in range(8):
            K.cp("dve", K.COB[:, c, :W], K.ACC[:, c, :W], ["ACC%d" % c], ["Z%d" % c])
            K.tt("pool", K.SQC[:, c, :W], K.ACC[:, c, :W], K.ACC[:, c, :W], ALU.mult, ["ACC%d" % c], ["SQC%d" % c])
        for c in range(8):
            K.mm(PS[:, 6, :W], K.ONES[:], K.COB[:, c, :W], c == 0, c == 7, ["ONES", "Z%d" % c], ["ps6"])
        for c in range(8):
            K.mm(PS[:, 7, :W], K.ONES[:], K.SQC[:, c, :W], c == 0, c == 7, ["ONES", "SQC%d" % c], ["ps7"])
        K.act(K.MEAN[:, :W], PS[:, 6, :W], AF.Identity, ["ps6"], ["MEAN"], scale=1.0 / D)
        K.tt("pool", K.VAR[:, :W], K.MEAN[:, :W], K.MEAN[:, :W], ALU.mult, ["MEAN"], ["VAR"])
        K.stt(K.VAR[:, :W], PS[:, 7, :W], 1.0 / D, K.VAR[:, :W], ALU.mult, ALU.subtract, ["ps7", "VAR"], ["VAR"])
        K.act(K.LNV2[:, :W], K.VAR[:, :W], AF.Ln, ["VAR"], ["LNV2"], bias=K.EPSB[:], scale=1.0)
        K.act(K.RSTD2[:, :W], K.LNV2[:, :W], AF.Exp, ["LNV2"], ["RSTD2"], scale=-0.5)
        for c in range(8):
            pb = c % 2
            K.tt("dve", K.TA[pb][:, :W], K.ACC[:, c, :W], K.MEAN[:, :W], ALU.subtract,
                 ["ACC%d" % c, "MEAN"], ["TA0"])
            K.stt(K.TB[pb][:, :W], K.TA[pb][:, :W], K.PV5[:, 1, c:c + 1], K.RSTD2[:, :W], ALU.mult, ALU.mult,
                  ["TA0", "PV5", "RSTD2"], ["TB%d" % pb])
            K.act(K.TE[pb][:, :W], K.TB[pb][:, :W], AF.Exp, ["TB%d" % pb, "PV5n"], ["TE%d" % pb],
                  bias=K.PV5[:, 4, c:c + 1], scale=-1.0)
            K.act(K.TE[pb][:, :W], K.TE[pb][:, :W], AF.Ln, ["TE%d" % pb, "ONEB"], ["TE%d" % pb], bias=K.ONEB[:], scale=1.0)
            K.act(K.TE[pb][:, :W], K.TE[pb][:, :W], AF.Exp, ["TE%d" % pb], ["TE%d" % pb], scale=-1.0)
            K.stt(K.Z[:, c, :W], K.TB[pb][:, :W], K.PV5[:, 2, c:c + 1], K.TE[pb][:, :W], ALU.add, ALU.mult,
                  ["TB%d" % pb, "PV5", "TE%d" % pb], ["Z%d" % c])
        for d in range(8):
            bank = d % 2
            for c in range(8):
                K.mm(PS[:, bank, :W], K.WPW2[:, c, d * 128:(d + 1) * 128], K.Z[:, c, :W], c == 0, c == 7,
                     ["WPW2", "Z%d" % c], ["ps%d" % bank])
            K.act(K.T3[bank][:, :W], PS[:, bank, :W], AF.Identity, ["ps%d" % bank, mk, "GB"], ["T3%d" % bank],
                  bias=K.GB[:, d, v:v + 1], scale=MOD[:, 16 + d, v:v + 1])
            K.tt("dve", X[:, d, t0:t0 + W], X[:, d, t0:t0 + W], K.T3[bank][:, :W], ALU.add,
                 ["X%d.%d" % (tc, d), "T3%d" % bank], ["X%d.%d" % (tc, d)])


def _epre_io(K):
    win = K.dram_in("win", [D, 1536], F32)
    rope = K.dram_in("rope", [128, 2, NLAT], F32)
    qkg = K.dram_in("qkg", [128, 8], F32)
    blk = K.dram_in("blk", [128, 128], BF16)
    perm = K.dram_in("perm", [128, 128], F32)
    cs64 = K.dram_in("cs64", [128, 256], BF16)
    kt_o = K.dram_out("kt_o", [128, 2, NT], BF16)
    v_o = K.dram_out("v_o", [128, 18, 2, 2, 65], BF16)
    ab_o = K.dram_out("ab_o", [128, 18, 512], BF16)
    qt_o = K.dram_out("qt_o", [128, 6, NT], BF16)
    return win, rope, qkg, blk, perm, cs64, kt_o, v_o, ab_o, qt_o


def _epre_run(K, L, io):
    win, rope, qkg, blk, perm, cs64, kt_o, v_o, ab_o, qt_o = io
    epre_segment(K, L, win, rope, qkg, blk, perm, cs64, [kt_o[:, k_, :] for k_ in range(2)],
                 [v_o[:, :, a_, :, :] for a_ in range(2)], [ab_o[:, 6 * c_:6 * c_ + 6, :] for c_ in range(3)])
    K.P.dma("sp", qt_o, K.QT[:], r=["QT%d.%d" % (tc, fc) for tc in range(5) for fc in range(6)],
            w=["qt_o"], final=True)


def build_pA():
    nc = bass.Bass("TRN2", target_bir_lowering=False)
    with ExitStack() as es:
        K = Ctx(nc, es)
        xin = K.dram_in("x_in", [128, 8, NT], F32)
        io = _epre_io(K)
        alloc_common(K)
        alloc_eps(K)
        load_mod(K, 0, "modA")
        load_x(K, xin)
        alloc_norm(K)
        alloc_epre(K)
        _epre_run(K, 0, io)
        K.P.emit()
    return nc


def build_pB():
    nc = bass.Bass("TRN2", target_bir_lowering=False)
    with ExitStack() as es:
        K = Ctx(nc, es)
        xin = K.dram_in("x_in", [128, 8, NT], F32)
        qt_in = K.dram_in("qt_in", [128, 6, NT], BF16)
        kt_all = K.dram_in("kt_all", [4, 128, 2, NT], BF16)
        v_all = K.dram_in("v_all", [4, 128, 18, 2, 2, 65], BF16)
        ab_all = K.dram_in("ab_all", [4, 128, 18, 512], BF16)
        dft = K.dram_in("dft", [64, 128, 2, NLAT], BF16)
        dftc = K.dram_in("dftc", [2, 128, 2, 256], BF16)
        wout = K.dram_in("wout", [D, D], F32)
        w1 = K.dram_in("w1", [D, 4 * D], F32)
        w2 = K.dram_in("w2", [4 * D, D], F32)
        wpw1 = K.dram_in("wpw1", [D, 2 * D], F32)
        bpw1 = K.dram_in("bpw1", [128, 16], F32)
        xo = K.dram_out("x_o", [128, 8, NT], F32)
        uo = K.dram_out("u_o", [128, 8, UW], BF16)
        alloc_common(K)
        alloc_eps(K)
        load_mod(K, 0, "modA")
        load_mod(K, 1, "modB")
        load_x(K, xin)
        with ExitStack() as ph:
            K.es = ph
            alloc_epost(K)
            K.P.dma("sp", K.QT[:], qt_in, w=["QT%d.%d" % (tc, fc) for tc in range(5) for fc in range(6)])
            epost_segment(K, 0, [kt_all[:, :, k_, :] for k_ in range(2)],
                          [v_all[:, :, :, a_, :, :] for a_ in range(2)],
                          [ab_all[:, :, 6 * c_:6 * c_ + 6, :] for c_ in range(3)], dft, dftc, wout)
            K.P.barrier()
        with ExitStack() as ph:
            K.es = ph
            alloc_norm(K)
            alloc_mlp(K)
            mlp_segment(K, 0, w1, w2)
            K.P.barrier()
        with ExitStack() as ph:
            K.es = ph
            alloc_norm(K)
            alloc_opre(K)
            opre_segment(K, 1, wpw1, bpw1)
            K.P.dma("sp", uo, K.U[:], r=["U%d" % tc for tc in range(5)], w=["u_o"], final=True)
            K.P.barrier()
        K.es = es
        store_x(K, xo)
        K.P.emit()
    return nc


def build_pC(last):
    nc = bass.Bass("TRN2", target_bir_lowering=False)
    ntc = 4 if last else 5
    with ExitStack() as es:
        K = Ctx(nc, es)
        xin = K.dram_in("x_in", [128, 8, NT], F32)
        u_in = K.dram_in("u_in", [128, 8, UW], BF16)
        wdw = K.dram_in("wdw", [128, 8, 31], F32)
        pv4 = K.dram_in("pv4", [128, 4, 8], F32)
        wpw2 = K.dram_in("wpw2", [D, D], F32)
        ident = K.dram_in("ident", [128, 128], BF16)
        w1 = K.dram_in("w1", [D, 4 * D], F32)
        w2 = K.dram_in("w2", [4 * D, D], F32)
        if not last:
            io = _epre_io(K)
            xo = K.dram_out("x_o", [128, 8, NT], F32)
        else:
            xo = K.dram_out("x_o", [128, 8, NLAT], F32)
        alloc_common(K)
        alloc_eps(K)
        load_mod(K, 0, "modA")
        if not last:
            load_mod(K, 1, "modB")
        load_x(K, xin)
        with ExitStack() as ph:
            K.es = ph
            alloc_opost(K)
            K.P.dma("sp", K.U[:], u_in, w=["U"])
            opost_segment(K, 0, wdw, pv4, wpw2, ntc, ident=ident)
            K.P.barrier()
        with ExitStack() as ph:
            K.es = ph
            alloc_norm(K)
            alloc_mlp(K)
            mlp_segment(K, 0, w1, w2, ntc)
            K.P.barrier()
        if not last:
            with ExitStack() as ph:
                K.es = ph
                alloc_norm(K)
                alloc_epre(K)
                _epre_run(K, 1, io)
                K.P.barrier()
        K.es = es
        store_x(K, xo, NLAT if last else NT)
        K.P.emit()
    return nc


_PROGS = {}


def _prog(name):
    if name not in _PROGS:
        _PROGS[name] = {"mod": build_mod, "A": build_pA, "B": build_pB,
                        "C": lambda: build_pC(False), "D": lambda: build_pC(True)}[name]()
    return _PROGS[name]


def _run(name, in_maps):
    res = run_bass_kernel_spmd(_prog(name), in_maps, core_ids=list(range(NCORES)))
    return res.results


def kernel_multi(x, c, ctx, c_ctx, ada_w, ada_b, norm1_g, norm2_g, mlp_w1, mlp_w2,
           attn_w_in, q_norm_g, k_norm_g, attn_w_out,
           conv_w_pw1, conv_b_pw1, conv_w_dw, conv_b_dw, conv_ln_g, conv_ln_b,
           conv_w_pw2, conv_b_pw2):
    f32 = lambda a: np.ascontiguousarray(np.asarray(a, dtype=np.float32))
    x, c, ctx, c_ctx = f32(x), f32(c), f32(ctx), f32(c_ctx)
    ada_w, ada_b, norm1_g, norm2_g = f32(ada_w), f32(ada_b), f32(norm1_g), f32(norm2_g)
    mlp_w1, mlp_w2, attn_w_in, attn_w_out = f32(mlp_w1), f32(mlp_w2), f32(attn_w_in), f32(attn_w_out)
    conv_w_pw1, conv_w_pw2, conv_w_dw = f32(conv_w_pw1), f32(conv_w_pw2), f32(conv_w_dw)
    C = consts()
    cores = [(i // 4, i % 4) for i in range(NCORES)]
    maps = []
    for b, r in cores:
        cvec = np.ascontiguousarray(np.stack([fm_vec(c[b]), fm_vec(c_ctx)], axis=-1))
        adab = fm_vec(ada_b[r])
        adab = np.ascontiguousarray(np.repeat(adab[:, :, None], 2, axis=2))
        ng = np.ascontiguousarray(np.stack([fm_vec(norm1_g[r]), fm_vec(norm2_g[r])], axis=1))
        maps.append(dict(adaw=ada_w[r], adab=adab, cvec=cvec, ng=ng))
    rm = _run("mod", maps)
    mod = {(b, L): np.asarray(rm[b * 4 + L]["modo"]) for b in range(2) for L in range(4)}

    def qkg_of(j):
        g = np.zeros((128, 8), np.float32)
        for fc in range(8):
            src = np.asarray(q_norm_g[j] if fc < 6 else k_norm_g[j], np.float32)
            g[:64, fc] = src
            g[64:, fc] = src
        return g

    def epre_inputs(j, r):
        return dict(win=perm_win(attn_w_in[j]), rope=C["rope"][r], qkg=qkg_of(j), blk=C["blk"],
                    perm=C["perm"], cs64=C["cs64"])

    def gather(res, key, b):
        return np.ascontiguousarray(np.stack([np.asarray(res[b * 4 + rr][key]) for rr in range(4)], 0))

    def epost_inputs(res, j, b, r, i):
        return dict(qt_in=np.asarray(res[i]["qt_o"]), kt_all=gather(res, "kt_o", b), v_all=gather(res, "v_o", b),
                    ab_all=gather(res, "ab_o", b), dft=C["dft"][r], dftc=C["dftc"], wout=attn_w_out[j])

    def u_with_halo(res, b, r):
        u = np.array(np.asarray(res[b * 4 + r]["u_o"]))
        if r > 0:
            ul = np.asarray(res[b * 4 + r - 1]["u_o"])
            u[:, :, 0:15] = ul[:, :, NLAT:NLAT + 15]
        if r < 3:
            ur = np.asarray(res[b * 4 + r + 1]["u_o"])
            u[:, :, NLAT + 15:NLAT + 30] = ur[:, :, 15:30]
        return np.ascontiguousarray(u)

    def opost_inputs(j):
        wdw = np.ascontiguousarray(conv_w_dw[j].reshape(31, 8, 128).transpose(2, 1, 0))
        pv4 = np.ascontiguousarray(np.stack([fm_vec(conv_b_dw[j]), fm_vec(conv_ln_g[j]), fm_vec(conv_ln_b[j]),
                                             fm_vec(conv_b_pw2[j])], axis=1))
        return dict(wdw=wdw, pv4=pv4, wpw2=conv_w_pw2[j], ident=_bf(np.eye(128, dtype=np.float32)))

    maps = []
    for b, r in cores:
        xt = np.concatenate([x[b, r * NLAT:(r + 1) * NLAT], ctx[b]], axis=0)
        m = dict(x_in=fm_tokens(xt), modA=mod[(b, 0)])
        m.update(epre_inputs(0, r))
        maps.append(m)
    xcur = [m["x_in"] for m in maps]
    res = _run("A", maps)
    for L in (0, 2):
        j = L // 2
        maps = []
        for i, (b, r) in enumerate(cores):
            m = dict(x_in=xcur[i], modA=mod[(b, L)], modB=mod[(b, L + 1)], w1=mlp_w1[L], w2=mlp_w2[L],
                     wpw1=conv_w_pw1[j], bpw1=fm_vec(conv_b_pw1[j]))
            m.update(epost_inputs(res, j, b, r, i))
            maps.append(m)
        res = _run("B", maps)
        xcur = [np.asarray(res[i]["x_o"]) for i in range(NCORES)]
        last = L == 2
        maps = []
        for i, (b, r) in enumerate(cores):
            m = dict(x_in=xcur[i], u_in=u_with_halo(res, b, r), modA=mod[(b, L + 1)],
                     w1=mlp_w1[L + 1], w2=mlp_w2[L + 1])
            m.update(opost_inputs(j))
            if not last:
                m["modB"] = mod[(b, L + 2)]
                m.update(epre_inputs(j + 1, r))
            maps.append(m)
        res = _run("D" if last else "C", maps)
        if not last:
            xcur = [np.asarray(res[i]["x_o"]) for i in range(NCORES)]
    out = np.zeros((2, 4 * NLAT, D), np.float32)
    for i, (b, r) in enumerate(cores):
        xo = np.asarray(res[i]["x_o"])
        out[b, r * NLAT:(r + 1) * NLAT] = xo.transpose(2, 1, 0).reshape(NLAT, D)
    return out


GROUPS = [[0, 1, 2, 3], [4, 5, 6, 7]]


def mod_segment(K, adaw, adab, cvec, ng, m_loc, m_gat):
    nc, P, PS = K.nc, K.P, K.PS
    CV = K.sb("CV", [128, 8, 2], F32)
    ABs = K.sb("ABs", [128, 48, 2], F32)
    NG = K.sb("NG", [128, 2, 8], F32)
    E1 = K.sb("E1", [128, 8, 2], F32)
    S = K.sb("S", [128, 8, 2], BF16)
    MODL = K.sb("MODL", [128, 48, 2], F32)
    WM = [K.sb("WM%d" % i, [128, 8, 1024], BF16) for i in range(2)]
    P.dma("sp", CV[:], cvec, w=["CV"])
    P.dma("sp", ABs[:], adab, w=["ABs"])
    P.dma("sp", NG[:], ng, w=["NG"])
    K.act(E1[:], CV[:], AF.Exp, ["CV"], ["E1"], scale=-1.0)
    K.ts("dve", E1[:], E1[:], 1.0, None, ALU.add, None, ["E1"], ["E1"])
    K.recip(E1[:], E1[:], ["E1"], ["E1"])
    K.tt("dve", S[:], CV[:], E1[:], ALU.mult, ["CV", "E1"], ["S"])
    adaw_v = adaw.rearrange("(kc p) n -> p kc n", p=128)
    for j in range(6):
        wm = WM[j % 2]
        wk = "WM%d" % (j % 2)
        P.dma("pool", wm[:], adaw_v[:, :, j * 1024:(j + 1) * 1024], w=[wk])
        bank = j % 2
        for c in range(8):
            for kc in range(8):
                K.mm(PS[:, bank, c * 2:c * 2 + 2], wm[:, kc, c * 128:(c + 1) * 128], S[:, kc, :],
                     kc == 0, kc == 7, [wk, "S"], ["ps%d" % bank])
        K.tt("dve", MODL[:, j * 8:(j + 1) * 8, :],
             PS[:, bank, 0:16].rearrange("p (c v) -> p c v", v=2),
             ABs[:, j * 8:(j + 1) * 8, :], ALU.add, ["ps%d" % bank, "ABs"], ["MODL%d" % j])
    for j, gi in ((1, 0), (4, 1)):
        for v in range(2):
            K.stt(MODL[:, j * 8:(j + 1) * 8, v], MODL[:, j * 8:(j + 1) * 8, v], 1.0, NG[:, gi, :],
                  ALU.add, ALU.mult, ["MODL%d" % j, "NG"], ["MODL%d" % j])
    P.dma("sp", m_loc, MODL[:].rearrange("p a v -> p (a v)"), r=["MODL%d" % j for j in range(6)], w=["m_loc"])
    P.cc("AllGather", GROUPS, m_loc, m_gat, r=["m_loc"], w=["m_gat"])
    mg = m_gat.rearrange("(r p) (a v) -> r p a v", p=128, v=2)
    for L in range(4):
        P.dma("sp", K.MOD[L][:], mg[L], r=["m_gat"], w=["MOD%d" % L])


def halo_segment(K, e_loc, e_gat):
    P = K.P
    U = K.U
    E4 = K.sb("E4", [128, 4, 8, 2, 15], BF16)
    HAL = K.sb("HAL", [128, 2, 8, 15], F32)
    ED = K.sb("ED", [128, 8, 2, 15], BF16)
    ukeys = ["U%d" % tc for tc in range(5)]
    K.cp("pool", ED[:, :, 0, :], U[:, :, 15:30], ukeys, ["ED"])
    K.cp("pool", ED[:, :, 1, :], U[:, :, NLAT:NLAT + 15], ukeys, ["ED"])
    P.dma("sp", e_loc, ED[:].rearrange("p c s e -> p (c s e)"), r=["ED"], w=["e_loc"])
    P.cc("AllGather", GROUPS, e_loc, e_gat, r=["e_loc"], w=["e_gat"])
    P.dma("sp", E4[:], e_gat.rearrange("(r p) (c s e) -> p r c s e", p=128, c=8, s=2), r=["e_gat"], w=["E4"])
    for side in range(2):
        src_s = 1 - side
        for rr in range(4):
            mcol = K.HMASK[:, side * 4 + rr:side * 4 + rr + 1]
            if rr == 0:
                K.ts("dve", HAL[:, side], E4[:, rr, :, src_s, :], mcol, None, ALU.mult, None,
                     ["E4", "HMASK"], ["HAL%d" % side])
            else:
                K.stt(HAL[:, side], E4[:, rr, :, src_s, :], mcol, HAL[:, side], ALU.mult, ALU.add,
                      ["E4", "HMASK", "HAL%d" % side], ["HAL%d" % side])
    K.cp("dve", U[:, :, 0:15], HAL[:, 0], ["HAL0"], ["U0"])
    K.cp("dve", U[:, :, NLAT + 15:NLAT + 30], HAL[:, 1], ["HAL1"], ["U3"])


def build_fused():
    nc = bass.Bass("TRN2", target_bir_lowering=False)
    with ExitStack() as es:
        K = Ctx(nc, es)
        P = K.P
        di = K.dram_in
        xin = di("x_in", [128, 8, NT], F32)
        adaw = di("adaw", [D, 6 * D], F32)
        adab = di("adab", [128, 48, 2], F32)
        cvec = di("cvec", [128, 8, 2], F32)
        ng = di("ng", [128, 2, 8], F32)
        hmask = di("hmask", [128, 8], F32)
        rope = di("rope", [128, 2, NLAT], F32)
        blk = di("blk", [128, 128], BF16)
        perm = di("perm", [128, 128], F32)
        cs64 = di("cs64", [128, 256], BF16)
        ident = di("ident", [128, 128], BF16)
        dft = di("dft", [64, 128, 2, NLAT], BF16)
        dftc = di("dftc", [2, 128, 2, 256], BF16)
        w1 = [di("w1_%d" % L, [D, 4 * D], F32) for L in range(4)]
        w2 = [di("w2_%d" % L, [4 * D, D], F32) for L in range(4)]
        win = [di("win_%d" % j, [D, 1536], F32) for j in range(2)]
        qkg = [di("qkg_%d" % j, [128, 8], F32) for j in range(2)]
        wout = [di("wout_%d" % j, [D, D], F32) for j in range(2)]
        wpw1 = [di("wpw1_%d" % j, [D, 2 * D], F32) for j in range(2)]
        bpw1 = [di("bpw1_%d" % j, [128, 16], F32) for j in range(2)]
        wdw = [di("wdw_%d" % j, [128, 8, 31], F32) for j in range(2)]
        pv4 = [di("pv4_%d" % j, [128, 4, 8], F32) for j in range(2)]
        wpw2 = [di("wpw2_%d" % j, [D, D], F32) for j in range(2)]
        xo = K.dram_out("x_o", [128, 8, NLAT], F32)

        def internal(name, shape, dt):
            return nc.dram_tensor(name, list(shape), dt, kind="Internal").ap()

        m_loc = internal("m_loc", [128, 96], F32)
        m_gat = internal("m_gat", [512, 96], F32)
        alloc_common(K)
        alloc_eps(K)
        K.HMASK = K.sb("HMASK", [128, 8], F32)
        P.dma("sp", K.HMASK[:], hmask, w=["HMASK"])
        for L in range(4):
            K.MOD[L] = K.sb("MODL%d" % L, [128, 48, 2], F32)
        load_x(K, xin)
        with ExitStack() as ph:
            K.es = ph
            mod_segment(K, adaw, adab, cvec, ng, m_loc, m_gat)
            P.barrier()
        K.es = es
        K.BLK = K.sb("BLK", [128, 128], BF16)
        K.PERM = K.sb("PERM", [128, 128], F32)
        K.CS64 = K.sb("CS64", [128, 256], BF16)
        P.dma("sp", K.BLK[:], blk, w=["BLK"])
        P.dma("sp", K.PERM[:], perm, w=["PERM"])
        P.dma("sp", K.CS64[:], cs64, w=["CS64"])
        K.IDENT = K.sb("IDENT", [128, 128], BF16)
        P.dma("sp", K.IDENT[:], ident, w=["IDENT"])
        for L in range(4):
            j = L // 2
            ntc = 4 if L == 3 else 5
            if L % 2 == 0:
                kt_loc = [internal("kt_loc%d_%d" % (L, k_), [128, NT], BF16) for k_ in range(2)]
                kt_gat = [internal("kt_gat%d_%d" % (L, k_), [512, NT], BF16) for k_ in range(2)]
                v_loc = [internal("v_loc%d_%d" % (L, k_), [128, 18 * 130], BF16) for k_ in range(2)]
                v_gat = [internal("v_gat%d_%d" % (L, k_), [512, 18 * 130], BF16) for k_ in range(2)]
                ab_loc = [internal("ab_loc%d_%d" % (L, k_), [128, 6 * 512], BF16) for k_ in range(3)]
                ab_gat = [internal("ab_gat%d_%d" % (L, k_), [512, 6 * 512], BF16) for k_ in range(3)]
                with ExitStack() as ql:
                    K.es = ql
                    K.QT = K.sb("QT", [128, 6, NT], BF16)
                    with ExitStack() as ph:
                        K.es = ph
                        alloc_norm(K)
                        alloc_epre_noqt(K)
                        abk = [[], [], []]
                        for tc in range(5):
                            t0_, W_ = TCS[tc]
                            for t_ in range(0, W_ // 128, 2):
                                abk[(t0_ // 128 + t_) // 6].append("ab_o%d.%d" % (tc, t_))

                        def after_tc(tc, ab_loc=ab_loc, ab_gat=ab_gat, abk=abk):
                            k_ = {1: 0, 2: 1, 4: 2}.get(tc)
                            if k_ is not None:
                                P.cc("AllGather", GROUPS, ab_loc[k_], ab_gat[k_], r=abk[k_], w=["ab_gat%d" % k_])

                        epre_segment(K, L, win[j], rope, qkg[j], blk, perm, cs64,
                                     kt_loc,
                                     [v.rearrange("p (t s e) -> p t s e", t=18, s=2) for v in v_loc],
                                     [a.rearrange("p (t n) -> p t n", t=6) for a in ab_loc],
                                     final=False, load_consts=False, after_tc=after_tc)
                        for k_ in range(2):
                            P.cc("AllGather", GROUPS, kt_loc[k_], kt_gat[k_], r=["kt_o%d" % k_], w=["kt_gat%d" % k_])
                            P.cc("AllGather", GROUPS, v_loc[k_], v_gat[k_],
                                 r=["v_o%d.%d" % (tc, k_) for tc in range(5)], w=["v_gat%d" % k_])
                        P.barrier(keep=["kt_gat0", "kt_gat1", "v_gat0", "v_gat1"])
                    with ExitStack() as ph:
                        K.es = ph
                        alloc_epost(K, with_qt=False)
                        epost_segment(K, L,
                                      [k_.rearrange("(r p) t -> r p t", p=128) for k_ in kt_gat],
                                      [v.rearrange("(r p) (t s e) -> r p t s e", p=128, t=18, s=2) for v in v_gat],
                                      [a.rearrange("(r p) (t n) -> r p t n", p=128, t=6) for a in ab_gat],
                                      dft, dftc, wout[j])
                        P.barrier()
            else:
                e_loc = internal("e_loc%d" % L, [128, 240], BF16)
                e_gat = internal("e_gat%d" % L, [512, 240], BF16)
                with ExitStack() as ul:
                    K.es = ul
                    K.U = K.sb("U", [128, 8, UW], BF16)
                    with ExitStack() as ph:
                        K.es = ph
                        alloc_norm(K)
                        alloc_opre(K, with_u=False)
                        opre_segment(K, L, wpw1[j], bpw1[j], ntc)
                        halo_segment(K, e_loc, e_gat)
                        P.barrier()
                    with ExitStack() as ph:
                        K.es = ph
                        alloc_opost(K, with_u=False)
                        opost_segment(K, L, wdw[j], pv4[j], wpw2[j], ntc, ukeys=True)
                        P.barrier()
            with ExitStack() as ph:
                K.es = ph
                alloc_norm(K)
                alloc_mlp(K)
                def store_tc(tc):
                    t0_, W_ = TCS[tc]
                    P.dma("sp", xo[:, :, t0_:t0_ + W_], K.X[:, :, t0_:t0_ + W_], r=xkeys(tc), w=["xo%d" % tc], final=True)

                mlp_segment(K, L, w1[L], w2[L], ntc, after_last=store_tc if L == 3 else None)
                P.barrier()
        K.es = es
        P.emit()
    return nc


def kernel(x, c, ctx, c_ctx, ada_w, ada_b, norm1_g, norm2_g, mlp_w1, mlp_w2,
           attn_w_in, q_norm_g, k_norm_g, attn_w_out,
           conv_w_pw1, conv_b_pw1, conv_w_dw, conv_b_dw, conv_ln_g, conv_ln_b,
           conv_w_pw2, conv_b_pw2):
    f32 = lambda a: np.ascontiguousarray(np.asarray(a, dtype=np.float32))
    x, c, ctx, c_ctx = f32(x), f32(c), f32(ctx), f32(c_ctx)
    ada_w, ada_b, norm1_g, norm2_g = f32(ada_w), f32(ada_b), f32(norm1_g), f32(norm2_g)
    mlp_w1, mlp_w2, attn_w_in, attn_w_out = f32(mlp_w1), f32(mlp_w2), f32(attn_w_in), f32(attn_w_out)
    conv_w_pw1, conv_w_pw2, conv_w_dw = f32(conv_w_pw1), f32(conv_w_pw2), f32(conv_w_dw)
    C = consts()
    shared = {}
    for L in range(4):
        shared["w1_%d" % L] = mlp_w1[L]
        shared["w2_%d" % L] = mlp_w2[L]
    for j in range(2):
        g = np.zeros((128, 8), np.float32)
        for fc in range(8):
            src = np.asarray(q_norm_g[j] if fc < 6 else k_norm_g[j], np.float32)
            g[:64, fc] = src
            g[64:, fc] = src
        shared["win_%d" % j] = perm_win(attn_w_in[j])
        shared["qkg_%d" % j] = g
        shared["wout_%d" % j] = attn_w_out[j]
        shared["wpw1_%d" % j] = conv_w_pw1[j]
        shared["bpw1_%d" % j] = fm_vec(conv_b_pw1[j])
        shared["wdw_%d" % j] = np.ascontiguousarray(conv_w_dw[j].reshape(31, 8, 128).transpose(2, 1, 0))
        shared["pv4_%d" % j] = np.ascontiguousarray(np.stack(
            [fm_vec(conv_b_dw[j]), fm_vec(conv_ln_g[j]), fm_vec(conv_ln_b[j]), fm_vec(conv_b_pw2[j])], axis=1))
        shared["wpw2_%d" % j] = conv_w_pw2[j]
    shared.update(blk=C["blk"], perm=C["perm"], cs64=C["cs64"], dftc=C["dftc"],
                  ident=_bf(np.eye(128, dtype=np.float32)))
    maps = []
    for i in range(NCORES):
        b, r = i // 4, i % 4
        xt = np.concatenate([x[b, r * NLAT:(r + 1) * NLAT], ctx[b]], axis=0)
        adab = fm_vec(ada_b[r])
        hm = np.zeros((128, 8), np.float32)
        if r > 0:
            hm[:, r - 1] = 1.0
        if r < 3:
            hm[:, 4 + r + 1] = 1.0
        m = dict(shared)
        m.update(x_in=fm_tokens(xt), adaw=ada_w[r],
                 adab=np.ascontiguousarray(np.repeat(adab[:, :, None], 2, axis=2)),
                 cvec=np.ascontiguousarray(np.stack([fm_vec(c[b]), fm_vec(c_ctx)], axis=-1)),
                 ng=np.ascontiguousarray(np.stack([fm_vec(norm1_g[r]), fm_vec(norm2_g[r])], axis=1)),
                 hmask=hm, rope=C["rope"][r], dft=C["dft"][r])
        maps.append(m)
    if "F" not in _PROGS:
        _PROGS["F"] = build_fused()
    res = run_bass_kernel_spmd(_PROGS["F"], maps, core_ids=list(range(NCORES))).results
    out = np.zeros((2, 4 * NLAT, D), np.float32)
    for i in range(NCORES):
        b, r = i // 4, i % 4
        xo = np.asarray(res[i]["x_o"])
        out[b, r * NLAT:(r + 1) * NLAT] = xo.transpose(2, 1, 0).reshape(NLAT, D)
    return out
```

```python
import math
from contextlib import ExitStack

import numpy as np
import ml_dtypes
import concourse.bass as bass
import concourse.mybir as mybir
from concourse.bass_utils import run_bass_kernel_spmd

F32 = mybir.dt.float32
BF16 = mybir.dt.bfloat16
AF = mybir.ActivationFunctionType
ALU = mybir.AluOpType
NPBF = ml_dtypes.bfloat16

D = 1024
NLAT = 2048
NCTX = 256
NT = NLAT + NCTX
TCS = [(0, 512), (512, 512), (1024, 512), (1536, 512), (2048, 256)]
EPS = 1e-6
NCORES = 8


class _Op:
    __slots__ = ("eng", "fn", "deps", "dma", "sem", "semval", "signal", "count", "idx", "final", "inc")


class Prog:
    ENGS = ("pe", "act", "dve", "pool", "sp")

    def __init__(self, nc):
        self.nc = nc
        self.ops = []
        self.state = {}
        self.dma_sem_of = {}
        self.dma_sem_cnt = []
        self.finals = []

    def _add(self, eng, fn, r, w, dma=False, final=False, inc=16):
        op = _Op()
        op.inc = inc
        op.eng, op.fn, op.dma, op.final = eng, fn, dma, final
        op.signal = False
        op.count = None
        op.idx = len(self.ops)
        deps = {}
        for k in r:
            st = self.state.setdefault(k, [None, []])
            if st[0] is not None:
                deps[st[0]] = "raw"
        for k in w:
            st = self.state.setdefault(k, [None, []])
            if st[0] is not None:
                deps[st[0]] = "waw"
            for ri in st[1]:
                if ri not in deps:
                    deps[ri] = "war"
        for k in r:
            rl = self.state[k][1]
            if not dma:
                rl[:] = [ri for ri in rl if self.ops[ri].dma or self.ops[ri].eng != eng]
            rl.append(op.idx)
        for k in w:
            self.state[k] = [op.idx, []]
        op.deps = []
        latest = {}
        for di, kind in deps.items():
            dop = self.ops[di]
            if dop.dma:
                op.deps.append(di)
            elif dop.eng == eng:
                if eng == "pe":
                    continue
                latest[dop.eng] = max(latest.get(dop.eng, -1), di)
            else:
                latest[dop.eng] = max(latest.get(dop.eng, -1), di)
        for di in latest.values():
            self.ops[di].signal = True
            op.deps.append(di)
        if dma:
            key = w[0]
            if key not in self.dma_sem_of:
                self.dma_sem_of[key] = len(self.dma_sem_cnt)
                self.dma_sem_cnt.append(0)
            si = self.dma_sem_of[key]
            self.dma_sem_cnt[si] += inc
            op.sem = si
            op.semval = self.dma_sem_cnt[si]
            if final:
                self.finals.append(op.idx)
        self.ops.append(op)
        return op

    def barrier(self, keep=()):
        last = {}
        lastdma = {}
        kept = {}
        for k in keep:
            st = self.state.get(k)
            if st is not None and st[0] is not None and self.ops[st[0]].dma:
                kept[k] = st[0]
        skip_sems = set(self.ops[i].sem for i in kept.values())
        seen = getattr(self, "_bar_seen", {})
        for op in self.ops:
            if op.dma:
                if op.sem not in skip_sems and op.semval > seen.get(op.sem, 0):
                    lastdma[op.sem] = op.idx
            elif op.fn is not None:
                last[op.eng] = op.idx
        for si, li in lastdma.items():
            seen[si] = self.ops[li].semval
        self._bar_seen = seen
        for e in self.ENGS:
            op = _Op()
            op.eng, op.fn, op.dma, op.final = e, None, False, False
            op.signal = False
            op.count = None
            op.idx = len(self.ops)
            op.deps = []
            for e2, li in last.items():
                if e2 != e:
                    self.ops[li].signal = True
                    op.deps.append(li)
            for si, li in lastdma.items():
                op.deps.append(li)
            self.ops.append(op)
        self.state = {k: [i, []] for k, i in kept.items()}

    def op(self, eng, fn, r=(), w=()):
        return self._add(eng, fn, tuple(r), tuple(w))

    def dma(self, q, out, in_, r=(), w=(), final=False):
        assert q in ("sp", "pool")
        return self._add(q, lambda e: e.dma_start(out=out, in_=in_), tuple(r), tuple(w),
                         dma=True, final=final)

    def cc(self, kind, groups, in_ap, out_ap, r=(), w=()):
        return self._add("pool", lambda e: e.collective_compute(kind, ALU.bypass, replica_groups=groups,
                                                                ins=[in_ap], outs=[out_ap]),
                         tuple(r), tuple(w), dma=True, inc=1)

    def emit(self):
        nc = self.nc
        cnt = {e: 0 for e in self.ENGS}
        for op in self.ops:
            if op.dma or op.fn is None:
                continue
            if op.signal:
                cnt[op.eng] += 1
                op.count = cnt[op.eng]
        for e in self.ENGS:
            assert cnt[e] < 60000, (e, cnt[e])
        for v in self.dma_sem_cnt:
            assert v < 60000, v
        with ExitStack() as es:
            esem = {e: es.enter_context(nc.semaphore("s_" + e)) for e in ("pe", "act", "dve", "pool")}
            dsem = [es.enter_context(nc.semaphore("d%d" % i)) for i in range(len(self.dma_sem_cnt))]
            block = es.enter_context(nc.Block())
            ops = self.ops
            finals = self.finals

            def run(ename, e):
                waited = {}
                for op in ops:
                    if op.eng != ename:
                        continue
                    for di in op.deps:
                        dop = ops[di]
                        if dop.dma:
                            sem, val, key = dsem[dop.sem], dop.semval, ("d", dop.sem)
                        else:
                            sem, val, key = esem[dop.eng], dop.count, ("e", dop.eng)
                        if waited.get(key, 0) >= val:
                            continue
                        waited[key] = val
                        e.wait_ge(sem, val)
                    if op.fn is None:
                        continue
                    ins = op.fn(e)
                    if op.dma:
                        ins.then_inc(dsem[op.sem], op.inc)
                    elif op.signal:
                        ins.then_inc(esem[ename], 1)
                if ename == "sp":
                    for fi in finals:
                        fop = ops[fi]
                        e.wait_ge(dsem[fop.sem], fop.semval)

            @block.tensor
            def _(e):
                run("pe", e)

            @block.scalar
            def _(e):
                run("act", e)

            @block.vector
            def _(e):
                run("dve", e)

            @block.gpsimd
            def _(e):
                run("pool", e)

            @block.sync
            def _(e):
                run("sp", e)


class Ctx:
    def __init__(self, nc, es):
        self.nc = nc
        self.es = es
        self.P = Prog(nc)
        self.din = {}
        self.dout = {}

    def dram_in(self, name, shape, dt):
        t = self.nc.dram_tensor(name, list(shape), dt, kind="ExternalInput").ap()
        self.din[name] = t
        return t

    def dram_out(self, name, shape, dt):
        t = self.nc.dram_tensor(name, list(shape), dt, kind="ExternalOutput").ap()
        self.dout[name] = t
        return t

    def sb(self, name, shape, dt):
        self.nsb = getattr(self, "nsb", 0) + 1
        return self.es.enter_context(self.nc.sbuf_tensor("%s_%d" % (name, self.nsb), list(shape), dt))

    def mm(self, out, lhsT, rhs, start, stop, r, w):
        self.P.op("pe", lambda e: e.matmul(out, lhsT, rhs, start=start, stop=stop), r, w)

    def act(self, out, in_, func, r, w, bias=None, scale=None):
        kw = {}
        if bias is not None:
            kw["bias"] = bias
        if scale is not None:
            kw["scale"] = scale
        self.P.op("act", lambda e: e.activation(out, in_, func, **kw), r, w)

    def tt(self, eng, out, in0, in1, op, r, w):
        self.P.op(eng, lambda e: e.tensor_tensor(out, in0, in1, op), r, w)

    def ts(self, eng, out, in0, s1, s2, op0, op1, r, w):
        if op1 is None:
            self.P.op(eng, lambda e: e.tensor_scalar(out, in0, s1, None, op0), r, w)
        else:
            self.P.op(eng, lambda e: e.tensor_scalar(out, in0, s1, s2, op0, op1), r, w)

    def stt(self, out, in0, scalar, in1, op0, op1, r, w):
        self.P.op("dve", lambda e: e.scalar_tensor_tensor(out, in0, scalar, in1, op0, op1), r, w)

    def cp(self, eng, out, in_, r, w):
        self.P.op(eng, lambda e: e.tensor_copy(out, in_), r, w)

    def recip(self, out, in_, r, w):
        self.P.op("dve", lambda e: e.reciprocal(out, in_), r, w)

    def memset(self, eng, ap, val, w):
        self.P.op(eng, lambda e: e.memset(ap, val), (), w)


def _bf(a):
    return np.ascontiguousarray(a).astype(NPBF)


def build_mod():
    nc = bass.Bass("TRN2", target_bir_lowering=False)
    with ExitStack() as es:
        K = Ctx(nc, es)
        P = K.P
        adaw = K.dram_in("adaw", [D, 6 * D], F32)
        adab = K.dram_in("adab", [128, 48, 2], F32)
        cvec = K.dram_in("cvec", [128, 8, 2], F32)
        ng = K.dram_in("ng", [128, 2, 8], F32)
        modo = K.dram_out("modo", [128, 48, 2], F32)
        CV = K.sb("CV", [128, 8, 2], F32)
        ABs = K.sb("ABs", [128, 48, 2], F32)
        NG = K.sb("NG", [128, 2, 8], F32)
        E1 = K.sb("E1", [128, 8, 2], F32)
        S = K.sb("S", [128, 8, 2], BF16)
        MOD = K.sb("MOD", [128, 48, 2], F32)
        WM = [K.sb("WM%d" % i, [128, 8, 1024], BF16) for i in range(2)]
        PS = es.enter_context(nc.psum_tensor("PS", [128, 8, 512], F32))
        P.dma("sp", CV[:], cvec, w=["CV"])
        P.dma("sp", ABs[:], adab, w=["ABs"])
        P.dma("sp", NG[:], ng, w=["NG"])
        K.act(E1[:], CV[:], AF.Exp, ["CV"], ["E1"], scale=-1.0)
        K.ts("dve", E1[:], E1[:], 1.0, None, ALU.add, None, ["E1"], ["E1"])
        K.recip(E1[:], E1[:], ["E1"], ["E1"])
        K.tt("dve", S[:], CV[:], E1[:], ALU.mult, ["CV", "E1"], ["S"])
        adaw_v = adaw.rearrange("(kc p) n -> p kc n", p=128)
        for j in range(6):
            wm = WM[j % 2]
            wk = "WM%d" % (j % 2)
            P.dma("pool", wm[:], adaw_v[:, :, j * 1024:(j + 1) * 1024], w=[wk])
            bank = j % 2
            for c in range(8):
                for kc in range(8):
                    K.mm(PS[:, bank, c * 2:c * 2 + 2], wm[:, kc, c * 128:(c + 1) * 128], S[:, kc, :],
                         kc == 0, kc == 7, [wk, "S"], ["ps%d" % bank])
            K.tt("dve", MOD[:, j * 8:(j + 1) * 8, :],
                 PS[:, bank, 0:16].rearrange("p (c v) -> p c v", v=2),
                 ABs[:, j * 8:(j + 1) * 8, :], ALU.add, ["ps%d" % bank, "ABs"], ["MOD%d" % j])
        for j, gi in ((1, 0), (4, 1)):
            for v in range(2):
                K.stt(MOD[:, j * 8:(j + 1) * 8, v], MOD[:, j * 8:(j + 1) * 8, v], 1.0, NG[:, gi, :],
                      ALU.add, ALU.mult, ["MOD%d" % j, "NG"], ["MOD%d" % j])
        P.dma("sp", modo, MOD[:], r=["MOD%d" % j for j in range(6)], w=["modo"], final=True)
        P.emit()
    return nc


def alloc_common(K):
    nc, es = K.nc, K.es
    K.X = K.sb("X", [128, 8, NT], F32)
    K.PS = es.enter_context(nc.psum_tensor("PS", [128, 8, 512], F32))
    K.ONES = K.sb("ONES", [128, 128], BF16)
    K.MOD = {}
    K.memset("pool", K.ONES[:], 1.0, ["ONES"])


def alloc_norm(K):
    K.SQ = K.sb("SQ", [128, 8, 512], BF16)
    K.LNV = K.sb("LNV", [128, 512], F32)
    K.RSTD = K.sb("RSTD", [128, 512], F32)
    K.TMP = [K.sb("TMP%d" % i, [128, 512], F32) for i in range(2)]


def load_mod(K, L, name=None):
    t = K.dram_in(name or ("mod%d" % L), [128, 48, 2], F32)
    K.MOD[L] = K.sb("MODL%d" % L, [128, 48, 2], F32)
    K.P.dma("sp", K.MOD[L][:], t, w=["MOD%d" % L])


def xkeys(tc):
    return ["X%d.%d" % (tc, c) for c in range(8)]


def norm_sq(K, tc):
    t0, W = TCS[tc]
    X = K.X
    for c in range(8):
        eng = "dve" if c % 2 == 0 else "pool"
        K.tt(eng, K.SQ[:, c, :W], X[:, c, t0:t0 + W], X[:, c, t0:t0 + W], ALU.mult,
             ["X%d.%d" % (tc, c)], ["SQ%d" % c])


def norm_rest(K, L, which, tc, out_fn, out_keys):
    t0, W = TCS[tc]
    v = 1 if tc == 4 else 0
    MOD = K.MOD[L]
    ja, jb = (1, 0) if which == 0 else (4, 3)
    mk = "MOD%d" % L
    X, PS = K.X, K.PS
    for c in range(8):
        K.mm(PS[:, 7, :W], K.ONES[:], K.SQ[:, c, :W], c == 0, c == 7, ["ONES", "SQ%d" % c], ["ps7"])
    K.act(K.LNV[:, :W], PS[:, 7, :W], AF.Ln, ["ps7"], ["LNV"], bias=K.EPSB[:], scale=1.0 / D)
    K.act(K.RSTD[:, :W], K.LNV[:, :W], AF.Exp, ["LNV"], ["RSTD"], scale=-0.5)
    for c in range(8):
        tb = c % 2
        K.stt(K.TMP[tb][:, :W], X[:, c, t0:t0 + W], MOD[:, ja * 8 + c, v:v + 1], K.RSTD[:, :W],
              ALU.mult, ALU.mult, ["X%d.%d" % (tc, c), mk, "RSTD"], ["TMP%d" % tb])
        K.act(out_fn(c), K.TMP[tb][:, :W], AF.Identity, ["TMP%d" % tb, mk], out_keys(c),
              bias=MOD[:, jb * 8 + c, v:v + 1], scale=1.0)


def norm_mod(K, L, which, tc, out_fn, out_keys):
    norm_sq(K, tc)
    norm_rest(K, L, which, tc, out_fn, out_keys)


def alloc_eps(K):
    K.EPSB = K.sb("EPSB", [128, 1], F32)
    K.memset("pool", K.EPSB[:], EPS, ["EPSB"])
    K.ONEB = K.sb("ONEB", [128, 1], F32)
    K.memset("pool", K.ONEB[:], 1.0, ["ONEB"])


def mlp_segment(K, L, w1, w2, ntc=5, after_last=None):
    X, PS = K.X, K.PS
    H = K.H
    MOD = K.MOD[L]
    mk = "MOD%d" % L
    def norm2(tc):
        t0, W = TCS[tc]
        norm_mod(K, L, 1, tc, lambda c, t0=t0, W=W: H[:, c, t0:t0 + W], lambda c, tc=tc: ["H%d" % tc])

    norm2(0)
    w1v = w1.rearrange("(kc p) n -> p kc n", p=128)
    w2v = w2.rearrange("(fc p) n -> p fc n", p=128)
    items = [(e, tc) for e in range(8) for tc in range(ntc)]

    def load(e):
        K.P.dma("pool", K.W1E[e % 2][:], w1v[:, :, e * 512:(e + 1) * 512], w=["W1E%d" % (e % 2)])
        K.P.dma("pool", K.W2E[e % 2][:], w2v[:, e * 4:(e + 1) * 4, :], w=["W2E%d" % (e % 2)])

    def part1(i):
        e, tc = items[i]
        t0, W = TCS[tc]
        ab = i % 2
        for f in range(4):
            bank = f % 2
            for kc in range(8):
                K.mm(PS[:, bank, :W], K.W1E[e % 2][:, kc, f * 128:(f + 1) * 128], H[:, kc, t0:t0 + W],
                     kc == 0, kc == 7, ["W1E%d" % (e % 2), "H%d" % tc], ["ps%d" % bank])
            K.act(K.RL[f % 2][:, :W], PS[:, bank, :W], AF.Relu, ["ps%d" % bank], ["RL%d" % (f % 2)])
            eng = "pool" if f % 2 == 0 else "dve"
            K.tt(eng, K.AH[ab][:, f, :W], K.RL[f % 2][:, :W], K.RL[f % 2][:, :W], ALU.mult,
                 ["RL%d" % (f % 2)], ["AH%d.%d" % (ab, f)])

    def part2(i):
        e, tc = items[i]
        t0, W = TCS[tc]
        v = 1 if tc == 4 else 0
        ab = i % 2
        for d in range(8):
            bank = 2 + d % 2
            for f in range(4):
                K.mm(PS[:, bank, :W], K.W2E[e % 2][:, f, d * 128:(d + 1) * 128], K.AH[ab][:, f, :W],
                     f == 0, f == 3, ["W2E%d" % (e % 2), "AH%d.%d" % (ab, f)], ["ps%d" % bank])
            K.stt(X[:, d, t0:t0 + W], PS[:, bank, :W], MOD[:, 40 + d, v:v + 1], X[:, d, t0:t0 + W],
                  ALU.mult, ALU.add, ["ps%d" % bank, mk, "X%d.%d" % (tc, d)], ["X%d.%d" % (tc, d)])

    load(0)
    part1(0)
    for i in range(len(items)):
        e0, tc0 = items[i]
        if tc0 == 1 and e0 + 1 < 8:
            load(e0 + 1)
        if i + 1 < len(items):
            if items[i + 1][0] == 0:
                norm2(items[i + 1][1])
            part1(i + 1)
        part2(i)
        if e0 == 7 and after_last is not None:
            after_last(tc0)


def alloc_mlp(K):
    K.H = K.sb("H", [128, 8, NT], BF16)
    K.W1E = [K.sb("W1E%d" % i, [128, 8, 512], BF16) for i in range(2)]
    K.W2E = [K.sb("W2E%d" % i, [128, 4, 1024], BF16) for i in range(2)]
    K.RL = [K.sb("RL%d" % i, [128, 512], F32) for i in range(2)]
    K.AH = [K.sb("AH%d" % i, [128, 4, 512], BF16) for i in range(2)]


def alloc_epre_noqt(K):
    alloc_epre(K, with_qt=False)


def alloc_epre(K, with_qt=True):
    K.WIN = K.sb("WIN", [128, 8, 1536], BF16)
    K.HC = [K.sb("HC%d" % i, [128, 8, 512], BF16) for i in range(2)]
    if with_qt:
        K.QT = K.sb("QT", [128, 6, NT], BF16)
    K.KTL = K.sb("KTL", [128, 2, NT], BF16)
    K.VTC = [K.sb("VTC%d" % i, [128, 4, 2, 2, 65], BF16) for i in range(2)]
    K.ABC = [K.sb("ABC%d" % i, [128, 4, 512], BF16) for i in range(2)]
    K.FTC = K.sb("FTC", [128, 2, 512], BF16)
    K.ROPE = [K.sb("ROPE0", [128, 2, 512], F32)] * 2
    K.PSB = [K.sb("PSB%d" % i, [128, 512], F32) for i in range(2)]
    K.SQ1 = [K.sb("SQ1%d" % i, [128, 512], BF16) for i in range(2)]
    K.QN = [K.sb("QN%d" % i, [128, 512], F32) for i in range(2)]
    K.T1 = [K.sb("T10", [128, 512], F32)] * 2
    K.T2 = [K.sb("T20", [128, 512], F32)] * 2
    K.LN2 = [K.sb("LN20", [128, 512], F32)] * 2
    K.RS2 = [K.sb("RS20", [128, 512], F32)] * 2
    if not hasattr(K, "BLK"):
        K.BLK = K.sb("BLK", [128, 128], BF16)
        K.PERM = K.sb("PERM", [128, 128], F32)
        K.CS64 = K.sb("CS64", [128, 256], BF16)
    K.QKG = K.sb("QKG", [128, 8], F32)


def epre_segment(K, L, win, rope, qkg, blk, perm, cs64, kt_o, v_o, ab_o, final=True, load_consts=True, after_tc=None):
    X, PS, P = K.X, K.PS, K.P
    P.dma("pool", K.WIN[:], win.rearrange("(kc p) n -> p kc n", p=128), w=["WIN"])
    if load_consts:
        P.dma("sp", K.BLK[:], blk, w=["BLK"])
        P.dma("sp", K.PERM[:], perm, w=["PERM"])
        P.dma("sp", K.CS64[:], cs64, w=["CS64"])
    P.dma("sp", K.QKG[:], qkg, w=["QKG"])
    K.memset("pool", K.VTC[0][:], 1.0, ["VTC0"])
    K.memset("pool", K.VTC[1][:], 1.0, ["VTC1"])
    def nrm(tc_, part):
        W_ = TCS[tc_][1]
        hcb = K.HC[tc_ % 2]
        hkb = "HC%d" % (tc_ % 2)
        if part == 0:
            norm_sq(K, tc_)
        else:
            norm_rest(K, L, 0, tc_, lambda c, hcb=hcb, W_=W_: hcb[:, c, :W_], lambda c, hkb=hkb: [hkb])

    nrm(0, 0)
    nrm(0, 1)
    for tc in range(5):
        t0, W = TCS[tc]
        hb = tc % 2
        hc = K.HC[hb]
        hk = "HC%d" % hb
        if tc < 4:
            P.dma("sp", K.ROPE[0][:], rope[:, :, t0:t0 + W], w=["ROPE0"])
        def stA(fc):
            pb = fc % 2
            b0 = fc % 2
            for kc in range(8):
                K.mm(PS[:, b0, :W], K.WIN[:, kc, fc * 128:(fc + 1) * 128], hc[:, kc, :W],
                     kc == 0, kc == 7, ["WIN", hk], ["ps%d" % b0])
            K.act(K.PSB[pb][:, :W], PS[:, b0, :W], AF.Identity, ["ps%d" % b0], ["PSB%d" % pb])
            K.tt("pool", K.SQ1[pb][:, :W], K.PSB[pb][:, :W], K.PSB[pb][:, :W], ALU.mult,
                 ["PSB%d" % pb], ["SQ1%d" % pb])

        def stB(fc):
            pb = fc % 2
            b1 = 2 + fc % 2
            K.mm(PS[:, b1, :W], K.BLK[:], K.SQ1[pb][:, :W], True, True, ["BLK", "SQ1%d" % pb], ["ps%d" % b1])
            K.act(K.LN2[pb][:, :W], PS[:, b1, :W], AF.Ln, ["ps%d" % b1], ["LN20"],
                  bias=K.EPSB[:], scale=1.0 / 64)
            K.act(K.RS2[pb][:, :W], K.LN2[pb][:, :W], AF.Exp, ["LN20"], ["RS20"], scale=-0.5)
            K.stt(K.QN[pb][:, :W], K.PSB[pb][:, :W], K.QKG[:, fc:fc + 1], K.RS2[pb][:, :W],
                  ALU.mult, ALU.mult, ["PSB%d" % pb, "QKG", "RS20"], ["QN%d" % pb])

        def stC(fc):
            pb = fc % 2
            b2 = 4 + fc % 2
            if fc < 6:
                dest, dk = K.QT[:, fc, t0:t0 + W], "QT%d.%d" % (tc, fc)
            else:
                dest, dk = K.KTL[:, fc - 6, t0:t0 + W], "KTL%d" % (fc - 6)
            if tc < 4:
                rp = K.ROPE[0]
                rk = "ROPE0"
                K.mm(PS[:, b2, :W], K.PERM[:], K.QN[pb][:, :W], True, True, ["PERM", "QN%d" % pb], ["ps%d" % b2])
                K.tt("dve", K.T1[pb][:, :W], K.QN[pb][:, :W], rp[:, 0, :W], ALU.mult,
                     ["QN%d" % pb, rk], ["T10"])
                K.tt("dve", K.T2[pb][:, :W], PS[:, b2, :W], rp[:, 1, :W], ALU.mult,
                     ["ps%d" % b2, rk], ["T20"])
                K.tt("pool", dest, K.T1[pb][:, :W], K.T2[pb][:, :W], ALU.add,
                     ["T10", "T20"], [dk])
            else:
                K.cp("pool", dest, K.QN[pb][:, :W], ["QN%d" % pb], [dk])

        stA(0)
        if tc + 1 < 5:
            nrm(tc + 1, 0)
        for fc in range(8):
            if fc + 1 < 8:
                stA(fc + 1)
            stB(fc)
            if fc >= 1:
                stC(fc - 1)
            if fc == 1 and tc + 1 < 5:
                nrm(tc + 1, 1)
        stC(7)
        rb = [6, 4, 5]
        ri = [0]

        def nb():
            b = rb[ri[0] % 3]
            ri[0] += 1
            return b

        for tt_ in range(W // 128):
            gt = t0 // 128 + tt_
            b = nb()
            for kc in range(8):
                K.mm(PS[:, b, 0:256], hc[:, kc, tt_ * 128:(tt_ + 1) * 128], K.WIN[:, kc, 1024:1280],
                     kc == 0, kc == 7, ["WIN", hk], ["ps%d" % b])
            K.act(K.VTC[tc % 2][:, tt_, :, :, 0:64], PS[:, b, 0:256].rearrange("p (a s e) -> p a s e", a=2, s=2),
                  AF.Identity, ["ps%d" % b], ["VTC%d" % (tc % 2)])
        for half in range(2):
            b = nb()
            for kc in range(8):
                K.mm(PS[:, b, :W], K.WIN[:, kc, 1280 + half * 128:1280 + (half + 1) * 128], hc[:, kc, :W],
                     kc == 0, kc == 7, ["WIN", hk], ["ps%d" % b])
            K.act(K.FTC[:, half, :W], PS[:, b, :W], AF.Identity, ["ps%d" % b], ["FTC%d" % half])
        for tt_ in range(W // 128):
            gt = t0 // 128 + tt_
            for half in range(2):
                b = nb()
                K.mm(PS[:, b, 0:256], K.FTC[:, half, tt_ * 128:(tt_ + 1) * 128], K.CS64[:],
                     True, True, ["FTC%d" % half, "CS64"], ["ps%d" % b])
                K.cp("dve", K.ABC[tc % 2][:, tt_, half * 256:(half + 1) * 256], PS[:, b, 0:256],
                     ["ps%d" % b], ["ABC%d" % (tc % 2)])
        nt_ = W // 128
        g0 = t0 // 128
        for a_ in range(2):
            P.dma("sp", v_o[a_][:, g0:g0 + nt_], K.VTC[tc % 2][:, :nt_, a_], r=["VTC%d" % (tc % 2)],
                  w=["v_o%d.%d" % (tc, a_)], final=final)
        for t_ in range(0, nt_, 2):
            ch, off = (g0 + t_) // 6, (g0 + t_) % 6
            P.dma("sp", ab_o[ch][:, off:off + 2, :], K.ABC[tc % 2][:, t_:t_ + 2, :], r=["ABC%d" % (tc % 2)],
                  w=["ab_o%d.%d" % (tc, t_)], final=final)
        if after_tc is not None:
            after_tc(tc)
    for k_ in range(2):
        P.dma("sp", kt_o[k_], K.KTL[:, k_, :], r=["KTL%d" % k_], w=["kt_o%d" % k_], final=final)


def fm_vec(v):
    v = np.asarray(v, np.float32)
    return np.ascontiguousarray(v.reshape(-1, 128).T)


def fm_tokens(a):
    T = a.shape[0]
    return np.ascontiguousarray(a.reshape(T, 8, 128).transpose(2, 1, 0))


_CONST = {}


def consts():
    if _CONST:
        return _CONST
    blk = np.zeros((128, 128), np.float32)
    blk[:64, :64] = 1.0
    blk[64:, 64:] = 1.0
    perm = np.zeros((128, 128), np.float32)
    for j in range(64):
        perm[2 * j + 1, 2 * j] = -1.0
        perm[2 * j, 2 * j + 1] = 1.0
    n = np.arange(64)
    ang = 2 * np.pi * np.outer(n, n) / 64.0
    c64, s64 = np.cos(ang), np.sin(ang)
    cs = np.zeros((128, 256), np.float64)
    for g in range(2):
        cs[g * 64:(g + 1) * 64, g * 64:(g + 1) * 64] = c64
        cs[g * 64:(g + 1) * 64, 128 + g * 64:128 + (g + 1) * 64] = s64
    _CONST["blk"] = _bf(blk)
    _CONST["perm"] = perm
    _CONST["cs64"] = _bf(cs)
    freqs = 10000.0 ** (-np.arange(16, dtype=np.float32) / 16)
    ropes = []
    for r in range(4):
        t = np.arange(r * NLAT, (r + 1) * NLAT)
        row = (t // 64).astype(np.float32)
        col = (t % 64).astype(np.float32)
        ang = np.concatenate([row[:, None] * freqs, col[:, None] * freqs], axis=-1).astype(np.float32)
        cos, sin = np.cos(ang), np.sin(ang)
        tab = np.zeros((128, 2, NLAT), np.float32)
        for p in range(128):
            jj = (p % 64) // 2
            tab[p, 0] = cos[:, jj]
            tab[p, 1] = sin[:, jj]
        ropes.append(tab)
    _CONST["rope"] = ropes
    tabs = []
    l = np.arange(8192, dtype=np.int64)
    sc = 1.0 / math.sqrt(8192 * 64)
    for r in range(4):
        k = np.arange(r * NLAT, (r + 1) * NLAT, dtype=np.int64)
        m = (l[:, None] * k[None, :]) % 8192
        a = 2 * np.pi * m / 8192.0
        tab = np.stack([np.cos(a) * sc, -np.sin(a) * sc], axis=1)
        tabs.append(_bf(tab.reshape(64, 128, 2, NLAT)))
    _CONST["dft"] = tabs
    l2 = np.arange(256, dtype=np.int64)
    m = (l2[:, None] * l2[None, :]) % 256
    a = 2 * np.pi * m / 256.0
    sc2 = 1.0 / math.sqrt(256 * 64)
    _CONST["dftc"] = _bf(np.stack([np.cos(a) * sc2, -np.sin(a) * sc2], axis=1).reshape(2, 128, 2, 256))
    return _CONST


def perm_win(w):
    cols = []
    for a in range(6):
        cols += list(range(a * 64, a * 64 + 64)) + list(range((a + 6) * 64, (a + 6) * 64 + 64))
    for kv in (0, 2, 1, 3):
        cols += list(range(768 + kv * 64, 768 + kv * 64 + 64))
    for kv in (0, 2, 1, 3):
        cols += list(range(1024 + kv * 64, 1024 + kv * 64 + 64))
    cols += list(range(1280, 1536))
    return np.ascontiguousarray(w[:, cols])


def build_p0():
    nc = bass.Bass("TRN2", target_bir_lowering=False)
    with ExitStack() as es:
        K = Ctx(nc, es)
        xin = K.dram_in("x_in", [128, 8, NT], F32)
        win = K.dram_in("win", [D, 1536], F32)
        rope = K.dram_in("rope", [128, 2, NLAT], F32)
        qkg = K.dram_in("qkg", [128, 8], F32)
        blk = K.dram_in("blk", [128, 128], BF16)
        perm = K.dram_in("perm", [128, 128], F32)
        cs64 = K.dram_in("cs64", [128, 256], BF16)
        kt_o = K.dram_out("kt_o", [128, 2, NT], BF16)
        v_o = K.dram_out("v_o", [128, 18, 2, 2, 65], BF16)
        ab_o = K.dram_out("ab_o", [128, 18, 512], BF16)
        qt_o = K.dram_out("qt_o", [128, 6, NT], BF16)
        alloc_common(K)
        alloc_eps(K)
        alloc_norm(K)
        alloc_epre(K)
        load_mod(K, 0)
        for c in range(8):
            K.P.dma("sp", K.X[:, c, :], xin[:, c, :], w=["X%d.%d" % (tc, c) for tc in range(5)])
        epre_segment(K, 0, win, rope, qkg, blk, perm, cs64, kt_o, v_o, ab_o)
        K.P.dma("sp", qt_o, K.QT[:], r=["QT%d.%d" % (tc, fc) for tc in range(5) for fc in range(6)],
                w=["qt_o"], final=True)
        K.P.emit()
    return nc


def alloc_epost(K, with_qt=True):
    if with_qt:
        K.QT = K.sb("QT", [128, 6, NT], BF16)
    K.KT = K.sb("KT", [128, 8448], BF16)
    K.V = K.sb("V", [128, 66, 2, 65], BF16)
    K.WO = K.sb("WO", [64, 12, 1024], BF16)
    K.WOF = K.sb("WOF", [128, 2, 1024], BF16)
    K.CAT = K.sb("CAT", [64, 6, 512], BF16)
    K.PB = [K.sb("PB%d" % i, [128, 2, 512], BF16) for i in range(3)]
    K.FT = K.sb("FT", [128, 2, NT], BF16)
    K.ABG = [K.sb("ABG%d" % i, [128, 2, 512], BF16) for i in range(2)]
    K.TABC = K.sb("TABC", [128, 2, 2, 256], BF16)
    K.ABGC = K.sb("ABGC", [128, 2, 512], BF16)
    K.OSB = K.sb("OSB", [65, 2, 512], F32)
    K.RDL = K.sb("RDL", [65, 2, 512], F32)
    K.RD = K.sb("RD", [65, 2, 512], BF16)
    K.ONESR = K.sb("ONESR", [128, 64], BF16)


def epost_segment(K, L, kt_all, v_all, ab_all, dft, dftc, wout, gk=()):
    X, PS, P = K.X, K.PS, K.P
    MOD = K.MOD[L]
    mk = "MOD%d" % L
    K.memset("pool", K.ONESR[:], 1.0, ["ONESR"])
    P.dma("pool", K.WO[:], wout[0:768, :].rearrange("(h d) n -> d h n", d=64), w=["WO"])
    P.dma("pool", K.WOF[:], wout[768:1024, :].rearrange("(c p) n -> p c n", p=128), w=["WOF"])
    TAB = [K.KT[:, 0:8192].rearrange("p (a b c) -> p a b c", a=2, b=2),
           K.V[:].rearrange("p a b c -> p (a b c)")[:, 0:8192].rearrange("p (a b c) -> p a b c", a=2, b=2)]
    TK = ["KTa", "Va"]
    for g in range(32):
        r = g // 8
        tl = (2 * g) % 16
        tb = g % 2
        P.dma("sp", TAB[tb], dft[2 * g:2 * g + 2].rearrange("l p s k -> p l s k"), w=[TK[tb]])
        P.dma("sp", K.ABG[tb][:], ab_all[tl // 6][r, :, tl % 6:tl % 6 + 2, :], r=gk, w=["ABG%d" % tb])
        for li in range(2):
            for half in range(2):
                for s in range(2):
                    for kc in range(4):
                        bank = half * 4 + kc
                        K.mm(PS[:, bank, :], K.ABG[tb][:, li, half * 256 + s * 128:half * 256 + (s + 1) * 128],
                             TAB[tb][:, li, s, kc * 512:(kc + 1) * 512],
                             g == 0 and li == 0 and s == 0, g == 31 and li == 1 and s == 1,
                             ["ABG%d" % tb, TK[tb]], ["ps%d" % bank])
    for half in range(2):
        for kc in range(4):
            bank = half * 4 + kc
            if bank % 2 == 0:
                K.act(K.FT[:, half, kc * 512:(kc + 1) * 512], PS[:, bank, :], AF.Identity, ["ps%d" % bank], ["FT%d" % kc])
            else:
                K.cp("dve", K.FT[:, half, kc * 512:(kc + 1) * 512], PS[:, bank, :], ["ps%d" % bank], ["FT%d" % kc])
    P.dma("sp", K.TABC[:], dftc.rearrange("l p s k -> p l s k"), w=["TABC"])
    P.dma("sp", K.ABGC[:], ab_all[2][0, :, 4:6, :], r=gk, w=["ABGC"])
    for half in range(2):
        for li in range(2):
            for s in range(2):
                K.mm(PS[:, half, 0:256], K.ABGC[:, li, half * 256 + s * 128:half * 256 + (s + 1) * 128],
                     K.TABC[:, li, s, :], li == 0 and s == 0, li == 1 and s == 1, ["ABGC", "TABC"], ["ps%d" % half])
        K.act(K.FT[:, half, 2048:2304], PS[:, half, 0:256], AF.Identity, ["ps%d" % half], ["FT4"])
    P.barrier()
    jobs = [(kp, tc, a3) for kp in range(2) for tc in range(5) for a3 in range(3)]

    def load_kv(kp):
        for r in range(4):
            P.dma("sp", K.KT[:, r * 2048:(r + 1) * 2048], kt_all[kp][r, :, 0:2048], r=["kt_gat%d" % kp], w=["KT%d" % r])
            P.dma("sp", K.V[:, r * 16:(r + 1) * 16, :, :], v_all[kp][r, :, 0:16, :, :], r=["v_gat%d" % kp], w=["V%d" % r])
        P.dma("sp", K.KT[:, 8192:8448], kt_all[kp][0, :, 2048:2304], r=["kt_gat%d" % kp], w=["KT4"])
        P.dma("sp", K.V[:, 64:66, :, :], v_all[kp][0, :, 16:18, :, :], r=["v_gat%d" % kp], w=["V4"])

    def kts_of(tc):
        return list(range(66)) if tc < 4 else [64, 65]

    def S(job, i, g0):
        kp, tc, a3 = job
        t0, W = TCS[tc]
        a = 3 * kp + a3
        qk = "QT%d.%d" % (tc, a)
        kt = kts_of(tc)[i]
        sb = (g0 + i) % 3
        kk = "KT%d" % min(kt // 16, 4)
        K.mm(PS[:, 2 * sb, :W], K.KT[0:64, kt * 128:(kt + 1) * 128], K.QT[0:64, a, t0:t0 + W],
             True, True, [kk, qk], ["ps%d" % (2 * sb)])
        K.mm(PS[:, 2 * sb + 1, :W], K.KT[64:128, kt * 128:(kt + 1) * 128], K.QT[64:128, a, t0:t0 + W],
             True, True, [kk, qk], ["ps%d" % (2 * sb + 1)])

    def prologue(job, g0):
        n = len(kts_of(job[1]))
        S(job, 0, g0)
        if n > 1:
            S(job, 1, g0)

    def body(job, g0, pending=None):
        kp, tc, a3 = job
        t0, W = TCS[tc]
        kts = kts_of(tc)
        n = len(kts)
        had_pending = pending is not None
        for i in range(n):
            if i == 2 and pending is not None:
                pending()
                pending = None
                S(job, 2, g0)
                if n > 3:
                    S(job, 3, g0)
            sb = (g0 + i) % 3
            kt = kts[i]
            vk = "V%d" % min(kt // 16, 4)
            if i + 2 < n and (not had_pending or i >= 2):
                S(job, i + 2, g0)
            K.act(K.PB[sb][:, :, :W], PS[:, 2 * sb:2 * sb + 2, :W], AF.Exp,
                  ["ps%d" % (2 * sb), "ps%d" % (2 * sb + 1)], ["PB%d" % sb], scale=0.125)
            for s_ in range(2):
                K.mm(PS[0:65, 6 + s_, :W], K.V[:, kt, s_, :], K.PB[sb][:, s_, :W],
                     i == 0, i == n - 1, [vk, "PB%d" % sb], ["ps%d" % (6 + s_)])
        if pending is not None:
            pending()
        K.cp("dve", K.OSB[0:65, :, :W], PS[0:65, 6:8, :W], ["ps6", "ps7"], ["OSB"])

    def finish(job, g0):
        kp, tc, a3 = job
        t0, W = TCS[tc]
        n = len(kts_of(tc))
        bs = (g0 + n) % 3
        v = 1 if tc == 4 else 0
        K.act(K.RDL[64:65, :, :W], K.OSB[64:65, :, :W], AF.Ln, ["OSB"], ["RDL"])
        K.act(K.RD[64:65, :, :W], K.RDL[64:65, :, :W], AF.Exp, ["RDL"], ["RD"], scale=-1.0)
        for s_ in range(2):
            K.mm(PS[0:64, 2 * bs + s_, :W], K.ONESR[64:65, 0:64], K.RD[64:65, s_, :W], True, True,
                 ["ONESR", "RD"], ["ps%d" % (2 * bs + s_)])
        K.tt("dve", K.CAT[0:64, 2 * a3:2 * a3 + 2, :W], K.OSB[0:64, :, :W], PS[0:64, 2 * bs:2 * bs + 2, :W],
             ALU.mult, ["OSB", "ps%d" % (2 * bs), "ps%d" % (2 * bs + 1)], ["CAT%d" % (2 * a3), "CAT%d" % (2 * a3 + 1)])
        if a3 == 2:
            for d in range(8):
                bank = 2 * bs + d % 2
                for slot in range(6):
                    head = 3 * kp + slot // 2 + 6 * (slot % 2)
                    K.mm(PS[:, bank, :W], K.WO[0:64, head, d * 128:(d + 1) * 128], K.CAT[0:64, slot, :W],
                         slot == 0, slot == 5 and kp == 1, ["WO", "CAT%d" % slot], ["ps%d" % bank])
                if kp == 0:
                    for half in range(2):
                        K.mm(PS[:, bank, :W], K.WOF[:, half, d * 128:(d + 1) * 128], K.FT[:, half, t0:t0 + W],
                             False, half == 1, ["WOF", "FT%d" % tc], ["ps%d" % bank])
                K.stt(X[:, d, t0:t0 + W], PS[:, bank, :W], MOD[:, 16 + d, v:v + 1], X[:, d, t0:t0 + W],
                      ALU.mult, ALU.add, ["ps%d" % bank, mk, "X%d.%d" % (tc, d)], ["X%d.%d" % (tc, d)])

    g0 = 0
    load_kv(0)
    prologue(jobs[0], g0)
    pending = None
    for ji, job in enumerate(jobs):
        n = len(kts_of(job[1]))
        body(job, g0, pending)
        pending = (lambda job=job, g0=g0: finish(job, g0))
        g0n = g0 + n + 1
        if ji + 1 < len(jobs):
            nj = jobs[ji + 1]
            if nj[0] != job[0]:
                pending()
                pending = None
                load_kv(nj[0])
            prologue(nj, g0n)
        g0 = g0n
    if pending is not None:
        pending()


def load_x(K, xin):
    for c in range(8):
        K.P.dma("sp", K.X[:, c, :], xin[:, c, :], w=["X%d.%d" % (tc, c) for tc in range(5)])


def store_x(K, xo, n=NT):
    for c in range(8):
        K.P.dma("sp", xo[:, c, :], K.X[:, c, 0:n], r=["X%d.%d" % (tc, c) for tc in range(5)],
                w=["xo%d" % c], final=True)


UW = NLAT + 30 + NCTX + 30


def ucol(tc):
    return 15 + TCS[tc][0] if tc < 4 else NLAT + 30 + 15


def alloc_opre(K, with_u=True):
    K.WPW1 = K.sb("WPW1", [128, 8, 2048], BF16)
    K.HC = [K.sb("HC%d" % i, [128, 8, 512], BF16) for i in range(2)]
    if with_u:
        K.U = K.sb("U", [128, 8, UW], BF16)
    K.BPW1 = K.sb("BPW1", [128, 16], F32)
    K.NEGB = K.sb("NEGB", [128, 16], F32)
    K.EG = [K.sb("EG%d" % i, [128, 512], F32) for i in range(2)]
    K.SG = [K.sb("SG%d" % i, [128, 512], F32) for i in range(2)]


def opre_segment(K, L, wpw1, bpw1, ntc=5):
    X, PS, P = K.X, K.PS, K.P
    P.dma("pool", K.WPW1[:], wpw1.rearrange("(kc p) n -> p kc n", p=128), w=["WPW1"])
    P.dma("sp", K.BPW1[:], bpw1, w=["BPW1"])
    K.ts("dve", K.NEGB[:], K.BPW1[:], -1.0, None, ALU.mult, None, ["BPW1"], ["NEGB"])
    K.memset("pool", K.U[:], 0.0, ["U%d" % tc for tc in range(5)])
    def nrm(tc, part):
        W = TCS[tc][1]
        hcb = K.HC[tc % 2]
        hkb = "HC%d" % (tc % 2)
        if part == 0:
            norm_sq(K, tc)
        else:
            norm_rest(K, L, 0, tc, lambda c, hcb=hcb, W=W: hcb[:, c, :W], lambda c, hkb=hkb: [hkb])

    nrm(0, 0)
    nrm(0, 1)
    for tc in range(ntc):
        t0, W = TCS[tc]
        hc = K.HC[tc % 2]
        hk = "HC%d" % (tc % 2)
        u0 = ucol(tc)
        for c in range(8):
            if tc + 1 < ntc and c == 0:
                nrm(tc + 1, 0)
            if tc + 1 < ntc and c == 2:
                nrm(tc + 1, 1)
            pb = c % 2
            bv, bg = c % 2, 2 + c % 2
            for kc in range(8):
                K.mm(PS[:, bv, :W], K.WPW1[:, kc, c * 128:(c + 1) * 128], hc[:, kc, :W],
                     kc == 0, kc == 7, ["WPW1", hk], ["ps%d" % bv])
            for kc in range(8):
                K.mm(PS[:, bg, :W], K.WPW1[:, kc, 1024 + c * 128:1024 + (c + 1) * 128], hc[:, kc, :W],
                     kc == 0, kc == 7, ["WPW1", hk], ["ps%d" % bg])
            K.act(K.EG[pb][:, :W], PS[:, bg, :W], AF.Exp, ["ps%d" % bg, "NEGB"], ["EG%d" % pb],
                  bias=K.NEGB[:, 8 + c:9 + c], scale=-1.0)
            K.act(K.EG[pb][:, :W], K.EG[pb][:, :W], AF.Ln, ["EG%d" % pb, "ONEB"], ["EG%d" % pb], bias=K.ONEB[:], scale=1.0)
            K.act(K.SG[pb][:, :W], K.EG[pb][:, :W], AF.Exp, ["EG%d" % pb], ["SG%d" % pb], scale=-1.0)
            K.stt(K.U[:, c, u0:u0 + W], PS[:, bv, :W], K.BPW1[:, c:c + 1], K.SG[pb][:, :W],
                  ALU.add, ALU.mult, ["ps%d" % bv, "BPW1", "SG%d" % pb], ["U%d" % tc])


def alloc_opost(K, with_u=True):
    if with_u:
        K.U = K.sb("U", [128, 8, UW], BF16)
    K.WPW2 = K.sb("WPW2", [128, 8, 1024], BF16)
    K.ACC = K.sb("ACC", [128, 8, 512], F32)
    K.SQC = K.sb("SQC", [128, 8, 512], BF16)
    K.Z = K.sb("Z", [128, 8, 512], BF16)
    K.COB = K.Z
    K.DIAG = [K.sb("DIAG%d" % i, [128, 31, 128], BF16) for i in range(2)]
    if not hasattr(K, "IDENT"):
        K.IDENT = K.sb("IDENT", [128, 128], BF16)
    K.WDW = K.sb("WDW", [128, 8, 31], F32)
    K.PV5 = K.sb("PV5", [128, 5, 8], F32)
    K.GB = K.sb("GB", [128, 8, 2], F32)
    K.MEAN = K.sb("MEAN", [128, 512], F32)
    K.VAR = K.sb("VAR", [128, 512], F32)
    K.LNV2 = K.sb("LNV2", [128, 512], F32)
    K.RSTD2 = K.sb("RSTD2", [128, 512], F32)
    K.TA = [K.sb("TA0", [128, 512], F32)] * 2
    K.TB = [K.sb("TB%d" % i, [128, 512], F32) for i in range(2)]
    K.TE = [K.sb("TE%d" % i, [128, 512], F32) for i in range(2)]
    K.T3 = [K.sb("T3%d" % i, [128, 512], F32) for i in range(2)]


def opost_segment(K, L, wdw, pv4, wpw2, ntc=5, ukeys=False, ident=None):
    X, PS, P = K.X, K.PS, K.P
    if ident is not None:
        P.dma("sp", K.IDENT[:], ident, w=["IDENT"])
    MOD = K.MOD[L]
    mk = "MOD%d" % L
    P.dma("pool", K.WPW2[:], wpw2.rearrange("(kc p) n -> p kc n", p=128), w=["WPW2"])
    P.dma("sp", K.WDW[:], wdw, w=["WDW"])
    P.dma("sp", K.PV5[:, 0:4, :], pv4, w=["PV5"])
    K.ts("dve", K.PV5[:, 4, :], K.PV5[:, 2, :], -1.0, None, ALU.mult, None, ["PV5"], ["PV5n"])
    for v in range(2):
        K.tt("dve", K.GB[:, :, v], MOD[:, 16:24, v], K.PV5[:, 3, :], ALU.mult, [mk, "PV5"], ["GB"])
    def conv_mm(tc, c):
        t0, W = TCS[tc]
        s0 = ucol(tc) - 15
        db = c % 2
        dk = "DIAG%d" % db
        idv = K.IDENT[:]
        wv = K.WDW[:, c, :]
        in0 = bass.AP(idv.tensor, idv.offset, [list(idv.ap[0]), [0, 31], [1, 128]])
        in1 = bass.AP(wv.tensor, wv.offset, [list(wv.ap[0]), [1, 31], [0, 128]])
        K.tt("dve", K.DIAG[db][:], in0, in1, ALU.mult, ["IDENT", "WDW"], [dk])
        bank = 2 + c % 4
        for j in range(31):
            K.mm(PS[:, bank, :W], K.DIAG[db][:, j, :], K.U[:, c, s0 + j:s0 + j + W], j == 0, j == 30,
                 [dk, "U"], ["ps%d" % bank])

    def conv_ev(tc, c):
        W = TCS[tc][1]
        bank = 2 + c % 4
        K.act(K.ACC[:, c, :W], PS[:, bank, :W], AF.Identity, ["ps%d" % bank, "PV5"], ["ACC%d" % c],
              bias=K.PV5[:, 0, c:c + 1], scale=1.0)

    def conv_tail(tc):
        W = TCS[tc][1]
        for c in range(8):
            K.cp("dve", K.COB[:, c, :W], K.ACC[:, c, :W], ["ACC%d" % c], ["Z%d" % c])
            K.tt("pool", K.SQC[:, c, :W], K.ACC[:, c, :W], K.ACC[:, c, :W], ALU.mult, ["ACC%d" % c], ["SQC%d" % c])

    def stats(tc):
        W = TCS[tc][1]
        for c in range(8):
            K.mm(PS[:, 6, :W], K.ONES[:], K.COB[:, c, :W], c == 0, c == 7, ["ONES", "Z%d" % c], ["ps6"])
        for c in range(8):
            K.mm(PS[:, 7, :W], K.ONES[:], K.SQC[:, c, :W], c == 0, c == 7, ["ONES", "SQC%d" % c], ["ps7"])
        K.act(K.MEAN[:, :W], PS[:, 6, :W], AF.Identity, ["ps6"], ["MEAN"], scale=1.0 / D)
        K.tt("pool", K.VAR[:, :W], K.MEAN[:, :W], K.MEAN[:, :W], ALU.mult, ["MEAN"], ["VAR"])
        K.stt(K.VAR[:, :W], PS[:, 7, :W], 1.0 / D, K.VAR[:, :W], ALU.mult, ALU.subtract, ["ps7", "VAR"], ["VAR"])
        K.act(K.LNV2[:, :W], K.VAR[:, :W], AF.Ln, ["VAR"], ["LNV2"], bias=K.EPSB[:], scale=1.0)
        K.act(K.RSTD2[:, :W], K.LNV2[:, :W], AF.Exp, ["LNV2"], ["RSTD2"], scale=-0.5)

    def ln_chain(tc):
        W = TCS[tc][1]
        for c in range(8):
            pb = c % 2
            K.tt("dve", K.TA[pb][:, :W], K.ACC[:, c, :W], K.MEAN[:, :W], ALU.subtract,
                 ["ACC%d" % c, "MEAN"], ["TA0"])
            K.stt(K.TB[pb][:, :W], K.TA[pb][:, :W], K.PV5[:, 1, c:c + 1], K.RSTD2[:, :W], ALU.mult, ALU.mult,
                  ["TA0", "PV5", "RSTD2"], ["TB%d" % pb])
            K.act(K.TE[pb][:, :W], K.TB[pb][:, :W], AF.Exp, ["TB%d" % pb, "PV5n"], ["TE%d" % pb],
                  bias=K.PV5[:, 4, c:c + 1], scale=-1.0)
            K.act(K.TE[pb][:, :W], K.TE[pb][:, :W], AF.Ln, ["TE%d" % pb, "ONEB"], ["TE%d" % pb], bias=K.ONEB[:], scale=1.0)
            K.act(K.TE[pb][:, :W], K.TE[pb][:, :W], AF.Exp, ["TE%d" % pb], ["TE%d" % pb], scale=-1.0)
            K.stt(K.Z[:, c, :W], K.TB[pb][:, :W], K.PV5[:, 2, c:c + 1], K.TE[pb][:, :W], ALU.add, ALU.mult,
                  ["TB%d" % pb, "PV5", "TE%d" % pb], ["Z%d" % c])

    def pw2(tc):
        t0, W = TCS[tc]
        v = 1 if tc == 4 else 0
        for d in range(8):
            bank = d % 2
            for c in range(8):
                K.mm(PS[:, bank, :W], K.WPW2[:, c, d * 128:(d + 1) * 128], K.Z[:, c, :W], c == 0, c == 7,
                     ["WPW2", "Z%d" % c], ["ps%d" % bank])
            K.act(K.T3[bank][:, :W], PS[:, bank, :W], AF.Identity, ["ps%d" % bank, mk, "GB"], ["T3%d" % bank],
                  bias=K.GB[:, d, v:v + 1], scale=MOD[:, 16 + d, v:v + 1])
            K.tt("dve", X[:, d, t0:t0 + W], X[:, d, t0:t0 + W], K.T3[bank][:, :W], ALU.add,
                 ["X%d.%d" % (tc, d), "T3%d" % bank], ["X%d.%d" % (tc, d)])

    for c in range(8):
        conv_mm(0, c)
        conv_ev(0, c)
    conv_tail(0)
    for tc in range(ntc):
        nxt = tc + 1 < ntc
        stats(tc)
        if nxt:
            conv_mm(tc + 1, 0)
            conv_mm(tc + 1, 1)
        ln_chain(tc)
        if nxt:
            conv_ev(tc + 1, 0)
            conv_ev(tc + 1, 1)
        pw2(tc)
        if nxt:
            for c in range(2, 8):
                conv_mm(tc + 1, c)
                conv_ev(tc + 1, c)
            conv_tail(tc + 1)


def _epre_io(K):
    win = K.dram_in("win", [D, 1536], F32)
    rope = K.dram_in("rope", [128, 2, NLAT], F32)
    qkg = K.dram_in("qkg", [128, 8], F32)
    blk = K.dram_in("blk", [128, 128], BF16)
    perm = K.dram_in("perm", [128, 128], F32)
    cs64 = K.dram_in("cs64", [128, 256], BF16)
    kt_o = K.dram_out("kt_o", [128, 2, NT], BF16)
    v_o = K.dram_out("v_o", [128, 18, 2, 2, 65], BF16)
    ab_o = K.dram_out("ab_o", [128, 18, 512], BF16)
    qt_o = K.dram_out("qt_o", [128, 6, NT], BF16)
    return win, rope, qkg, blk, perm, cs64, kt_o, v_o, ab_o, qt_o


def _epre_run(K, L, io):
    win, rope, qkg, blk, perm, cs64, kt_o, v_o, ab_o, qt_o = io
    epre_segment(K, L, win, rope, qkg, blk, perm, cs64, [kt_o[:, k_, :] for k_ in range(2)],
                 [v_o[:, :, a_, :, :] for a_ in range(2)], [ab_o[:, 6 * c_:6 * c_ + 6, :] for c_ in range(3)])
    K.P.dma("sp", qt_o, K.QT[:], r=["QT%d.%d" % (tc, fc) for tc in range(5) for fc in range(6)],
            w=["qt_o"], final=True)


def build_pA():
    nc = bass.Bass("TRN2", target_bir_lowering=False)
    with ExitStack() as es:
        K = Ctx(nc, es)
        xin = K.dram_in("x_in", [128, 8, NT], F32)
        io = _epre_io(K)
        alloc_common(K)
        alloc_eps(K)
        load_mod(K, 0, "modA")
        load_x(K, xin)
        alloc_norm(K)
        alloc_epre(K)
        _epre_run(K, 0, io)
        K.P.emit()
    return nc


def build_pB():
    nc = bass.Bass("TRN2", target_bir_lowering=False)
    with ExitStack() as es:
        K = Ctx(nc, es)
        xin = K.dram_in("x_in", [128, 8, NT], F32)
        qt_in = K.dram_in("qt_in", [128, 6, NT], BF16)
        kt_all = K.dram_in("kt_all", [4, 128, 2, NT], BF16)
        v_all = K.dram_in("v_all", [4, 128, 18, 2, 2, 65], BF16)
        ab_all = K.dram_in("ab_all", [4, 128, 18, 512], BF16)
        dft = K.dram_in("dft", [64, 128, 2, NLAT], BF16)
        dftc = K.dram_in("dftc", [2, 128, 2, 256], BF16)
        wout = K.dram_in("wout", [D, D], F32)
        w1 = K.dram_in("w1", [D, 4 * D], F32)
        w2 = K.dram_in("w2", [4 * D, D], F32)
        wpw1 = K.dram_in("wpw1", [D, 2 * D], F32)
        bpw1 = K.dram_in("bpw1", [128, 16], F32)
        xo = K.dram_out("x_o", [128, 8, NT], F32)
        uo = K.dram_out("u_o", [128, 8, UW], BF16)
        alloc_common(K)
        alloc_eps(K)
        load_mod(K, 0, "modA")
        load_mod(K, 1, "modB")
        load_x(K, xin)
        with ExitStack() as ph:
            K.es = ph
            alloc_epost(K)
            K.P.dma("sp", K.QT[:], qt_in, w=["QT%d.%d" % (tc, fc) for tc in range(5) for fc in range(6)])
            epost_segment(K, 0, [kt_all[:, :, k_, :] for k_ in range(2)],
                          [v_all[:, :, :, a_, :, :] for a_ in range(2)],
                          [ab_all[:, :, 6 * c_:6 * c_ + 6, :] for c_ in range(3)], dft, dftc, wout)
            K.P.barrier()
        with ExitStack() as ph:
            K.es = ph
            alloc_norm(K)
            alloc_mlp(K)
            mlp_segment(K, 0, w1, w2)
            K.P.barrier()
        with ExitStack() as ph:
            K.es = ph
            alloc_norm(K)
            alloc_opre(K)
            opre_segment(K, 1, wpw1, bpw1)
            K.P.dma("sp", uo, K.U[:], r=["U%d" % tc for tc in range(5)], w=["u_o"], final=True)
            K.P.barrier()
        K.es = es
        store_x(K, xo)
        K.P.emit()
    return nc


def build_pC(last):
    nc = bass.Bass("TRN2", target_bir_lowering=False)
    ntc = 4 if last else 5
    with ExitStack() as es:
        K = Ctx(nc, es)
        xin = K.dram_in("x_in", [128, 8, NT], F32)
        u_in = K.dram_in("u_in", [128, 8, UW], BF16)
        wdw = K.dram_in("wdw", [128, 8, 31], F32)
        pv4 = K.dram_in("pv4", [128, 4, 8], F32)
        wpw2 = K.dram_in("wpw2", [D, D], F32)
        ident = K.dram_in("ident", [128, 128], BF16)
        w1 = K.dram_in("w1", [D, 4 * D], F32)
        w2 = K.dram_in("w2", [4 * D, D], F32)
        if not last:
            io = _epre_io(K)
            xo = K.dram_out("x_o", [128, 8, NT], F32)
        else:
            xo = K.dram_out("x_o", [128, 8, NLAT], F32)
        alloc_common(K)
        alloc_eps(K)
        load_mod(K, 0, "modA")
        if not last:
            load_mod(K, 1, "modB")
        load_x(K, xin)
        with ExitStack() as ph:
            K.es = ph
            alloc_opost(K)
            K.P.dma("sp", K.U[:], u_in, w=["U"])
            opost_segment(K, 0, wdw, pv4, wpw2, ntc, ident=ident)
            K.P.barrier()
        with ExitStack() as ph:
            K.es = ph
            alloc_norm(K)
            alloc_mlp(K)
            mlp_segment(K, 0, w1, w2, ntc)
            K.P.barrier()
        if not last:
            with ExitStack() as ph:
                K.es = ph
                alloc_norm(K)
                alloc_epre(K)
                _epre_run(K, 1, io)
                K.P.barrier()
        K.es = es
        store_x(K, xo, NLAT if last else NT)
        K.P.emit()
    return nc


_PROGS = {}


def _prog(name):
    if name not in _PROGS:
        _PROGS[name] = {"mod": build_mod, "A": build_pA, "B": build_pB,
                        "C": lambda: build_pC(False), "D": lambda: build_pC(True)}[name]()
    return _PROGS[name]


def _run(name, in_maps):
    res = run_bass_kernel_spmd(_prog(name), in_maps, core_ids=list(range(NCORES)))
    return res.results


def kernel_multi(x, c, ctx, c_ctx, ada_w, ada_b, norm1_g, norm2_g, mlp_w1, mlp_w2,
           attn_w_in, q_norm_g, k_norm_g, attn_w_out,
           conv_w_pw1, conv_b_pw1, conv_w_dw, conv_b_dw, conv_ln_g, conv_ln_b,
           conv_w_pw2, conv_b_pw2):
    f32 = lambda a: np.ascontiguousarray(np.asarray(a, dtype=np.float32))
    x, c, ctx, c_ctx = f32(x), f32(c), f32(ctx), f32(c_ctx)
    ada_w, ada_b, norm1_g, norm2_g = f32(ada_w), f32(ada_b), f32(norm1_g), f32(norm2_g)
    mlp_w1, mlp_w2, attn_w_in, attn_w_out = f32(mlp_w1), f32(mlp_w2), f32(attn_w_in), f32(attn_w_out)
    conv_w_pw1, conv_w_pw2, conv_w_dw = f32(conv_w_pw1), f32(conv_w_pw2), f32(conv_w_dw)
    C = consts()
    cores = [(i // 4, i % 4) for i in range(NCORES)]
    maps = []
    for b, r in cores:
        cvec = np.ascontiguousarray(np.stack([fm_vec(c[b]), fm_vec(c_ctx)], axis=-1))
        adab = fm_vec(ada_b[r])
        adab = np.ascontiguousarray(np.repeat(adab[:, :, None], 2, axis=2))
        ng = np.ascontiguousarray(np.stack([fm_vec(norm1_g[r]), fm_vec(norm2_g[r])], axis=1))
        maps.append(dict(adaw=ada_w[r], adab=adab, cvec=cvec, ng=ng))
    rm = _run("mod", maps)
    mod = {(b, L): np.asarray(rm[b * 4 + L]["modo"]) for b in range(2) for L in range(4)}

    def qkg_of(j):
        g = np.zeros((128, 8), np.float32)
        for fc in range(8):
            src = np.asarray(q_norm_g[j] if fc < 6 else k_norm_g[j], np.float32)
            g[:64, fc] = src
            g[64:, fc] = src
        return g

    def epre_inputs(j, r):
        return dict(win=perm_win(attn_w_in[j]), rope=C["rope"][r], qkg=qkg_of(j), blk=C["blk"],
                    perm=C["perm"], cs64=C["cs64"])

    def gather(res, key, b):
        return np.ascontiguousarray(np.stack([np.asarray(res[b * 4 + rr][key]) for rr in range(4)], 0))

    def epost_inputs(res, j, b, r, i):
        return dict(qt_in=np.asarray(res[i]["qt_o"]), kt_all=gather(res, "kt_o", b), v_all=gather(res, "v_o", b),
                    ab_all=gather(res, "ab_o", b), dft=C["dft"][r], dftc=C["dftc"], wout=attn_w_out[j])

    def u_with_halo(res, b, r):
        u = np.array(np.asarray(res[b * 4 + r]["u_o"]))
        if r > 0:
            ul = np.asarray(res[b * 4 + r - 1]["u_o"])
            u[:, :, 0:15] = ul[:, :, NLAT:NLAT + 15]
        if r < 3:
            ur = np.asarray(res[b * 4 + r + 1]["u_o"])
            u[:, :, NLAT + 15:NLAT + 30] = ur[:, :, 15:30]
        return np.ascontiguousarray(u)

    def opost_inputs(j):
        wdw = np.ascontiguousarray(conv_w_dw[j].reshape(31, 8, 128).transpose(2, 1, 0))
        pv4 = np.ascontiguousarray(np.stack([fm_vec(conv_b_dw[j]), fm_vec(conv_ln_g[j]), fm_vec(conv_ln_b[j]),
                                             fm_vec(conv_b_pw2[j])], axis=1))
        return dict(wdw=wdw, pv4=pv4, wpw2=conv_w_pw2[j], ident=_bf(np.eye(128, dtype=np.float32)))

    maps = []
    for b, r in cores:
        xt = np.concatenate([x[b, r * NLAT:(r + 1) * NLAT], ctx[b]], axis=0)
        m = dict(x_in=fm_tokens(xt), modA=mod[(b, 0)])
        m.update(epre_inputs(0, r))
        maps.append(m)
    xcur = [m["x_in"] for m in maps]
    res = _run("A", maps)
    for L in (0, 2):
        j = L // 2
        maps = []
        for i, (b, r) in enumerate(cores):
            m = dict(x_in=xcur[i], modA=mod[(b, L)], modB=mod[(b, L + 1)], w1=mlp_w1[L], w2=mlp_w2[L],
                     wpw1=conv_w_pw1[j], bpw1=fm_vec(conv_b_pw1[j]))
            m.update(epost_inputs(res, j, b, r, i))
            maps.append(m)
        res = _run("B", maps)
        xcur = [np.asarray(res[i]["x_o"]) for i in range(NCORES)]
        last = L == 2
        maps = []
        for i, (b, r) in enumerate(cores):
            m = dict(x_in=xcur[i], u_in=u_with_halo(res, b, r), modA=mod[(b, L + 1)],
                     w1=mlp_w1[L + 1], w2=mlp_w2[L + 1])
            m.update(opost_inputs(j))
            if not last:
                m["modB"] = mod[(b, L + 2)]
                m.update(epre_inputs(j + 1, r))
            maps.append(m)
        res = _run("D" if last else "C", maps)
        if not last:
            xcur = [np.asarray(res[i]["x_o"]) for i in range(NCORES)]
    out = np.zeros((2, 4 * NLAT, D), np.float32)
    for i, (b, r) in enumerate(cores):
        xo = np.asarray(res[i]["x_o"])
        out[b, r * NLAT:(r + 1) * NLAT] = xo.transpose(2, 1, 0).reshape(NLAT, D)
    return out


GROUPS = [[0, 1, 2, 3], [4, 5, 6, 7]]


def mod_segment(K, adaw, adab, cvec, ng, m_loc, m_gat):
    nc, P, PS = K.nc, K.P, K.PS
    CV = K.sb("CV", [128, 8, 2], F32)
    ABs = K.sb("ABs", [128, 48, 2], F32)
    NG = K.sb("NG", [128, 2, 8], F32)
    E1 = K.sb("E1", [128, 8, 2], F32)
    S = K.sb("S", [128, 8, 2], BF16)
    MODL = K.sb("MODL", [128, 48, 2], F32)
    WM = [K.sb("WM%d" % i, [128, 8, 1024], BF16) for i in range(2)]
    P.dma("sp", CV[:], cvec, w=["CV"])
    P.dma("sp", ABs[:], adab, w=["ABs"])
    P.dma("sp", NG[:], ng, w=["NG"])
    K.act(E1[:], CV[:], AF.Exp, ["CV"], ["E1"], scale=-1.0)
    K.ts("dve", E1[:], E1[:], 1.0, None, ALU.add, None, ["E1"], ["E1"])
    K.recip(E1[:], E1[:], ["E1"], ["E1"])
    K.tt("dve", S[:], CV[:], E1[:], ALU.mult, ["CV", "E1"], ["S"])
    adaw_v = adaw.rearrange("(kc p) n -> p kc n", p=128)
    for j in range(6):
        wm = WM[j % 2]
        wk = "WM%d" % (j % 2)
        P.dma("pool", wm[:], adaw_v[:, :, j * 1024:(j + 1) * 1024], w=[wk])
        bank = j % 2
        for c in range(8):
            for kc in range(8):
                K.mm(PS[:, bank, c * 2:c * 2 + 2], wm[:, kc, c * 128:(c + 1) * 128], S[:, kc, :],
                     kc == 0, kc == 7, [wk, "S"], ["ps%d" % bank])
        K.tt("dve", MODL[:, j * 8:(j + 1) * 8, :],
             PS[:, bank, 0:16].rearrange("p (c v) -> p c v", v=2),
             ABs[:, j * 8:(j + 1) * 8, :], ALU.add, ["ps%d" % bank, "ABs"], ["MODL%d" % j])
    for j, gi in ((1, 0), (4, 1)):
        for v in range(2):
            K.stt(MODL[:, j * 8:(j + 1) * 8, v], MODL[:, j * 8:(j + 1) * 8, v], 1.0, NG[:, gi, :],
                  ALU.add, ALU.mult, ["MODL%d" % j, "NG"], ["MODL%d" % j])
    P.dma("sp", m_loc, MODL[:].rearrange("p a v -> p (a v)"), r=["MODL%d" % j for j in range(6)], w=["m_loc"])
    P.cc("AllGather", GROUPS, m_loc, m_gat, r=["m_loc"], w=["m_gat"])
    mg = m_gat.rearrange("(r p) (a v) -> r p a v", p=128, v=2)
    for L in range(4):
        P.dma("sp", K.MOD[L][:], mg[L], r=["m_gat"], w=["MOD%d" % L])


def halo_segment(K, e_loc, e_gat):
    P = K.P
    U = K.U
    E4 = K.sb("E4", [128, 4, 8, 2, 15], BF16)
    HAL = K.sb("HAL", [128, 2, 8, 15], F32)
    ED = K.sb("ED", [128, 8, 2, 15], BF16)
    ukeys = ["U%d" % tc for tc in range(5)]
    K.cp("pool", ED[:, :, 0, :], U[:, :, 15:30], ukeys, ["ED"])
    K.cp("pool", ED[:, :, 1, :], U[:, :, NLAT:NLAT + 15], ukeys, ["ED"])
    P.dma("sp", e_loc, ED[:].rearrange("p c s e -> p (c s e)"), r=["ED"], w=["e_loc"])
    P.cc("AllGather", GROUPS, e_loc, e_gat, r=["e_loc"], w=["e_gat"])
    P.dma("sp", E4[:], e_gat.rearrange("(r p) (c s e) -> p r c s e", p=128, c=8, s=2), r=["e_gat"], w=["E4"])
    for side in range(2):
        src_s = 1 - side
        for rr in range(4):
            mcol = K.HMASK[:, side * 4 + rr:side * 4 + rr + 1]
            if rr == 0:
                K.ts("dve", HAL[:, side], E4[:, rr, :, src_s, :], mcol, None, ALU.mult, None,
                     ["E4", "HMASK"], ["HAL%d" % side])
            else:
                K.stt(HAL[:, side], E4[:, rr, :, src_s, :], mcol, HAL[:, side], ALU.mult, ALU.add,
                      ["E4", "HMASK", "HAL%d" % side], ["HAL%d" % side])
    K.cp("dve", U[:, :, 0:15], HAL[:, 0], ["HAL0"], ["U0"])
    K.cp("dve", U[:, :, NLAT + 15:NLAT + 30], HAL[:, 1], ["HAL1"], ["U3"])


def build_fused():
    nc = bass.Bass("TRN2", target_bir_lowering=False)
    with ExitStack() as es:
        K = Ctx(nc, es)
        P = K.P
        di = K.dram_in
        xin = di("x_in", [128, 8, NT], F32)
        adaw = di("adaw", [D, 6 * D], F32)
        adab = di("adab", [128, 48, 2], F32)
        cvec = di("cvec", [128, 8, 2], F32)
        ng = di("ng", [128, 2, 8], F32)
        hmask = di("hmask", [128, 8], F32)
        rope = di("rope", [128, 2, NLAT], F32)
        blk = di("blk", [128, 128], BF16)
        perm = di("perm", [128, 128], F32)
        cs64 = di("cs64", [128, 256], BF16)
        ident = di("ident", [128, 128], BF16)
        dft = di("dft", [64, 128, 2, NLAT], BF16)
        dftc = di("dftc", [2, 128, 2, 256], BF16)
        w1 = [di("w1_%d" % L, [D, 4 * D], F32) for L in range(4)]
        w2 = [di("w2_%d" % L, [4 * D, D], F32) for L in range(4)]
        win = [di("win_%d" % j, [D, 1536], F32) for j in range(2)]
        qkg = [di("qkg_%d" % j, [128, 8], F32) for j in range(2)]
        wout = [di("wout_%d" % j, [D, D], F32) for j in range(2)]
        wpw1 = [di("wpw1_%d" % j, [D, 2 * D], F32) for j in range(2)]
        bpw1 = [di("bpw1_%d" % j, [128, 16], F32) for j in range(2)]
        wdw = [di("wdw_%d" % j, [128, 8, 31], F32) for j in range(2)]
        pv4 = [di("pv4_%d" % j, [128, 4, 8], F32) for j in range(2)]
        wpw2 = [di("wpw2_%d" % j, [D, D], F32) for j in range(2)]
        xo = K.dram_out("x_o", [128, 8, NLAT], F32)

        def internal(name, shape, dt):
            return nc.dram_tensor(name, list(shape), dt, kind="Internal").ap()

        m_loc = internal("m_loc", [128, 96], F32)
        m_gat = internal("m_gat", [512, 96], F32)
        alloc_common(K)
        alloc_eps(K)
        K.HMASK = K.sb("HMASK", [128, 8], F32)
        P.dma("sp", K.HMASK[:], hmask, w=["HMASK"])
        for L in range(4):
            K.MOD[L] = K.sb("MODL%d" % L, [128, 48, 2], F32)
        load_x(K, xin)
        with ExitStack() as ph:
            K.es = ph
            mod_segment(K, adaw, adab, cvec, ng, m_loc, m_gat)
            P.barrier()
        K.es = es
        K.BLK = K.sb("BLK", [128, 128], BF16)
        K.PERM = K.sb("PERM", [128, 128], F32)
        K.CS64 = K.sb("CS64", [128, 256], BF16)
        P.dma("sp", K.BLK[:], blk, w=["BLK"])
        P.dma("sp", K.PERM[:], perm, w=["PERM"])
        P.dma("sp", K.CS64[:], cs64, w=["CS64"])
        K.IDENT = K.sb("IDENT", [128, 128], BF16)
        P.dma("sp", K.IDENT[:], ident, w=["IDENT"])
        for L in range(4):
            j = L // 2
            ntc = 4 if L == 3 else 5
            if L % 2 == 0:
                kt_loc = [internal("kt_loc%d_%d" % (L, k_), [128, NT], BF16) for k_ in range(2)]
                kt_gat = [internal("kt_gat%d_%d" % (L, k_), [512, NT], BF16) for k_ in range(2)]
                v_loc = [internal("v_loc%d_%d" % (L, k_), [128, 18 * 130], BF16) for k_ in range(2)]
                v_gat = [internal("v_gat%d_%d" % (L, k_), [512, 18 * 130], BF16) for k_ in range(2)]
                ab_loc = [internal("ab_loc%d_%d" % (L, k_), [128, 6 * 512], BF16) for k_ in range(3)]
                ab_gat = [internal("ab_gat%d_%d" % (L, k_), [512, 6 * 512], BF16) for k_ in range(3)]
                with ExitStack() as ql:
                    K.es = ql
                    K.QT = K.sb("QT", [128, 6, NT], BF16)
                    with ExitStack() as ph:
                        K.es = ph
                        alloc_norm(K)
                        alloc_epre_noqt(K)
                        abk = [[], [], []]
                        for tc in range(5):
                            t0_, W_ = TCS[tc]
                            for t_ in range(0, W_ // 128, 2):
                                abk[(t0_ // 128 + t_) // 6].append("ab_o%d.%d" % (tc, t_))

                        def after_tc(tc, ab_loc=ab_loc, ab_gat=ab_gat, abk=abk):
                            k_ = {1: 0, 2: 1, 4: 2}.get(tc)
                            if k_ is not None:
                                P.cc("AllGather", GROUPS, ab_loc[k_], ab_gat[k_], r=abk[k_], w=["ab_gat%d" % k_])

                        epre_segment(K, L, win[j], rope, qkg[j], blk, perm, cs64,
                                     kt_loc,
                                     [v.rearrange("p (t s e) -> p t s e", t=18, s=2) for v in v_loc],
                                     [a.rearrange("p (t n) -> p t n", t=6) for a in ab_loc],
                                     final=False, load_consts=False, after_tc=after_tc)
                        for k_ in range(2):
                            P.cc("AllGather", GROUPS, kt_loc[k_], kt_gat[k_], r=["kt_o%d" % k_], w=["kt_gat%d" % k_])
                            P.cc("AllGather", GROUPS, v_loc[k_], v_gat[k_],
                                 r=["v_o%d.%d" % (tc, k_) for tc in range(5)], w=["v_gat%d" % k_])
                        P.barrier(keep=["kt_gat0", "kt_gat1", "v_gat0", "v_gat1"])
                    with ExitStack() as ph:
                        K.es = ph
                        alloc_epost(K, with_qt=False)
                        epost_segment(K, L,
                                      [k_.rearrange("(r p) t -> r p t", p=128) for k_ in kt_gat],
                                      [v.rearrange("(r p) (t s e) -> r p t s e", p=128, t=18, s=2) for v in v_gat],
                                      [a.rearrange("(r p) (t n) -> r p t n", p=128, t=6) for a in ab_gat],
                                      dft, dftc, wout[j])
                        P.barrier()
            else:
                e_loc = internal("e_loc%d" % L, [128, 240], BF16)
                e_gat = internal("e_gat%d" % L, [512, 240], BF16)
                with ExitStack() as ul:
                    K.es = ul
                    K.U = K.sb("U", [128, 8, UW], BF16)
                    with ExitStack() as ph:
                        K.es = ph
                        alloc_norm(K)
                        alloc_opre(K, with_u=False)
                        opre_segment(K, L, wpw1[j], bpw1[j], ntc)
                        halo_segment(K, e_loc, e_gat)
                        P.barrier()
                    with ExitStack() as ph:
                        K.es = ph
                        alloc_opost(K, with_u=False)
                        opost_segment(K, L, wdw[j], pv4[j], wpw2[j], ntc, ukeys=True)
                        P.barrier()
            with ExitStack() as ph:
                K.es = ph
                alloc_norm(K)
                alloc_mlp(K)
                def store_tc(tc):
                    t0_, W_ = TCS[tc]
                    P.dma("sp", xo[:, :, t0_:t0_ + W_], K.X[:, :, t0_:t0_ + W_], r=xkeys(tc), w=["xo%d" % tc], final=True)

                mlp_segment(K, L, w1[L], w2[L], ntc, after_last=store_tc if L == 3 else None)
                P.barrier()
        K.es = es
        P.emit()
    return nc


def kernel(x, c, ctx, c_ctx, ada_w, ada_b, norm1_g, norm2_g, mlp_w1, mlp_w2,
           attn_w_in, q_norm_g, k_norm_g, attn_w_out,
           conv_w_pw1, conv_b_pw1, conv_w_dw, conv_b_dw, conv_ln_g, conv_ln_b,
           conv_w_pw2, conv_b_pw2):
    f32 = lambda a: np.ascontiguousarray(np.asarray(a, dtype=np.float32))
    x, c, ctx, c_ctx = f32(x), f32(c), f32(ctx), f32(c_ctx)
    ada_w, ada_b, norm1_g, norm2_g = f32(ada_w), f32(ada_b), f32(norm1_g), f32(norm2_g)
    mlp_w1, mlp_w2, attn_w_in, attn_w_out = f32(mlp_w1), f32(mlp_w2), f32(attn_w_in), f32(attn_w_out)
    conv_w_pw1, conv_w_pw2, conv_w_dw = f32(conv_w_pw1), f32(conv_w_pw2), f32(conv_w_dw)
    C = consts()
    shared = {}
    for L in range(4):
        shared["w1_%d" % L] = mlp_w1[L]
        shared["w2_%d" % L] = mlp_w2[L]
    for j in range(2):
        g = np.zeros((128, 8), np.float32)
        for fc in range(8):
            src = np.asarray(q_norm_g[j] if fc < 6 else k_norm_g[j], np.float32)
            g[:64, fc] = src
            g[64:, fc] = src
        shared["win_%d" % j] = perm_win(attn_w_in[j])
        shared["qkg_%d" % j] = g
        shared["wout_%d" % j] = attn_w_out[j]
        shared["wpw1_%d" % j] = conv_w_pw1[j]
        shared["bpw1_%d" % j] = fm_vec(conv_b_pw1[j])
        shared["wdw_%d" % j] = np.ascontiguousarray(conv_w_dw[j].reshape(31, 8, 128).transpose(2, 1, 0))
        shared["pv4_%d" % j] = np.ascontiguousarray(np.stack(
            [fm_vec(conv_b_dw[j]), fm_vec(conv_ln_g[j]), fm_vec(conv_ln_b[j]), fm_vec(conv_b_pw2[j])], axis=1))
        shared["wpw2_%d" % j] = conv_w_pw2[j]
    shared.update(blk=C["blk"], perm=C["perm"], cs64=C["cs64"], dftc=C["dftc"],
                  ident=_bf(np.eye(128, dtype=np.float32)))
    maps = []
    for i in range(NCORES):
        b, r = i // 4, i % 4
        xt = np.concatenate([x[b, r * NLAT:(r + 1) * NLAT], ctx[b]], axis=0)
        adab = fm_vec(ada_b[r])
        hm = np.zeros((128, 8), np.float32)
        if r > 0:
            hm[:, r - 1] = 1.0
        if r < 3:
            hm[:, 4 + r + 1] = 1.0
        m = dict(shared)
        m.update(x_in=fm_tokens(xt), adaw=ada_w[r],
                 adab=np.ascontiguousarray(np.repeat(adab[:, :, None], 2, axis=2)),
                 cvec=np.ascontiguousarray(np.stack([fm_vec(c[b]), fm_vec(c_ctx)], axis=-1)),
                 ng=np.ascontiguousarray(np.stack([fm_vec(norm1_g[r]), fm_vec(norm2_g[r])], axis=1)),
                 hmask=hm, rope=C["rope"][r], dft=C["dft"][r])
        maps.append(m)
    if "F" not in _PROGS:
        _PROGS["F"] = build_fused()
    res = run_bass_kernel_spmd(_PROGS["F"], maps, core_ids=list(range(NCORES))).results
    out = np.zeros((2, 4 * NLAT, D), np.float32)
    for i in range(NCORES):
        b, r = i // 4, i % 4
        xo = np.asarray(res[i]["x_o"])
        out[b, r * NLAT:(r + 1) * NLAT] = xo.transpose(2, 1, 0).reshape(NLAT, D)
    return out
```

```python
import math
from contextlib import ExitStack

import numpy as np
import ml_dtypes
import concourse.bass as bass
import concourse.mybir as mybir
from concourse.bass_utils import run_bass_kernel_spmd

F32 = mybir.dt.float32
BF16 = mybir.dt.bfloat16
AF = mybir.ActivationFunctionType
ALU = mybir.AluOpType
NPBF = ml_dtypes.bfloat16

D = 1024
NLAT = 2048
NCTX = 256
NT = NLAT + NCTX
TCS = [(0, 512), (512, 512), (1024, 512), (1536, 512), (2048, 256)]
EPS = 1e-6
NCORES = 8


class _Op:
    __slots__ = ("eng", "fn", "deps", "dma", "sem", "semval", "signal", "count", "idx", "final", "inc")


class Prog:
    ENGS = ("pe", "act", "dve", "pool", "sp")

    def __init__(self, nc):
        self.nc = nc
        self.ops = []
        self.state = {}
        self.dma_sem_of = {}
        self.dma_sem_cnt = []
        self.finals = []

    def _add(self, eng, fn, r, w, dma=False, final=False, inc=16):
        op = _Op()
        op.inc = inc
        op.eng, op.fn, op.dma, op.final = eng, fn, dma, final
        op.signal = False
        op.count = None
        op.idx = len(self.ops)
        deps = {}
        for k in r:
            st = self.state.setdefault(k, [None, []])
            if st[0] is not None:
                deps[st[0]] = "raw"
        for k in w:
            st = self.state.setdefault(k, [None, []])
            if st[0] is not None:
                deps[st[0]] = "waw"
            for ri in st[1]:
                if ri not in deps:
                    deps[ri] = "war"
        for k in r:
            rl = self.state[k][1]
            if not dma:
                rl[:] = [ri for ri in rl if self.ops[ri].dma or self.ops[ri].eng != eng]
            rl.append(op.idx)
        for k in w:
            self.state[k] = [op.idx, []]
        op.deps = []
        latest = {}
        for di, kind in deps.items():
            dop = self.ops[di]
            if dop.dma:
                op.deps.append(di)
            elif dop.eng == eng:
                if eng == "pe":
                    continue
                if kind == "war":
                    continue
                latest[dop.eng] = max(latest.get(dop.eng, -1), di)
            else:
                latest[dop.eng] = max(latest.get(dop.eng, -1), di)
        for di in latest.values():
            self.ops[di].signal = True
            op.deps.append(di)
        if dma:
            key = w[0]
            if key not in self.dma_sem_of:
                self.dma_sem_of[key] = len(self.dma_sem_cnt)
                self.dma_sem_cnt.append(0)
            si = self.dma_sem_of[key]
            self.dma_sem_cnt[si] += inc
            op.sem = si
            op.semval = self.dma_sem_cnt[si]
            if final:
                self.finals.append(op.idx)
        self.ops.append(op)
        return op

    def barrier(self, keep=()):
        last = {}
        lastdma = {}
        kept = {}
        for k in keep:
            st = self.state.get(k)
            if st is not None and st[0] is not None and self.ops[st[0]].dma:
                kept[k] = st[0]
        skip_sems = set(self.ops[i].sem for i in kept.values())
        seen = getattr(self, "_bar_seen", {})
        for op in self.ops:
            if op.dma:
                if op.sem not in skip_sems and op.semval > seen.get(op.sem, 0):
                    lastdma[op.sem] = op.idx
            elif op.fn is not None:
                last[op.eng] = op.idx
        for si, li in lastdma.items():
            seen[si] = self.ops[li].semval
        self._bar_seen = seen
        for e in self.ENGS:
            op = _Op()
            op.eng, op.fn, op.dma, op.final = e, None, False, False
            op.signal = False
            op.count = None
            op.idx = len(self.ops)
            op.deps = []
            for e2, li in last.items():
                if e2 != e:
                    self.ops[li].signal = True
                    op.deps.append(li)
            for si, li in lastdma.items():
                op.deps.append(li)
            self.ops.append(op)
        self.state = {k: [i, []] for k, i in kept.items()}

    def op(self, eng, fn, r=(), w=()):
        return self._add(eng, fn, tuple(r), tuple(w))

    def dma(self, q, out, in_, r=(), w=(), final=False):
        assert q in ("sp", "pool")
        return self._add(q, lambda e: e.dma_start(out=out, in_=in_), tuple(r), tuple(w),
                         dma=True, final=final)

    def cc(self, kind, groups, in_ap, out_ap, r=(), w=()):
        return self._add("pool", lambda e: e.collective_compute(kind, ALU.bypass, replica_groups=groups,
                                                                ins=[in_ap], outs=[out_ap]),
                         tuple(r), tuple(w), dma=True, inc=1)

    def emit(self):
        nc = self.nc
        cnt = {e: 0 for e in self.ENGS}
        for op in self.ops:
            if op.dma or op.fn is None:
                continue
            if op.signal:
                cnt[op.eng] += 1
                op.count = cnt[op.eng]
        for e in self.ENGS:
            assert cnt[e] < 60000, (e, cnt[e])
        for v in self.dma_sem_cnt:
            assert v < 60000, v
        with ExitStack() as es:
            esem = {e: es.enter_context(nc.semaphore("s_" + e)) for e in ("pe", "act", "dve", "pool")}
            dsem = [es.enter_context(nc.semaphore("d%d" % i)) for i in range(len(self.dma_sem_cnt))]
            block = es.enter_context(nc.Block())
            ops = self.ops
            finals = self.finals

            def run(ename, e):
                waited = {}
                for op in ops:
                    if op.eng != ename:
                        continue
                    for di in op.deps:
                        dop = ops[di]
                        if dop.dma:
                            sem, val, key = dsem[dop.sem], dop.semval, ("d", dop.sem)
                        else:
                            sem, val, key = esem[dop.eng], dop.count, ("e", dop.eng)
                        if waited.get(key, 0) >= val:
                            continue
                        waited[key] = val
                        e.wait_ge(sem, val)
                    if op.fn is None:
                        continue
                    ins = op.fn(e)
                    if op.dma:
                        ins.then_inc(dsem[op.sem], op.inc)
                    elif op.signal:
                        ins.then_inc(esem[ename], 1)
                if ename == "sp":
                    for fi in finals:
                        fop = ops[fi]
                        e.wait_ge(dsem[fop.sem], fop.semval)

            @block.tensor
            def _(e):
                run("pe", e)

            @block.scalar
            def _(e):
                run("act", e)

            @block.vector
            def _(e):
                run("dve", e)

            @block.gpsimd
            def _(e):
                run("pool", e)

            @block.sync
            def _(e):
                run("sp", e)


class Ctx:
    def __init__(self, nc, es):
        self.nc = nc
        self.es = es
        self.P = Prog(nc)
        self.din = {}
        self.dout = {}

    def dram_in(self, name, shape, dt):
        t = self.nc.dram_tensor(name, list(shape), dt, kind="ExternalInput").ap()
        self.din[name] = t
        return t

    def dram_out(self, name, shape, dt):
        t = self.nc.dram_tensor(name, list(shape), dt, kind="ExternalOutput").ap()
        self.dout[name] = t
        return t

    def sb(self, name, shape, dt):
        self.nsb = getattr(self, "nsb", 0) + 1
        return self.es.enter_context(self.nc.sbuf_tensor("%s_%d" % (name, self.nsb), list(shape), dt))

    def mm(self, out, lhsT, rhs, start, stop, r, w):
        self.P.op("pe", lambda e: e.matmul(out, lhsT, rhs, start=start, stop=stop), r, w)

    def act(self, out, in_, func, r, w, bias=None, scale=None):
        kw = {}
        if bias is not None:
            kw["bias"] = bias
        if scale is not None:
            kw["scale"] = scale
        self.P.op("act", lambda e: e.activation(out, in_, func, **kw), r, w)

    def tt(self, eng, out, in0, in1, op, r, w):
        self.P.op(eng, lambda e: e.tensor_tensor(out, in0, in1, op), r, w)

    def ts(self, eng, out, in0, s1, s2, op0, op1, r, w):
        if op1 is None:
            self.P.op(eng, lambda e: e.tensor_scalar(out, in0, s1, None, op0), r, w)
        else:
            self.P.op(eng, lambda e: e.tensor_scalar(out, in0, s1, s2, op0, op1), r, w)

    def stt(self, out, in0, scalar, in1, op0, op1, r, w):
        self.P.op("dve", lambda e: e.scalar_tensor_tensor(out, in0, scalar, in1, op0, op1), r, w)

    def cp(self, eng, out, in_, r, w):
        self.P.op(eng, lambda e: e.tensor_copy(out, in_), r, w)

    def recip(self, out, in_, r, w):
        self.P.op("dve", lambda e: e.reciprocal(out, in_), r, w)

    def memset(self, eng, ap, val, w):
        self.P.op(eng, lambda e: e.memset(ap, val), (), w)


def _bf(a):
    return np.ascontiguousarray(a).astype(NPBF)


def build_mod():
    nc = bass.Bass("TRN2", target_bir_lowering=False)
    with ExitStack() as es:
        K = Ctx(nc, es)
        P = K.P
        adaw = K.dram_in("adaw", [D, 6 * D], F32)
        adab = K.dram_in("adab", [128, 48, 2], F32)
        cvec = K.dram_in("cvec", [128, 8, 2], F32)
        ng = K.dram_in("ng", [128, 2, 8], F32)
        modo = K.dram_out("modo", [128, 48, 2], F32)
        CV = K.sb("CV", [128, 8, 2], F32)
        ABs = K.sb("ABs", [128, 48, 2], F32)
        NG = K.sb("NG", [128, 2, 8], F32)
        E1 = K.sb("E1", [128, 8, 2], F32)
        S = K.sb("S", [128, 8, 2], BF16)
        MOD = K.sb("MOD", [128, 48, 2], F32)
        WM = [K.sb("WM%d" % i, [128, 8, 1024], BF16) for i in range(2)]
        PS = es.enter_context(nc.psum_tensor("PS", [128, 8, 512], F32))
        P.dma("sp", CV[:], cvec, w=["CV"])
        P.dma("sp", ABs[:], adab, w=["ABs"])
        P.dma("sp", NG[:], ng, w=["NG"])
        K.act(E1[:], CV[:], AF.Exp, ["CV"], ["E1"], scale=-1.0)
        K.ts("dve", E1[:], E1[:], 1.0, None, ALU.add, None, ["E1"], ["E1"])
        K.recip(E1[:], E1[:], ["E1"], ["E1"])
        K.tt("dve", S[:], CV[:], E1[:], ALU.mult, ["CV", "E1"], ["S"])
        adaw_v = adaw.rearrange("(kc p) n -> p kc n", p=128)
        for j in range(6):
            wm = WM[j % 2]
            wk = "WM%d" % (j % 2)
            P.dma("pool", wm[:], adaw_v[:, :, j * 1024:(j + 1) * 1024], w=[wk])
            bank = j % 2
            for c in range(8):
                for kc in range(8):
                    K.mm(PS[:, bank, c * 2:c * 2 + 2], wm[:, kc, c * 128:(c + 1) * 128], S[:, kc, :],
                         kc == 0, kc == 7, [wk, "S"], ["ps%d" % bank])
            K.tt("dve", MOD[:, j * 8:(j + 1) * 8, :],
                 PS[:, bank, 0:16].rearrange("p (c v) -> p c v", v=2),
                 ABs[:, j * 8:(j + 1) * 8, :], ALU.add, ["ps%d" % bank, "ABs"], ["MOD%d" % j])
        for j, gi in ((1, 0), (4, 1)):
            for v in range(2):
                K.stt(MOD[:, j * 8:(j + 1) * 8, v], MOD[:, j * 8:(j + 1) * 8, v], 1.0, NG[:, gi, :],
                      ALU.add, ALU.mult, ["MOD%d" % j, "NG"], ["MOD%d" % j])
        P.dma("sp", modo, MOD[:], r=["MOD%d" % j for j in range(6)], w=["modo"], final=True)
        P.emit()
    return nc


def alloc_common(K):
    nc, es = K.nc, K.es
    K.X = K.sb("X", [128, 8, NT], F32)
    K.PS = es.enter_context(nc.psum_tensor("PS", [128, 8, 512], F32))
    K.ONES = K.sb("ONES", [128, 128], BF16)
    K.MOD = {}
    K.memset("pool", K.ONES[:], 1.0, ["ONES"])


def alloc_norm(K):
    K.SQ = K.sb("SQ", [128, 8, 512], BF16)
    K.LNV = K.sb("LNV", [128, 512], F32)
    K.RSTD = K.sb("RSTD", [128, 512], F32)
    K.TMP = [K.sb("TMP%d" % i, [128, 512], F32) for i in range(2)]


def load_mod(K, L, name=None):
    t = K.dram_in(name or ("mod%d" % L), [128, 48, 2], F32)
    K.MOD[L] = K.sb("MODL%d" % L, [128, 48, 2], F32)
    K.P.dma("sp", K.MOD[L][:], t, w=["MOD%d" % L])


def xkeys(tc):
    return ["X%d.%d" % (tc, c) for c in range(8)]


def norm_sq(K, tc):
    t0, W = TCS[tc]
    X = K.X
    for c in range(8):
        eng = "dve" if c % 2 == 0 else "pool"
        K.tt(eng, K.SQ[:, c, :W], X[:, c, t0:t0 + W], X[:, c, t0:t0 + W], ALU.mult,
             ["X%d.%d" % (tc, c)], ["SQ%d" % c])


def norm_rest(K, L, which, tc, out_fn, out_keys):
    t0, W = TCS[tc]
    v = 1 if tc == 4 else 0
    MOD = K.MOD[L]
    ja, jb = (1, 0) if which == 0 else (4, 3)
    mk = "MOD%d" % L
    X, PS = K.X, K.PS
    for c in range(8):
        K.mm(PS[:, 7, :W], K.ONES[:], K.SQ[:, c, :W], c == 0, c == 7, ["ONES", "SQ%d" % c], ["ps7"])
    K.act(K.LNV[:, :W], PS[:, 7, :W], AF.Ln, ["ps7"], ["LNV"], bias=K.EPSB[:], scale=1.0 / D)
    K.act(K.RSTD[:, :W], K.LNV[:, :W], AF.Exp, ["LNV"], ["RSTD"], scale=-0.5)
    for c in range(8):
        tb = c % 2
        K.stt(K.TMP[tb][:, :W], X[:, c, t0:t0 + W], MOD[:, ja * 8 + c, v:v + 1], K.RSTD[:, :W],
              ALU.mult, ALU.mult, ["X%d.%d" % (tc, c), mk, "RSTD"], ["TMP%d" % tb])
        K.act(out_fn(c), K.TMP[tb][:, :W], AF.Identity, ["TMP%d" % tb, mk], out_keys(c),
              bias=MOD[:, jb * 8 + c, v:v + 1], scale=1.0)


def norm_mod(K, L, which, tc, out_fn, out_keys):
    norm_sq(K, tc)
    norm_rest(K, L, which, tc, out_fn, out_keys)


def alloc_eps(K):
    K.EPSB = K.sb("EPSB", [128, 1], F32)
    K.memset("pool", K.EPSB[:], EPS, ["EPSB"])
    K.ONEB = K.sb("ONEB", [128, 1], F32)
    K.memset("pool", K.ONEB[:], 1.0, ["ONEB"])


def mlp_segment(K, L, w1, w2, ntc=5, after_last=None):
    X, PS = K.X, K.PS
    H = K.H
    MOD = K.MOD[L]
    mk = "MOD%d" % L
    def norm2(tc):
        t0, W = TCS[tc]
        norm_mod(K, L, 1, tc, lambda c, t0=t0, W=W: H[:, c, t0:t0 + W], lambda c, tc=tc: ["H%d" % tc])

    norm2(0)
    w1v = w1.rearrange("(kc p) n -> p kc n", p=128)
    w2v = w2.rearrange("(fc p) n -> p fc n", p=128)
    items = [(e, tc) for e in range(8) for tc in range(ntc)]

    def load(e):
        K.P.dma("pool", K.W1E[e % 2][:], w1v[:, :, e * 512:(e + 1) * 512], w=["W1E%d" % (e % 2)])
        K.P.dma("pool", K.W2E[e % 2][:], w2v[:, e * 4:(e + 1) * 4, :], w=["W2E%d" % (e % 2)])

    def part1(i):
        e, tc = items[i]
        t0, W = TCS[tc]
        ab = i % 2
        for f in range(4):
            bank = f % 2
            for kc in range(8):
                K.mm(PS[:, bank, :W], K.W1E[e % 2][:, kc, f * 128:(f + 1) * 128], H[:, kc, t0:t0 + W],
                     kc == 0, kc == 7, ["W1E%d" % (e % 2), "H%d" % tc], ["ps%d" % bank])
            K.act(K.RL[f % 2][:, :W], PS[:, bank, :W], AF.Relu, ["ps%d" % bank], ["RL%d" % (f % 2)])
            eng = "pool" if f % 2 == 0 else "dve"
            K.tt(eng, K.AH[ab][:, f, :W], K.RL[f % 2][:, :W], K.RL[f % 2][:, :W], ALU.mult,
                 ["RL%d" % (f % 2)], ["AH%d.%d" % (ab, f)])

    def part2(i):
        e, tc = items[i]
        t0, W = TCS[tc]
        v = 1 if tc == 4 else 0
        ab = i % 2
        for d in range(8):
            bank = 2 + d % 2
            for f in range(4):
                K.mm(PS[:, bank, :W], K.W2E[e % 2][:, f, d * 128:(d + 1) * 128], K.AH[ab][:, f, :W],
                     f == 0, f == 3, ["W2E%d" % (e % 2), "AH%d.%d" % (ab, f)], ["ps%d" % bank])
            K.stt(X[:, d, t0:t0 + W], PS[:, bank, :W], MOD[:, 40 + d, v:v + 1], X[:, d, t0:t0 + W],
                  ALU.mult, ALU.add, ["ps%d" % bank, mk, "X%d.%d" % (tc, d)], ["X%d.%d" % (tc, d)])

    load(0)
    part1(0)
    for i in range(len(items)):
        e0, tc0 = items[i]
        if tc0 == 1 and e0 + 1 < 8:
            load(e0 + 1)
        if i + 1 < len(items):
            if items[i + 1][0] == 0:
                norm2(items[i + 1][1])
            part1(i + 1)
        part2(i)
        if e0 == 7 and after_last is not None:
            after_last(tc0)


def alloc_mlp(K):
    K.H = K.sb("H", [128, 8, NT], BF16)
    K.W1E = [K.sb("W1E%d" % i, [128, 8, 512], BF16) for i in range(2)]
    K.W2E = [K.sb("W2E%d" % i, [128, 4, 1024], BF16) for i in range(2)]
    K.RL = [K.sb("RL%d" % i, [128, 512], F32) for i in range(2)]
    K.AH = [K.sb("AH%d" % i, [128, 4, 512], BF16) for i in range(2)]


def alloc_epre_noqt(K):
    alloc_epre(K, with_qt=False)


def alloc_epre(K, with_qt=True):
    K.WIN = K.sb("WIN", [128, 8, 1536], BF16)
    K.HC = [K.sb("HC%d" % i, [128, 8, 512], BF16) for i in range(2)]
    if with_qt:
        K.QT = K.sb("QT", [128, 6, NT], BF16)
    K.KTL = K.sb("KTL", [128, 2, NT], BF16)
    K.VTC = [K.sb("VTC%d" % i, [128, 4, 2, 2, 65], BF16) for i in range(2)]
    K.ABC = [K.sb("ABC%d" % i, [128, 4, 512], BF16) for i in range(2)]
    K.FTC = K.sb("FTC", [128, 2, 512], BF16)
    K.ROPE = [K.sb("ROPE0", [128, 2, 512], F32)] * 2
    K.PSB = [K.sb("PSB%d" % i, [128, 512], F32) for i in range(2)]
    K.SQ1 = [K.sb("SQ1%d" % i, [128, 512], BF16) for i in range(2)]
    K.QN = [K.sb("QN%d" % i, [128, 512], F32) for i in range(2)]
    K.T1 = [K.sb("T10", [128, 512], F32)] * 2
    K.T2 = [K.sb("T20", [128, 512], F32)] * 2
    K.LN2 = [K.sb("LN20", [128, 512], F32)] * 2
    K.RS2 = [K.sb("RS20", [128, 512], F32)] * 2
    if not hasattr(K, "BLK"):
        K.BLK = K.sb("BLK", [128, 128], BF16)
        K.PERM = K.sb("PERM", [128, 128], F32)
        K.CS64 = K.sb("CS64", [128, 256], BF16)
    K.QKG = K.sb("QKG", [128, 8], F32)


def epre_segment(K, L, win, rope, qkg, blk, perm, cs64, kt_o, v_o, ab_o, final=True, load_consts=True, after_tc=None):
    X, PS, P = K.X, K.PS, K.P
    P.dma("pool", K.WIN[:], win.rearrange("(kc p) n -> p kc n", p=128), w=["WIN"])
    if load_consts:
        P.dma("sp", K.BLK[:], blk, w=["BLK"])
        P.dma("sp", K.PERM[:], perm, w=["PERM"])
        P.dma("sp", K.CS64[:], cs64, w=["CS64"])
    P.dma("sp", K.QKG[:], qkg, w=["QKG"])
    K.memset("pool", K.VTC[0][:], 1.0, ["VTC0"])
    K.memset("pool", K.VTC[1][:], 1.0, ["VTC1"])
    def nrm(tc_, part):
        W_ = TCS[tc_][1]
        hcb = K.HC[tc_ % 2]
        hkb = "HC%d" % (tc_ % 2)
        if part == 0:
            norm_sq(K, tc_)
        else:
            norm_rest(K, L, 0, tc_, lambda c, hcb=hcb, W_=W_: hcb[:, c, :W_], lambda c, hkb=hkb: [hkb])

    nrm(0, 0)
    nrm(0, 1)
    for tc in range(5):
        t0, W = TCS[tc]
        hb = tc % 2
        hc = K.HC[hb]
        hk = "HC%d" % hb
        if tc < 4:
            P.dma("sp", K.ROPE[0][:], rope[:, :, t0:t0 + W], w=["ROPE0"])
        def stA(fc):
            pb = fc % 2
            b0 = fc % 2
            for kc in range(8):
                K.mm(PS[:, b0, :W], K.WIN[:, kc, fc * 128:(fc + 1) * 128], hc[:, kc, :W],
                     kc == 0, kc == 7, ["WIN", hk], ["ps%d" % b0])
            K.act(K.PSB[pb][:, :W], PS[:, b0, :W], AF.Identity, ["ps%d" % b0], ["PSB%d" % pb])
            K.tt("pool", K.SQ1[pb][:, :W], K.PSB[pb][:, :W], K.PSB[pb][:, :W], ALU.mult,
                 ["PSB%d" % pb], ["SQ1%d" % pb])

        def stB(fc):
            pb = fc % 2
            b1 = 2 + fc % 2
            K.mm(PS[:, b1, :W], K.BLK[:], K.SQ1[pb][:, :W], True, True, ["BLK", "SQ1%d" % pb], ["ps%d" % b1])
            K.act(K.LN2[pb][:, :W], PS[:, b1, :W], AF.Ln, ["ps%d" % b1], ["LN20"],
                  bias=K.EPSB[:], scale=1.0 / 64)
            K.act(K.RS2[pb][:, :W], K.LN2[pb][:, :W], AF.Exp, ["LN20"], ["RS20"], scale=-0.5)
            K.stt(K.QN[pb][:, :W], K.PSB[pb][:, :W], K.QKG[:, fc:fc + 1], K.RS2[pb][:, :W],
                  ALU.mult, ALU.mult, ["PSB%d" % pb, "QKG", "RS20"], ["QN%d" % pb])

        def stC(fc):
            pb = fc % 2
            b2 = 4 + fc % 2
            if fc < 6:
                dest, dk = K.QT[:, fc, t0:t0 + W], "QT%d.%d" % (tc, fc)
            else:
                dest, dk = K.KTL[:, fc - 6, t0:t0 + W], "KTL%d" % (fc - 6)
            if tc < 4:
                rp = K.ROPE[0]
                rk = "ROPE0"
                K.mm(PS[:, b2, :W], K.PERM[:], K.QN[pb][:, :W], True, True, ["PERM", "QN%d" % pb], ["ps%d" % b2])
                K.tt("dve", K.T1[pb][:, :W], K.QN[pb][:, :W], rp[:, 0, :W], ALU.mult,
                     ["QN%d" % pb, rk], ["T10"])
                K.tt("dve", K.T2[pb][:, :W], PS[:, b2, :W], rp[:, 1, :W], ALU.mult,
                     ["ps%d" % b2, rk], ["T20"])
                K.tt("pool", dest, K.T1[pb][:, :W], K.T2[pb][:, :W], ALU.add,
                     ["T10", "T20"], [dk])
            else:
                K.cp("pool", dest, K.QN[pb][:, :W], ["QN%d" % pb], [dk])

        stA(0)
        if tc + 1 < 5:
            nrm(tc + 1, 0)
        for fc in range(8):
            if fc + 1 < 8:
                stA(fc + 1)
            stB(fc)
            if fc >= 1:
                stC(fc - 1)
            if fc == 1 and tc + 1 < 5:
                nrm(tc + 1, 1)
        stC(7)
        rb = [6, 4, 5]
        ri = [0]

        def nb():
            b = rb[ri[0] % 3]
            ri[0] += 1
            return b

        for tt_ in range(W // 128):
            gt = t0 // 128 + tt_
            b = nb()
            for kc in range(8):
                K.mm(PS[:, b, 0:256], hc[:, kc, tt_ * 128:(tt_ + 1) * 128], K.WIN[:, kc, 1024:1280],
                     kc == 0, kc == 7, ["WIN", hk], ["ps%d" % b])
            K.act(K.VTC[tc % 2][:, tt_, :, :, 0:64], PS[:, b, 0:256].rearrange("p (a s e) -> p a s e", a=2, s=2),
                  AF.Identity, ["ps%d" % b], ["VTC%d" % (tc % 2)])
        for half in range(2):
            b = nb()
            for kc in range(8):
                K.mm(PS[:, b, :W], K.WIN[:, kc, 1280 + half * 128:1280 + (half + 1) * 128], hc[:, kc, :W],
                     kc == 0, kc == 7, ["WIN", hk], ["ps%d" % b])
            K.act(K.FTC[:, half, :W], PS[:, b, :W], AF.Identity, ["ps%d" % b], ["FTC%d" % half])
        for tt_ in range(W // 128):
            gt = t0 // 128 + tt_
            for half in range(2):
                b = nb()
                K.mm(PS[:, b, 0:256], K.FTC[:, half, tt_ * 128:(tt_ + 1) * 128], K.CS64[:],
                     True, True, ["FTC%d" % half, "CS64"], ["ps%d" % b])
                K.cp("dve", K.ABC[tc % 2][:, tt_, half * 256:(half + 1) * 256], PS[:, b, 0:256],
                     ["ps%d" % b], ["ABC%d" % (tc % 2)])
        nt_ = W // 128
        g0 = t0 // 128
        for a_ in range(2):
            P.dma("sp", v_o[a_][:, g0:g0 + nt_], K.VTC[tc % 2][:, :nt_, a_], r=["VTC%d" % (tc % 2)],
                  w=["v_o%d.%d" % (tc, a_)], final=final)
        for t_ in range(0, nt_, 2):
            ch, off = (g0 + t_) // 6, (g0 + t_) % 6
            P.dma("sp", ab_o[ch][:, off:off + 2, :], K.ABC[tc % 2][:, t_:t_ + 2, :], r=["ABC%d" % (tc % 2)],
                  w=["ab_o%d.%d" % (tc, t_)], final=final)
        if after_tc is not None:
            after_tc(tc)
    for k_ in range(2):
        P.dma("sp", kt_o[k_], K.KTL[:, k_, :], r=["KTL%d" % k_], w=["kt_o%d" % k_], final=final)


def fm_vec(v):
    v = np.asarray(v, np.float32)
    return np.ascontiguousarray(v.reshape(-1, 128).T)


def fm_tokens(a):
    T = a.shape[0]
    return np.ascontiguousarray(a.reshape(T, 8, 128).transpose(2, 1, 0))


_CONST = {}


def consts():
    if _CONST:
        return _CONST
    blk = np.zeros((128, 128), np.float32)
    blk[:64, :64] = 1.0
    blk[64:, 64:] = 1.0
    perm = np.zeros((128, 128), np.float32)
    for j in range(64):
        perm[2 * j + 1, 2 * j] = -1.0
        perm[2 * j, 2 * j + 1] = 1.0
    n = np.arange(64)
    ang = 2 * np.pi * np.outer(n, n) / 64.0
    c64, s64 = np.cos(ang), np.sin(ang)
    cs = np.zeros((128, 256), np.float64)
    for g in range(2):
        cs[g * 64:(g + 1) * 64, g * 64:(g + 1) * 64] = c64
        cs[g * 64:(g + 1) * 64, 128 + g * 64:128 + (g + 1) * 64] = s64
    _CONST["blk"] = _bf(blk)
    _CONST["perm"] = perm
    _CONST["cs64"] = _bf(cs)
    freqs = 10000.0 ** (-np.arange(16, dtype=np.float32) / 16)
    ropes = []
    for r in range(4):
        t = np.arange(r * NLAT, (r + 1) * NLAT)
        row = (t // 64).astype(np.float32)
        col = (t % 64).astype(np.float32)
        ang = np.concatenate([row[:, None] * freqs, col[:, None] * freqs], axis=-1).astype(np.float32)
        cos, sin = np.cos(ang), np.sin(ang)
        tab = np.zeros((128, 2, NLAT), np.float32)
        for p in range(128):
            jj = (p % 64) // 2
            tab[p, 0] = cos[:, jj]
            tab[p, 1] = sin[:, jj]
        ropes.append(tab)
    _CONST["rope"] = ropes
    tabs = []
    l = np.arange(8192, dtype=np.int64)
    sc = 1.0 / math.sqrt(8192 * 64)
    for r in range(4):
        k = np.arange(r * NLAT, (r + 1) * NLAT, dtype=np.int64)
        m = (l[:, None] * k[None, :]) % 8192
        a = 2 * np.pi * m / 8192.0
        tab = np.stack([np.cos(a) * sc, -np.sin(a) * sc], axis=1)
        tabs.append(_bf(tab.reshape(64, 128, 2, NLAT)))
    _CONST["dft"] = tabs
    l2 = np.arange(256, dtype=np.int64)
    m = (l2[:, None] * l2[None, :]) % 256
    a = 2 * np.pi * m / 256.0
    sc2 = 1.0 / math.sqrt(256 * 64)
    _CONST["dftc"] = _bf(np.stack([np.cos(a) * sc2, -np.sin(a) * sc2], axis=1).reshape(2, 128, 2, 256))
    return _CONST


def perm_win(w):
    cols = []
    for a in range(6):
        cols += list(range(a * 64, a * 64 + 64)) + list(range((a + 6) * 64, (a + 6) * 64 + 64))
    for kv in (0, 2, 1, 3):
        cols += list(range(768 + kv * 64, 768 + kv * 64 + 64))
    for kv in (0, 2, 1, 3):
        cols += list(range(1024 + kv * 64, 1024 + kv * 64 + 64))
    cols += list(range(1280, 1536))
    return np.ascontiguousarray(w[:, cols])


def build_p0():
    nc = bass.Bass("TRN2", target_bir_lowering=False)
    with ExitStack() as es:
        K = Ctx(nc, es)
        xin = K.dram_in("x_in", [128, 8, NT], F32)
        win = K.dram_in("win", [D, 1536], F32)
        rope = K.dram_in("rope", [128, 2, NLAT], F32)
        qkg = K.dram_in("qkg", [128, 8], F32)
        blk = K.dram_in("blk", [128, 128], BF16)
        perm = K.dram_in("perm", [128, 128], F32)
        cs64 = K.dram_in("cs64", [128, 256], BF16)
        kt_o = K.dram_out("kt_o", [128, 2, NT], BF16)
        v_o = K.dram_out("v_o", [128, 18, 2, 2, 65], BF16)
        ab_o = K.dram_out("ab_o", [128, 18, 512], BF16)
        qt_o = K.dram_out("qt_o", [128, 6, NT], BF16)
        alloc_common(K)
        alloc_eps(K)
        alloc_norm(K)
        alloc_epre(K)
        load_mod(K, 0)
        for c in range(8):
            K.P.dma("sp", K.X[:, c, :], xin[:, c, :], w=["X%d.%d" % (tc, c) for tc in range(5)])
        epre_segment(K, 0, win, rope, qkg, blk, perm, cs64, kt_o, v_o, ab_o)
        K.P.dma("sp", qt_o, K.QT[:], r=["QT%d.%d" % (tc, fc) for tc in range(5) for fc in range(6)],
                w=["qt_o"], final=True)
        K.P.emit()
    return nc


def alloc_epost(K, with_qt=True):
    if with_qt:
        K.QT = K.sb("QT", [128, 6, NT], BF16)
    K.KT = K.sb("KT", [128, 8448], BF16)
    K.V = K.sb("V", [128, 66, 2, 65], BF16)
    K.WO = K.sb("WO", [64, 12, 1024], BF16)
    K.WOF = K.sb("WOF", [128, 2, 1024], BF16)
    K.CAT = K.sb("CAT", [64, 6, 512], BF16)
    K.PB = [K.sb("PB%d" % i, [128, 2, 512], BF16) for i in range(3)]
    K.FT = K.sb("FT", [128, 2, NT], BF16)
    K.ABG = [K.sb("ABG%d" % i, [128, 2, 512], BF16) for i in range(2)]
    K.TABC = K.sb("TABC", [128, 2, 2, 256], BF16)
    K.ABGC = K.sb("ABGC", [128, 2, 512], BF16)
    K.OSB = K.sb("OSB", [65, 2, 512], F32)
    K.RDL = K.sb("RDL", [65, 2, 512], F32)
    K.RD = K.sb("RD", [65, 2, 512], BF16)
    K.ONESR = K.sb("ONESR", [128, 64], BF16)


def epost_segment(K, L, kt_all, v_all, ab_all, dft, dftc, wout, gk=()):
    X, PS, P = K.X, K.PS, K.P
    MOD = K.MOD[L]
    mk = "MOD%d" % L
    K.memset("pool", K.ONESR[:], 1.0, ["ONESR"])
    P.dma("pool", K.WO[:], wout[0:768, :].rearrange("(h d) n -> d h n", d=64), w=["WO"])
    P.dma("pool", K.WOF[:], wout[768:1024, :].rearrange("(c p) n -> p c n", p=128), w=["WOF"])
    TAB = [K.KT[:, 0:8192].rearrange("p (a b c) -> p a b c", a=2, b=2),
           K.V[:].rearrange("p a b c -> p (a b c)")[:, 0:8192].rearrange("p (a b c) -> p a b c", a=2, b=2)]
    TK = ["KTa", "Va"]
    for g in range(32):
        r = g // 8
        tl = (2 * g) % 16
        tb = g % 2
        P.dma("sp", TAB[tb], dft[2 * g:2 * g + 2].rearrange("l p s k -> p l s k"), w=[TK[tb]])
        P.dma("sp", K.ABG[tb][:], ab_all[tl // 6][r, :, tl % 6:tl % 6 + 2, :], r=gk, w=["ABG%d" % tb])
        for li in range(2):
            for half in range(2):
                for s in range(2):
                    for kc in range(4):
                        bank = half * 4 + kc
                        K.mm(PS[:, bank, :], K.ABG[tb][:, li, half * 256 + s * 128:half * 256 + (s + 1) * 128],
                             TAB[tb][:, li, s, kc * 512:(kc + 1) * 512],
                             g == 0 and li == 0 and s == 0, g == 31 and li == 1 and s == 1,
                             ["ABG%d" % tb, TK[tb]], ["ps%d" % bank])
    for half in range(2):
        for kc in range(4):
            bank = half * 4 + kc
            if bank % 2 == 0:
                K.act(K.FT[:, half, kc * 512:(kc + 1) * 512], PS[:, bank, :], AF.Identity, ["ps%d" % bank], ["FT%d" % kc])
            else:
                K.cp("dve", K.FT[:, half, kc * 512:(kc + 1) * 512], PS[:, bank, :], ["ps%d" % bank], ["FT%d" % kc])
    P.dma("sp", K.TABC[:], dftc.rearrange("l p s k -> p l s k"), w=["TABC"])
    P.dma("sp", K.ABGC[:], ab_all[2][0, :, 4:6, :], r=gk, w=["ABGC"])
    for half in range(2):
        for li in range(2):
            for s in range(2):
                K.mm(PS[:, half, 0:256], K.ABGC[:, li, half * 256 + s * 128:half * 256 + (s + 1) * 128],
                     K.TABC[:, li, s, :], li == 0 and s == 0, li == 1 and s == 1, ["ABGC", "TABC"], ["ps%d" % half])
        K.act(K.FT[:, half, 2048:2304], PS[:, half, 0:256], AF.Identity, ["ps%d" % half], ["FT4"])
    P.barrier()
    jobs = [(kp, tc, a3) for kp in range(2) for tc in range(5) for a3 in range(3)]

    def load_kv(kp):
        for r in range(4):
            P.dma("sp", K.KT[:, r * 2048:(r + 1) * 2048], kt_all[kp][r, :, 0:2048], r=["kt_gat%d" % kp], w=["KT%d" % r])
            P.dma("sp", K.V[:, r * 16:(r + 1) * 16, :, :], v_all[kp][r, :, 0:16, :, :], r=["v_gat%d" % kp], w=["V%d" % r])
        P.dma("sp", K.KT[:, 8192:8448], kt_all[kp][0, :, 2048:2304], r=["kt_gat%d" % kp], w=["KT4"])
        P.dma("sp", K.V[:, 64:66, :, :], v_all[kp][0, :, 16:18, :, :], r=["v_gat%d" % kp], w=["V4"])

    def kts_of(tc):
        return list(range(66)) if tc < 4 else [64, 65]

    def S(job, i, g0):
        kp, tc, a3 = job
        t0, W = TCS[tc]
        a = 3 * kp + a3
        qk = "QT%d.%d" % (tc, a)
        kt = kts_of(tc)[i]
        sb = (g0 + i) % 3
        kk = "KT%d" % min(kt // 16, 4)
        K.mm(PS[:, 2 * sb, :W], K.KT[0:64, kt * 128:(kt + 1) * 128], K.QT[0:64, a, t0:t0 + W],
             True, True, [kk, qk], ["ps%d" % (2 * sb)])
        K.mm(PS[:, 2 * sb + 1, :W], K.KT[64:128, kt * 128:(kt + 1) * 128], K.QT[64:128, a, t0:t0 + W],
             True, True, [kk, qk], ["ps%d" % (2 * sb + 1)])

    def prologue(job, g0):
        n = len(kts_of(job[1]))
        S(job, 0, g0)
        if n > 1:
            S(job, 1, g0)

    def body(job, g0, pending=None):
        kp, tc, a3 = job
        t0, W = TCS[tc]
        kts = kts_of(tc)
        n = len(kts)
        had_pending = pending is not None
        for i in range(n):
            if i == 2 and pending is not None:
                pending()
                pending = None
                S(job, 2, g0)
                if n > 3:
                    S(job, 3, g0)
            sb = (g0 + i) % 3
            kt = kts[i]
            vk = "V%d" % min(kt // 16, 4)
            if i + 2 < n and (not had_pending or i >= 2):
                S(job, i + 2, g0)
            K.act(K.PB[sb][:, :, :W], PS[:, 2 * sb:2 * sb + 2, :W], AF.Exp,
                  ["ps%d" % (2 * sb), "ps%d" % (2 * sb + 1)], ["PB%d" % sb], scale=0.125)
            for s_ in range(2):
                K.mm(PS[0:65, 6 + s_, :W], K.V[:, kt, s_, :], K.PB[sb][:, s_, :W],
                     i == 0, i == n - 1, [vk, "PB%d" % sb], ["ps%d" % (6 + s_)])
        if pending is not None:
            pending()
        K.cp("dve", K.OSB[0:65, :, :W], PS[0:65, 6:8, :W], ["ps6", "ps7"], ["OSB"])

    def finish(job, g0):
        kp, tc, a3 = job
        t0, W = TCS[tc]
        n = len(kts_of(tc))
        bs = (g0 + n) % 3
        v = 1 if tc == 4 else 0
        K.act(K.RDL[64:65, :, :W], K.OSB[64:65, :, :W], AF.Ln, ["OSB"], ["RDL"])
        K.act(K.RD[64:65, :, :W], K.RDL[64:65, :, :W], AF.Exp, ["RDL"], ["RD"], scale=-1.0)
        for s_ in range(2):
            K.mm(PS[0:64, 2 * bs + s_, :W], K.ONESR[64:65, 0:64], K.RD[64:65, s_, :W], True, True,
                 ["ONESR", "RD"], ["ps%d" % (2 * bs + s_)])
        K.tt("dve", K.CAT[0:64, 2 * a3:2 * a3 + 2, :W], K.OSB[0:64, :, :W], PS[0:64, 2 * bs:2 * bs + 2, :W],
             ALU.mult, ["OSB", "ps%d" % (2 * bs), "ps%d" % (2 * bs + 1)], ["CAT%d" % (2 * a3), "CAT%d" % (2 * a3 + 1)])
        if a3 == 2:
            for d in range(8):
                bank = 2 * bs + d % 2
                for slot in range(6):
                    head = 3 * kp + slot // 2 + 6 * (slot % 2)
                    K.mm(PS[:, bank, :W], K.WO[0:64, head, d * 128:(d + 1) * 128], K.CAT[0:64, slot, :W],
                         slot == 0, slot == 5 and kp == 1, ["WO", "CAT%d" % slot], ["ps%d" % bank])
                if kp == 0:
                    for half in range(2):
                        K.mm(PS[:, bank, :W], K.WOF[:, half, d * 128:(d + 1) * 128], K.FT[:, half, t0:t0 + W],
                             False, half == 1, ["WOF", "FT%d" % tc], ["ps%d" % bank])
                K.stt(X[:, d, t0:t0 + W], PS[:, bank, :W], MOD[:, 16 + d, v:v + 1], X[:, d, t0:t0 + W],
                      ALU.mult, ALU.add, ["ps%d" % bank, mk, "X%d.%d" % (tc, d)], ["X%d.%d" % (tc, d)])

    g0 = 0
    load_kv(0)
    prologue(jobs[0], g0)
    pending = None
    for ji, job in enumerate(jobs):
        n = len(kts_of(job[1]))
        body(job, g0, pending)
        pending = (lambda job=job, g0=g0: finish(job, g0))
        g0n = g0 + n + 1
        if ji + 1 < len(jobs):
            nj = jobs[ji + 1]
            if nj[0] != job[0]:
                pending()
                pending = None
                load_kv(nj[0])
            prologue(nj, g0n)
        g0 = g0n
    if pending is not None:
        pending()


def load_x(K, xin):
    for c in range(8):
        K.P.dma("sp", K.X[:, c, :], xin[:, c, :], w=["X%d.%d" % (tc, c) for tc in range(5)])


def store_x(K, xo, n=NT):
    for c in range(8):
        K.P.dma("sp", xo[:, c, :], K.X[:, c, 0:n], r=["X%d.%d" % (tc, c) for tc in range(5)],
                w=["xo%d" % c], final=True)


UW = NLAT + 30 + NCTX + 30


def ucol(tc):
    return 15 + TCS[tc][0] if tc < 4 else NLAT + 30 + 15


def alloc_opre(K, with_u=True):
    K.WPW1 = K.sb("WPW1", [128, 8, 2048], BF16)
    K.HC = [K.sb("HC%d" % i, [128, 8, 512], BF16) for i in range(2)]
    if with_u:
        K.U = K.sb("U", [128, 8, UW], BF16)
    K.BPW1 = K.sb("BPW1", [128, 16], F32)
    K.NEGB = K.sb("NEGB", [128, 16], F32)
    K.EG = [K.sb("EG%d" % i, [128, 512], F32) for i in range(2)]
    K.SG = [K.sb("SG%d" % i, [128, 512], F32) for i in range(2)]


def opre_segment(K, L, wpw1, bpw1, ntc=5):
    X, PS, P = K.X, K.PS, K.P
    P.dma("pool", K.WPW1[:], wpw1.rearrange("(kc p) n -> p kc n", p=128), w=["WPW1"])
    P.dma("sp", K.BPW1[:], bpw1, w=["BPW1"])
    K.ts("dve", K.NEGB[:], K.BPW1[:], -1.0, None, ALU.mult, None, ["BPW1"], ["NEGB"])
    K.memset("pool", K.U[:], 0.0, ["U%d" % tc for tc in range(5)])
    def nrm(tc, part):
        W = TCS[tc][1]
        hcb = K.HC[tc % 2]
        hkb = "HC%d" % (tc % 2)
        if part == 0:
            norm_sq(K, tc)
        else:
            norm_rest(K, L, 0, tc, lambda c, hcb=hcb, W=W: hcb[:, c, :W], lambda c, hkb=hkb: [hkb])

    nrm(0, 0)
    nrm(0, 1)
    for tc in range(ntc):
        t0, W = TCS[tc]
        hc = K.HC[tc % 2]
        hk = "HC%d" % (tc % 2)
        u0 = ucol(tc)
        for c in range(8):
            if tc + 1 < ntc and c == 0:
                nrm(tc + 1, 0)
            if tc + 1 < ntc and c == 2:
                nrm(tc + 1, 1)
            pb = c % 2
            bv, bg = c % 2, 2 + c % 2
            for kc in range(8):
                K.mm(PS[:, bv, :W], K.WPW1[:, kc, c * 128:(c + 1) * 128], hc[:, kc, :W],
                     kc == 0, kc == 7, ["WPW1", hk], ["ps%d" % bv])
            for kc in range(8):
                K.mm(PS[:, bg, :W], K.WPW1[:, kc, 1024 + c * 128:1024 + (c + 1) * 128], hc[:, kc, :W],
                     kc == 0, kc == 7, ["WPW1", hk], ["ps%d" % bg])
            K.act(K.EG[pb][:, :W], PS[:, bg, :W], AF.Exp, ["ps%d" % bg, "NEGB"], ["EG%d" % pb],
                  bias=K.NEGB[:, 8 + c:9 + c], scale=-1.0)
            K.act(K.EG[pb][:, :W], K.EG[pb][:, :W], AF.Ln, ["EG%d" % pb, "ONEB"], ["EG%d" % pb], bias=K.ONEB[:], scale=1.0)
            K.act(K.SG[pb][:, :W], K.EG[pb][:, :W], AF.Exp, ["EG%d" % pb], ["SG%d" % pb], scale=-1.0)
            K.stt(K.U[:, c, u0:u0 + W], PS[:, bv, :W], K.BPW1[:, c:c + 1], K.SG[pb][:, :W],
                  ALU.add, ALU.mult, ["ps%d" % bv, "BPW1", "SG%d" % pb], ["U%d" % tc])


def alloc_opost(K, with_u=True):
    if with_u:
        K.U = K.sb("U", [128, 8, UW], BF16)
    K.WPW2 = K.sb("WPW2", [128, 8, 1024], BF16)
    K.ACC = K.sb("ACC", [128, 8, 512], F32)
    K.SQC = K.sb("SQC", [128, 8, 512], BF16)
    K.Z = K.sb("Z", [128, 8, 512], BF16)
    K.COB = K.Z
    K.DIAG = [K.sb("DIAG%d" % i, [128, 31, 128], BF16) for i in range(2)]
    if not hasattr(K, "IDENT"):
        K.IDENT = K.sb("IDENT", [128, 128], BF16)
    K.WDW = K.sb("WDW", [128, 8, 31], F32)
    K.PV5 = K.sb("PV5", [128, 5, 8], F32)
    K.GB = K.sb("GB", [128, 8, 2], F32)
    K.MEAN = K.sb("MEAN", [128, 512], F32)
    K.VAR = K.sb("VAR", [128, 512], F32)
    K.LNV2 = K.sb("LNV2", [128, 512], F32)
    K.RSTD2 = K.sb("RSTD2", [128, 512], F32)
    K.TA = [K.sb("TA0", [128, 512], F32)] * 2
    K.TB = [K.sb("TB%d" % i, [128, 512], F32) for i in range(2)]
    K.TE = [K.sb("TE%d" % i, [128, 512], F32) for i in range(2)]
    K.T3 = [K.sb("T3%d" % i, [128, 512], F32) for i in range(2)]


def opost_segment(K, L, wdw, pv4, wpw2, ntc=5, ukeys=False, ident=None):
    X, PS, P = K.X, K.PS, K.P
    if ident is not None:
        P.dma("sp", K.IDENT[:], ident, w=["IDENT"])
    MOD = K.MOD[L]
    mk = "MOD%d" % L
    P.dma("pool", K.WPW2[:], wpw2.rearrange("(kc p) n -> p kc n", p=128), w=["WPW2"])
    P.dma("sp", K.WDW[:], wdw, w=["WDW"])
    P.dma("sp", K.PV5[:, 0:4, :], pv4, w=["PV5"])
    K.ts("dve", K.PV5[:, 4, :], K.PV5[:, 2, :], -1.0, None, ALU.mult, None, ["PV5"], ["PV5n"])
    for v in range(2):
        K.tt("dve", K.GB[:, :, v], MOD[:, 16:24, v], K.PV5[:, 3, :], ALU.mult, [mk, "PV5"], ["GB"])
    def conv_mm(tc, c):
        t0, W = TCS[tc]
        s0 = ucol(tc) - 15
        db = c % 2
        dk = "DIAG%d" % db
        idv = K.IDENT[:]
        wv = K.WDW[:, c, :]
        in0 = bass.AP(idv.tensor, idv.offset, [list(idv.ap[0]), [0, 31], [1, 128]])
        in1 = bass.AP(wv.tensor, wv.offset, [list(wv.ap[0]), [1, 31], [0, 128]])
        K.tt("dve", K.DIAG[db][:], in0, in1, ALU.mult, ["IDENT", "WDW"], [dk])
        bank = 2 + c % 4
        for j in range(31):
            K.mm(PS[:, bank, :W], K.DIAG[db][:, j, :], K.U[:, c, s0 + j:s0 + j + W], j == 0, j == 30,
                 [dk, "U"], ["ps%d" % bank])

    def conv_ev(tc, c):
        W = TCS[tc][1]
        bank = 2 + c % 4
        K.act(K.ACC[:, c, :W], PS[:, bank, :W], AF.Identity, ["ps%d" % bank, "PV5"], ["ACC%d" % c],
              bias=K.PV5[:, 0, c:c + 1], scale=1.0)

    def conv_tail(tc):
        W = TCS[tc][1]
        for c in range(8):
            K.cp("dve", K.COB[:, c, :W], K.ACC[:, c, :W], ["ACC%d" % c], ["Z%d" % c])
            K.tt("pool", K.SQC[:, c, :W], K.ACC[:, c, :W], K.ACC[:, c, :W], ALU.mult, ["ACC%d" % c], ["SQC%d" % c])

    def stats(tc):
        W = TCS[tc][1]
        for c in range(8):
            K.mm(PS[:, 6, :W], K.ONES[:], K.COB[:, c, :W], c == 0, c == 7, ["ONES", "Z%d" % c], ["ps6"])
        for c in range(8):
            K.mm(PS[:, 7, :W], K.ONES[:], K.SQC[:, c, :W], c == 0, c == 7, ["ONES", "SQC%d" % c], ["ps7"])
        K.act(K.MEAN[:, :W], PS[:, 6, :W], AF.Identity, ["ps6"], ["MEAN"], scale=1.0 / D)
        K.tt("pool", K.VAR[:, :W], K.MEAN[:, :W], K.MEAN[:, :W], ALU.mult, ["MEAN"], ["VAR"])
        K.stt(K.VAR[:, :W], PS[:, 7, :W], 1.0 / D, K.VAR[:, :W], ALU.mult, ALU.subtract, ["ps7", "VAR"], ["VAR"])
        K.act(K.LNV2[:, :W], K.VAR[:, :W], AF.Ln, ["VAR"], ["LNV2"], bias=K.EPSB[:], scale=1.0)
        K.act(K.RSTD2[:, :W], K.LNV2[:, :W], AF.Exp, ["LNV2"], ["RSTD2"], scale=-0.5)

    def ln_chain(tc):
        W = TCS[tc][1]
        for c in range(8):
            pb = c % 2
            K.tt("dve", K.TA[pb][:, :W], K.ACC[:, c, :W], K.MEAN[:, :W], ALU.subtract,
                 ["ACC%d" % c, "MEAN"], ["TA0"])
            K.stt(K.TB[pb][:, :W], K.TA[pb][:, :W], K.PV5[:, 1, c:c + 1], K.RSTD2[:, :W], ALU.mult, ALU.mult,
                  ["TA0", "PV5", "RSTD2"], ["TB%d" % pb])
            K.act(K.TE[pb][:, :W], K.TB[pb][:, :W], AF.Exp, ["TB%d" % pb, "PV5n"], ["TE%d" % pb],
                  bias=K.PV5[:, 4, c:c + 1], scale=-1.0)
            K.act(K.TE[pb][:, :W], K.TE[pb][:, :W], AF.Ln, ["TE%d" % pb, "ONEB"], ["TE%d" % pb], bias=K.ONEB[:], scale=1.0)
            K.act(K.TE[pb][:, :W], K.TE[pb][:, :W], AF.Exp, ["TE%d" % pb], ["TE%d" % pb], scale=-1.0)
            K.stt(K.Z[:, c, :W], K.TB[pb][:, :W], K.PV5[:, 2, c:c + 1], K.TE[pb][:, :W], ALU.add, ALU.mult,
                  ["TB%d" % pb, "PV5", "TE%d" % pb], ["Z%d" % c])

    def pw2(tc):
        t0, W = TCS[tc]
        v = 1 if tc == 4 else 0
        for d in range(8):
            bank = d % 2
            for c in range(8):
                K.mm(PS[:, bank, :W], K.WPW2[:, c, d * 128:(d + 1) * 128], K.Z[:, c, :W], c == 0, c == 7,
                     ["WPW2", "Z%d" % c], ["ps%d" % bank])
            K.act(K.T3[bank][:, :W], PS[:, bank, :W], AF.Identity, ["ps%d" % bank, mk, "GB"], ["T3%d" % bank],
                  bias=K.GB[:, d, v:v + 1], scale=MOD[:, 16 + d, v:v + 1])
            K.tt("dve", X[:, d, t0:t0 + W], X[:, d, t0:t0 + W], K.T3[bank][:, :W], ALU.add,
                 ["X%d.%d" % (tc, d), "T3%d" % bank], ["X%d.%d" % (tc, d)])

    for c in range(8):
        conv_mm(0, c)
        conv_ev(0, c)
    conv_tail(0)
    for tc in range(ntc):
        nxt = tc + 1 < ntc
        stats(tc)
        if nxt:
            conv_mm(tc + 1, 0)
            conv_mm(tc + 1, 1)
        ln_chain(tc)
        if nxt:
            conv_ev(tc + 1, 0)
            conv_ev(tc + 1, 1)
        pw2(tc)
        if nxt:
            for c in range(2, 8):
                conv_mm(tc + 1, c)
                conv_ev(tc + 1, c)
            conv_tail(tc + 1)


def _epre_io(K):
    win = K.dram_in("win", [D, 1536], F32)
    rope = K.dram_in("rope", [128, 2, NLAT], F32)
    qkg = K.dram_in("qkg", [128, 8], F32)
    blk = K.dram_in("blk", [128, 128], BF16)
    perm = K.dram_in("perm", [128, 128], F32)
    cs64 = K.dram_in("cs64", [128, 256], BF16)
    kt_o = K.dram_out("kt_o", [128, 2, NT], BF16)
    v_o = K.dram_out("v_o", [128, 18, 2, 2, 65], BF16)
    ab_o = K.dram_out("ab_o", [128, 18, 512], BF16)
    qt_o = K.dram_out("qt_o", [128, 6, NT], BF16)
    return win, rope, qkg, blk, perm, cs64, kt_o, v_o, ab_o, qt_o


def _epre_run(K, L, io):
    win, rope, qkg, blk, perm, cs64, kt_o, v_o, ab_o, qt_o = io
    epre_segment(K, L, win, rope, qkg, blk, perm, cs64, [kt_o[:, k_, :] for k_ in range(2)],
                 [v_o[:, :, a_, :, :] for a_ in range(2)], [ab_o[:, 6 * c_:6 * c_ + 6, :] for c_ in range(3)])
    K.P.dma("sp", qt_o, K.QT[:], r=["QT%d.%d" % (tc, fc) for tc in range(5) for fc in range(6)],
            w=["qt_o"], final=True)


def build_pA():
    nc = bass.Bass("TRN2", target_bir_lowering=False)
    with ExitStack() as es:
        K = Ctx(nc, es)
        xin = K.dram_in("x_in", [128, 8, NT], F32)
        io = _epre_io(K)
        alloc_common(K)
        alloc_eps(K)
        load_mod(K, 0, "modA")
        load_x(K, xin)
        alloc_norm(K)
        alloc_epre(K)
        _epre_run(K, 0, io)
        K.P.emit()
    return nc


def build_pB():
    nc = bass.Bass("TRN2", target_bir_lowering=False)
    with ExitStack() as es:
        K = Ctx(nc, es)
        xin = K.dram_in("x_in", [128, 8, NT], F32)
        qt_in = K.dram_in("qt_in", [128, 6, NT], BF16)
        kt_all = K.dram_in("kt_all", [4, 128, 2, NT], BF16)
        v_all = K.dram_in("v_all", [4, 128, 18, 2, 2, 65], BF16)
        ab_all = K.dram_in("ab_all", [4, 128, 18, 512], BF16)
        dft = K.dram_in("dft", [64, 128, 2, NLAT], BF16)
        dftc = K.dram_in("dftc", [2, 128, 2, 256], BF16)
        wout = K.dram_in("wout", [D, D], F32)
        w1 = K.dram_in("w1", [D, 4 * D], F32)
        w2 = K.dram_in("w2", [4 * D, D], F32)
        wpw1 = K.dram_in("wpw1", [D, 2 * D], F32)
        bpw1 = K.dram_in("bpw1", [128, 16], F32)
        xo = K.dram_out("x_o", [128, 8, NT], F32)
        uo = K.dram_out("u_o", [128, 8, UW], BF16)
        alloc_common(K)
        alloc_eps(K)
        load_mod(K, 0, "modA")
        load_mod(K, 1, "modB")
        load_x(K, xin)
        with ExitStack() as ph:
            K.es = ph
            alloc_epost(K)
            K.P.dma("sp", K.QT[:], qt_in, w=["QT%d.%d" % (tc, fc) for tc in range(5) for fc in range(6)])
            epost_segment(K, 0, [kt_all[:, :, k_, :] for k_ in range(2)],
                          [v_all[:, :, :, a_, :, :] for a_ in range(2)],
                          [ab_all[:, :, 6 * c_:6 * c_ + 6, :] for c_ in range(3)], dft, dftc, wout)
            K.P.barrier()
        with ExitStack() as ph:
            K.es = ph
            alloc_norm(K)
            alloc_mlp(K)
            mlp_segment(K, 0, w1, w2)
            K.P.barrier()
        with ExitStack() as ph:
            K.es = ph
            alloc_norm(K)
            alloc_opre(K)
            opre_segment(K, 1, wpw1, bpw1)
            K.P.dma("sp", uo, K.U[:], r=["U%d" % tc for tc in range(5)], w=["u_o"], final=True)
            K.P.barrier()
        K.es = es
        store_x(K, xo)
        K.P.emit()
    return nc


def build_pC(last):
    nc = bass.Bass("TRN2", target_bir_lowering=False)
    ntc = 4 if last else 5
    with ExitStack() as es:
        K = Ctx(nc, es)
        xin = K.dram_in("x_in", [128, 8, NT], F32)
        u_in = K.dram_in("u_in", [128, 8, UW], BF16)
        wdw = K.dram_in("wdw", [128, 8, 31], F32)
        pv4 = K.dram_in("pv4", [128, 4, 8], F32)
        wpw2 = K.dram_in("wpw2", [D, D], F32)
        ident = K.dram_in("ident", [128, 128], BF16)
        w1 = K.dram_in("w1", [D, 4 * D], F32)
        w2 = K.dram_in("w2", [4 * D, D], F32)
        if not last:
            io = _epre_io(K)
            xo = K.dram_out("x_o", [128, 8, NT], F32)
        else:
            xo = K.dram_out("x_o", [128, 8, NLAT], F32)
        alloc_common(K)
        alloc_eps(K)
        load_mod(K, 0, "modA")
        if not last:
            load_mod(K, 1, "modB")
        load_x(K, xin)
        with ExitStack() as ph:
            K.es = ph
            alloc_opost(K)
            K.P.dma("sp", K.U[:], u_in, w=["U"])
            opost_segment(K, 0, wdw, pv4, wpw2, ntc, ident=ident)
            K.P.barrier()
        with ExitStack() as ph:
            K.es = ph
            alloc_norm(K)
            alloc_mlp(K)
            mlp_segment(K, 0, w1, w2, ntc)
            K.P.barrier()
        if not last:
            with ExitStack() as ph:
                K.es = ph
                alloc_norm(K)
                alloc_epre(K)
                _epre_run(K, 1, io)
                K.P.barrier()
        K.es = es
        store_x(K, xo, NLAT if last else NT)
        K.P.emit()
    return nc


_PROGS = {}


def _prog(name):
    if name not in _PROGS:
        _PROGS[name] = {"mod": build_mod, "A": build_pA, "B": build_pB,
                        "C": lambda: build_pC(False), "D": lambda: build_pC(True)}[name]()
    return _PROGS[name]


def _run(name, in_maps):
    res = run_bass_kernel_spmd(_prog(name), in_maps, core_ids=list(range(NCORES)))
    return res.results


def kernel_multi(x, c, ctx, c_ctx, ada_w, ada_b, norm1_g, norm2_g, mlp_w1, mlp_w2,
           attn_w_in, q_norm_g, k_norm_g, attn_w_out,
           conv_w_pw1, conv_b_pw1, conv_w_dw, conv_b_dw, conv_ln_g, conv_ln_b,
           conv_w_pw2, conv_b_pw2):
    f32 = lambda a: np.ascontiguousarray(np.asarray(a, dtype=np.float32))
    x, c, ctx, c_ctx = f32(x), f32(c), f32(ctx), f32(c_ctx)
    ada_w, ada_b, norm1_g, norm2_g = f32(ada_w), f32(ada_b), f32(norm1_g), f32(norm2_g)
    mlp_w1, mlp_w2, attn_w_in, attn_w_out = f32(mlp_w1), f32(mlp_w2), f32(attn_w_in), f32(attn_w_out)
    conv_w_pw1, conv_w_pw2, conv_w_dw = f32(conv_w_pw1), f32(conv_w_pw2), f32(conv_w_dw)
    C = consts()
    cores = [(i // 4, i % 4) for i in range(NCORES)]
    maps = []
    for b, r in cores:
        cvec = np.ascontiguousarray(np.stack([fm_vec(c[b]), fm_vec(c_ctx)], axis=-1))
        adab = fm_vec(ada_b[r])
        adab = np.ascontiguousarray(np.repeat(adab[:, :, None], 2, axis=2))
        ng = np.ascontiguousarray(np.stack([fm_vec(norm1_g[r]), fm_vec(norm2_g[r])], axis=1))
        maps.append(dict(adaw=ada_w[r], adab=adab, cvec=cvec, ng=ng))
    rm = _run("mod", maps)
    mod = {(b, L): np.asarray(rm[b * 4 + L]["modo"]) for b in range(2) for L in range(4)}

    def qkg_of(j):
        g = np.zeros((128, 8), np.float32)
        for fc in range(8):
            src = np.asarray(q_norm_g[j] if fc < 6 else k_norm_g[j], np.float32)
            g[:64, fc] = src
            g[64:, fc] = src
        return g

    def epre_inputs(j, r):
        return dict(win=perm_win(attn_w_in[j]), rope=C["rope"][r], qkg=qkg_of(j), blk=C["blk"],
                    perm=C["perm"], cs64=C["cs64"])

    def gather(res, key, b):
        return np.ascontiguousarray(np.stack([np.asarray(res[b * 4 + rr][key]) for rr in range(4)], 0))

    def epost_inputs(res, j, b, r, i):
        return dict(qt_in=np.asarray(res[i]["qt_o"]), kt_all=gather(res, "kt_o", b), v_all=gather(res, "v_o", b),
                    ab_all=gather(res, "ab_o", b), dft=C["dft"][r], dftc=C["dftc"], wout=attn_w_out[j])

    def u_with_halo(res, b, r):
        u = np.array(np.asarray(res[b * 4 + r]["u_o"]))
        if r > 0:
            ul = np.asarray(res[b * 4 + r - 1]["u_o"])
            u[:, :, 0:15] = ul[:, :, NLAT:NLAT + 15]
        if r < 3:
            ur = np.asarray(res[b * 4 + r + 1]["u_o"])
            u[:, :, NLAT + 15:NLAT + 30] = ur[:, :, 15:30]
        return np.ascontiguousarray(u)

    def opost_inputs(j):
        wdw = np.ascontiguousarray(conv_w_dw[j].reshape(31, 8, 128).transpose(2, 1, 0))
        pv4 = np.ascontiguousarray(np.stack([fm_vec(conv_b_dw[j]), fm_vec(conv_ln_g[j]), fm_vec(conv_ln_b[j]),
                                             fm_vec(conv_b_pw2[j])], axis=1))
        return dict(wdw=wdw, pv4=pv4, wpw2=conv_w_pw2[j], ident=_bf(np.eye(128, dtype=np.float32)))

    maps = []
    for b, r in cores:
        xt = np.concatenate([x[b, r * NLAT:(r + 1) * NLAT], ctx[b]], axis=0)
        m = dict(x_in=fm_tokens(xt), modA=mod[(b, 0)])
        m.update(epre_inputs(0, r))
        maps.append(m)
    xcur = [m["x_in"] for m in maps]
    res = _run("A", maps)
    for L in (0, 2):
        j = L // 2
        maps = []
        for i, (b, r) in enumerate(cores):
            m = dict(x_in=xcur[i], modA=mod[(b, L)], modB=mod[(b, L + 1)], w1=mlp_w1[L], w2=mlp_w2[L],
                     wpw1=conv_w_pw1[j], bpw1=fm_vec(conv_b_pw1[j]))
            m.update(epost_inputs(res, j, b, r, i))
            maps.append(m)
        res = _run("B", maps)
        xcur = [np.asarray(res[i]["x_o"]) for i in range(NCORES)]
        last = L == 2
        maps = []
        for i, (b, r) in enumerate(cores):
            m = dict(x_in=xcur[i], u_in=u_with_halo(res, b, r), modA=mod[(b, L + 1)],
                     w1=mlp_w1[L + 1], w2=mlp_w2[L + 1])
            m.update(opost_inputs(j))
            if not last:
                m["modB"] = mod[(b, L + 2)]
                m.update(epre_inputs(j + 1, r))
            maps.append(m)
        res = _run("D" if last else "C", maps)
        if not last:
            xcur = [np.asarray(res[i]["x_o"]) for i in range(NCORES)]
    out = np.zeros((2, 4 * NLAT, D), np.float32)
    for i, (b, r) in enumerate(cores):
        xo = np.asarray(res[i]["x_o"])
        out[b, r * NLAT:(r + 1) * NLAT] = xo.transpose(2, 1, 0).reshape(NLAT, D)
    return out


GROUPS = [[0, 1, 2, 3], [4, 5, 6, 7]]


def mod_segment(K, adaw, adab, cvec, ng, m_loc, m_gat):
    nc, P, PS = K.nc, K.P, K.PS
    CV = K.sb("CV", [128, 8, 2], F32)
    ABs = K.sb("ABs", [128, 48, 2], F32)
    NG = K.sb("NG", [128, 2, 8], F32)
    E1 = K.sb("E1", [128, 8, 2], F32)
    S = K.sb("S", [128, 8, 2], BF16)
    MODL = K.sb("MODL", [128, 48, 2], F32)
    WM = [K.sb("WM%d" % i, [128, 8, 1024], BF16) for i in range(2)]
    P.dma("sp", CV[:], cvec, w=["CV"])
    P.dma("sp", ABs[:], adab, w=["ABs"])
    P.dma("sp", NG[:], ng, w=["NG"])
    K.act(E1[:], CV[:], AF.Exp, ["CV"], ["E1"], scale=-1.0)
    K.ts("dve", E1[:], E1[:], 1.0, None, ALU.add, None, ["E1"], ["E1"])
    K.recip(E1[:], E1[:], ["E1"], ["E1"])
    K.tt("dve", S[:], CV[:], E1[:], ALU.mult, ["CV", "E1"], ["S"])
    adaw_v = adaw.rearrange("(kc p) n -> p kc n", p=128)
    for j in range(6):
        wm = WM[j % 2]
        wk = "WM%d" % (j % 2)
        P.dma("pool", wm[:], adaw_v[:, :, j * 1024:(j + 1) * 1024], w=[wk])
        bank = j % 2
        for c in range(8):
            for kc in range(8):
                K.mm(PS[:, bank, c * 2:c * 2 + 2], wm[:, kc, c * 128:(c + 1) * 128], S[:, kc, :],
                     kc == 0, kc == 7, [wk, "S"], ["ps%d" % bank])
        K.tt("dve", MODL[:, j * 8:(j + 1) * 8, :],
             PS[:, bank, 0:16].rearrange("p (c v) -> p c v", v=2),
             ABs[:, j * 8:(j + 1) * 8, :], ALU.add, ["ps%d" % bank, "ABs"], ["MODL%d" % j])
    for j, gi in ((1, 0), (4, 1)):
        for v in range(2):
            K.stt(MODL[:, j * 8:(j + 1) * 8, v], MODL[:, j * 8:(j + 1) * 8, v], 1.0, NG[:, gi, :],
                  ALU.add, ALU.mult, ["MODL%d" % j, "NG"], ["MODL%d" % j])
    P.dma("sp", m_loc, MODL[:].rearrange("p a v -> p (a v)"), r=["MODL%d" % j for j in range(6)], w=["m_loc"])
    P.cc("AllGather", GROUPS, m_loc, m_gat, r=["m_loc"], w=["m_gat"])
    mg = m_gat.rearrange("(r p) (a v) -> r p a v", p=128, v=2)
    for L in range(4):
        P.dma("sp", K.MOD[L][:], mg[L], r=["m_gat"], w=["MOD%d" % L])


def halo_segment(K, e_loc, e_gat):
    P = K.P
    U = K.U
    E4 = K.sb("E4", [128, 4, 8, 2, 15], BF16)
    HAL = K.sb("HAL", [128, 2, 8, 15], F32)
    ED = K.sb("ED", [128, 8, 2, 15], BF16)
    ukeys = ["U%d" % tc for tc in range(5)]
    K.cp("pool", ED[:, :, 0, :], U[:, :, 15:30], ukeys, ["ED"])
    K.cp("pool", ED[:, :, 1, :], U[:, :, NLAT:NLAT + 15], ukeys, ["ED"])
    P.dma("sp", e_loc, ED[:].rearrange("p c s e -> p (c s e)"), r=["ED"], w=["e_loc"])
    P.cc("AllGather", GROUPS, e_loc, e_gat, r=["e_loc"], w=["e_gat"])
    P.dma("sp", E4[:], e_gat.rearrange("(r p) (c s e) -> p r c s e", p=128, c=8, s=2), r=["e_gat"], w=["E4"])
    for side in range(2):
        src_s = 1 - side
        for rr in range(4):
            mcol = K.HMASK[:, side * 4 + rr:side * 4 + rr + 1]
            if rr == 0:
                K.ts("dve", HAL[:, side], E4[:, rr, :, src_s, :], mcol, None, ALU.mult, None,
                     ["E4", "HMASK"], ["HAL%d" % side])
            else:
                K.stt(HAL[:, side], E4[:, rr, :, src_s, :], mcol, HAL[:, side], ALU.mult, ALU.add,
                      ["E4", "HMASK", "HAL%d" % side], ["HAL%d" % side])
    K.cp("dve", U[:, :, 0:15], HAL[:, 0], ["HAL0"], ["U0"])
    K.cp("dve", U[:, :, NLAT + 15:NLAT + 30], HAL[:, 1], ["HAL1"], ["U3"])


def build_fused():
    nc = bass.Bass("TRN2", target_bir_lowering=False)
    with ExitStack() as es:
        K = Ctx(nc, es)
        P = K.P
        di = K.dram_in
        xin = di("x_in", [128, 8, NT], F32)
        adaw = di("adaw", [D, 6 * D], F32)
        adab = di("adab", [128, 48, 2], F32)
        cvec = di("cvec", [128, 8, 2], F32)
        ng = di("ng", [128, 2, 8], F32)
        hmask = di("hmask", [128, 8], F32)
        rope = di("rope", [128, 2, NLAT], F32)
        blk = di("blk", [128, 128], BF16)
        perm = di("perm", [128, 128], F32)
        cs64 = di("cs64", [128, 256], BF16)
        ident = di("ident", [128, 128], BF16)
        dft = di("dft", [64, 128, 2, NLAT], BF16)
        dftc = di("dftc", [2, 128, 2, 256], BF16)
        w1 = [di("w1_%d" % L, [D, 4 * D], F32) for L in range(4)]
        w2 = [di("w2_%d" % L, [4 * D, D], F32) for L in range(4)]
        win = [di("win_%d" % j, [D, 1536], F32) for j in range(2)]
        qkg = [di("qkg_%d" % j, [128, 8], F32) for j in range(2)]
        wout = [di("wout_%d" % j, [D, D], F32) for j in range(2)]
        wpw1 = [di("wpw1_%d" % j, [D, 2 * D], F32) for j in range(2)]
        bpw1 = [di("bpw1_%d" % j, [128, 16], F32) for j in range(2)]
        wdw = [di("wdw_%d" % j, [128, 8, 31], F32) for j in range(2)]
        pv4 = [di("pv4_%d" % j, [128, 4, 8], F32) for j in range(2)]
        wpw2 = [di("wpw2_%d" % j, [D, D], F32) for j in range(2)]
        xo = K.dram_out("x_o", [128, 8, NLAT], F32)

        def internal(name, shape, dt):
            return nc.dram_tensor(name, list(shape), dt, kind="Internal").ap()

        m_loc = internal("m_loc", [128, 96], F32)
        m_gat = internal("m_gat", [512, 96], F32)
        alloc_common(K)
        alloc_eps(K)
        K.HMASK = K.sb("HMASK", [128, 8], F32)
        P.dma("sp", K.HMASK[:], hmask, w=["HMASK"])
        for L in range(4):
            K.MOD[L] = K.sb("MODL%d" % L, [128, 48, 2], F32)
        load_x(K, xin)
        with ExitStack() as ph:
            K.es = ph
            mod_segment(K, adaw, adab, cvec, ng, m_loc, m_gat)
            P.barrier()
        K.es = es
        K.BLK = K.sb("BLK", [128, 128], BF16)
        K.PERM = K.sb("PERM", [128, 128], F32)
        K.CS64 = K.sb("CS64", [128, 256], BF16)
        P.dma("sp", K.BLK[:], blk, w=["BLK"])
        P.dma("sp", K.PERM[:], perm, w=["PERM"])
        P.dma("sp", K.CS64[:], cs64, w=["CS64"])
        K.IDENT = K.sb("IDENT", [128, 128], BF16)
        P.dma("sp", K.IDENT[:], ident, w=["IDENT"])
        for L in range(4):
            j = L // 2
            ntc = 4 if L == 3 else 5
            if L % 2 == 0:
                kt_loc = [internal("kt_loc%d_%d" % (L, k_), [128, NT], BF16) for k_ in range(2)]
                kt_gat = [internal("kt_gat%d_%d" % (L, k_), [512, NT], BF16) for k_ in range(2)]
                v_loc = [internal("v_loc%d_%d" % (L, k_), [128, 18 * 130], BF16) for k_ in range(2)]
                v_gat = [internal("v_gat%d_%d" % (L, k_), [512, 18 * 130], BF16) for k_ in range(2)]
                ab_loc = [internal("ab_loc%d_%d" % (L, k_), [128, 6 * 512], BF16) for k_ in range(3)]
                ab_gat = [internal("ab_gat%d_%d" % (L, k_), [512, 6 * 512], BF16) for k_ in range(3)]
                with ExitStack() as ql:
                    K.es = ql
                    K.QT = K.sb("QT", [128, 6, NT], BF16)
                    with ExitStack() as ph:
                        K.es = ph
                        alloc_norm(K)
                        alloc_epre_noqt(K)
                        abk = [[], [], []]
                        for tc in range(5):
                            t0_, W_ = TCS[tc]
                            for t_ in range(0, W_ // 128, 2):
                                abk[(t0_ // 128 + t_) // 6].append("ab_o%d.%d" % (tc, t_))

                        def after_tc(tc, ab_loc=ab_loc, ab_gat=ab_gat, abk=abk):
                            k_ = {1: 0, 2: 1, 4: 2}.get(tc)
                            if k_ is not None:
                                P.cc("AllGather", GROUPS, ab_loc[k_], ab_gat[k_], r=abk[k_], w=["ab_gat%d" % k_])

                        epre_segment(K, L, win[j], rope, qkg[j], blk, perm, cs64,
                                     kt_loc,
                                     [v.rearrange("p (t s e) -> p t s e", t=18, s=2) for v in v_loc],
                                     [a.rearrange("p (t n) -> p t n", t=6) for a in ab_loc],
                                     final=False, load_consts=False, after_tc=after_tc)
                        for k_ in range(2):
                            P.cc("AllGather", GROUPS, kt_loc[k_], kt_gat[k_], r=["kt_o%d" % k_], w=["kt_gat%d" % k_])
                            P.cc("AllGather", GROUPS, v_loc[k_], v_gat[k_],
                                 r=["v_o%d.%d" % (tc, k_) for tc in range(5)], w=["v_gat%d" % k_])
                        P.barrier(keep=["kt_gat0", "kt_gat1", "v_gat0", "v_gat1"])
                    with ExitStack() as ph:
                        K.es = ph
                        alloc_epost(K, with_qt=False)
                        epost_segment(K, L,
                                      [k_.rearrange("(r p) t -> r p t", p=128) for k_ in kt_gat],
                                      [v.rearrange("(r p) (t s e) -> r p t s e", p=128, t=18, s=2) for v in v_gat],
                                      [a.rearrange("(r p) (t n) -> r p t n", p=128, t=6) for a in ab_gat],
                                      dft, dftc, wout[j])
                        P.barrier()
            else:
                e_loc = internal("e_loc%d" % L, [128, 240], BF16)
                e_gat = internal("e_gat%d" % L, [512, 240], BF16)
                with ExitStack() as ul:
                    K.es = ul
                    K.U = K.sb("U", [128, 8, UW], BF16)
                    with ExitStack() as ph:
                        K.es = ph
                        alloc_norm(K)
                        alloc_opre(K, with_u=False)
                        opre_segment(K, L, wpw1[j], bpw1[j], ntc)
                        halo_segment(K, e_loc, e_gat)
                        P.barrier()
                    with ExitStack() as ph:
                        K.es = ph
                        alloc_opost(K, with_u=False)
                        opost_segment(K, L, wdw[j], pv4[j], wpw2[j], ntc, ukeys=True)
                        P.barrier()
            with ExitStack() as ph:
                K.es = ph
                alloc_norm(K)
                alloc_mlp(K)
                def store_tc(tc):
                    t0_, W_ = TCS[tc]
                    P.dma("sp", xo[:, :, t0_:t0_ + W_], K.X[:, :, t0_:t0_ + W_], r=xkeys(tc), w=["xo%d" % tc], final=True)

                mlp_segment(K, L, w1[L], w2[L], ntc, after_last=store_tc if L == 3 else None)
                P.barrier()
        K.es = es
        P.emit()
    return nc


def kernel(x, c, ctx, c_ctx, ada_w, ada_b, norm1_g, norm2_g, mlp_w1, mlp_w2,
           attn_w_in, q_norm_g, k_norm_g, attn_w_out,
           conv_w_pw1, conv_b_pw1, conv_w_dw, conv_b_dw, conv_ln_g, conv_ln_b,
           conv_w_pw2, conv_b_pw2):
    f32 = lambda a: np.ascontiguousarray(np.asarray(a, dtype=np.float32))
    x, c, ctx, c_ctx = f32(x), f32(c), f32(ctx), f32(c_ctx)
    ada_w, ada_b, norm1_g, norm2_g = f32(ada_w), f32(ada_b), f32(norm1_g), f32(norm2_g)
    mlp_w1, mlp_w2, attn_w_in, attn_w_out = f32(mlp_w1), f32(mlp_w2), f32(attn_w_in), f32(attn_w_out)
    conv_w_pw1, conv_w_pw2, conv_w_dw = f32(conv_w_pw1), f32(conv_w_pw2), f32(conv_w_dw)
    C = consts()
    shared = {}
    for L in range(4):
        shared["w1_%d" % L] = mlp_w1[L]
        shared["w2_%d" % L] = mlp_w2[L]
    for j in range(2):
        g = np.zeros((128, 8), np.float32)
        for fc in range(8):
            src = np.asarray(q_norm_g[j] if fc < 6 else k_norm_g[j], np.float32)
            g[:64, fc] = src
            g[64:, fc] = src
        shared["win_%d" % j] = perm_win(attn_w_in[j])
        shared["qkg_%d" % j] = g
        shared["wout_%d" % j] = attn_w_out[j]
        shared["wpw1_%d" % j] = conv_w_pw1[j]
        shared["bpw1_%d" % j] = fm_vec(conv_b_pw1[j])
        shared["wdw_%d" % j] = np.ascontiguousarray(conv_w_dw[j].reshape(31, 8, 128).transpose(2, 1, 0))
        shared["pv4_%d" % j] = np.ascontiguousarray(np.stack(
            [fm_vec(conv_b_dw[j]), fm_vec(conv_ln_g[j]), fm_vec(conv_ln_b[j]), fm_vec(conv_b_pw2[j])], axis=1))
        shared["wpw2_%d" % j] = conv_w_pw2[j]
    shared.update(blk=C["blk"], perm=C["perm"], cs64=C["cs64"], dftc=C["dftc"],
                  ident=_bf(np.eye(128, dtype=np.float32)))
    maps = []
    for i in range(NCORES):
        b, r = i // 4, i % 4
        xt = np.concatenate([x[b, r * NLAT:(r + 1) * NLAT], ctx[b]], axis=0)
        adab = fm_vec(ada_b[r])
        hm = np.zeros((128, 8), np.float32)
        if r > 0:
            hm[:, r - 1] = 1.0
        if r < 3:
            hm[:, 4 + r + 1] = 1.0
        m = dict(shared)
        m.update(x_in=fm_tokens(xt), adaw=ada_w[r],
                 adab=np.ascontiguousarray(np.repeat(adab[:, :, None], 2, axis=2)),
                 cvec=np.ascontiguousarray(np.stack([fm_vec(c[b]), fm_vec(c_ctx)], axis=-1)),
                 ng=np.ascontiguousarray(np.stack([fm_vec(norm1_g[r]), fm_vec(norm2_g[r])], axis=1)),
                 hmask=hm, rope=C["rope"][r], dft=C["dft"][r])
        maps.append(m)
    if "F" not in _PROGS:
        _PROGS["F"] = build_fused()
    res = run_bass_kernel_spmd(_PROGS["F"], maps, core_ids=list(range(NCORES))).results
    out = np.zeros((2, 4 * NLAT, D), np.float32)
    for i in range(NCORES):
        b, r = i // 4, i % 4
        xo = np.asarray(res[i]["x_o"])
        out[b, r * NLAT:(r + 1) * NLAT] = xo.transpose(2, 1, 0).reshape(NLAT, D)
    return out
```

```python
import math
from contextlib import ExitStack

import numpy as np
import ml_dtypes
import concourse.bass as bass
import concourse.mybir as mybir
from concourse.bass_utils import run_bass_kernel_spmd

F32 = mybir.dt.float32
BF16 = mybir.dt.bfloat16
AF = mybir.ActivationFunctionType
ALU = mybir.AluOpType
NPBF = ml_dtypes.bfloat16

D = 1024
NLAT = 2048
NCTX = 256
NT = NLAT + NCTX
TCS = [(0, 512), (512, 512), (1024, 512), (1536, 512), (2048, 256)]
EPS = 1e-6
NCORES = 8


class _Op:
    __slots__ = ("eng", "fn", "deps", "dma", "sem", "semval", "signal", "count", "idx", "final", "inc")


class Prog:
    ENGS = ("pe", "act", "dve", "pool", "sp")

    def __init__(self, nc):
        self.nc = nc
        self.ops = []
        self.state = {}
        self.dma_sem_of = {}
        self.dma_sem_cnt = []
        self.finals = []

    def _add(self, eng, fn, r, w, dma=False, final=False, inc=16):
        op = _Op()
        op.inc = inc
        op.eng, op.fn, op.dma, op.final = eng, fn, dma, final
        op.signal = False
        op.count = None
        op.idx = len(self.ops)
        deps = {}
        for k in r:
            st = self.state.setdefault(k, [None, []])
            if st[0] is not None:
                deps[st[0]] = "raw"
        for k in w:
            st = self.state.setdefault(k, [None, []])
            if st[0] is not None:
                deps[st[0]] = "waw"
            for ri in st[1]:
                if ri not in deps:
                    deps[ri] = "war"
        for k in r:
            rl = self.state[k][1]
            if not dma:
                rl[:] = [ri for ri in rl if self.ops[ri].dma or self.ops[ri].eng != eng]
            rl.append(op.idx)
        for k in w:
            self.state[k] = [op.idx, []]
        op.deps = []
        latest = {}
        for di, kind in deps.items():
            dop = self.ops[di]
            if dop.dma:
                op.deps.append(di)
            elif dop.eng == eng:
                if eng == "pe":
                    continue
                latest[dop.eng] = max(latest.get(dop.eng, -1), di)
            else:
                latest[dop.eng] = max(latest.get(dop.eng, -1), di)
        for di in latest.values():
            self.ops[di].signal = True
            op.deps.append(di)
        if dma:
            key = w[0]
            if key not in self.dma_sem_of:
                self.dma_sem_of[key] = len(self.dma_sem_cnt)
                self.dma_sem_cnt.append(0)
            si = self.dma_sem_of[key]
            self.dma_sem_cnt[si] += inc
            op.sem = si
            op.semval = self.dma_sem_cnt[si]
            if final:
                self.finals.append(op.idx)
        self.ops.append(op)
        return op

    def barrier(self, keep=()):
        last = {}
        lastdma = {}
        kept = {}
        for k in keep:
            st = self.state.get(k)
            if st is not None and st[0] is not None and self.ops[st[0]].dma:
                kept[k] = st[0]
        skip_sems = set(self.ops[i].sem for i in kept.values())
        seen = getattr(self, "_bar_seen", {})
        for op in self.ops:
            if op.dma:
                if op.sem not in skip_sems and op.semval > seen.get(op.sem, 0):
                    lastdma[op.sem] = op.idx
            elif op.fn is not None:
                last[op.eng] = op.idx
        for si, li in lastdma.items():
            seen[si] = self.ops[li].semval
        self._bar_seen = seen
        for e in self.ENGS:
            op = _Op()
            op.eng, op.fn, op.dma, op.final = e, None, False, False
            op.signal = False
            op.count = None
            op.idx = len(self.ops)
            op.deps = []
            for e2, li in last.items():
                if e2 != e:
                    self.ops[li].signal = True
                    op.deps.append(li)
            for si, li in lastdma.items():
                op.deps.append(li)
            self.ops.append(op)
        self.state = {k: [i, []] for k, i in kept.items()}

    def op(self, eng, fn, r=(), w=()):
        return self._add(eng, fn, tuple(r), tuple(w))

    def dma(self, q, out, in_, r=(), w=(), final=False):
        assert q in ("sp", "pool")
        return self._add(q, lambda e: e.dma_start(out=out, in_=in_), tuple(r), tuple(w),
                         dma=True, final=final)

    def cc(self, kind, groups, in_ap, out_ap, r=(), w=()):
        return self._add("pool", lambda e: e.collective_compute(kind, ALU.bypass, replica_groups=groups,
                                                                ins=[in_ap], outs=[out_ap]),
                         tuple(r), tuple(w), dma=True, inc=1)

    def emit(self):
        nc = self.nc
        cnt = {e: 0 for e in self.ENGS}
        for op in self.ops:
            if op.dma or op.fn is None:
                continue
            if op.signal:
                cnt[op.eng] += 1
                op.count = cnt[op.eng]
        for e in self.ENGS:
            assert cnt[e] < 60000, (e, cnt[e])
        for v in self.dma_sem_cnt:
            assert v < 60000, v
        with ExitStack() as es:
            esem = {e: es.enter_context(nc.semaphore("s_" + e)) for e in ("pe", "act", "dve", "pool")}
            dsem = [es.enter_context(nc.semaphore("d%d" % i)) for i in range(len(self.dma_sem_cnt))]
            block = es.enter_context(nc.Block())
            ops = self.ops
            finals = self.finals

            def run(ename, e):
                waited = {}
                for op in ops:
                    if op.eng != ename:
                        continue
                    for di in op.deps:
                        dop = ops[di]
                        if dop.dma:
                            sem, val, key = dsem[dop.sem], dop.semval, ("d", dop.sem)
                        else:
                            sem, val, key = esem[dop.eng], dop.count, ("e", dop.eng)
                        if waited.get(key, 0) >= val:
                            continue
                        waited[key] = val
                        e.wait_ge(sem, val)
                    if op.fn is None:
                        continue
                    ins = op.fn(e)
                    if op.dma:
                        ins.then_inc(dsem[op.sem], op.inc)
                    elif op.signal:
                        ins.then_inc(esem[ename], 1)
                if ename == "sp":
                    for fi in finals:
                        fop = ops[fi]
                        e.wait_ge(dsem[fop.sem], fop.semval)

            @block.tensor
            def _(e):
                run("pe", e)

            @block.scalar
            def _(e):
                run("act", e)

            @block.vector
            def _(e):
                run("dve", e)

            @block.gpsimd
            def _(e):
                run("pool", e)

            @block.sync
            def _(e):
                run("sp", e)


class Ctx:
    def __init__(self, nc, es):
        self.nc = nc
        self.es = es
        self.P = Prog(nc)
        self.din = {}
        self.dout = {}

    def dram_in(self, name, shape, dt):
        t = self.nc.dram_tensor(name, list(shape), dt, kind="ExternalInput").ap()
        self.din[name] = t
        return t

    def dram_out(self, name, shape, dt):
        t = self.nc.dram_tensor(name, list(shape), dt, kind="ExternalOutput").ap()
        self.dout[name] = t
        return t

    def sb(self, name, shape, dt):
        self.nsb = getattr(self, "nsb", 0) + 1
        return self.es.enter_context(self.nc.sbuf_tensor("%s_%d" % (name, self.nsb), list(shape), dt))

    def mm(self, out, lhsT, rhs, start, stop, r, w):
        self.P.op("pe", lambda e: e.matmul(out, lhsT, rhs, start=start, stop=stop), r, w)

    def act(self, out, in_, func, r, w, bias=None, scale=None):
        kw = {}
        if bias is not None:
            kw["bias"] = bias
        if scale is not None:
            kw["scale"] = scale
        self.P.op("act", lambda e: e.activation(out, in_, func, **kw), r, w)

    def tt(self, eng, out, in0, in1, op, r, w):
        self.P.op(eng, lambda e: e.tensor_tensor(out, in0, in1, op), r, w)

    def ts(self, eng, out, in0, s1, s2, op0, op1, r, w):
        if op1 is None:
            self.P.op(eng, lambda e: e.tensor_scalar(out, in0, s1, None, op0), r, w)
        else:
            self.P.op(eng, lambda e: e.tensor_scalar(out, in0, s1, s2, op0, op1), r, w)

    def stt(self, out, in0, scalar, in1, op0, op1, r, w):
        self.P.op("dve", lambda e: e.scalar_tensor_tensor(out, in0, scalar, in1, op0, op1), r, w)

    def cp(self, eng, out, in_, r, w):
        self.P.op(eng, lambda e: e.tensor_copy(out, in_), r, w)

    def recip(self, out, in_, r, w):
        self.P.op("dve", lambda e: e.reciprocal(out, in_), r, w)

    def memset(self, eng, ap, val, w):
        self.P.op(eng, lambda e: e.memset(ap, val), (), w)


def _bf(a):
    return np.ascontiguousarray(a).astype(NPBF)


def build_mod():
    nc = bass.Bass("TRN2", target_bir_lowering=False)
    with ExitStack() as es:
        K = Ctx(nc, es)
        P = K.P
        adaw = K.dram_in("adaw", [D, 6 * D], F32)
        adab = K.dram_in("adab", [128, 48, 2], F32)
        cvec = K.dram_in("cvec", [128, 8, 2], F32)
        ng = K.dram_in("ng", [128, 2, 8], F32)
        modo = K.dram_out("modo", [128, 48, 2], F32)
        CV = K.sb("CV", [128, 8, 2], F32)
        ABs = K.sb("ABs", [128, 48, 2], F32)
        NG = K.sb("NG", [128, 2, 8], F32)
        E1 = K.sb("E1", [128, 8, 2], F32)
        S = K.sb("S", [128, 8, 2], BF16)
        MOD = K.sb("MOD", [128, 48, 2], F32)
        WM = [K.sb("WM%d" % i, [128, 8, 1024], BF16) for i in range(2)]
        PS = es.enter_context(nc.psum_tensor("PS", [128, 8, 512], F32))
        P.dma("sp", CV[:], cvec, w=["CV"])
        P.dma("sp", ABs[:], adab, w=["ABs"])
        P.dma("sp", NG[:], ng, w=["NG"])
        K.act(E1[:], CV[:], AF.Exp, ["CV"], ["E1"], scale=-1.0)
        K.ts("dve", E1[:], E1[:], 1.0, None, ALU.add, None, ["E1"], ["E1"])
        K.recip(E1[:], E1[:], ["E1"], ["E1"])
        K.tt("dve", S[:], CV[:], E1[:], ALU.mult, ["CV", "E1"], ["S"])
        adaw_v = adaw.rearrange("(kc p) n -> p kc n", p=128)
        for j in range(6):
            wm = WM[j % 2]
            wk = "WM%d" % (j % 2)
            P.dma("pool", wm[:], adaw_v[:, :, j * 1024:(j + 1) * 1024], w=[wk])
            bank = j % 2
            for c in range(8):
                for kc in range(8):
                    K.mm(PS[:, bank, c * 2:c * 2 + 2], wm[:, kc, c * 128:(c + 1) * 128], S[:, kc, :],
                         kc == 0, kc == 7, [wk, "S"], ["ps%d" % bank])
            K.tt("dve", MOD[:, j * 8:(j + 1) * 8, :],
                 PS[:, bank, 0:16].rearrange("p (c v) -> p c v", v=2),
                 ABs[:, j * 8:(j + 1) * 8, :], ALU.add, ["ps%d" % bank, "ABs"], ["MOD%d" % j])
        for j, gi in ((1, 0), (4, 1)):
            for v in range(2):
                K.stt(MOD[:, j * 8:(j + 1) * 8, v], MOD[:, j * 8:(j + 1) * 8, v], 1.0, NG[:, gi, :],
                      ALU.add, ALU.mult, ["MOD%d" % j, "NG"], ["MOD%d" % j])
        P.dma("sp", modo, MOD[:], r=["MOD%d" % j for j in range(6)], w=["modo"], final=True)
        P.emit()
    return nc


def alloc_common(K):
    nc, es = K.nc, K.es
    K.X = K.sb("X", [128, 8, NT], F32)
    K.PS = es.enter_context(nc.psum_tensor("PS", [128, 8, 512], F32))
    K.ONES = K.sb("ONES", [128, 128], BF16)
    K.MOD = {}
    K.memset("pool", K.ONES[:], 1.0, ["ONES"])


def alloc_norm(K):
    K.SQ = K.sb("SQ", [128, 8, 512], BF16)
    K.LNV = K.sb("LNV", [128, 512], F32)
    K.RSTD = K.sb("RSTD", [128, 512], F32)
    K.TMP = [K.sb("TMP%d" % i, [128, 512], F32) for i in range(2)]


def load_mod(K, L, name=None):
    t = K.dram_in(name or ("mod%d" % L), [128, 48, 2], F32)
    K.MOD[L] = K.sb("MODL%d" % L, [128, 48, 2], F32)
    K.P.dma("sp", K.MOD[L][:], t, w=["MOD%d" % L])


def xkeys(tc):
    return ["X%d.%d" % (tc, c) for c in range(8)]


def norm_sq(K, tc):
    t0, W = TCS[tc]
    X = K.X
    for c in range(8):
        eng = "dve" if c % 2 == 0 else "pool"
        K.tt(eng, K.SQ[:, c, :W], X[:, c, t0:t0 + W], X[:, c, t0:t0 + W], ALU.mult,
             ["X%d.%d" % (tc, c)], ["SQ%d" % c])


def norm_rest(K, L, which, tc, out_fn, out_keys):
    t0, W = TCS[tc]
    v = 1 if tc == 4 else 0
    MOD = K.MOD[L]
    ja, jb = (1, 0) if which == 0 else (4, 3)
    mk = "MOD%d" % L
    X, PS = K.X, K.PS
    for c in range(8):
        K.mm(PS[:, 7, :W], K.ONES[:], K.SQ[:, c, :W], c == 0, c == 7, ["ONES", "SQ%d" % c], ["ps7"])
    K.act(K.LNV[:, :W], PS[:, 7, :W], AF.Ln, ["ps7"], ["LNV"], bias=K.EPSB[:], scale=1.0 / D)
    K.act(K.RSTD[:, :W], K.LNV[:, :W], AF.Exp, ["LNV"], ["RSTD"], scale=-0.5)
    for c in range(8):
        tb = c % 2
        K.stt(K.TMP[tb][:, :W], X[:, c, t0:t0 + W], MOD[:, ja * 8 + c, v:v + 1], K.RSTD[:, :W],
              ALU.mult, ALU.mult, ["X%d.%d" % (tc, c), mk, "RSTD"], ["TMP%d" % tb])
        K.act(out_fn(c), K.TMP[tb][:, :W], AF.Identity, ["TMP%d" % tb, mk], out_keys(c),
              bias=MOD[:, jb * 8 + c, v:v + 1], scale=1.0)


def norm_mod(K, L, which, tc, out_fn, out_keys):
    norm_sq(K, tc)
    norm_rest(K, L, which, tc, out_fn, out_keys)


def alloc_eps(K):
    K.EPSB = K.sb("EPSB", [128, 1], F32)
    K.memset("pool", K.EPSB[:], EPS, ["EPSB"])
    K.ONEB = K.sb("ONEB", [128, 1], F32)
    K.memset("pool", K.ONEB[:], 1.0, ["ONEB"])


def mlp_segment(K, L, w1, w2, ntc=5, after_last=None):
    X, PS = K.X, K.PS
    H = K.H
    MOD = K.MOD[L]
    mk = "MOD%d" % L
    def norm2(tc):
        t0, W = TCS[tc]
        norm_mod(K, L, 1, tc, lambda c, t0=t0, W=W: H[:, c, t0:t0 + W], lambda c, tc=tc: ["H%d" % tc])

    norm2(0)
    w1v = w1.rearrange("(kc p) n -> p kc n", p=128)
    w2v = w2.rearrange("(fc p) n -> p fc n", p=128)
    items = [(e, tc) for e in range(8) for tc in range(ntc)]

    def load(e):
        K.P.dma("pool", K.W1E[e % 2][:], w1v[:, :, e * 512:(e + 1) * 512], w=["W1E%d" % (e % 2)])
        K.P.dma("pool", K.W2E[e % 2][:], w2v[:, e * 4:(e + 1) * 4, :], w=["W2E%d" % (e % 2)])

    def part1(i):
        e, tc = items[i]
        t0, W = TCS[tc]
        ab = i % 2
        for f in range(4):
            bank = f % 2
            for kc in range(8):
                K.mm(PS[:, bank, :W], K.W1E[e % 2][:, kc, f * 128:(f + 1) * 128], H[:, kc, t0:t0 + W],
                     kc == 0, kc == 7, ["W1E%d" % (e % 2), "H%d" % tc], ["ps%d" % bank])
            K.act(K.RL[f % 2][:, :W], PS[:, bank, :W], AF.Relu, ["ps%d" % bank], ["RL%d" % (f % 2)])
            eng = "pool" if f % 2 == 0 else "dve"
            K.tt(eng, K.AH[ab][:, f, :W], K.RL[f % 2][:, :W], K.RL[f % 2][:, :W], ALU.mult,
                 ["RL%d" % (f % 2)], ["AH%d.%d" % (ab, f)])

    def part2(i):
        e, tc = items[i]
        t0, W = TCS[tc]
        v = 1 if tc == 4 else 0
        ab = i % 2
        for d in range(8):
            bank = 2 + d % 2
            for f in range(4):
                K.mm(PS[:, bank, :W], K.W2E[e % 2][:, f, d * 128:(d + 1) * 128], K.AH[ab][:, f, :W],
                     f == 0, f == 3, ["W2E%d" % (e % 2), "AH%d.%d" % (ab, f)], ["ps%d" % bank])
            K.stt(X[:, d, t0:t0 + W], PS[:, bank, :W], MOD[:, 40 + d, v:v + 1], X[:, d, t0:t0 + W],
                  ALU.mult, ALU.add, ["ps%d" % bank, mk, "X%d.%d" % (tc, d)], ["X%d.%d" % (tc, d)])

    load(0)
    part1(0)
    for i in range(len(items)):
        e0, tc0 = items[i]
        if tc0 == 1 and e0 + 1 < 8:
            load(e0 + 1)
        if i + 1 < len(items):
            if items[i + 1][0] == 0:
                norm2(items[i + 1][1])
            part1(i + 1)
        part2(i)
        if e0 == 7 and after_last is not None:
            after_last(tc0)


def alloc_mlp(K):
    K.H = K.sb("H", [128, 8, NT], BF16)
    K.W1E = [K.sb("W1E%d" % i, [128, 8, 512], BF16) for i in range(2)]
    K.W2E = [K.sb("W2E%d" % i, [128, 4, 1024], BF16) for i in range(2)]
    K.RL = [K.sb("RL%d" % i, [128, 512], F32) for i in range(2)]
    K.AH = [K.sb("AH%d" % i, [128, 4, 512], BF16) for i in range(2)]


def alloc_epre_noqt(K):
    alloc_epre(K, with_qt=False)


def alloc_epre(K, with_qt=True):
    K.WIN = K.sb("WIN", [128, 8, 1536], BF16)
    K.HC = [K.sb("HC%d" % i, [128, 8, 512], BF16) for i in range(2)]
    if with_qt:
        K.QT = K.sb("QT", [128, 6, NT], BF16)
    K.KTL = K.sb("KTL", [128, 2, NT], BF16)
    K.VTC = [K.sb("VTC%d" % i, [128, 4, 2, 2, 65], BF16) for i in range(2)]
    K.ABC = [K.sb("ABC%d" % i, [128, 4, 512], BF16) for i in range(2)]
    K.FTC = K.sb("FTC", [128, 2, 512], BF16)
    K.ROPE = [K.sb("ROPE0", [128, 2, 512], F32)] * 2
    K.PSB = [K.sb("PSB%d" % i, [128, 512], F32) for i in range(2)]
    K.SQ1 = [K.sb("SQ1%d" % i, [128, 512], BF16) for i in range(2)]
    K.QN = [K.sb("QN%d" % i, [128, 512], F32) for i in range(2)]
    K.T1 = [K.sb("T10", [128, 512], F32)] * 2
    K.T2 = [K.sb("T20", [128, 512], F32)] * 2
    K.LN2 = [K.sb("LN20", [128, 512], F32)] * 2
    K.RS2 = [K.sb("RS20", [128, 512], F32)] * 2
    if not hasattr(K, "BLK"):
        K.BLK = K.sb("BLK", [128, 128], BF16)
        K.PERM = K.sb("PERM", [128, 128], F32)
        K.CS64 = K.sb("CS64", [128, 256], BF16)
    K.QKG = K.sb("QKG", [128, 8], F32)


def epre_segment(K, L, win, rope, qkg, blk, perm, cs64, kt_o, v_o, ab_o, final=True, load_consts=True, after_tc=None):
    X, PS, P = K.X, K.PS, K.P
    P.dma("pool", K.WIN[:], win.rearrange("(kc p) n -> p kc n", p=128), w=["WIN"])
    if load_consts:
        P.dma("sp", K.BLK[:], blk, w=["BLK"])
        P.dma("sp", K.PERM[:], perm, w=["PERM"])
        P.dma("sp", K.CS64[:], cs64, w=["CS64"])
    P.dma("sp", K.QKG[:], qkg, w=["QKG"])
    K.memset("pool", K.VTC[0][:], 1.0, ["VTC0"])
    K.memset("pool", K.VTC[1][:], 1.0, ["VTC1"])
    def nrm(tc_, part):
        W_ = TCS[tc_][1]
        hcb = K.HC[tc_ % 2]
        hkb = "HC%d" % (tc_ % 2)
        if part == 0:
            norm_sq(K, tc_)
        else:
            norm_rest(K, L, 0, tc_, lambda c, hcb=hcb, W_=W_: hcb[:, c, :W_], lambda c, hkb=hkb: [hkb])

    nrm(0, 0)
    nrm(0, 1)
    for tc in range(5):
        t0, W = TCS[tc]
        hb = tc % 2
        hc = K.HC[hb]
        hk = "HC%d" % hb
        if tc < 4:
            P.dma("sp", K.ROPE[0][:], rope[:, :, t0:t0 + W], w=["ROPE0"])
        def stA(fc):
            pb = fc % 2
            b0 = fc % 2
            for kc in range(8):
                K.mm(PS[:, b0, :W], K.WIN[:, kc, fc * 128:(fc + 1) * 128], hc[:, kc, :W],
                     kc == 0, kc == 7, ["WIN", hk], ["ps%d" % b0])
            K.act(K.PSB[pb][:, :W], PS[:, b0, :W], AF.Identity, ["ps%d" % b0], ["PSB%d" % pb])
            K.tt("pool", K.SQ1[pb][:, :W], K.PSB[pb][:, :W], K.PSB[pb][:, :W], ALU.mult,
                 ["PSB%d" % pb], ["SQ1%d" % pb])

        def stB(fc):
            pb = fc % 2
            b1 = 2 + fc % 2
            K.mm(PS[:, b1, :W], K.BLK[:], K.SQ1[pb][:, :W], True, True, ["BLK", "SQ1%d" % pb], ["ps%d" % b1])
            K.act(K.LN2[pb][:, :W], PS[:, b1, :W], AF.Ln, ["ps%d" % b1], ["LN20"],
                  bias=K.EPSB[:], scale=1.0 / 64)
            K.act(K.RS2[pb][:, :W], K.LN2[pb][:, :W], AF.Exp, ["LN20"], ["RS20"], scale=-0.5)
            K.stt(K.QN[pb][:, :W], K.PSB[pb][:, :W], K.QKG[:, fc:fc + 1], K.RS2[pb][:, :W],
                  ALU.mult, ALU.mult, ["PSB%d" % pb, "QKG", "RS20"], ["QN%d" % pb])

        def stC(fc):
            pb = fc % 2
            b2 = 4 + fc % 2
            if fc < 6:
                dest, dk = K.QT[:, fc, t0:t0 + W], "QT%d.%d" % (tc, fc)
            else:
                dest, dk = K.KTL[:, fc - 6, t0:t0 + W], "KTL%d" % (fc - 6)
            if tc < 4:
                rp = K.ROPE[0]
                rk = "ROPE0"
                K.mm(PS[:, b2, :W], K.PERM[:], K.QN[pb][:, :W], True, True, ["PERM", "QN%d" % pb], ["ps%d" % b2])
                K.tt("dve", K.T1[pb][:, :W], K.QN[pb][:, :W], rp[:, 0, :W], ALU.mult,
                     ["QN%d" % pb, rk], ["T10"])
                K.tt("dve", K.T2[pb][:, :W], PS[:, b2, :W], rp[:, 1, :W], ALU.mult,
                     ["ps%d" % b2, rk], ["T20"])
                K.tt("pool", dest, K.T1[pb][:, :W], K.T2[pb][:, :W], ALU.add,
                     ["T10", "T20"], [dk])
            else:
                K.cp("pool", dest, K.QN[pb][:, :W], ["QN%d" % pb], [dk])

        stA(0)
        if tc + 1 < 5:
            nrm(tc + 1, 0)
        for fc in range(8):
            if fc + 1 < 8:
                stA(fc + 1)
            stB(fc)
            if fc >= 1:
                stC(fc - 1)
            if fc == 1 and tc + 1 < 5:
                nrm(tc + 1, 1)
        stC(7)
        rb = [6, 4, 5]
        ri = [0]

        def nb():
            b = rb[ri[0] % 3]
            ri[0] += 1
            return b

        for tt_ in range(W // 128):
            gt = t0 // 128 + tt_
            b = nb()
            for kc in range(8):
                K.mm(PS[:, b, 0:256], hc[:, kc, tt_ * 128:(tt_ + 1) * 128], K.WIN[:, kc, 1024:1280],
                     kc == 0, kc == 7, ["WIN", hk], ["ps%d" % b])
            K.act(K.VTC[tc % 2][:, tt_, :, :, 0:64], PS[:, b, 0:256].rearrange("p (a s e) -> p a s e", a=2, s=2),
                  AF.Identity, ["ps%d" % b], ["VTC%d" % (tc % 2)])
        for half in range(2):
            b = nb()
            for kc in range(8):
                K.mm(PS[:, b, :W], K.WIN[:, kc, 1280 + half * 128:1280 + (half + 1) * 128], hc[:, kc, :W],
                     kc == 0, kc == 7, ["WIN", hk], ["ps%d" % b])
            K.act(K.FTC[:, half, :W], PS[:, b, :W], AF.Identity, ["ps%d" % b], ["FTC%d" % half])
        for tt_ in range(W // 128):
            gt = t0 // 128 + tt_
            for half in range(2):
                b = nb()
                K.mm(PS[:, b, 0:256], K.FTC[:, half, tt_ * 128:(tt_ + 1) * 128], K.CS64[:],
                     True, True, ["FTC%d" % half, "CS64"], ["ps%d" % b])
                K.cp("dve", K.ABC[tc % 2][:, tt_, half * 256:(half + 1) * 256], PS[:, b, 0:256],
                     ["ps%d" % b], ["ABC%d" % (tc % 2)])
        nt_ = W // 128
        g0 = t0 // 128
        for a_ in range(2):
            P.dma("sp", v_o[a_][:, g0:g0 + nt_], K.VTC[tc % 2][:, :nt_, a_], r=["VTC%d" % (tc % 2)],
                  w=["v_o%d.%d" % (tc, a_)], final=final)
        for t_ in range(0, nt_, 2):
            ch, off = (g0 + t_) // 6, (g0 + t_) % 6
            P.dma("sp", ab_o[ch][:, off:off + 2, :], K.ABC[tc % 2][:, t_:t_ + 2, :], r=["ABC%d" % (tc % 2)],
                  w=["ab_o%d.%d" % (tc, t_)], final=final)
        if after_tc is not None:
            after_tc(tc)
    for k_ in range(2):
        P.dma("sp", kt_o[k_], K.KTL[:, k_, :], r=["KTL%d" % k_], w=["kt_o%d" % k_], final=final)


def fm_vec(v):
    v = np.asarray(v, np.float32)
    return np.ascontiguousarray(v.reshape(-1, 128).T)


def fm_tokens(a):
    T = a.shape[0]
    return np.ascontiguousarray(a.reshape(T, 8, 128).transpose(2, 1, 0))


_CONST = {}


def consts():
    if _CONST:
        return _CONST
    blk = np.zeros((128, 128), np.float32)
    blk[:64, :64] = 1.0
    blk[64:, 64:] = 1.0
    perm = np.zeros((128, 128), np.float32)
    for j in range(64):
        perm[2 * j + 1, 2 * j] = -1.0
        perm[2 * j, 2 * j + 1] = 1.0
    n = np.arange(64)
    ang = 2 * np.pi * np.outer(n, n) / 64.0
    c64, s64 = np.cos(ang), np.sin(ang)
    cs = np.zeros((128, 256), np.float64)
    for g in range(2):
        cs[g * 64:(g + 1) * 64, g * 64:(g + 1) * 64] = c64
        cs[g * 64:(g + 1) * 64, 128 + g * 64:128 + (g + 1) * 64] = s64
    _CONST["blk"] = _bf(blk)
    _CONST["perm"] = perm
    _CONST["cs64"] = _bf(cs)
    freqs = 10000.0 ** (-np.arange(16, dtype=np.float32) / 16)
    ropes = []
    for r in range(4):
        t = np.arange(r * NLAT, (r + 1) * NLAT)
        row = (t // 64).astype(np.float32)
        col = (t % 64).astype(np.float32)
        ang = np.concatenate([row[:, None] * freqs, col[:, None] * freqs], axis=-1).astype(np.float32)
        cos, sin = np.cos(ang), np.sin(ang)
        tab = np.zeros((128, 2, NLAT), np.float32)
        for p in range(128):
            jj = (p % 64) // 2
            tab[p, 0] = cos[:, jj]
            tab[p, 1] = sin[:, jj]
        ropes.append(tab)
    _CONST["rope"] = ropes
    tabs = []
    l = np.arange(8192, dtype=np.int64)
    sc = 1.0 / math.sqrt(8192 * 64)
    for r in range(4):
        k = np.arange(r * NLAT, (r + 1) * NLAT, dtype=np.int64)
        m = (l[:, None] * k[None, :]) % 8192
        a = 2 * np.pi * m / 8192.0
        tab = np.stack([np.cos(a) * sc, -np.sin(a) * sc], axis=1)
        tabs.append(_bf(tab.reshape(64, 128, 2, NLAT)))
    _CONST["dft"] = tabs
    l2 = np.arange(256, dtype=np.int64)
    m = (l2[:, None] * l2[None, :]) % 256
    a = 2 * np.pi * m / 256.0
    sc2 = 1.0 / math.sqrt(256 * 64)
    _CONST["dftc"] = _bf(np.stack([np.cos(a) * sc2, -np.sin(a) * sc2], axis=1).reshape(2, 128, 2, 256))
    return _CONST


def perm_win(w):
    cols = []
    for a in range(6):
        cols += list(range(a * 64, a * 64 + 64)) + list(range((a + 6) * 64, (a + 6) * 64 + 64))
    for kv in (0, 2, 1, 3):
        cols += list(range(768 + kv * 64, 768 + kv * 64 + 64))
    for kv in (0, 2, 1, 3):
        cols += list(range(1024 + kv * 64, 1024 + kv * 64 + 64))
    cols += list(range(1280, 1536))
    return np.ascontiguousarray(w[:, cols])


def build_p0():
    nc = bass.Bass("TRN2", target_bir_lowering=False)
    with ExitStack() as es:
        K = Ctx(nc, es)
        xin = K.dram_in("x_in", [128, 8, NT], F32)
        win = K.dram_in("win", [D, 1536], F32)
        rope = K.dram_in("rope", [128, 2, NLAT], F32)
        qkg = K.dram_in("qkg", [128, 8], F32)
        blk = K.dram_in("blk", [128, 128], BF16)
        perm = K.dram_in("perm", [128, 128], F32)
        cs64 = K.dram_in("cs64", [128, 256], BF16)
        kt_o = K.dram_out("kt_o", [128, 2, NT], BF16)
        v_o = K.dram_out("v_o", [128, 18, 2, 2, 65], BF16)
        ab_o = K.dram_out("ab_o", [128, 18, 512], BF16)
        qt_o = K.dram_out("qt_o", [128, 6, NT], BF16)
        alloc_common(K)
        alloc_eps(K)
        alloc_norm(K)
        alloc_epre(K)
        load_mod(K, 0)
        for c in range(8):
            K.P.dma("sp", K.X[:, c, :], xin[:, c, :], w=["X%d.%d" % (tc, c) for tc in range(5)])
        epre_segment(K, 0, win, rope, qkg, blk, perm, cs64, kt_o, v_o, ab_o)
        K.P.dma("sp", qt_o, K.QT[:], r=["QT%d.%d" % (tc, fc) for tc in range(5) for fc in range(6)],
                w=["qt_o"], final=True)
        K.P.emit()
    return nc


def alloc_epost(K, with_qt=True):
    if with_qt:
        K.QT = K.sb("QT", [128, 6, NT], BF16)
    K.KT = K.sb("KT", [128, 8448], BF16)
    K.V = K.sb("V", [128, 66, 2, 65], BF16)
    K.WO = K.sb("WO", [64, 12, 1024], BF16)
    K.WOF = K.sb("WOF", [128, 2, 1024], BF16)
    K.CAT = K.sb("CAT", [64, 6, 512], BF16)
    K.PB = [K.sb("PB%d" % i, [128, 2, 512], BF16) for i in range(3)]
    K.FT = K.sb("FT", [128, 2, NT], BF16)
    K.ABG = [K.sb("ABG%d" % i, [128, 2, 512], BF16) for i in range(2)]
    K.TABC = K.sb("TABC", [128, 2, 2, 256], BF16)
    K.ABGC = K.sb("ABGC", [128, 2, 512], BF16)
    K.OSB = K.sb("OSB", [65, 2, 512], F32)
    K.RDL = K.sb("RDL", [65, 2, 512], F32)
    K.RD = K.sb("RD", [65, 2, 512], BF16)
    K.ONESR = K.sb("ONESR", [128, 64], BF16)


def epost_segment(K, L, kt_all, v_all, ab_all, dft, dftc, wout, gk=()):
    X, PS, P = K.X, K.PS, K.P
    MOD = K.MOD[L]
    mk = "MOD%d" % L
    K.memset("pool", K.ONESR[:], 1.0, ["ONESR"])
    P.dma("pool", K.WO[:], wout[0:768, :].rearrange("(h d) n -> d h n", d=64), w=["WO"])
    P.dma("pool", K.WOF[:], wout[768:1024, :].rearrange("(c p) n -> p c n", p=128), w=["WOF"])
    TAB = [K.KT[:, 0:8192].rearrange("p (a b c) -> p a b c", a=2, b=2),
           K.V[:].rearrange("p a b c -> p (a b c)")[:, 0:8192].rearrange("p (a b c) -> p a b c", a=2, b=2)]
    TK = ["KTa", "Va"]
    for g in range(32):
        r = g // 8
        tl = (2 * g) % 16
        tb = g % 2
        P.dma("sp", TAB[tb], dft[2 * g:2 * g + 2].rearrange("l p s k -> p l s k"), w=[TK[tb]])
        P.dma("sp", K.ABG[tb][:], ab_all[tl // 6][r, :, tl % 6:tl % 6 + 2, :], r=gk, w=["ABG%d" % tb])
        for li in range(2):
            for half in range(2):
                for s in range(2):
                    for kc in range(4):
                        bank = half * 4 + kc
                        K.mm(PS[:, bank, :], K.ABG[tb][:, li, half * 256 + s * 128:half * 256 + (s + 1) * 128],
                             TAB[tb][:, li, s, kc * 512:(kc + 1) * 512],
                             g == 0 and li == 0 and s == 0, g == 31 and li == 1 and s == 1,
                             ["ABG%d" % tb, TK[tb]], ["ps%d" % bank])
    for half in range(2):
        for kc in range(4):
            bank = half * 4 + kc
            if bank % 2 == 0:
                K.act(K.FT[:, half, kc * 512:(kc + 1) * 512], PS[:, bank, :], AF.Identity, ["ps%d" % bank], ["FT%d" % kc])
            else:
                K.cp("dve", K.FT[:, half, kc * 512:(kc + 1) * 512], PS[:, bank, :], ["ps%d" % bank], ["FT%d" % kc])
    P.dma("sp", K.TABC[:], dftc.rearrange("l p s k -> p l s k"), w=["TABC"])
    P.dma("sp", K.ABGC[:], ab_all[2][0, :, 4:6, :], r=gk, w=["ABGC"])
    for half in range(2):
        for li in range(2):
            for s in range(2):
                K.mm(PS[:, half, 0:256], K.ABGC[:, li, half * 256 + s * 128:half * 256 + (s + 1) * 128],
                     K.TABC[:, li, s, :], li == 0 and s == 0, li == 1 and s == 1, ["ABGC", "TABC"], ["ps%d" % half])
        K.act(K.FT[:, half, 2048:2304], PS[:, half, 0:256], AF.Identity, ["ps%d" % half], ["FT4"])
    P.barrier()
    jobs = [(kp, tc, a3) for kp in range(2) for tc in range(5) for a3 in range(3)]

    def load_kv(kp):
        for r in range(4):
            P.dma("sp", K.KT[:, r * 2048:(r + 1) * 2048], kt_all[kp][r, :, 0:2048], r=["kt_gat%d" % kp], w=["KT%d" % r])
            P.dma("sp", K.V[:, r * 16:(r + 1) * 16, :, :], v_all[kp][r, :, 0:16, :, :], r=["v_gat%d" % kp], w=["V%d" % r])
        P.dma("sp", K.KT[:, 8192:8448], kt_all[kp][0, :, 2048:2304], r=["kt_gat%d" % kp], w=["KT4"])
        P.dma("sp", K.V[:, 64:66, :, :], v_all[kp][0, :, 16:18, :, :], r=["v_gat%d" % kp], w=["V4"])

    def kts_of(tc):
        return list(range(66)) if tc < 4 else [64, 65]

    def S(job, i, g0):
        kp, tc, a3 = job
        t0, W = TCS[tc]
        a = 3 * kp + a3
        qk = "QT%d.%d" % (tc, a)
        kt = kts_of(tc)[i]
        sb = (g0 + i) % 3
        kk = "KT%d" % min(kt // 16, 4)
        K.mm(PS[:, 2 * sb, :W], K.KT[0:64, kt * 128:(kt + 1) * 128], K.QT[0:64, a, t0:t0 + W],
             True, True, [kk, qk], ["ps%d" % (2 * sb)])
        K.mm(PS[:, 2 * sb + 1, :W], K.KT[64:128, kt * 128:(kt + 1) * 128], K.QT[64:128, a, t0:t0 + W],
             True, True, [kk, qk], ["ps%d" % (2 * sb + 1)])

    def prologue(job, g0):
        n = len(kts_of(job[1]))
        S(job, 0, g0)
        if n > 1:
            S(job, 1, g0)

    def body(job, g0, pending=None):
        kp, tc, a3 = job
        t0, W = TCS[tc]
        kts = kts_of(tc)
        n = len(kts)
        had_pending = pending is not None
        for i in range(n):
            sb = (g0 + i) % 3
            kt = kts[i]
            vk = "V%d" % min(kt // 16, 4)
            if i + 2 < n:
                S(job, i + 2, g0)
            K.act(K.PB[sb][:, :, :W], PS[:, 2 * sb:2 * sb + 2, :W], AF.Exp,
                  ["ps%d" % (2 * sb), "ps%d" % (2 * sb + 1)], ["PB%d" % sb], scale=0.125)
            for s_ in range(2):
                K.mm(PS[0:65, 6 + s_, :W], K.V[:, kt, s_, :], K.PB[sb][:, s_, :W],
                     i == 0, i == n - 1, [vk, "PB%d" % sb], ["ps%d" % (6 + s_)])
            if i == 0 and pending is not None:
                pending()
                pending = None
        if pending is not None:
            pending()
        K.cp("dve", K.OSB[0:65, :, :W], PS[0:65, 6:8, :W], ["ps6", "ps7"], ["OSB"])

    def finish(job, g0):
        kp, tc, a3 = job
        t0, W = TCS[tc]
        n = len(kts_of(tc))
        bs = (g0 + n + 1) % 3
        v = 1 if tc == 4 else 0
        K.act(K.RDL[64:65, :, :W], K.OSB[64:65, :, :W], AF.Ln, ["OSB"], ["RDL"])
        K.act(K.RD[64:65, :, :W], K.RDL[64:65, :, :W], AF.Exp, ["RDL"], ["RD"], scale=-1.0)
        for s_ in range(2):
            K.mm(PS[0:64, 2 * bs + s_, :W], K.ONESR[64:65, 0:64], K.RD[64:65, s_, :W], True, True,
                 ["ONESR", "RD"], ["ps%d" % (2 * bs + s_)])
        K.tt("dve", K.CAT[0:64, 2 * a3:2 * a3 + 2, :W], K.OSB[0:64, :, :W], PS[0:64, 2 * bs:2 * bs + 2, :W],
             ALU.mult, ["OSB", "ps%d" % (2 * bs), "ps%d" % (2 * bs + 1)], ["CAT%d" % (2 * a3), "CAT%d" % (2 * a3 + 1)])
        if a3 == 2:
            for d in range(8):
                bank = 2 * bs + d % 2
                for slot in range(6):
                    head = 3 * kp + slot // 2 + 6 * (slot % 2)
                    K.mm(PS[:, bank, :W], K.WO[0:64, head, d * 128:(d + 1) * 128], K.CAT[0:64, slot, :W],
                         slot == 0, slot == 5 and kp == 1, ["WO", "CAT%d" % slot], ["ps%d" % bank])
                if kp == 0:
                    for half in range(2):
                        K.mm(PS[:, bank, :W], K.WOF[:, half, d * 128:(d + 1) * 128], K.FT[:, half, t0:t0 + W],
                             False, half == 1, ["WOF", "FT%d" % tc], ["ps%d" % bank])
                K.stt(X[:, d, t0:t0 + W], PS[:, bank, :W], MOD[:, 16 + d, v:v + 1], X[:, d, t0:t0 + W],
                      ALU.mult, ALU.add, ["ps%d" % bank, mk, "X%d.%d" % (tc, d)], ["X%d.%d" % (tc, d)])

    g0 = 0
    load_kv(0)
    prologue(jobs[0], g0)
    pending = None
    for ji, job in enumerate(jobs):
        n = len(kts_of(job[1]))
        body(job, g0, pending)
        pending = (lambda job=job, g0=g0: finish(job, g0))
        g0n = g0 + n + 1
        if ji + 1 < len(jobs):
            nj = jobs[ji + 1]
            if nj[0] != job[0]:
                pending()
                pending = None
                load_kv(nj[0])
            prologue(nj, g0n)
        g0 = g0n
    if pending is not None:
        pending()


def load_x(K, xin):
    for c in range(8):
        K.P.dma("sp", K.X[:, c, :], xin[:, c, :], w=["X%d.%d" % (tc, c) for tc in range(5)])


def store_x(K, xo, n=NT):
    for c in range(8):
        K.P.dma("sp", xo[:, c, :], K.X[:, c, 0:n], r=["X%d.%d" % (tc, c) for tc in range(5)],
                w=["xo%d" % c], final=True)


UW = NLAT + 30 + NCTX + 30


def ucol(tc):
    return 15 + TCS[tc][0] if tc < 4 else NLAT + 30 + 15


def alloc_opre(K, with_u=True):
    K.WPW1 = K.sb("WPW1", [128, 8, 2048], BF16)
    K.HC = [K.sb("HC%d" % i, [128, 8, 512], BF16) for i in range(2)]
    if with_u:
        K.U = K.sb("U", [128, 8, UW], BF16)
    K.BPW1 = K.sb("BPW1", [128, 16], F32)
    K.NEGB = K.sb("NEGB", [128, 16], F32)
    K.EG = [K.sb("EG%d" % i, [128, 512], F32) for i in range(2)]
    K.SG = [K.sb("SG%d" % i, [128, 512], F32) for i in range(2)]


def opre_segment(K, L, wpw1, bpw1, ntc=5):
    X, PS, P = K.X, K.PS, K.P
    P.dma("pool", K.WPW1[:], wpw1.rearrange("(kc p) n -> p kc n", p=128), w=["WPW1"])
    P.dma("sp", K.BPW1[:], bpw1, w=["BPW1"])
    K.ts("dve", K.NEGB[:], K.BPW1[:], -1.0, None, ALU.mult, None, ["BPW1"], ["NEGB"])
    K.memset("pool", K.U[:], 0.0, ["U%d" % tc for tc in range(5)])
    def nrm(tc, part):
        W = TCS[tc][1]
        hcb = K.HC[tc % 2]
        hkb = "HC%d" % (tc % 2)
        if part == 0:
            norm_sq(K, tc)
        else:
            norm_rest(K, L, 0, tc, lambda c, hcb=hcb, W=W: hcb[:, c, :W], lambda c, hkb=hkb: [hkb])

    nrm(0, 0)
    nrm(0, 1)
    for tc in range(ntc):
        t0, W = TCS[tc]
        hc = K.HC[tc % 2]
        hk = "HC%d" % (tc % 2)
        u0 = ucol(tc)
        for c in range(8):
            if tc + 1 < ntc and c == 0:
                nrm(tc + 1, 0)
            if tc + 1 < ntc and c == 2:
                nrm(tc + 1, 1)
            pb = c % 2
            bv, bg = c % 2, 2 + c % 2
            for kc in range(8):
                K.mm(PS[:, bv, :W], K.WPW1[:, kc, c * 128:(c + 1) * 128], hc[:, kc, :W],
                     kc == 0, kc == 7, ["WPW1", hk], ["ps%d" % bv])
            for kc in range(8):
                K.mm(PS[:, bg, :W], K.WPW1[:, kc, 1024 + c * 128:1024 + (c + 1) * 128], hc[:, kc, :W],
                     kc == 0, kc == 7, ["WPW1", hk], ["ps%d" % bg])
            K.act(K.EG[pb][:, :W], PS[:, bg, :W], AF.Exp, ["ps%d" % bg, "NEGB"], ["EG%d" % pb],
                  bias=K.NEGB[:, 8 + c:9 + c], scale=-1.0)
            K.act(K.EG[pb][:, :W], K.EG[pb][:, :W], AF.Ln, ["EG%d" % pb, "ONEB"], ["EG%d" % pb], bias=K.ONEB[:], scale=1.0)
            K.act(K.SG[pb][:, :W], K.EG[pb][:, :W], AF.Exp, ["EG%d" % pb], ["SG%d" % pb], scale=-1.0)
            K.stt(K.U[:, c, u0:u0 + W], PS[:, bv, :W], K.BPW1[:, c:c + 1], K.SG[pb][:, :W],
                  ALU.add, ALU.mult, ["ps%d" % bv, "BPW1", "SG%d" % pb], ["U%d" % tc])


def alloc_opost(K, with_u=True):
    if with_u:
        K.U = K.sb("U", [128, 8, UW], BF16)
    K.WPW2 = K.sb("WPW2", [128, 8, 1024], BF16)
    K.ACC = K.sb("ACC", [128, 8, 512], F32)
    K.SQC = K.sb("SQC", [128, 8, 512], BF16)
    K.Z = K.sb("Z", [128, 8, 512], BF16)
    K.COB = K.Z
    K.DIAG = [K.sb("DIAG%d" % i, [128, 31, 128], BF16) for i in range(2)]
    if not hasattr(K, "IDENT"):
        K.IDENT = K.sb("IDENT", [128, 128], BF16)
    K.WDW = K.sb("WDW", [128, 8, 31], F32)
    K.PV5 = K.sb("PV5", [128, 5, 8], F32)
    K.GB = K.sb("GB", [128, 8, 2], F32)
    K.MEAN = K.sb("MEAN", [128, 512], F32)
    K.VAR = K.sb("VAR", [128, 512], F32)
    K.LNV2 = K.sb("LNV2", [128, 512], F32)
    K.RSTD2 = K.sb("RSTD2", [128, 512], F32)
    K.TA = [K.sb("TA0", [128, 512], F32)] * 2
    K.TB = [K.sb("TB%d" % i, [128, 512], F32) for i in range(2)]
    K.TE = [K.sb("TE%d" % i, [128, 512], F32) for i in range(2)]
    K.T3 = [K.sb("T3%d" % i, [128, 512], F32) for i in range(2)]


def opost_segment(K, L, wdw, pv4, wpw2, ntc=5, ukeys=False, ident=None):
    X, PS, P = K.X, K.PS, K.P
    if ident is not None:
        P.dma("sp", K.IDENT[:], ident, w=["IDENT"])
    MOD = K.MOD[L]
    mk = "MOD%d" % L
    P.dma("pool", K.WPW2[:], wpw2.rearrange("(kc p) n -> p kc n", p=128), w=["WPW2"])
    P.dma("sp", K.WDW[:], wdw, w=["WDW"])
    P.dma("sp", K.PV5[:, 0:4, :], pv4, w=["PV5"])
    K.ts("dve", K.PV5[:, 4, :], K.PV5[:, 2, :], -1.0, None, ALU.mult, None, ["PV5"], ["PV5n"])
    for v in range(2):
        K.tt("dve", K.GB[:, :, v], MOD[:, 16:24, v], K.PV5[:, 3, :], ALU.mult, [mk, "PV5"], ["GB"])
    def conv_mm(tc, c):
        t0, W = TCS[tc]
        s0 = ucol(tc) - 15
        db = c % 2
        dk = "DIAG%d" % db
        idv = K.IDENT[:]
        wv = K.WDW[:, c, :]
        in0 = bass.AP(idv.tensor, idv.offset, [list(idv.ap[0]), [0, 31], [1, 128]])
        in1 = bass.AP(wv.tensor, wv.offset, [list(wv.ap[0]), [1, 31], [0, 128]])
        K.tt("dve", K.DIAG[db][:], in0, in1, ALU.mult, ["IDENT", "WDW"], [dk])
        bank = 2 + c % 4
        for j in range(31):
            K.mm(PS[:, bank, :W], K.DIAG[db][:, j, :], K.U[:, c, s0 + j:s0 + j + W], j == 0, j == 30,
                 [dk, "U"], ["ps%d" % bank])

    def conv_ev(tc, c):
        W = TCS[tc][1]
        bank = 2 + c % 4
        K.act(K.ACC[:, c, :W], PS[:, bank, :W], AF.Identity, ["ps%d" % bank, "PV5"], ["ACC%d" % c],
              bias=K.PV5[:, 0, c:c + 1], scale=1.0)

    def conv_tail(tc):
        W = TCS[tc][1]
        for c in range(8):
            K.cp("dve", K.COB[:, c, :W], K.ACC[:, c, :W], ["ACC%d" % c], ["Z%d" % c])
            K.tt("pool", K.SQC[:, c, :W], K.ACC[:, c, :W], K.ACC[:, c, :W], ALU.mult, ["ACC%d" % c], ["SQC%d" % c])

    def stats(tc):
        W = TCS[tc][1]
        for c in range(8):
            K.mm(PS[:, 6, :W], K.ONES[:], K.COB[:, c, :W], c == 0, c == 7, ["ONES", "Z%d" % c], ["ps6"])
        for c in range(8):
            K.mm(PS[:, 7, :W], K.ONES[:], K.SQC[:, c, :W], c == 0, c == 7, ["ONES", "SQC%d" % c], ["ps7"])
        K.act(K.MEAN[:, :W], PS[:, 6, :W], AF.Identity, ["ps6"], ["MEAN"], scale=1.0 / D)
        K.tt("pool", K.VAR[:, :W], K.MEAN[:, :W], K.MEAN[:, :W], ALU.mult, ["MEAN"], ["VAR"])
        K.stt(K.VAR[:, :W], PS[:, 7, :W], 1.0 / D, K.VAR[:, :W], ALU.mult, ALU.subtract, ["ps7", "VAR"], ["VAR"])
        K.act(K.LNV2[:, :W], K.VAR[:, :W], AF.Ln, ["VAR"], ["LNV2"], bias=K.EPSB[:], scale=1.0)
        K.act(K.RSTD2[:, :W], K.LNV2[:, :W], AF.Exp, ["LNV2"], ["RSTD2"], scale=-0.5)

    def ln_chain(tc):
        W = TCS[tc][1]
        for c in range(8):
            pb = c % 2
            K.tt("dve", K.TA[pb][:, :W], K.ACC[:, c, :W], K.MEAN[:, :W], ALU.subtract,
                 ["ACC%d" % c, "MEAN"], ["TA0"])
            K.stt(K.TB[pb][:, :W], K.TA[pb][:, :W], K.PV5[:, 1, c:c + 1], K.RSTD2[:, :W], ALU.mult, ALU.mult,
                  ["TA0", "PV5", "RSTD2"], ["TB%d" % pb])
            K.act(K.TE[pb][:, :W], K.TB[pb][:, :W], AF.Exp, ["TB%d" % pb, "PV5n"], ["TE%d" % pb],
                  bias=K.PV5[:, 4, c:c + 1], scale=-1.0)
            K.act(K.TE[pb][:, :W], K.TE[pb][:, :W], AF.Ln, ["TE%d" % pb, "ONEB"], ["TE%d" % pb], bias=K.ONEB[:], scale=1.0)
            K.act(K.TE[pb][:, :W], K.TE[pb][:, :W], AF.Exp, ["TE%d" % pb], ["TE%d" % pb], scale=-1.0)
            K.stt(K.Z[:, c, :W], K.TB[pb][:, :W], K.PV5[:, 2, c:c + 1], K.TE[pb][:, :W], ALU.add, ALU.mult,
                  ["TB%d" % pb, "PV5", "TE%d" % pb], ["Z%d" % c])

    def pw2(tc):
        t0, W = TCS[tc]
        v = 1 if tc == 4 else 0
        for d in range(8):
            bank = d % 2
            for c in range(8):
                K.mm(PS[:, bank, :W], K.WPW2[:, c, d * 128:(d + 1) * 128], K.Z[:, c, :W], c == 0, c == 7,
                     ["WPW2", "Z%d" % c], ["ps%d" % bank])
            K.act(K.T3[bank][:, :W], PS[:, bank, :W], AF.Identity, ["ps%d" % bank, mk, "GB"], ["T3%d" % bank],
                  bias=K.GB[:, d, v:v + 1], scale=MOD[:, 16 + d, v:v + 1])
            K.tt("dve", X[:, d, t0:t0 + W], X[:, d, t0:t0 + W], K.T3[bank][:, :W], ALU.add,
                 ["X%d.%d" % (tc, d), "T3%d" % bank], ["X%d.%d" % (tc, d)])

    for c in range(8):
        conv_mm(0, c)
        conv_ev(0, c)
    conv_tail(0)
    for tc in range(ntc):
        nxt = tc + 1 < ntc
        stats(tc)
        if nxt:
            conv_mm(tc + 1, 0)
            conv_mm(tc + 1, 1)
        ln_chain(tc)
        if nxt:
            conv_ev(tc + 1, 0)
            conv_ev(tc + 1, 1)
        pw2(tc)
        if nxt:
            for c in range(2, 8):
                conv_mm(tc + 1, c)
                conv_ev(tc + 1, c)
            conv_tail(tc + 1)


def _epre_io(K):
    win = K.dram_in("win", [D, 1536], F32)
    rope = K.dram_in("rope", [128, 2, NLAT], F32)
    qkg = K.dram_in("qkg", [128, 8], F32)
    blk = K.dram_in("blk", [128, 128], BF16)
    perm = K.dram_in("perm", [128, 128], F32)
    cs64 = K.dram_in("cs64", [128, 256], BF16)
    kt_o = K.dram_out("kt_o", [128, 2, NT], BF16)
    v_o = K.dram_out("v_o", [128, 18, 2, 2, 65], BF16)
    ab_o = K.dram_out("ab_o", [128, 18, 512], BF16)
    qt_o = K.dram_out("qt_o", [128, 6, NT], BF16)
    return win, rope, qkg, blk, perm, cs64, kt_o, v_o, ab_o, qt_o


def _epre_run(K, L, io):
    win, rope, qkg, blk, perm, cs64, kt_o, v_o, ab_o, qt_o = io
    epre_segment(K, L, win, rope, qkg, blk, perm, cs64, [kt_o[:, k_, :] for k_ in range(2)],
                 [v_o[:, :, a_, :, :] for a_ in range(2)], [ab_o[:, 6 * c_:6 * c_ + 6, :] for c_ in range(3)])
    K.P.dma("sp", qt_o, K.QT[:], r=["QT%d.%d" % (tc, fc) for tc in range(5) for fc in range(6)],
            w=["qt_o"], final=True)


def build_pA():
    nc = bass.Bass("TRN2", target_bir_lowering=False)
    with ExitStack() as es:
        K = Ctx(nc, es)
        xin = K.dram_in("x_in", [128, 8, NT], F32)
        io = _epre_io(K)
        alloc_common(K)
        alloc_eps(K)
        load_mod(K, 0, "modA")
        load_x(K, xin)
        alloc_norm(K)
        alloc_epre(K)
        _epre_run(K, 0, io)
        K.P.emit()
    return nc


def build_pB():
    nc = bass.Bass("TRN2", target_bir_lowering=False)
    with ExitStack() as es:
        K = Ctx(nc, es)
        xin = K.dram_in("x_in", [128, 8, NT], F32)
        qt_in = K.dram_in("qt_in", [128, 6, NT], BF16)
        kt_all = K.dram_in("kt_all", [4, 128, 2, NT], BF16)
        v_all = K.dram_in("v_all", [4, 128, 18, 2, 2, 65], BF16)
        ab_all = K.dram_in("ab_all", [4, 128, 18, 512], BF16)
        dft = K.dram_in("dft", [64, 128, 2, NLAT], BF16)
        dftc = K.dram_in("dftc", [2, 128, 2, 256], BF16)
        wout = K.dram_in("wout", [D, D], F32)
        w1 = K.dram_in("w1", [D, 4 * D], F32)
        w2 = K.dram_in("w2", [4 * D, D], F32)
        wpw1 = K.dram_in("wpw1", [D, 2 * D], F32)
        bpw1 = K.dram_in("bpw1", [128, 16], F32)
        xo = K.dram_out("x_o", [128, 8, NT], F32)
        uo = K.dram_out("u_o", [128, 8, UW], BF16)
        alloc_common(K)
        alloc_eps(K)
        load_mod(K, 0, "modA")
        load_mod(K, 1, "modB")
        load_x(K, xin)
        with ExitStack() as ph:
            K.es = ph
            alloc_epost(K)
            K.P.dma("sp", K.QT[:], qt_in, w=["QT%d.%d" % (tc, fc) for tc in range(5) for fc in range(6)])
            epost_segment(K, 0, [kt_all[:, :, k_, :] for k_ in range(2)],
                          [v_all[:, :, :, a_, :, :] for a_ in range(2)],
                          [ab_all[:, :, 6 * c_:6 * c_ + 6, :] for c_ in range(3)], dft, dftc, wout)
            K.P.barrier()
        with ExitStack() as ph:
            K.es = ph
            alloc_norm(K)
            alloc_mlp(K)
            mlp_segment(K, 0, w1, w2)
            K.P.barrier()
        with ExitStack() as ph:
            K.es = ph
            alloc_norm(K)
            alloc_opre(K)
            opre_segment(K, 1, wpw1, bpw1)
            K.P.dma("sp", uo, K.U[:], r=["U%d" % tc for tc in range(5)], w=["u_o"], final=True)
            K.P.barrier()
        K.es = es
        store_x(K, xo)
        K.P.emit()
    return nc


def build_pC(last):
    nc = bass.Bass("TRN2", target_bir_lowering=False)
    ntc = 4 if last else 5
    with ExitStack() as es:
        K = Ctx(nc, es)
        xin = K.dram_in("x_in", [128, 8, NT], F32)
        u_in = K.dram_in("u_in", [128, 8, UW], BF16)
        wdw = K.dram_in("wdw", [128, 8, 31], F32)
        pv4 = K.dram_in("pv4", [128, 4, 8], F32)
        wpw2 = K.dram_in("wpw2", [D, D], F32)
        ident = K.dram_in("ident", [128, 128], BF16)
        w1 = K.dram_in("w1", [D, 4 * D], F32)
        w2 = K.dram_in("w2", [4 * D, D], F32)
        if not last:
            io = _epre_io(K)
            xo = K.dram_out("x_o", [128, 8, NT], F32)
        else:
            xo = K.dram_out("x_o", [128, 8, NLAT], F32)
        alloc_common(K)
        alloc_eps(K)
        load_mod(K, 0, "modA")
        if not last:
            load_mod(K, 1, "modB")
        load_x(K, xin)
        with ExitStack() as ph:
            K.es = ph
            alloc_opost(K)
            K.P.dma("sp", K.U[:], u_in, w=["U"])
            opost_segment(K, 0, wdw, pv4, wpw2, ntc, ident=ident)
            K.P.barrier()
        with ExitStack() as ph:
            K.es = ph
            alloc_norm(K)
            alloc_mlp(K)
            mlp_segment(K, 0, w1, w2, ntc)
            K.P.barrier()
        if not last:
            with ExitStack() as ph:
                K.es = ph
                alloc_norm(K)
                alloc_epre(K)
                _epre_run(K, 1, io)
                K.P.barrier()
        K.es = es
        store_x(K, xo, NLAT if last else NT)
        K.P.emit()
    return nc


_PROGS = {}


def _prog(name):
    if name not in _PROGS:
        _PROGS[name] = {"mod": build_mod, "A": build_pA, "B": build_pB,
                        "C": lambda: build_pC(False), "D": lambda: build_pC(True)}[name]()
    return _PROGS[name]


def _run(name, in_maps):
    res = run_bass_kernel_spmd(_prog(name), in_maps, core_ids=list(range(NCORES)))
    return res.results


def kernel_multi(x, c, ctx, c_ctx, ada_w, ada_b, norm1_g, norm2_g, mlp_w1, mlp_w2,
           attn_w_in, q_norm_g, k_norm_g, attn_w_out,
           conv_w_pw1, conv_b_pw1, conv_w_dw, conv_b_dw, conv_ln_g, conv_ln_b,
           conv_w_pw2, conv_b_pw2):
    f32 = lambda a: np.ascontiguousarray(np.asarray(a, dtype=np.float32))
    x, c, ctx, c_ctx = f32(x), f32(c), f32(ctx), f32(c_ctx)
    ada_w, ada_b, norm1_g, norm2_g = f32(ada_w), f32(ada_b), f32(norm1_g), f32(norm2_g)
    mlp_w1, mlp_w2, attn_w_in, attn_w_out = f32(mlp_w1), f32(mlp_w2), f32(attn_w_in), f32(attn_w_out)
    conv_w_pw1, conv_w_pw2, conv_w_dw = f32(conv_w_pw1), f32(conv_w_pw2), f32(conv_w_dw)
    C = consts()
    cores = [(i // 4, i % 4) for i in range(NCORES)]
    maps = []
    for b, r in cores:
        cvec = np.ascontiguousarray(np.stack([fm_vec(c[b]), fm_vec(c_ctx)], axis=-1))
        adab = fm_vec(ada_b[r])
        adab = np.ascontiguousarray(np.repeat(adab[:, :, None], 2, axis=2))
        ng = np.ascontiguousarray(np.stack([fm_vec(norm1_g[r]), fm_vec(norm2_g[r])], axis=1))
        maps.append(dict(adaw=ada_w[r], adab=adab, cvec=cvec, ng=ng))
    rm = _run("mod", maps)
    mod = {(b, L): np.asarray(rm[b * 4 + L]["modo"]) for b in range(2) for L in range(4)}

    def qkg_of(j):
        g = np.zeros((128, 8), np.float32)
        for fc in range(8):
            src = np.asarray(q_norm_g[j] if fc < 6 else k_norm_g[j], np.float32)
            g[:64, fc] = src
            g[64:, fc] = src
        return g

    def epre_inputs(j, r):
        return dict(win=perm_win(attn_w_in[j]), rope=C["rope"][r], qkg=qkg_of(j), blk=C["blk"],
                    perm=C["perm"], cs64=C["cs64"])

    def gather(res, key, b):
        return np.ascontiguousarray(np.stack([np.asarray(res[b * 4 + rr][key]) for rr in range(4)], 0))

    def epost_inputs(res, j, b, r, i):
        return dict(qt_in=np.asarray(res[i]["qt_o"]), kt_all=gather(res, "kt_o", b), v_all=gather(res, "v_o", b),
                    ab_all=gather(res, "ab_o", b), dft=C["dft"][r], dftc=C["dftc"], wout=attn_w_out[j])

    def u_with_halo(res, b, r):
        u = np.array(np.asarray(res[b * 4 + r]["u_o"]))
        if r > 0:
            ul = np.asarray(res[b * 4 + r - 1]["u_o"])
            u[:, :, 0:15] = ul[:, :, NLAT:NLAT + 15]
        if r < 3:
            ur = np.asarray(res[b * 4 + r + 1]["u_o"])
            u[:, :, NLAT + 15:NLAT + 30] = ur[:, :, 15:30]
        return np.ascontiguousarray(u)

    def opost_inputs(j):
        wdw = np.ascontiguousarray(conv_w_dw[j].reshape(31, 8, 128).transpose(2, 1, 0))
        pv4 = np.ascontiguousarray(np.stack([fm_vec(conv_b_dw[j]), fm_vec(conv_ln_g[j]), fm_vec(conv_ln_b[j]),
                                             fm_vec(conv_b_pw2[j])], axis=1))
        return dict(wdw=wdw, pv4=pv4, wpw2=conv_w_pw2[j], ident=_bf(np.eye(128, dtype=np.float32)))

    maps = []
    for b, r in cores:
        xt = np.concatenate([x[b, r * NLAT:(r + 1) * NLAT], ctx[b]], axis=0)
        m = dict(x_in=fm_tokens(xt), modA=mod[(b, 0)])
        m.update(epre_inputs(0, r))
        maps.append(m)
    xcur = [m["x_in"] for m in maps]
    res = _run("A", maps)
    for L in (0, 2):
        j = L // 2
        maps = []
        for i, (b, r) in enumerate(cores):
            m = dict(x_in=xcur[i], modA=mod[(b, L)], modB=mod[(b, L + 1)], w1=mlp_w1[L], w2=mlp_w2[L],
                     wpw1=conv_w_pw1[j], bpw1=fm_vec(conv_b_pw1[j]))
            m.update(epost_inputs(res, j, b, r, i))
            maps.append(m)
        res = _run("B", maps)
        xcur = [np.asarray(res[i]["x_o"]) for i in range(NCORES)]
        last = L == 2
        maps = []
        for i, (b, r) in enumerate(cores):
            m = dict(x_in=xcur[i], u_in=u_with_halo(res, b, r), modA=mod[(b, L + 1)],
                     w1=mlp_w1[L + 1], w2=mlp_w2[L + 1])
            m.update(opost_inputs(j))
            if not last:
                m["modB"] = mod[(b, L + 2)]
                m.update(epre_inputs(j + 1, r))
            maps.append(m)
        res = _run("D" if last else "C", maps)
        if not last:
            xcur = [np.asarray(res[i]["x_o"]) for i in range(NCORES)]
    out = np.zeros((2, 4 * NLAT, D), np.float32)
    for i, (b, r) in enumerate(cores):
        xo = np.asarray(res[i]["x_o"])
        out[b, r * NLAT:(r + 1) * NLAT] = xo.transpose(2, 1, 0).reshape(NLAT, D)
    return out


GROUPS = [[0, 1, 2, 3], [4, 5, 6, 7]]


def mod_segment(K, adaw, adab, cvec, ng, m_loc, m_gat):
    nc, P, PS = K.nc, K.P, K.PS
    CV = K.sb("CV", [128, 8, 2], F32)
    ABs = K.sb("ABs", [128, 48, 2], F32)
    NG = K.sb("NG", [128, 2, 8], F32)
    E1 = K.sb("E1", [128, 8, 2], F32)
    S = K.sb("S", [128, 8, 2], BF16)
    MODL = K.sb("MODL", [128, 48, 2], F32)
    WM = [K.sb("WM%d" % i, [128, 8, 1024], BF16) for i in range(2)]
    P.dma("sp", CV[:], cvec, w=["CV"])
    P.dma("sp", ABs[:], adab, w=["ABs"])
    P.dma("sp", NG[:], ng, w=["NG"])
    K.act(E1[:], CV[:], AF.Exp, ["CV"], ["E1"], scale=-1.0)
    K.ts("dve", E1[:], E1[:], 1.0, None, ALU.add, None, ["E1"], ["E1"])
    K.recip(E1[:], E1[:], ["E1"], ["E1"])
    K.tt("dve", S[:], CV[:], E1[:], ALU.mult, ["CV", "E1"], ["S"])
    adaw_v = adaw.rearrange("(kc p) n -> p kc n", p=128)
    for j in range(6):
        wm = WM[j % 2]
        wk = "WM%d" % (j % 2)
        P.dma("pool", wm[:], adaw_v[:, :, j * 1024:(j + 1) * 1024], w=[wk])
        bank = j % 2
        for c in range(8):
            for kc in range(8):
                K.mm(PS[:, bank, c * 2:c * 2 + 2], wm[:, kc, c * 128:(c + 1) * 128], S[:, kc, :],
                     kc == 0, kc == 7, [wk, "S"], ["ps%d" % bank])
        K.tt("dve", MODL[:, j * 8:(j + 1) * 8, :],
             PS[:, bank, 0:16].rearrange("p (c v) -> p c v", v=2),
             ABs[:, j * 8:(j + 1) * 8, :], ALU.add, ["ps%d" % bank, "ABs"], ["MODL%d" % j])
    for j, gi in ((1, 0), (4, 1)):
        for v in range(2):
            K.stt(MODL[:, j * 8:(j + 1) * 8, v], MODL[:, j * 8:(j + 1) * 8, v], 1.0, NG[:, gi, :],
                  ALU.add, ALU.mult, ["MODL%d" % j, "NG"], ["MODL%d" % j])
    P.dma("sp", m_loc, MODL[:].rearrange("p a v -> p (a v)"), r=["MODL%d" % j for j in range(6)], w=["m_loc"])
    P.cc("AllGather", GROUPS, m_loc, m_gat, r=["m_loc"], w=["m_gat"])
    mg = m_gat.rearrange("(r p) (a v) -> r p a v", p=128, v=2)
    for L in range(4):
        P.dma("sp", K.MOD[L][:], mg[L], r=["m_gat"], w=["MOD%d" % L])


def halo_segment(K, e_loc, e_gat):
    P = K.P
    U = K.U
    E4 = K.sb("E4", [128, 4, 8, 2, 15], BF16)
    HAL = K.sb("HAL", [128, 2, 8, 15], F32)
    ED = K.sb("ED", [128, 8, 2, 15], BF16)
    ukeys = ["U%d" % tc for tc in range(5)]
    K.cp("pool", ED[:, :, 0, :], U[:, :, 15:30], ukeys, ["ED"])
    K.cp("pool", ED[:, :, 1, :], U[:, :, NLAT:NLAT + 15], ukeys, ["ED"])
    P.dma("sp", e_loc, ED[:].rearrange("p c s e -> p (c s e)"), r=["ED"], w=["e_loc"])
    P.cc("AllGather", GROUPS, e_loc, e_gat, r=["e_loc"], w=["e_gat"])
    P.dma("sp", E4[:], e_gat.rearrange("(r p) (c s e) -> p r c s e", p=128, c=8, s=2), r=["e_gat"], w=["E4"])
    for side in range(2):
        src_s = 1 - side
        for rr in range(4):
            mcol = K.HMASK[:, side * 4 + rr:side * 4 + rr + 1]
            if rr == 0:
                K.ts("dve", HAL[:, side], E4[:, rr, :, src_s, :], mcol, None, ALU.mult, None,
                     ["E4", "HMASK"], ["HAL%d" % side])
            else:
                K.stt(HAL[:, side], E4[:, rr, :, src_s, :], mcol, HAL[:, side], ALU.mult, ALU.add,
                      ["E4", "HMASK", "HAL%d" % side], ["HAL%d" % side])
    K.cp("dve", U[:, :, 0:15], HAL[:, 0], ["HAL0"], ["U0"])
    K.cp("dve", U[:, :, NLAT + 15:NLAT + 30], HAL[:, 1], ["HAL1"], ["U3"])


def build_fused():
    nc = bass.Bass("TRN2", target_bir_lowering=False)
    with ExitStack() as es:
        K = Ctx(nc, es)
        P = K.P
        di = K.dram_in
        xin = di("x_in", [128, 8, NT], F32)
        adaw = di("adaw", [D, 6 * D], F32)
        adab = di("adab", [128, 48, 2], F32)
        cvec = di("cvec", [128, 8, 2], F32)
        ng = di("ng", [128, 2, 8], F32)
        hmask = di("hmask", [128, 8], F32)
        rope = di("rope", [128, 2, NLAT], F32)
        blk = di("blk", [128, 128], BF16)
        perm = di("perm", [128, 128], F32)
        cs64 = di("cs64", [128, 256], BF16)
        ident = di("ident", [128, 128], BF16)
        dft = di("dft", [64, 128, 2, NLAT], BF16)
        dftc = di("dftc", [2, 128, 2, 256], BF16)
        w1 = [di("w1_%d" % L, [D, 4 * D], F32) for L in range(4)]
        w2 = [di("w2_%d" % L, [4 * D, D], F32) for L in range(4)]
        win = [di("win_%d" % j, [D, 1536], F32) for j in range(2)]
        qkg = [di("qkg_%d" % j, [128, 8], F32) for j in range(2)]
        wout = [di("wout_%d" % j, [D, D], F32) for j in range(2)]
        wpw1 = [di("wpw1_%d" % j, [D, 2 * D], F32) for j in range(2)]
        bpw1 = [di("bpw1_%d" % j, [128, 16], F32) for j in range(2)]
        wdw = [di("wdw_%d" % j, [128, 8, 31], F32) for j in range(2)]
        pv4 = [di("pv4_%d" % j, [128, 4, 8], F32) for j in range(2)]
        wpw2 = [di("wpw2_%d" % j, [D, D], F32) for j in range(2)]
        xo = K.dram_out("x_o", [128, 8, NLAT], F32)

        def internal(name, shape, dt):
            return nc.dram_tensor(name, list(shape), dt, kind="Internal").ap()

        m_loc = internal("m_loc", [128, 96], F32)
        m_gat = internal("m_gat", [512, 96], F32)
        alloc_common(K)
        alloc_eps(K)
        K.HMASK = K.sb("HMASK", [128, 8], F32)
        P.dma("sp", K.HMASK[:], hmask, w=["HMASK"])
        for L in range(4):
            K.MOD[L] = K.sb("MODL%d" % L, [128, 48, 2], F32)
        load_x(K, xin)
        with ExitStack() as ph:
            K.es = ph
            mod_segment(K, adaw, adab, cvec, ng, m_loc, m_gat)
            P.barrier()
        K.es = es
        K.BLK = K.sb("BLK", [128, 128], BF16)
        K.PERM = K.sb("PERM", [128, 128], F32)
        K.CS64 = K.sb("CS64", [128, 256], BF16)
        P.dma("sp", K.BLK[:], blk, w=["BLK"])
        P.dma("sp", K.PERM[:], perm, w=["PERM"])
        P.dma("sp", K.CS64[:], cs64, w=["CS64"])
        K.IDENT = K.sb("IDENT", [128, 128], BF16)
        P.dma("sp", K.IDENT[:], ident, w=["IDENT"])
        for L in range(4):
            j = L // 2
            ntc = 4 if L == 3 else 5
            if L % 2 == 0:
                kt_loc = [internal("kt_loc%d_%d" % (L, k_), [128, NT], BF16) for k_ in range(2)]
                kt_gat = [internal("kt_gat%d_%d" % (L, k_), [512, NT], BF16) for k_ in range(2)]
                v_loc = [internal("v_loc%d_%d" % (L, k_), [128, 18 * 130], BF16) for k_ in range(2)]
                v_gat = [internal("v_gat%d_%d" % (L, k_), [512, 18 * 130], BF16) for k_ in range(2)]
                ab_loc = [internal("ab_loc%d_%d" % (L, k_), [128, 6 * 512], BF16) for k_ in range(3)]
                ab_gat = [internal("ab_gat%d_%d" % (L, k_), [512, 6 * 512], BF16) for k_ in range(3)]
                with ExitStack() as ql:
                    K.es = ql
                    K.QT = K.sb("QT", [128, 6, NT], BF16)
                    with ExitStack() as ph:
                        K.es = ph
                        alloc_norm(K)
                        alloc_epre_noqt(K)
                        abk = [[], [], []]
                        for tc in range(5):
                            t0_, W_ = TCS[tc]
                            for t_ in range(0, W_ // 128, 2):
                                abk[(t0_ // 128 + t_) // 6].append("ab_o%d.%d" % (tc, t_))

                        def after_tc(tc, ab_loc=ab_loc, ab_gat=ab_gat, abk=abk):
                            k_ = {1: 0, 2: 1, 4: 2}.get(tc)
                            if k_ is not None:
                                P.cc("AllGather", GROUPS, ab_loc[k_], ab_gat[k_], r=abk[k_], w=["ab_gat%d" % k_])

                        epre_segment(K, L, win[j], rope, qkg[j], blk, perm, cs64,
                                     kt_loc,
                                     [v.rearrange("p (t s e) -> p t s e", t=18, s=2) for v in v_loc],
                                     [a.rearrange("p (t n) -> p t n", t=6) for a in ab_loc],
                                     final=False, load_consts=False, after_tc=after_tc)
                        for k_ in range(2):
                            P.cc("AllGather", GROUPS, kt_loc[k_], kt_gat[k_], r=["kt_o%d" % k_], w=["kt_gat%d" % k_])
                            P.cc("AllGather", GROUPS, v_loc[k_], v_gat[k_],
                                 r=["v_o%d.%d" % (tc, k_) for tc in range(5)], w=["v_gat%d" % k_])
                        P.barrier(keep=["kt_gat0", "kt_gat1", "v_gat0", "v_gat1"])
                    with ExitStack() as ph:
                        K.es = ph
                        alloc_epost(K, with_qt=False)
                        epost_segment(K, L,
                                      [k_.rearrange("(r p) t -> r p t", p=128) for k_ in kt_gat],
                                      [v.rearrange("(r p) (t s e) -> r p t s e", p=128, t=18, s=2) for v in v_gat],
                                      [a.rearrange("(r p) (t n) -> r p t n", p=128, t=6) for a in ab_gat],
                                      dft, dftc, wout[j])
                        P.barrier()
            else:
                e_loc = internal("e_loc%d" % L, [128, 240], BF16)
                e_gat = internal("e_gat%d" % L, [512, 240], BF16)
                with ExitStack() as ul:
                    K.es = ul
                    K.U = K.sb("U", [128, 8, UW], BF16)
                    with ExitStack() as ph:
                        K.es = ph
                        alloc_norm(K)
                        alloc_opre(K, with_u=False)
                        opre_segment(K, L, wpw1[j], bpw1[j], ntc)
                        halo_segment(K, e_loc, e_gat)
                        P.barrier()
                    with ExitStack() as ph:
                        K.es = ph
                        alloc_opost(K, with_u=False)
                        opost_segment(K, L, wdw[j], pv4[j], wpw2[j], ntc, ukeys=True)
                        P.barrier()
            with ExitStack() as ph:
                K.es = ph
                alloc_norm(K)
                alloc_mlp(K)
                def store_tc(tc):
                    t0_, W_ = TCS[tc]
                    P.dma("sp", xo[:, :, t0_:t0_ + W_], K.X[:, :, t0_:t0_ + W_], r=xkeys(tc), w=["xo%d" % tc], final=True)

                mlp_segment(K, L, w1[L], w2[L], ntc, after_last=store_tc if L == 3 else None)
                P.barrier()
        K.es = es
        P.emit()
    return nc


def kernel(x, c, ctx, c_ctx, ada_w, ada_b, norm1_g, norm2_g, mlp_w1, mlp_w2,
           attn_w_in, q_norm_g, k_norm_g, attn_w_out,
           conv_w_pw1, conv_b_pw1, conv_w_dw, conv_b_dw, conv_ln_g, conv_ln_b,
           conv_w_pw2, conv_b_pw2):
    f32 = lambda a: np.ascontiguousarray(np.asarray(a, dtype=np.float32))
    x, c, ctx, c_ctx = f32(x), f32(c), f32(ctx), f32(c_ctx)
    ada_w, ada_b, norm1_g, norm2_g = f32(ada_w), f32(ada_b), f32(norm1_g), f32(norm2_g)
    mlp_w1, mlp_w2, attn_w_in, attn_w_out = f32(mlp_w1), f32(mlp_w2), f32(attn_w_in), f32(attn_w_out)
    conv_w_pw1, conv_w_pw2, conv_w_dw = f32(conv_w_pw1), f32(conv_w_pw2), f32(conv_w_dw)
    C = consts()
    shared = {}
    for L in range(4):
        shared["w1_%d" % L] = mlp_w1[L]
        shared["w2_%d" % L] = mlp_w2[L]
    for j in range(2):
        g = np.zeros((128, 8), np.float32)
        for fc in range(8):
            src = np.asarray(q_norm_g[j] if fc < 6 else k_norm_g[j], np.float32)
            g[:64, fc] = src
            g[64:, fc] = src
        shared["win_%d" % j] = perm_win(attn_w_in[j])
        shared["qkg_%d" % j] = g
        shared["wout_%d" % j] = attn_w_out[j]
        shared["wpw1_%d" % j] = conv_w_pw1[j]
        shared["bpw1_%d" % j] = fm_vec(conv_b_pw1[j])
        shared["wdw_%d" % j] = np.ascontiguousarray(conv_w_dw[j].reshape(31, 8, 128).transpose(2, 1, 0))
        shared["pv4_%d" % j] = np.ascontiguousarray(np.stack(
            [fm_vec(conv_b_dw[j]), fm_vec(conv_ln_g[j]), fm_vec(conv_ln_b[j]), fm_vec(conv_b_pw2[j])], axis=1))
        shared["wpw2_%d" % j] = conv_w_pw2[j]
    shared.update(blk=C["blk"], perm=C["perm"], cs64=C["cs64"], dftc=C["dftc"],
                  ident=_bf(np.eye(128, dtype=np.float32)))
    maps = []
    for i in range(NCORES):
        b, r = i // 4, i % 4
        xt = np.concatenate([x[b, r * NLAT:(r + 1) * NLAT], ctx[b]], axis=0)
        adab = fm_vec(ada_b[r])
        hm = np.zeros((128, 8), np.float32)
        if r > 0:
            hm[:, r - 1] = 1.0
        if r < 3:
            hm[:, 4 + r + 1] = 1.0
        m = dict(shared)
        m.update(x_in=fm_tokens(xt), adaw=ada_w[r],
                 adab=np.ascontiguousarray(np.repeat(adab[:, :, None], 2, axis=2)),
                 cvec=np.ascontiguousarray(np.stack([fm_vec(c[b]), fm_vec(c_ctx)], axis=-1)),
                 ng=np.ascontiguousarray(np.stack([fm_vec(norm1_g[r]), fm_vec(norm2_g[r])], axis=1)),
                 hmask=hm, rope=C["rope"][r], dft=C["dft"][r])
        maps.append(m)
    if "F" not in _PROGS:
        _PROGS["F"] = build_fused()
    res = run_bass_kernel_spmd(_PROGS["F"], maps, core_ids=list(range(NCORES))).results
    out = np.zeros((2, 4 * NLAT, D), np.float32)
    for i in range(NCORES):
        b, r = i // 4, i % 4
        xo = np.asarray(res[i]["x_o"])
        out[b, r * NLAT:(r + 1) * NLAT] = xo.transpose(2, 1, 0).reshape(NLAT, D)
    return out
```

```python
import math
from contextlib import ExitStack

import numpy as np
import ml_dtypes
import concourse.bass as bass
import concourse.mybir as mybir
from concourse.bass_utils import run_bass_kernel_spmd

F32 = mybir.dt.float32
BF16 = mybir.dt.bfloat16
AF = mybir.ActivationFunctionType
ALU = mybir.AluOpType
NPBF = ml_dtypes.bfloat16

D = 1024
NLAT = 2048
NCTX = 256
NT = NLAT + NCTX
TCS = [(0, 512), (512, 512), (1024, 512), (1536, 512), (2048, 256)]
EPS = 1e-6
NCORES = 8


class _Op:
    __slots__ = ("eng", "fn", "deps", "dma", "sem", "semval", "signal", "count", "idx", "final", "inc")


class Prog:
    ENGS = ("pe", "act", "dve", "pool", "sp")

    def __init__(self, nc):
        self.nc = nc
        self.ops = []
        self.state = {}
        self.dma_sem_of = {}
        self.dma_sem_cnt = []
        self.finals = []

    def _add(self, eng, fn, r, w, dma=False, final=False, inc=16):
        op = _Op()
        op.inc = inc
        op.eng, op.fn, op.dma, op.final = eng, fn, dma, final
        op.signal = False
        op.count = None
        op.idx = len(self.ops)
        deps = {}
        for k in r:
            st = self.state.setdefault(k, [None, []])
            if st[0] is not None:
                deps[st[0]] = "raw"
        for k in w:
            st = self.state.setdefault(k, [None, []])
            if st[0] is not None:
                deps[st[0]] = "waw"
            for ri in st[1]:
                if ri not in deps:
                    deps[ri] = "war"
        for k in r:
            rl = self.state[k][1]
            if not dma:
                rl[:] = [ri for ri in rl if self.ops[ri].dma or self.ops[ri].eng != eng]
            rl.append(op.idx)
        for k in w:
            self.state[k] = [op.idx, []]
        op.deps = []
        latest = {}
        for di, kind in deps.items():
            dop = self.ops[di]
            if dop.dma:
                op.deps.append(di)
            elif dop.eng == eng:
                if eng == "pe":
                    continue
                latest[dop.eng] = max(latest.get(dop.eng, -1), di)
            else:
                latest[dop.eng] = max(latest.get(dop.eng, -1), di)
        for di in latest.values():
            self.ops[di].signal = True
            op.deps.append(di)
        if dma:
            key = w[0]
            if key not in self.dma_sem_of:
                self.dma_sem_of[key] = len(self.dma_sem_cnt)
                self.dma_sem_cnt.append(0)
            si = self.dma_sem_of[key]
            self.dma_sem_cnt[si] += inc
            op.sem = si
            op.semval = self.dma_sem_cnt[si]
            if final:
                self.finals.append(op.idx)
        self.ops.append(op)
        return op

    def barrier(self, keep=()):
        last = {}
        lastdma = {}
        kept = {}
        for k in keep:
            st = self.state.get(k)
            if st is not None and st[0] is not None and self.ops[st[0]].dma:
                kept[k] = st[0]
        skip_sems = set(self.ops[i].sem for i in kept.values())
        seen = getattr(self, "_bar_seen", {})
        for op in self.ops:
            if op.dma:
                if op.sem not in skip_sems and op.semval > seen.get(op.sem, 0):
                    lastdma[op.sem] = op.idx
            elif op.fn is not None:
                last[op.eng] = op.idx
        for si, li in lastdma.items():
            seen[si] = self.ops[li].semval
        self._bar_seen = seen
        for e in self.ENGS:
            op = _Op()
            op.eng, op.fn, op.dma, op.final = e, None, False, False
            op.signal = False
            op.count = None
            op.idx = len(self.ops)
            op.deps = []
            for e2, li in last.items():
                if e2 != e:
                    self.ops[li].signal = True
                    op.deps.append(li)
            for si, li in lastdma.items():
                op.deps.append(li)
            self.ops.append(op)
        self.state = {k: [i, []] for k, i in kept.items()}

    def op(self, eng, fn, r=(), w=()):
        return self._add(eng, fn, tuple(r), tuple(w))

    def dma(self, q, out, in_, r=(), w=(), final=False):
        assert q in ("sp", "pool")
        return self._add(q, lambda e: e.dma_start(out=out, in_=in_), tuple(r), tuple(w),
                         dma=True, final=final)

    def cc(self, kind, groups, in_ap, out_ap, r=(), w=()):
        return self._add("pool", lambda e: e.collective_compute(kind, ALU.bypass, replica_groups=groups,
                                                                ins=[in_ap], outs=[out_ap]),
                         tuple(r), tuple(w), dma=True, inc=1)

    def emit(self):
        nc = self.nc
        cnt = {e: 0 for e in self.ENGS}
        for op in self.ops:
            if op.dma or op.fn is None:
                continue
            if op.signal:
                cnt[op.eng] += 1
                op.count = cnt[op.eng]
        for e in self.ENGS:
            assert cnt[e] < 60000, (e, cnt[e])
        for v in self.dma_sem_cnt:
            assert v < 60000, v
        with ExitStack() as es:
            esem = {e: es.enter_context(nc.semaphore("s_" + e)) for e in ("pe", "act", "dve", "pool")}
            dsem = [es.enter_context(nc.semaphore("d%d" % i)) for i in range(len(self.dma_sem_cnt))]
            block = es.enter_context(nc.Block())
            ops = self.ops
            finals = self.finals

            def run(ename, e):
                waited = {}
                for op in ops:
                    if op.eng != ename:
                        continue
                    for di in op.deps:
                        dop = ops[di]
                        if dop.dma:
                            sem, val, key = dsem[dop.sem], dop.semval, ("d", dop.sem)
                        else:
                            sem, val, key = esem[dop.eng], dop.count, ("e", dop.eng)
                        if waited.get(key, 0) >= val:
                            continue
                        waited[key] = val
                        e.wait_ge(sem, val)
                    if op.fn is None:
                        continue
                    ins = op.fn(e)
                    if op.dma:
                        ins.then_inc(dsem[op.sem], op.inc)
                    elif op.signal:
                        ins.then_inc(esem[ename], 1)
                if ename == "sp":
                    for fi in finals:
                        fop = ops[fi]
                        e.wait_ge(dsem[fop.sem], fop.semval)

            @block.tensor
            def _(e):
                run("pe", e)

            @block.scalar
            def _(e):
                run("act", e)

            @block.vector
            def _(e):
                run("dve", e)

            @block.gpsimd
            def _(e):
                run("pool", e)

            @block.sync
            def _(e):
                run("sp", e)


class Ctx:
    def __init__(self, nc, es):
        self.nc = nc
        self.es = es
        self.P = Prog(nc)
        self.din = {}
        self.dout = {}

    def dram_in(self, name, shape, dt):
        t = self.nc.dram_tensor(name, list(shape), dt, kind="ExternalInput").ap()
        self.din[name] = t
        return t

    def dram_out(self, name, shape, dt):
        t = self.nc.dram_tensor(name, list(shape), dt, kind="ExternalOutput").ap()
        self.dout[name] = t
        return t

    def sb(self, name, shape, dt):
        self.nsb = getattr(self, "nsb", 0) + 1
        return self.es.enter_context(self.nc.sbuf_tensor("%s_%d" % (name, self.nsb), list(shape), dt))

    def mm(self, out, lhsT, rhs, start, stop, r, w):
        self.P.op("pe", lambda e: e.matmul(out, lhsT, rhs, start=start, stop=stop), r, w)

    def act(self, out, in_, func, r, w, bias=None, scale=None):
        kw = {}
        if bias is not None:
            kw["bias"] = bias
        if scale is not None:
            kw["scale"] = scale
        self.P.op("act", lambda e: e.activation(out, in_, func, **kw), r, w)

    def tt(self, eng, out, in0, in1, op, r, w):
        self.P.op(eng, lambda e: e.tensor_tensor(out, in0, in1, op), r, w)

    def ts(self, eng, out, in0, s1, s2, op0, op1, r, w):
        if op1 is None:
            self.P.op(eng, lambda e: e.tensor_scalar(out, in0, s1, None, op0), r, w)
        else:
            self.P.op(eng, lambda e: e.tensor_scalar(out, in0, s1, s2, op0, op1), r, w)

    def stt(self, out, in0, scalar, in1, op0, op1, r, w):
        self.P.op("dve", lambda e: e.scalar_tensor_tensor(out, in0, scalar, in1, op0, op1), r, w)

    def cp(self, eng, out, in_, r, w):
        self.P.op(eng, lambda e: e.tensor_copy(out, in_), r, w)

    def recip(self, out, in_, r, w):
        self.P.op("dve", lambda e: e.reciprocal(out, in_), r, w)

    def memset(self, eng, ap, val, w):
        self.P.op(eng, lambda e: e.memset(ap, val), (), w)


def _bf(a):
    return np.ascontiguousarray(a).astype(NPBF)


def build_mod():
    nc = bass.Bass("TRN2", target_bir_lowering=False)
    with ExitStack() as es:
        K = Ctx(nc, es)
        P = K.P
        adaw = K.dram_in("adaw", [D, 6 * D], F32)
        adab = K.dram_in("adab", [128, 48, 2], F32)
        cvec = K.dram_in("cvec", [128, 8, 2], F32)
        ng = K.dram_in("ng", [128, 2, 8], F32)
        modo = K.dram_out("modo", [128, 48, 2], F32)
        CV = K.sb("CV", [128, 8, 2], F32)
        ABs = K.sb("ABs", [128, 48, 2], F32)
        NG = K.sb("NG", [128, 2, 8], F32)
        E1 = K.sb("E1", [128, 8, 2], F32)
        S = K.sb("S", [128, 8, 2], BF16)
        MOD = K.sb("MOD", [128, 48, 2], F32)
        WM = [K.sb("WM%d" % i, [128, 8, 1024], BF16) for i in range(2)]
        PS = es.enter_context(nc.psum_tensor("PS", [128, 8, 512], F32))
        P.dma("sp", CV[:], cvec, w=["CV"])
        P.dma("sp", ABs[:], adab, w=["ABs"])
        P.dma("sp", NG[:], ng, w=["NG"])
        K.act(E1[:], CV[:], AF.Exp, ["CV"], ["E1"], scale=-1.0)
        K.ts("dve", E1[:], E1[:], 1.0, None, ALU.add, None, ["E1"], ["E1"])
        K.recip(E1[:], E1[:], ["E1"], ["E1"])
        K.tt("dve", S[:], CV[:], E1[:], ALU.mult, ["CV", "E1"], ["S"])
        adaw_v = adaw.rearrange("(kc p) n -> p kc n", p=128)
        for j in range(6):
            wm = WM[j % 2]
            wk = "WM%d" % (j % 2)
            P.dma("pool", wm[:], adaw_v[:, :, j * 1024:(j + 1) * 1024], w=[wk])
            bank = j % 2
            for c in range(8):
                for kc in range(8):
                    K.mm(PS[:, bank, c * 2:c * 2 + 2], wm[:, kc, c * 128:(c + 1) * 128], S[:, kc, :],
                         kc == 0, kc == 7, [wk, "S"], ["ps%d" % bank])
            K.tt("dve", MOD[:, j * 8:(j + 1) * 8, :],
                 PS[:, bank, 0:16].rearrange("p (c v) -> p c v", v=2),
                 ABs[:, j * 8:(j + 1) * 8, :], ALU.add, ["ps%d" % bank, "ABs"], ["MOD%d" % j])
        for j, gi in ((1, 0), (4, 1)):
            for v in range(2):
                K.stt(MOD[:, j * 8:(j + 1) * 8, v], MOD[:, j * 8:(j + 1) * 8, v], 1.0, NG[:, gi, :],
                      ALU.add, ALU.mult, ["MOD%d" % j, "NG"], ["MOD%d" % j])
        P.dma("sp", modo, MOD[:], r=["MOD%d" % j for j in range(6)], w=["modo"], final=True)
        P.emit()
    return nc


def alloc_common(K):
    nc, es = K.nc, K.es
    K.X = K.sb("X", [128, 8, NT], F32)
    K.PS = es.enter_context(nc.psum_tensor("PS", [128, 8, 512], F32))
    K.ONES = K.sb("ONES", [128, 128], BF16)
    K.MOD = {}
    K.memset("pool", K.ONES[:], 1.0, ["ONES"])


def alloc_norm(K):
    K.SQ = K.sb("SQ", [128, 8, 512], BF16)
    K.LNV = K.sb("LNV", [128, 512], F32)
    K.RSTD = K.sb("RSTD", [128, 512], F32)
    K.TMP = [K.sb("TMP%d" % i, [128, 512], F32) for i in range(2)]


def load_mod(K, L, name=None):
    t = K.dram_in(name or ("mod%d" % L), [128, 48, 2], F32)
    K.MOD[L] = K.sb("MODL%d" % L, [128, 48, 2], F32)
    K.P.dma("sp", K.MOD[L][:], t, w=["MOD%d" % L])


def xkeys(tc):
    return ["X%d.%d" % (tc, c) for c in range(8)]


def norm_sq(K, tc):
    t0, W = TCS[tc]
    X = K.X
    for c in range(8):
        eng = "dve" if c % 2 == 0 else "pool"
        K.tt(eng, K.SQ[:, c, :W], X[:, c, t0:t0 + W], X[:, c, t0:t0 + W], ALU.mult,
             ["X%d.%d" % (tc, c)], ["SQ%d" % c])


def norm_rest(K, L, which, tc, out_fn, out_keys):
    t0, W = TCS[tc]
    v = 1 if tc == 4 else 0
    MOD = K.MOD[L]
    ja, jb = (1, 0) if which == 0 else (4, 3)
    mk = "MOD%d" % L
    X, PS = K.X, K.PS
    for c in range(8):
        K.mm(PS[:, 7, :W], K.ONES[:], K.SQ[:, c, :W], c == 0, c == 7, ["ONES", "SQ%d" % c], ["ps7"])
    K.act(K.LNV[:, :W], PS[:, 7, :W], AF.Ln, ["ps7"], ["LNV"], bias=K.EPSB[:], scale=1.0 / D)
    K.act(K.RSTD[:, :W], K.LNV[:, :W], AF.Exp, ["LNV"], ["RSTD"], scale=-0.5)
    for c in range(8):
        tb = c % 2
        K.stt(K.TMP[tb][:, :W], X[:, c, t0:t0 + W], MOD[:, ja * 8 + c, v:v + 1], K.RSTD[:, :W],
              ALU.mult, ALU.mult, ["X%d.%d" % (tc, c), mk, "RSTD"], ["TMP%d" % tb])
        K.act(out_fn(c), K.TMP[tb][:, :W], AF.Identity, ["TMP%d" % tb, mk], out_keys(c),
              bias=MOD[:, jb * 8 + c, v:v + 1], scale=1.0)


def norm_mod(K, L, which, tc, out_fn, out_keys):
    norm_sq(K, tc)
    norm_rest(K, L, which, tc, out_fn, out_keys)


def alloc_eps(K):
    K.EPSB = K.sb("EPSB", [128, 1], F32)
    K.memset("pool", K.EPSB[:], EPS, ["EPSB"])
    K.ONEB = K.sb("ONEB", [128, 1], F32)
    K.memset("pool", K.ONEB[:], 1.0, ["ONEB"])


def mlp_segment(K, L, w1, w2, ntc=5, after_last=None):
    X, PS = K.X, K.PS
    H = K.H
    MOD = K.MOD[L]
    mk = "MOD%d" % L
    def norm2(tc):
        t0, W = TCS[tc]
        norm_mod(K, L, 1, tc, lambda c, t0=t0, W=W: H[:, c, t0:t0 + W], lambda c, tc=tc: ["H%d" % tc])

    norm2(0)
    w1v = w1.rearrange("(kc p) n -> p kc n", p=128)
    w2v = w2.rearrange("(fc p) n -> p fc n", p=128)
    items = [(e, tc) for e in range(8) for tc in range(ntc)]

    def load(e):
        K.P.dma("pool", K.W1E[e % 2][:], w1v[:, :, e * 512:(e + 1) * 512], w=["W1E%d" % (e % 2)])
        K.P.dma("pool", K.W2E[e % 2][:], w2v[:, e * 4:(e + 1) * 4, :], w=["W2E%d" % (e % 2)])

    def part1(i):
        e, tc = items[i]
        t0, W = TCS[tc]
        ab = i % 2
        for f in range(4):
            bank = f % 2
            for kc in range(8):
                K.mm(PS[:, bank, :W], K.W1E[e % 2][:, kc, f * 128:(f + 1) * 128], H[:, kc, t0:t0 + W],
                     kc == 0, kc == 7, ["W1E%d" % (e % 2), "H%d" % tc], ["ps%d" % bank])
            K.act(K.RL[f % 2][:, :W], PS[:, bank, :W], AF.Relu, ["ps%d" % bank], ["RL%d" % (f % 2)])
            eng = "pool" if f % 2 == 0 else "dve"
            K.tt(eng, K.AH[ab][:, f, :W], K.RL[f % 2][:, :W], K.RL[f % 2][:, :W], ALU.mult,
                 ["RL%d" % (f % 2)], ["AH%d.%d" % (ab, f)])

    def part2(i):
        e, tc = items[i]
        t0, W = TCS[tc]
        v = 1 if tc == 4 else 0
        ab = i % 2
        for d in range(8):
            bank = 2 + d % 2
            for f in range(4):
                K.mm(PS[:, bank, :W], K.W2E[e % 2][:, f, d * 128:(d + 1) * 128], K.AH[ab][:, f, :W],
                     f == 0, f == 3, ["W2E%d" % (e % 2), "AH%d.%d" % (ab, f)], ["ps%d" % bank])
            K.stt(X[:, d, t0:t0 + W], PS[:, bank, :W], MOD[:, 40 + d, v:v + 1], X[:, d, t0:t0 + W],
                  ALU.mult, ALU.add, ["ps%d" % bank, mk, "X%d.%d" % (tc, d)], ["X%d.%d" % (tc, d)])

    load(0)
    part1(0)
    for i in range(len(items)):
        e0, tc0 = items[i]
        if tc0 == 1 and e0 + 1 < 8:
            load(e0 + 1)
        if i + 1 < len(items):
            if items[i + 1][0] == 0:
                norm2(items[i + 1][1])
            part1(i + 1)
        part2(i)
        if e0 == 7 and after_last is not None:
            after_last(tc0)


def alloc_mlp(K):
    K.H = K.sb("H", [128, 8, NT], BF16)
    K.W1E = [K.sb("W1E%d" % i, [128, 8, 512], BF16) for i in range(2)]
    K.W2E = [K.sb("W2E%d" % i, [128, 4, 1024], BF16) for i in range(2)]
    K.RL = [K.sb("RL%d" % i, [128, 512], F32) for i in range(2)]
    K.AH = [K.sb("AH%d" % i, [128, 4, 512], BF16) for i in range(2)]


def alloc_epre_noqt(K):
    alloc_epre(K, with_qt=False)


def alloc_epre(K, with_qt=True):
    K.WIN = K.sb("WIN", [128, 8, 1536], BF16)
    K.HC = [K.sb("HC%d" % i, [128, 8, 512], BF16) for i in range(2)]
    if with_qt:
        K.QT = K.sb("QT", [128, 6, NT], BF16)
    K.KTL = K.sb("KTL", [128, 2, NT], BF16)
    K.VTC = [K.sb("VTC%d" % i, [128, 4, 2, 2, 65], BF16) for i in range(2)]
    K.ABC = [K.sb("ABC%d" % i, [128, 4, 512], BF16) for i in range(2)]
    K.FTC = K.sb("FTC", [128, 2, 512], BF16)
    K.ROPE = [K.sb("ROPE0", [128, 2, 512], F32)] * 2
    K.PSB = [K.sb("PSB%d" % i, [128, 512], F32) for i in range(2)]
    K.SQ1 = [K.sb("SQ1%d" % i, [128, 512], BF16) for i in range(2)]
    K.QN = [K.sb("QN%d" % i, [128, 512], F32) for i in range(2)]
    K.T1 = [K.sb("T10", [128, 512], F32)] * 2
    K.T2 = [K.sb("T20", [128, 512], F32)] * 2
    K.LN2 = [K.sb("LN20", [128, 512], F32)] * 2
    K.RS2 = [K.sb("RS20", [128, 512], F32)] * 2
    if not hasattr(K, "BLK"):
        K.BLK = K.sb("BLK", [128, 128], BF16)
        K.PERM = K.sb("PERM", [128, 128], F32)
        K.CS64 = K.sb("CS64", [128, 256], BF16)
    K.QKG = K.sb("QKG", [128, 8], F32)


def epre_segment(K, L, win, rope, qkg, blk, perm, cs64, kt_o, v_o, ab_o, final=True, load_consts=True, after_tc=None):
    X, PS, P = K.X, K.PS, K.P
    P.dma("pool", K.WIN[:], win.rearrange("(kc p) n -> p kc n", p=128), w=["WIN"])
    if load_consts:
        P.dma("sp", K.BLK[:], blk, w=["BLK"])
        P.dma("sp", K.PERM[:], perm, w=["PERM"])
        P.dma("sp", K.CS64[:], cs64, w=["CS64"])
    P.dma("sp", K.QKG[:], qkg, w=["QKG"])
    K.memset("pool", K.VTC[0][:], 1.0, ["VTC0"])
    K.memset("pool", K.VTC[1][:], 1.0, ["VTC1"])
    def nrm(tc_, part):
        W_ = TCS[tc_][1]
        hcb = K.HC[tc_ % 2]
        hkb = "HC%d" % (tc_ % 2)
        if part == 0:
            norm_sq(K, tc_)
        else:
            norm_rest(K, L, 0, tc_, lambda c, hcb=hcb, W_=W_: hcb[:, c, :W_], lambda c, hkb=hkb: [hkb])

    nrm(0, 0)
    nrm(0, 1)
    for tc in range(5):
        t0, W = TCS[tc]
        hb = tc % 2
        hc = K.HC[hb]
        hk = "HC%d" % hb
        if tc < 4:
            P.dma("sp", K.ROPE[0][:], rope[:, :, t0:t0 + W], w=["ROPE0"])
        def stA(fc):
            pb = fc % 2
            b0 = fc % 2
            for kc in range(8):
                K.mm(PS[:, b0, :W], K.WIN[:, kc, fc * 128:(fc + 1) * 128], hc[:, kc, :W],
                     kc == 0, kc == 7, ["WIN", hk], ["ps%d" % b0])
            K.act(K.PSB[pb][:, :W], PS[:, b0, :W], AF.Identity, ["ps%d" % b0], ["PSB%d" % pb])
            K.tt("pool", K.SQ1[pb][:, :W], K.PSB[pb][:, :W], K.PSB[pb][:, :W], ALU.mult,
                 ["PSB%d" % pb], ["SQ1%d" % pb])

        def stB(fc):
            pb = fc % 2
            b1 = 2 + fc % 2
            K.mm(PS[:, b1, :W], K.BLK[:], K.SQ1[pb][:, :W], True, True, ["BLK", "SQ1%d" % pb], ["ps%d" % b1])
            K.act(K.LN2[pb][:, :W], PS[:, b1, :W], AF.Ln, ["ps%d" % b1], ["LN20"],
                  bias=K.EPSB[:], scale=1.0 / 64)
            K.act(K.RS2[pb][:, :W], K.LN2[pb][:, :W], AF.Exp, ["LN20"], ["RS20"], scale=-0.5)
            K.stt(K.QN[pb][:, :W], K.PSB[pb][:, :W], K.QKG[:, fc:fc + 1], K.RS2[pb][:, :W],
                  ALU.mult, ALU.mult, ["PSB%d" % pb, "QKG", "RS20"], ["QN%d" % pb])

        def stC(fc):
            pb = fc % 2
            b2 = 4 + fc % 2
            if fc < 6:
                dest, dk = K.QT[:, fc, t0:t0 + W], "QT%d.%d" % (tc, fc)
            else:
                dest, dk = K.KTL[:, fc - 6, t0:t0 + W], "KTL%d" % (fc - 6)
            if tc < 4:
                rp = K.ROPE[0]
                rk = "ROPE0"
                K.mm(PS[:, b2, :W], K.PERM[:], K.QN[pb][:, :W], True, True, ["PERM", "QN%d" % pb], ["ps%d" % b2])
                K.tt("dve", K.T1[pb][:, :W], K.QN[pb][:, :W], rp[:, 0, :W], ALU.mult,
                     ["QN%d" % pb, rk], ["T10"])
                K.tt("dve", K.T2[pb][:, :W], PS[:, b2, :W], rp[:, 1, :W], ALU.mult,
                     ["ps%d" % b2, rk], ["T20"])
                K.tt("pool", dest, K.T1[pb][:, :W], K.T2[pb][:, :W], ALU.add,
                     ["T10", "T20"], [dk])
            else:
                K.cp("pool", dest, K.QN[pb][:, :W], ["QN%d" % pb], [dk])

        stA(0)
        if tc + 1 < 5:
            nrm(tc + 1, 0)
        for fc in range(8):
            if fc + 1 < 8:
                stA(fc + 1)
            stB(fc)
            if fc >= 1:
                stC(fc - 1)
            if fc == 1 and tc + 1 < 5:
                nrm(tc + 1, 1)
        stC(7)
        rb = [6, 4, 5]
        ri = [0]

        def nb():
            b = rb[ri[0] % 3]
            ri[0] += 1
            return b

        for tt_ in range(W // 128):
            gt = t0 // 128 + tt_
            b = nb()
            for kc in range(8):
                K.mm(PS[:, b, 0:256], hc[:, kc, tt_ * 128:(tt_ + 1) * 128], K.WIN[:, kc, 1024:1280],
                     kc == 0, kc == 7, ["WIN", hk], ["ps%d" % b])
            K.act(K.VTC[tc % 2][:, tt_, :, :, 0:64], PS[:, b, 0:256].rearrange("p (a s e) -> p a s e", a=2, s=2),
                  AF.Identity, ["ps%d" % b], ["VTC%d" % (tc % 2)])
        for half in range(2):
            b = nb()
            for kc in range(8):
                K.mm(PS[:, b, :W], K.WIN[:, kc, 1280 + half * 128:1280 + (half + 1) * 128], hc[:, kc, :W],
                     kc == 0, kc == 7, ["WIN", hk], ["ps%d" % b])
            K.act(K.FTC[:, half, :W], PS[:, b, :W], AF.Identity, ["ps%d" % b], ["FTC%d" % half])
        for tt_ in range(W // 128):
            gt = t0 // 128 + tt_
            for half in range(2):
                b = nb()
                K.mm(PS[:, b, 0:256], K.FTC[:, half, tt_ * 128:(tt_ + 1) * 128], K.CS64[:],
                     True, True, ["FTC%d" % half, "CS64"], ["ps%d" % b])
                K.cp("dve", K.ABC[tc % 2][:, tt_, half * 256:(half + 1) * 256], PS[:, b, 0:256],
                     ["ps%d" % b], ["ABC%d" % (tc % 2)])
        nt_ = W // 128
        g0 = t0 // 128
        for a_ in range(2):
            P.dma("sp", v_o[a_][:, g0:g0 + nt_], K.VTC[tc % 2][:, :nt_, a_], r=["VTC%d" % (tc % 2)],
                  w=["v_o%d.%d" % (tc, a_)], final=final)
        for t_ in range(0, nt_, 2):
            ch, off = (g0 + t_) // 6, (g0 + t_) % 6
            P.dma("sp", ab_o[ch][:, off:off + 2, :], K.ABC[tc % 2][:, t_:t_ + 2, :], r=["ABC%d" % (tc % 2)],
                  w=["ab_o%d.%d" % (tc, t_)], final=final)
        if after_tc is not None:
            after_tc(tc)
    for k_ in range(2):
        P.dma("sp", kt_o[k_], K.KTL[:, k_, :], r=["KTL%d" % k_], w=["kt_o%d" % k_], final=final)


def fm_vec(v):
    v = np.asarray(v, np.float32)
    return np.ascontiguousarray(v.reshape(-1, 128).T)


def fm_tokens(a):
    T = a.shape[0]
    return np.ascontiguousarray(a.reshape(T, 8, 128).transpose(2, 1, 0))


_CONST = {}


def consts():
    if _CONST:
        return _CONST
    blk = np.zeros((128, 128), np.float32)
    blk[:64, :64] = 1.0
    blk[64:, 64:] = 1.0
    perm = np.zeros((128, 128), np.float32)
    for j in range(64):
        perm[2 * j + 1, 2 * j] = -1.0
        perm[2 * j, 2 * j + 1] = 1.0
    n = np.arange(64)
    ang = 2 * np.pi * np.outer(n, n) / 64.0
    c64, s64 = np.cos(ang), np.sin(ang)
    cs = np.zeros((128, 256), np.float64)
    for g in range(2):
        cs[g * 64:(g + 1) * 64, g * 64:(g + 1) * 64] = c64
        cs[g * 64:(g + 1) * 64, 128 + g * 64:128 + (g + 1) * 64] = s64
    _CONST["blk"] = _bf(blk)
    _CONST["perm"] = perm
    _CONST["cs64"] = _bf(cs)
    freqs = 10000.0 ** (-np.arange(16, dtype=np.float32) / 16)
    ropes = []
    for r in range(4):
        t = np.arange(r * NLAT, (r + 1) * NLAT)
        row = (t // 64).astype(np.float32)
        col = (t % 64).astype(np.float32)
        ang = np.concatenate([row[:, None] * freqs, col[:, None] * freqs], axis=-1).astype(np.float32)
        cos, sin = np.cos(ang), np.sin(ang)
        tab = np.zeros((128, 2, NLAT), np.float32)
        for p in range(128):
            jj = (p % 64) // 2
            tab[p, 0] = cos[:, jj]
            tab[p, 1] = sin[:, jj]
        ropes.append(tab)
    _CONST["rope"] = ropes
    tabs = []
    l = np.arange(8192, dtype=np.int64)
    sc = 1.0 / math.sqrt(8192 * 64)
    for r in range(4):
        k = np.arange(r * NLAT, (r + 1) * NLAT, dtype=np.int64)
        m = (l[:, None] * k[None, :]) % 8192
        a = 2 * np.pi * m / 8192.0
        tab = np.stack([np.cos(a) * sc, -np.sin(a) * sc], axis=1)
        tabs.append(_bf(tab.reshape(64, 128, 2, NLAT)))
    _CONST["dft"] = tabs
    l2 = np.arange(256, dtype=np.int64)
    m = (l2[:, None] * l2[None, :]) % 256
    a = 2 * np.pi * m / 256.0
    sc2 = 1.0 / math.sqrt(256 * 64)
    _CONST["dftc"] = _bf(np.stack([np.cos(a) * sc2, -np.sin(a) * sc2], axis=1).reshape(2, 128, 2, 256))
    return _CONST


def perm_win(w):
    cols = []
    for a in range(6):
        cols += list(range(a * 64, a * 64 + 64)) + list(range((a + 6) * 64, (a + 6) * 64 + 64))
    for kv in (0, 2, 1, 3):
        cols += list(range(768 + kv * 64, 768 + kv * 64 + 64))
    for kv in (0, 2, 1, 3):
        cols += list(range(1024 + kv * 64, 1024 + kv * 64 + 64))
    cols += list(range(1280, 1536))
    return np.ascontiguousarray(w[:, cols])


def build_p0():
    nc = bass.Bass("TRN2", target_bir_lowering=False)
    with ExitStack() as es:
        K = Ctx(nc, es)
        xin = K.dram_in("x_in", [128, 8, NT], F32)
        win = K.dram_in("win", [D, 1536], F32)
        rope = K.dram_in("rope", [128, 2, NLAT], F32)
        qkg = K.dram_in("qkg", [128, 8], F32)
        blk = K.dram_in("blk", [128, 128], BF16)
        perm = K.dram_in("perm", [128, 128], F32)
        cs64 = K.dram_in("cs64", [128, 256], BF16)
        kt_o = K.dram_out("kt_o", [128, 2, NT], BF16)
        v_o = K.dram_out("v_o", [128, 18, 2, 2, 65], BF16)
        ab_o = K.dram_out("ab_o", [128, 18, 512], BF16)
        qt_o = K.dram_out("qt_o", [128, 6, NT], BF16)
        alloc_common(K)
        alloc_eps(K)
        alloc_norm(K)
        alloc_epre(K)
        load_mod(K, 0)
        for c in range(8):
            K.P.dma("sp", K.X[:, c, :], xin[:, c, :], w=["X%d.%d" % (tc, c) for tc in range(5)])
        epre_segment(K, 0, win, rope, qkg, blk, perm, cs64, kt_o, v_o, ab_o)
        K.P.dma("sp", qt_o, K.QT[:], r=["QT%d.%d" % (tc, fc) for tc in range(5) for fc in range(6)],
                w=["qt_o"], final=True)
        K.P.emit()
    return nc


def alloc_epost(K, with_qt=True):
    if with_qt:
        K.QT = K.sb("QT", [128, 6, NT], BF16)
    K.KT = K.sb("KT", [128, 8448], BF16)
    K.V = K.sb("V", [128, 66, 2, 65], BF16)
    K.WO = K.sb("WO", [64, 12, 1024], BF16)
    K.WOF = K.sb("WOF", [128, 2, 1024], BF16)
    K.CAT = K.sb("CAT", [64, 6, 512], BF16)
    K.PB = [K.sb("PB%d" % i, [128, 2, 512], BF16) for i in range(3)]
    K.FT = K.sb("FT", [128, 2, NT], BF16)
    K.ABG = [K.sb("ABG%d" % i, [128, 2, 512], BF16) for i in range(2)]
    K.TABC = K.sb("TABC", [128, 2, 2, 256], BF16)
    K.ABGC = K.sb("ABGC", [128, 2, 512], BF16)
    K.OSB = K.sb("OSB", [65, 2, 512], F32)
    K.RDL = K.sb("RDL", [65, 2, 512], F32)
    K.RD = K.sb("RD", [65, 2, 512], BF16)
    K.ONESR = K.sb("ONESR", [128, 64], BF16)


def epost_segment(K, L, kt_all, v_all, ab_all, dft, dftc, wout, gk=()):
    X, PS, P = K.X, K.PS, K.P
    MOD = K.MOD[L]
    mk = "MOD%d" % L
    K.memset("pool", K.ONESR[:], 1.0, ["ONESR"])
    P.dma("pool", K.WO[:], wout[0:768, :].rearrange("(h d) n -> d h n", d=64), w=["WO"])
    P.dma("pool", K.WOF[:], wout[768:1024, :].rearrange("(c p) n -> p c n", p=128), w=["WOF"])
    TAB = [K.KT[:, 0:8192].rearrange("p (a b c) -> p a b c", a=2, b=2),
           K.V[:].rearrange("p a b c -> p (a b c)")[:, 0:8192].rearrange("p (a b c) -> p a b c", a=2, b=2)]
    TK = ["KTa", "Va"]
    for g in range(32):
        r = g // 8
        tl = (2 * g) % 16
        tb = g % 2
        P.dma("sp", TAB[tb], dft[2 * g:2 * g + 2].rearrange("l p s k -> p l s k"), w=[TK[tb]])
        P.dma("sp", K.ABG[tb][:], ab_all[tl // 6][r, :, tl % 6:tl % 6 + 2, :], r=gk, w=["ABG%d" % tb])
        for li in range(2):
            for half in range(2):
                for s in range(2):
                    for kc in range(4):
                        bank = half * 4 + kc
                        K.mm(PS[:, bank, :], K.ABG[tb][:, li, half * 256 + s * 128:half * 256 + (s + 1) * 128],
                             TAB[tb][:, li, s, kc * 512:(kc + 1) * 512],
                             g == 0 and li == 0 and s == 0, g == 31 and li == 1 and s == 1,
                             ["ABG%d" % tb, TK[tb]], ["ps%d" % bank])
    for half in range(2):
        for kc in range(4):
            bank = half * 4 + kc
            if bank % 2 == 0:
                K.act(K.FT[:, half, kc * 512:(kc + 1) * 512], PS[:, bank, :], AF.Identity, ["ps%d" % bank], ["FT%d" % kc])
            else:
                K.cp("dve", K.FT[:, half, kc * 512:(kc + 1) * 512], PS[:, bank, :], ["ps%d" % bank], ["FT%d" % kc])
    P.dma("sp", K.TABC[:], dftc.rearrange("l p s k -> p l s k"), w=["TABC"])
    P.dma("sp", K.ABGC[:], ab_all[2][0, :, 4:6, :], r=gk, w=["ABGC"])
    for half in range(2):
        for li in range(2):
            for s in range(2):
                K.mm(PS[:, half, 0:256], K.ABGC[:, li, half * 256 + s * 128:half * 256 + (s + 1) * 128],
                     K.TABC[:, li, s, :], li == 0 and s == 0, li == 1 and s == 1, ["ABGC", "TABC"], ["ps%d" % half])
        K.act(K.FT[:, half, 2048:2304], PS[:, half, 0:256], AF.Identity, ["ps%d" % half], ["FT4"])
    P.barrier()
    jobs = [(kp, tc, a3) for kp in range(2) for tc in range(5) for a3 in range(3)]

    def load_kv(kp):
        for r in range(4):
            P.dma("sp", K.KT[:, r * 2048:(r + 1) * 2048], kt_all[kp][r, :, 0:2048], r=["kt_gat%d" % kp], w=["KT%d" % r])
            P.dma("sp", K.V[:, r * 16:(r + 1) * 16, :, :], v_all[kp][r, :, 0:16, :, :], r=["v_gat%d" % kp], w=["V%d" % r])
        P.dma("sp", K.KT[:, 8192:8448], kt_all[kp][0, :, 2048:2304], r=["kt_gat%d" % kp], w=["KT4"])
        P.dma("sp", K.V[:, 64:66, :, :], v_all[kp][0, :, 16:18, :, :], r=["v_gat%d" % kp], w=["V4"])

    def kts_of(tc):
        return list(range(66)) if tc < 4 else [64, 65]

    def S(job, i, g0):
        kp, tc, a3 = job
        t0, W = TCS[tc]
        a = 3 * kp + a3
        qk = "QT%d.%d" % (tc, a)
        kt = kts_of(tc)[i]
        sb = (g0 + i) % 3
        kk = "KT%d" % min(kt // 16, 4)
        K.mm(PS[:, 2 * sb, :W], K.KT[0:64, kt * 128:(kt + 1) * 128], K.QT[0:64, a, t0:t0 + W],
             True, True, [kk, qk], ["ps%d" % (2 * sb)])
        K.mm(PS[:, 2 * sb + 1, :W], K.KT[64:128, kt * 128:(kt + 1) * 128], K.QT[64:128, a, t0:t0 + W],
             True, True, [kk, qk], ["ps%d" % (2 * sb + 1)])

    def prologue(job, g0):
        n = len(kts_of(job[1]))
        S(job, 0, g0)
        if n > 1:
            S(job, 1, g0)

    def body(job, g0, pending=None):
        kp, tc, a3 = job
        t0, W = TCS[tc]
        kts = kts_of(tc)
        n = len(kts)
        had_pending = pending is not None
        for i in range(n):
            sb = (g0 + i) % 3
            kt = kts[i]
            vk = "V%d" % min(kt // 16, 4)
            if i + 2 < n:
                S(job, i + 2, g0)
            K.act(K.PB[sb][:, :, :W], PS[:, 2 * sb:2 * sb + 2, :W], AF.Exp,
                  ["ps%d" % (2 * sb), "ps%d" % (2 * sb + 1)], ["PB%d" % sb], scale=0.125)
            for s_ in range(2):
                K.mm(PS[0:65, 6 + s_, :W], K.V[:, kt, s_, :], K.PB[sb][:, s_, :W],
                     i == 0, i == n - 1, [vk, "PB%d" % sb], ["ps%d" % (6 + s_)])
            if i == 10 and pending is not None:
                pending(sb)
                pending = None
        if pending is not None:
            pending((g0 + n) % 3)
        K.cp("dve", K.OSB[0:65, :, :W], PS[0:65, 6:8, :W], ["ps6", "ps7"], ["OSB"])
        K.recip(K.RDL[64:65, :, :W], K.OSB[64:65, :, :W], ["OSB"], ["RDL"])
        K.cp("dve", K.RD[64:65, :, :W], K.RDL[64:65, :, :W], ["RDL"], ["RD"])

    def finish(job, bs):
        kp, tc, a3 = job
        t0, W = TCS[tc]
        v = 1 if tc == 4 else 0
        for s_ in range(2):
            K.mm(PS[0:64, 2 * bs + s_, :W], K.ONESR[64:65, 0:64], K.RD[64:65, s_, :W], True, True,
                 ["ONESR", "RD"], ["ps%d" % (2 * bs + s_)])
        K.tt("dve", K.CAT[0:64, 2 * a3:2 * a3 + 2, :W], K.OSB[0:64, :, :W], PS[0:64, 2 * bs:2 * bs + 2, :W],
             ALU.mult, ["OSB", "ps%d" % (2 * bs), "ps%d" % (2 * bs + 1)], ["CAT%d" % (2 * a3), "CAT%d" % (2 * a3 + 1)])
        if a3 == 2:
            for d in range(8):
                bank = 2 * bs + d % 2
                for slot in range(6):
                    head = 3 * kp + slot // 2 + 6 * (slot % 2)
                    K.mm(PS[:, bank, :W], K.WO[0:64, head, d * 128:(d + 1) * 128], K.CAT[0:64, slot, :W],
                         slot == 0, slot == 5 and kp == 1, ["WO", "CAT%d" % slot], ["ps%d" % bank])
                if kp == 0:
                    for half in range(2):
                        K.mm(PS[:, bank, :W], K.WOF[:, half, d * 128:(d + 1) * 128], K.FT[:, half, t0:t0 + W],
                             False, half == 1, ["WOF", "FT%d" % tc], ["ps%d" % bank])
                K.stt(X[:, d, t0:t0 + W], PS[:, bank, :W], MOD[:, 16 + d, v:v + 1], X[:, d, t0:t0 + W],
                      ALU.mult, ALU.add, ["ps%d" % bank, mk, "X%d.%d" % (tc, d)], ["X%d.%d" % (tc, d)])

    g0 = 0
    load_kv(0)
    prologue(jobs[0], g0)
    pending = None
    for ji, job in enumerate(jobs):
        n = len(kts_of(job[1]))
        body(job, g0, pending)
        pending = (lambda bs, job=job: finish(job, bs))
        g0n = g0 + n + 1
        if ji + 1 < len(jobs):
            nj = jobs[ji + 1]
            if nj[0] != job[0]:
                pending((g0 + n) % 3)
                pending = None
                load_kv(nj[0])
            prologue(nj, g0n)
        g0 = g0n
    if pending is not None:
        pending(g0 % 3)


def load_x(K, xin):
    for c in range(8):
        K.P.dma("sp", K.X[:, c, :], xin[:, c, :], w=["X%d.%d" % (tc, c) for tc in range(5)])


def store_x(K, xo, n=NT):
    for c in range(8):
        K.P.dma("sp", xo[:, c, :], K.X[:, c, 0:n], r=["X%d.%d" % (tc, c) for tc in range(5)],
                w=["xo%d" % c], final=True)


UW = NLAT + 30 + NCTX + 30


def ucol(tc):
    return 15 + TCS[tc][0] if tc < 4 else NLAT + 30 + 15


def alloc_opre(K, with_u=True):
    K.WPW1 = K.sb("WPW1", [128, 8, 2048], BF16)
    K.HC = [K.sb("HC%d" % i, [128, 8, 512], BF16) for i in range(2)]
    if with_u:
        K.U = K.sb("U", [128, 8, UW], BF16)
    K.BPW1 = K.sb("BPW1", [128, 16], F32)
    K.NEGB = K.sb("NEGB", [128, 16], F32)
    K.EG = [K.sb("EG%d" % i, [128, 512], F32) for i in range(2)]
    K.SG = [K.sb("SG%d" % i, [128, 512], F32) for i in range(2)]


def opre_segment(K, L, wpw1, bpw1, ntc=5):
    X, PS, P = K.X, K.PS, K.P
    P.dma("pool", K.WPW1[:], wpw1.rearrange("(kc p) n -> p kc n", p=128), w=["WPW1"])
    P.dma("sp", K.BPW1[:], bpw1, w=["BPW1"])
    K.ts("dve", K.NEGB[:], K.BPW1[:], -1.0, None, ALU.mult, None, ["BPW1"], ["NEGB"])
    K.memset("pool", K.U[:], 0.0, ["U%d" % tc for tc in range(5)])
    def nrm(tc, part):
        W = TCS[tc][1]
        hcb = K.HC[tc % 2]
        hkb = "HC%d" % (tc % 2)
        if part == 0:
            norm_sq(K, tc)
        else:
            norm_rest(K, L, 0, tc, lambda c, hcb=hcb, W=W: hcb[:, c, :W], lambda c, hkb=hkb: [hkb])

    nrm(0, 0)
    nrm(0, 1)
    for tc in range(ntc):
        t0, W = TCS[tc]
        hc = K.HC[tc % 2]
        hk = "HC%d" % (tc % 2)
        u0 = ucol(tc)
        for c in range(8):
            if tc + 1 < ntc and c == 0:
                nrm(tc + 1, 0)
            if tc + 1 < ntc and c == 2:
                nrm(tc + 1, 1)
            pb = c % 2
            bv, bg = c % 2, 2 + c % 2
            for kc in range(8):
                K.mm(PS[:, bv, :W], K.WPW1[:, kc, c * 128:(c + 1) * 128], hc[:, kc, :W],
                     kc == 0, kc == 7, ["WPW1", hk], ["ps%d" % bv])
            for kc in range(8):
                K.mm(PS[:, bg, :W], K.WPW1[:, kc, 1024 + c * 128:1024 + (c + 1) * 128], hc[:, kc, :W],
                     kc == 0, kc == 7, ["WPW1", hk], ["ps%d" % bg])
            K.act(K.EG[pb][:, :W], PS[:, bg, :W], AF.Exp, ["ps%d" % bg, "NEGB"], ["EG%d" % pb],
                  bias=K.NEGB[:, 8 + c:9 + c], scale=-1.0)
            K.act(K.EG[pb][:, :W], K.EG[pb][:, :W], AF.Ln, ["EG%d" % pb, "ONEB"], ["EG%d" % pb], bias=K.ONEB[:], scale=1.0)
            K.act(K.SG[pb][:, :W], K.EG[pb][:, :W], AF.Exp, ["EG%d" % pb], ["SG%d" % pb], scale=-1.0)
            K.stt(K.U[:, c, u0:u0 + W], PS[:, bv, :W], K.BPW1[:, c:c + 1], K.SG[pb][:, :W],
                  ALU.add, ALU.mult, ["ps%d" % bv, "BPW1", "SG%d" % pb], ["U%d" % tc])


def alloc_opost(K, with_u=True):
    if with_u:
        K.U = K.sb("U", [128, 8, UW], BF16)
    K.WPW2 = K.sb("WPW2", [128, 8, 1024], BF16)
    K.ACC = K.sb("ACC", [128, 8, 512], F32)
    K.SQC = K.sb("SQC", [128, 8, 512], BF16)
    K.Z = K.sb("Z", [128, 8, 512], BF16)
    K.COB = K.Z
    K.DIAG = [K.sb("DIAG%d" % i, [128, 31, 128], BF16) for i in range(2)]
    if not hasattr(K, "IDENT"):
        K.IDENT = K.sb("IDENT", [128, 128], BF16)
    K.WDW = K.sb("WDW", [128, 8, 31], F32)
    K.PV5 = K.sb("PV5", [128, 5, 8], F32)
    K.GB = K.sb("GB", [128, 8, 2], F32)
    K.MEAN = K.sb("MEAN", [128, 512], F32)
    K.VAR = K.sb("VAR", [128, 512], F32)
    K.LNV2 = K.sb("LNV2", [128, 512], F32)
    K.RSTD2 = K.sb("RSTD2", [128, 512], F32)
    K.TA = [K.sb("TA0", [128, 512], F32)] * 2
    K.TB = [K.sb("TB%d" % i, [128, 512], F32) for i in range(2)]
    K.TE = [K.sb("TE%d" % i, [128, 512], F32) for i in range(2)]
    K.T3 = [K.sb("T3%d" % i, [128, 512], F32) for i in range(2)]


def opost_segment(K, L, wdw, pv4, wpw2, ntc=5, ukeys=False, ident=None):
    X, PS, P = K.X, K.PS, K.P
    if ident is not None:
        P.dma("sp", K.IDENT[:], ident, w=["IDENT"])
    MOD = K.MOD[L]
    mk = "MOD%d" % L
    P.dma("pool", K.WPW2[:], wpw2.rearrange("(kc p) n -> p kc n", p=128), w=["WPW2"])
    P.dma("sp", K.WDW[:], wdw, w=["WDW"])
    P.dma("sp", K.PV5[:, 0:4, :], pv4, w=["PV5"])
    K.ts("dve", K.PV5[:, 4, :], K.PV5[:, 2, :], -1.0, None, ALU.mult, None, ["PV5"], ["PV5n"])
    for v in range(2):
        K.tt("dve", K.GB[:, :, v], MOD[:, 16:24, v], K.PV5[:, 3, :], ALU.mult, [mk, "PV5"], ["GB"])
    def conv_mm(tc, c):
        t0, W = TCS[tc]
        s0 = ucol(tc) - 15
        db = c % 2
        dk = "DIAG%d" % db
        idv = K.IDENT[:]
        wv = K.WDW[:, c, :]
        in0 = bass.AP(idv.tensor, idv.offset, [list(idv.ap[0]), [0, 31], [1, 128]])
        in1 = bass.AP(wv.tensor, wv.offset, [list(wv.ap[0]), [1, 31], [0, 128]])
        K.tt("dve", K.DIAG[db][:], in0, in1, ALU.mult, ["IDENT", "WDW"], [dk])
        bank = 2 + c % 4
        for j in range(31):
            K.mm(PS[:, bank, :W], K.DIAG[db][:, j, :], K.U[:, c, s0 + j:s0 + j + W], j == 0, j == 30,
                 [dk, "U"], ["ps%d" % bank])

    def conv_ev(tc, c):
        W = TCS[tc][1]
        bank = 2 + c % 4
        K.act(K.ACC[:, c, :W], PS[:, bank, :W], AF.Identity, ["ps%d" % bank, "PV5"], ["ACC%d" % c],
              bias=K.PV5[:, 0, c:c + 1], scale=1.0)

    def conv_tail(tc):
        W = TCS[tc][1]
        for c in range(8):
            K.cp("dve", K.COB[:, c, :W], K.ACC[:, c, :W], ["ACC%d" % c], ["Z%d" % c])
            K.tt("pool", K.SQC[:, c, :W], K.ACC[:, c, :W], K.ACC[:, c, :W], ALU.mult, ["ACC%d" % c], ["SQC%d" % c])

    def stats(tc):
        W = TCS[tc][1]
        for c in range(8):
            K.mm(PS[:, 6, :W], K.ONES[:], K.COB[:, c, :W], c == 0, c == 7, ["ONES", "Z%d" % c], ["ps6"])
        for c in range(8):
            K.mm(PS[:, 7, :W], K.ONES[:], K.SQC[:, c, :W], c == 0, c == 7, ["ONES", "SQC%d" % c], ["ps7"])
        K.act(K.MEAN[:, :W], PS[:, 6, :W], AF.Identity, ["ps6"], ["MEAN"], scale=1.0 / D)
        K.tt("pool", K.VAR[:, :W], K.MEAN[:, :W], K.MEAN[:, :W], ALU.mult, ["MEAN"], ["VAR"])
        K.stt(K.VAR[:, :W], PS[:, 7, :W], 1.0 / D, K.VAR[:, :W], ALU.mult, ALU.subtract, ["ps7", "VAR"], ["VAR"])
        K.act(K.LNV2[:, :W], K.VAR[:, :W], AF.Ln, ["VAR"], ["LNV2"], bias=K.EPSB[:], scale=1.0)
        K.act(K.RSTD2[:, :W], K.LNV2[:, :W], AF.Exp, ["LNV2"], ["RSTD2"], scale=-0.5)

    def ln_chain(tc):
        W = TCS[tc][1]
        for c in range(8):
            pb = c % 2
            K.tt("dve", K.TA[pb][:, :W], K.ACC[:, c, :W], K.MEAN[:, :W], ALU.subtract,
                 ["ACC%d" % c, "MEAN"], ["TA0"])
            K.stt(K.TB[pb][:, :W], K.TA[pb][:, :W], K.PV5[:, 1, c:c + 1], K.RSTD2[:, :W], ALU.mult, ALU.mult,
                  ["TA0", "PV5", "RSTD2"], ["TB%d" % pb])
            K.act(K.TE[pb][:, :W], K.TB[pb][:, :W], AF.Exp, ["TB%d" % pb, "PV5n"], ["TE%d" % pb],
                  bias=K.PV5[:, 4, c:c + 1], scale=-1.0)
            K.act(K.TE[pb][:, :W], K.TE[pb][:, :W], AF.Ln, ["TE%d" % pb, "ONEB"], ["TE%d" % pb], bias=K.ONEB[:], scale=1.0)
            K.act(K.TE[pb][:, :W], K.TE[pb][:, :W], AF.Exp, ["TE%d" % pb], ["TE%d" % pb], scale=-1.0)
            K.stt(K.Z[:, c, :W], K.TB[pb][:, :W], K.PV5[:, 2, c:c + 1], K.TE[pb][:, :W], ALU.add, ALU.mult,
                  ["TB%d" % pb, "PV5", "TE%d" % pb], ["Z%d" % c])

    def pw2(tc):
        t0, W = TCS[tc]
        v = 1 if tc == 4 else 0
        for d in range(8):
            bank = d % 2
            for c in range(8):
                K.mm(PS[:, bank, :W], K.WPW2[:, c, d * 128:(d + 1) * 128], K.Z[:, c, :W], c == 0, c == 7,
                     ["WPW2", "Z%d" % c], ["ps%d" % bank])
            K.act(K.T3[bank][:, :W], PS[:, bank, :W], AF.Identity, ["ps%d" % bank, mk, "GB"], ["T3%d" % bank],
                  bias=K.GB[:, d, v:v + 1], scale=MOD[:, 16 + d, v:v + 1])
            K.tt("dve", X[:, d, t0:t0 + W], X[:, d, t0:t0 + W], K.T3[bank][:, :W], ALU.add,
                 ["X%d.%d" % (tc, d), "T3%d" % bank], ["X%d.%d" % (tc, d)])

    for c in range(8):
        conv_mm(0, c)
        conv_ev(0, c)
    conv_tail(0)
    for tc in range(ntc):
        nxt = tc + 1 < ntc
        stats(tc)
        if nxt:
            conv_mm(tc + 1, 0)
            conv_mm(tc + 1, 1)
        ln_chain(tc)
        if nxt:
            conv_ev(tc + 1, 0)
            conv_ev(tc + 1, 1)
        pw2(tc)
        if nxt:
            for c in range(2, 8):
                conv_mm(tc + 1, c)
                conv_ev(tc + 1, c)
            conv_tail(tc + 1)


def _epre_io(K):
    win = K.dram_in("win", [D, 1536], F32)
    rope = K.dram_in("rope", [128, 2, NLAT], F32)
    qkg = K.dram_in("qkg", [128, 8], F32)
    blk = K.dram_in("blk", [128, 128], BF16)
    perm = K.dram_in("perm", [128, 128], F32)
    cs64 = K.dram_in("cs64", [128, 256], BF16)
    kt_o = K.dram_out("kt_o", [128, 2, NT], BF16)
    v_o = K.dram_out("v_o", [128, 18, 2, 2, 65], BF16)
    ab_o = K.dram_out("ab_o", [128, 18, 512], BF16)
    qt_o = K.dram_out("qt_o", [128, 6, NT], BF16)
    return win, rope, qkg, blk, perm, cs64, kt_o, v_o, ab_o, qt_o


def _epre_run(K, L, io):
    win, rope, qkg, blk, perm, cs64, kt_o, v_o, ab_o, qt_o = io
    epre_segment(K, L, win, rope, qkg, blk, perm, cs64, [kt_o[:, k_, :] for k_ in range(2)],
                 [v_o[:, :, a_, :, :] for a_ in range(2)], [ab_o[:, 6 * c_:6 * c_ + 6, :] for c_ in range(3)])
    K.P.dma("sp", qt_o, K.QT[:], r=["QT%d.%d" % (tc, fc) for tc in range(5) for fc in range(6)],
            w=["qt_o"], final=True)


def build_pA():
    nc = bass.Bass("TRN2", target_bir_lowering=False)
    with ExitStack() as es:
        K = Ctx(nc, es)
        xin = K.dram_in("x_in", [128, 8, NT], F32)
        io = _epre_io(K)
        alloc_common(K)
        alloc_eps(K)
        load_mod(K, 0, "modA")
        load_x(K, xin)
        alloc_norm(K)
        alloc_epre(K)
        _epre_run(K, 0, io)
        K.P.emit()
    return nc


def build_pB():
    nc = bass.Bass("TRN2", target_bir_lowering=False)
    with ExitStack() as es:
        K = Ctx(nc, es)
        xin = K.dram_in("x_in", [128, 8, NT], F32)
        qt_in = K.dram_in("qt_in", [128, 6, NT], BF16)
        kt_all = K.dram_in("kt_all", [4, 128, 2, NT], BF16)
        v_all = K.dram_in("v_all", [4, 128, 18, 2, 2, 65], BF16)
        ab_all = K.dram_in("ab_all", [4, 128, 18, 512], BF16)
        dft = K.dram_in("dft", [64, 128, 2, NLAT], BF16)
        dftc = K.dram_in("dftc", [2, 128, 2, 256], BF16)
        wout = K.dram_in("wout", [D, D], F32)
        w1 = K.dram_in("w1", [D, 4 * D], F32)
        w2 = K.dram_in("w2", [4 * D, D], F32)
        wpw1 = K.dram_in("wpw1", [D, 2 * D], F32)
        bpw1 = K.dram_in("bpw1", [128, 16], F32)
        xo = K.dram_out("x_o", [128, 8, NT], F32)
        uo = K.dram_out("u_o", [128, 8, UW], BF16)
        alloc_common(K)
        alloc_eps(K)
        load_mod(K, 0, "modA")
        load_mod(K, 1, "modB")
        load_x(K, xin)
        with ExitStack() as ph:
            K.es = ph
            alloc_epost(K)
            K.P.dma("sp", K.QT[:], qt_in, w=["QT%d.%d" % (tc, fc) for tc in range(5) for fc in range(6)])
            epost_segment(K, 0, [kt_all[:, :, k_, :] for k_ in range(2)],
                          [v_all[:, :, :, a_, :, :] for a_ in range(2)],
                          [ab_all[:, :, 6 * c_:6 * c_ + 6, :] for c_ in range(3)], dft, dftc, wout)
            K.P.barrier()
        with ExitStack() as ph:
            K.es = ph
            alloc_norm(K)
            alloc_mlp(K)
            mlp_segment(K, 0, w1, w2)
            K.P.barrier()
        with ExitStack() as ph:
            K.es = ph
            alloc_norm(K)
            alloc_opre(K)
            opre_segment(K, 1, wpw1, bpw1)
            K.P.dma("sp", uo, K.U[:], r=["U%d" % tc for tc in range(5)], w=["u_o"], final=True)
            K.P.barrier()
        K.es = es
        store_x(K, xo)
        K.P.emit()
    return nc


def build_pC(last):
    nc = bass.Bass("TRN2", target_bir_lowering=False)
    ntc = 4 if last else 5
    with ExitStack() as es:
        K = Ctx(nc, es)
        xin = K.dram_in("x_in", [128, 8, NT], F32)
        u_in = K.dram_in("u_in", [128, 8, UW], BF16)
        wdw = K.dram_in("wdw", [128, 8, 31], F32)
        pv4 = K.dram_in("pv4", [128, 4, 8], F32)
        wpw2 = K.dram_in("wpw2", [D, D], F32)
        ident = K.dram_in("ident", [128, 128], BF16)
        w1 = K.dram_in("w1", [D, 4 * D], F32)
        w2 = K.dram_in("w2", [4 * D, D], F32)
        if not last:
            io = _epre_io(K)
            xo = K.dram_out("x_o", [128, 8, NT], F32)
        else:
            xo = K.dram_out("x_o", [128, 8, NLAT], F32)
        alloc_common(K)
        alloc_eps(K)
        load_mod(K, 0, "modA")
        if not last:
            load_mod(K, 1, "modB")
        load_x(K, xin)
        with ExitStack() as ph:
            K.es = ph
            alloc_opost(K)
            K.P.dma("sp", K.U[:], u_in, w=["U"])
            opost_segment(K, 0, wdw, pv4, wpw2, ntc, ident=ident)
            K.P.barrier()
        with ExitStack() as ph:
            K.es = ph
            alloc_norm(K)
            alloc_mlp(K)
            mlp_segment(K, 0, w1, w2, ntc)
            K.P.barrier()
        if not last:
            with ExitStack() as ph:
                K.es = ph
                alloc_norm(K)
                alloc_epre(K)
                _epre_run(K, 1, io)
                K.P.barrier()
        K.es = es
        store_x(K, xo, NLAT if last else NT)
        K.P.emit()
    return nc


_PROGS = {}


def _prog(name):
    if name not in _PROGS:
        _PROGS[name] = {"mod": build_mod, "A": build_pA, "B": build_pB,
                        "C": lambda: build_pC(False), "D": lambda: build_pC(True)}[name]()
    return _PROGS[name]


def _run(name, in_maps):
    res = run_bass_kernel_spmd(_prog(name), in_maps, core_ids=list(range(NCORES)))
    return res.results


def kernel_multi(x, c, ctx, c_ctx, ada_w, ada_b, norm1_g, norm2_g, mlp_w1, mlp_w2,
           attn_w_in, q_norm_g, k_norm_g, attn_w_out,
           conv_w_pw1, conv_b_pw1, conv_w_dw, conv_b_dw, conv_ln_g, conv_ln_b,
           conv_w_pw2, conv_b_pw2):
    f32 = lambda a: np.ascontiguousarray(np.asarray(a, dtype=np.float32))
    x, c, ctx, c_ctx = f32(x), f32(c), f32(ctx), f32(c_ctx)
    ada_w, ada_b, norm1_g, norm2_g = f32(ada_w), f32(ada_b), f32(norm1_g), f32(norm2_g)
    mlp_w1, mlp_w2, attn_w_in, attn_w_out = f32(mlp_w1), f32(mlp_w2), f32(attn_w_in), f32(attn_w_out)
    conv_w_pw1, conv_w_pw2, conv_w_dw = f32(conv_w_pw1), f32(conv_w_pw2), f32(conv_w_dw)
    C = consts()
    cores = [(i // 4, i % 4) for i in range(NCORES)]
    maps = []
    for b, r in cores:
        cvec = np.ascontiguousarray(np.stack([fm_vec(c[b]), fm_vec(c_ctx)], axis=-1))
        adab = fm_vec(ada_b[r])
        adab = np.ascontiguousarray(np.repeat(adab[:, :, None], 2, axis=2))
        ng = np.ascontiguousarray(np.stack([fm_vec(norm1_g[r]), fm_vec(norm2_g[r])], axis=1))
        maps.append(dict(adaw=ada_w[r], adab=adab, cvec=cvec, ng=ng))
    rm = _run("mod", maps)
    mod = {(b, L): np.asarray(rm[b * 4 + L]["modo"]) for b in range(2) for L in range(4)}

    def qkg_of(j):
        g = np.zeros((128, 8), np.float32)
        for fc in range(8):
            src = np.asarray(q_norm_g[j] if fc < 6 else k_norm_g[j], np.float32)
            g[:64, fc] = src
            g[64:, fc] = src
        return g

    def epre_inputs(j, r):
        return dict(win=perm_win(attn_w_in[j]), rope=C["rope"][r], qkg=qkg_of(j), blk=C["blk"],
                    perm=C["perm"], cs64=C["cs64"])

    def gather(res, key, b):
        return np.ascontiguousarray(np.stack([np.asarray(res[b * 4 + rr][key]) for rr in range(4)], 0))

    def epost_inputs(res, j, b, r, i):
        return dict(qt_in=np.asarray(res[i]["qt_o"]), kt_all=gather(res, "kt_o", b), v_all=gather(res, "v_o", b),
                    ab_all=gather(res, "ab_o", b), dft=C["dft"][r], dftc=C["dftc"], wout=attn_w_out[j])

    def u_with_halo(res, b, r):
        u = np.array(np.asarray(res[b * 4 + r]["u_o"]))
        if r > 0:
            ul = np.asarray(res[b * 4 + r - 1]["u_o"])
            u[:, :, 0:15] = ul[:, :, NLAT:NLAT + 15]
        if r < 3:
            ur = np.asarray(res[b * 4 + r + 1]["u_o"])
            u[:, :, NLAT + 15:NLAT + 30] = ur[:, :, 15:30]
        return np.ascontiguousarray(u)

    def opost_inputs(j):
        wdw = np.ascontiguousarray(conv_w_dw[j].reshape(31, 8, 128).transpose(2, 1, 0))
        pv4 = np.ascontiguousarray(np.stack([fm_vec(conv_b_dw[j]), fm_vec(conv_ln_g[j]), fm_vec(conv_ln_b[j]),
                                             fm_vec(conv_b_pw2[j])], axis=1))
        return dict(wdw=wdw, pv4=pv4, wpw2=conv_w_pw2[j], ident=_bf(np.eye(128, dtype=np.float32)))

    maps = []
    for b, r in cores:
        xt = np.concatenate([x[b, r * NLAT:(r + 1) * NLAT], ctx[b]], axis=0)
        m = dict(x_in=fm_tokens(xt), modA=mod[(b, 0)])
        m.update(epre_inputs(0, r))
        maps.append(m)
    xcur = [m["x_in"] for m in maps]
    res = _run("A", maps)
    for L in (0, 2):
        j = L // 2
        maps = []
        for i, (b, r) in enumerate(cores):
            m = dict(x_in=xcur[i], modA=mod[(b, L)], modB=mod[(b, L + 1)], w1=mlp_w1[L], w2=mlp_w2[L],
                     wpw1=conv_w_pw1[j], bpw1=fm_vec(conv_b_pw1[j]))
            m.update(epost_inputs(res, j, b, r, i))
            maps.append(m)
        res = _run("B", maps)
        xcur = [np.asarray(res[i]["x_o"]) for i in range(NCORES)]
        last = L == 2
        maps = []
        for i, (b, r) in enumerate(cores):
            m = dict(x_in=xcur[i], u_in=u_with_halo(res, b, r), modA=mod[(b, L + 1)],
                     w1=mlp_w1[L + 1], w2=mlp_w2[L + 1])
            m.update(opost_inputs(j))
            if not last:
                m["modB"] = mod[(b, L + 2)]
                m.update(epre_inputs(j + 1, r))
            maps.append(m)
        res = _run("D" if last else "C", maps)
        if not last:
            xcur = [np.asarray(res[i]["x_o"]) for i in range(NCORES)]
    out = np.zeros((2, 4 * NLAT, D), np.float32)
    for i, (b, r) in enumerate(cores):
        xo = np.asarray(res[i]["x_o"])
        out[b, r * NLAT:(r + 1) * NLAT] = xo.transpose(2, 1, 0).reshape(NLAT, D)
    return out


GROUPS = [[0, 1, 2, 3], [4, 5, 6, 7]]


def mod_segment(K, adaw, adab, cvec, ng, m_loc, m_gat):
    nc, P, PS = K.nc, K.P, K.PS
    CV = K.sb("CV", [128, 8, 2], F32)
    ABs = K.sb("ABs", [128, 48, 2], F32)
    NG = K.sb("NG", [128, 2, 8], F32)
    E1 = K.sb("E1", [128, 8, 2], F32)
    S = K.sb("S", [128, 8, 2], BF16)
    MODL = K.sb("MODL", [128, 48, 2], F32)
    WM = [K.sb("WM%d" % i, [128, 8, 1024], BF16) for i in range(2)]
    P.dma("sp", CV[:], cvec, w=["CV"])
    P.dma("sp", ABs[:], adab, w=["ABs"])
    P.dma("sp", NG[:], ng, w=["NG"])
    K.act(E1[:], CV[:], AF.Exp, ["CV"], ["E1"], scale=-1.0)
    K.ts("dve", E1[:], E1[:], 1.0, None, ALU.add, None, ["E1"], ["E1"])
    K.recip(E1[:], E1[:], ["E1"], ["E1"])
    K.tt("dve", S[:], CV[:], E1[:], ALU.mult, ["CV", "E1"], ["S"])
    adaw_v = adaw.rearrange("(kc p) n -> p kc n", p=128)
    for j in range(6):
        wm = WM[j % 2]
        wk = "WM%d" % (j % 2)
        P.dma("pool", wm[:], adaw_v[:, :, j * 1024:(j + 1) * 1024], w=[wk])
        bank = j % 2
        for c in range(8):
            for kc in range(8):
                K.mm(PS[:, bank, c * 2:c * 2 + 2], wm[:, kc, c * 128:(c + 1) * 128], S[:, kc, :],
                     kc == 0, kc == 7, [wk, "S"], ["ps%d" % bank])
        K.tt("dve", MODL[:, j * 8:(j + 1) * 8, :],
             PS[:, bank, 0:16].rearrange("p (c v) -> p c v", v=2),
             ABs[:, j * 8:(j + 1) * 8, :], ALU.add, ["ps%d" % bank, "ABs"], ["MODL%d" % j])
    for j, gi in ((1, 0), (4, 1)):
        for v in range(2):
            K.stt(MODL[:, j * 8:(j + 1) * 8, v], MODL[:, j * 8:(j + 1) * 8, v], 1.0, NG[:, gi, :],
                  ALU.add, ALU.mult, ["MODL%d" % j, "NG"], ["MODL%d" % j])
    P.dma("sp", m_loc, MODL[:].rearrange("p a v -> p (a v)"), r=["MODL%d" % j for j in range(6)], w=["m_loc"])
    P.cc("AllGather", GROUPS, m_loc, m_gat, r=["m_loc"], w=["m_gat"])
    mg = m_gat.rearrange("(r p) (a v) -> r p a v", p=128, v=2)
    for L in range(4):
        P.dma("sp", K.MOD[L][:], mg[L], r=["m_gat"], w=["MOD%d" % L])


def halo_segment(K, e_loc, e_gat):
    P = K.P
    U = K.U
    E4 = K.sb("E4", [128, 4, 8, 2, 15], BF16)
    HAL = K.sb("HAL", [128, 2, 8, 15], F32)
    ED = K.sb("ED", [128, 8, 2, 15], BF16)
    ukeys = ["U%d" % tc for tc in range(5)]
    K.cp("pool", ED[:, :, 0, :], U[:, :, 15:30], ukeys, ["ED"])
    K.cp("pool", ED[:, :, 1, :], U[:, :, NLAT:NLAT + 15], ukeys, ["ED"])
    P.dma("sp", e_loc, ED[:].rearrange("p c s e -> p (c s e)"), r=["ED"], w=["e_loc"])
    P.cc("AllGather", GROUPS, e_loc, e_gat, r=["e_loc"], w=["e_gat"])
    P.dma("sp", E4[:], e_gat.rearrange("(r p) (c s e) -> p r c s e", p=128, c=8, s=2), r=["e_gat"], w=["E4"])
    for side in range(2):
        src_s = 1 - side
        for rr in range(4):
            mcol = K.HMASK[:, side * 4 + rr:side * 4 + rr + 1]
            if rr == 0:
                K.ts("dve", HAL[:, side], E4[:, rr, :, src_s, :], mcol, None, ALU.mult, None,
                     ["E4", "HMASK"], ["HAL%d" % side])
            else:
                K.stt(HAL[:, side], E4[:, rr, :, src_s, :], mcol, HAL[:, side], ALU.mult, ALU.add,
                      ["E4", "HMASK", "HAL%d" % side], ["HAL%d" % side])
    K.cp("dve", U[:, :, 0:15], HAL[:, 0], ["HAL0"], ["U0"])
    K.cp("dve", U[:, :, NLAT + 15:NLAT + 30], HAL[:, 1], ["HAL1"], ["U3"])


def build_fused():
    nc = bass.Bass("TRN2", target_bir_lowering=False)
    with ExitStack() as es:
        K = Ctx(nc, es)
        P = K.P
        di = K.dram_in
        xin = di("x_in", [128, 8, NT], F32)
        adaw = di("adaw", [D, 6 * D], F32)
        adab = di("adab", [128, 48, 2], F32)
        cvec = di("cvec", [128, 8, 2], F32)
        ng = di("ng", [128, 2, 8], F32)
        hmask = di("hmask", [128, 8], F32)
        rope = di("rope", [128, 2, NLAT], F32)
        blk = di("blk", [128, 128], BF16)
        perm = di("perm", [128, 128], F32)
        cs64 = di("cs64", [128, 256], BF16)
        ident = di("ident", [128, 128], BF16)
        dft = di("dft", [64, 128, 2, NLAT], BF16)
        dftc = di("dftc", [2, 128, 2, 256], BF16)
        w1 = [di("w1_%d" % L, [D, 4 * D], F32) for L in range(4)]
        w2 = [di("w2_%d" % L, [4 * D, D], F32) for L in range(4)]
        win = [di("win_%d" % j, [D, 1536], F32) for j in range(2)]
        qkg = [di("qkg_%d" % j, [128, 8], F32) for j in range(2)]
        wout = [di("wout_%d" % j, [D, D], F32) for j in range(2)]
        wpw1 = [di("wpw1_%d" % j, [D, 2 * D], F32) for j in range(2)]
        bpw1 = [di("bpw1_%d" % j, [128, 16], F32) for j in range(2)]
        wdw = [di("wdw_%d" % j, [128, 8, 31], F32) for j in range(2)]
        pv4 = [di("pv4_%d" % j, [128, 4, 8], F32) for j in range(2)]
        wpw2 = [di("wpw2_%d" % j, [D, D], F32) for j in range(2)]
        xo = K.dram_out("x_o", [128, 8, NLAT], F32)

        def internal(name, shape, dt):
            return nc.dram_tensor(name, list(shape), dt, kind="Internal").ap()

        m_loc = internal("m_loc", [128, 96], F32)
        m_gat = internal("m_gat", [512, 96], F32)
        alloc_common(K)
        alloc_eps(K)
        K.HMASK = K.sb("HMASK", [128, 8], F32)
        P.dma("sp", K.HMASK[:], hmask, w=["HMASK"])
        for L in range(4):
            K.MOD[L] = K.sb("MODL%d" % L, [128, 48, 2], F32)
        load_x(K, xin)
        with ExitStack() as ph:
            K.es = ph
            mod_segment(K, adaw, adab, cvec, ng, m_loc, m_gat)
            P.barrier()
        K.es = es
        K.BLK = K.sb("BLK", [128, 128], BF16)
        K.PERM = K.sb("PERM", [128, 128], F32)
        K.CS64 = K.sb("CS64", [128, 256], BF16)
        P.dma("sp", K.BLK[:], blk, w=["BLK"])
        P.dma("sp", K.PERM[:], perm, w=["PERM"])
        P.dma("sp", K.CS64[:], cs64, w=["CS64"])
        K.IDENT = K.sb("IDENT", [128, 128], BF16)
        P.dma("sp", K.IDENT[:], ident, w=["IDENT"])
        for L in range(4):
            j = L // 2
            ntc = 4 if L == 3 else 5
            if L % 2 == 0:
                kt_loc = [internal("kt_loc%d_%d" % (L, k_), [128, NT], BF16) for k_ in range(2)]
                kt_gat = [internal("kt_gat%d_%d" % (L, k_), [512, NT], BF16) for k_ in range(2)]
                v_loc = [internal("v_loc%d_%d" % (L, k_), [128, 18 * 130], BF16) for k_ in range(2)]
                v_gat = [internal("v_gat%d_%d" % (L, k_), [512, 18 * 130], BF16) for k_ in range(2)]
                ab_loc = [internal("ab_loc%d_%d" % (L, k_), [128, 6 * 512], BF16) for k_ in range(3)]
                ab_gat = [internal("ab_gat%d_%d" % (L, k_), [512, 6 * 512], BF16) for k_ in range(3)]
                with ExitStack() as ql:
                    K.es = ql
                    K.QT = K.sb("QT", [128, 6, NT], BF16)
                    with ExitStack() as ph:
                        K.es = ph
                        alloc_norm(K)
                        alloc_epre_noqt(K)
                        abk = [[], [], []]
                        for tc in range(5):
                            t0_, W_ = TCS[tc]
                            for t_ in range(0, W_ // 128, 2):
                                abk[(t0_ // 128 + t_) // 6].append("ab_o%d.%d" % (tc, t_))

                        def after_tc(tc, ab_loc=ab_loc, ab_gat=ab_gat, abk=abk):
                            k_ = {1: 0, 2: 1, 4: 2}.get(tc)
                            if k_ is not None:
                                P.cc("AllGather", GROUPS, ab_loc[k_], ab_gat[k_], r=abk[k_], w=["ab_gat%d" % k_])

                        epre_segment(K, L, win[j], rope, qkg[j], blk, perm, cs64,
                                     kt_loc,
                                     [v.rearrange("p (t s e) -> p t s e", t=18, s=2) for v in v_loc],
                                     [a.rearrange("p (t n) -> p t n", t=6) for a in ab_loc],
                                     final=False, load_consts=False, after_tc=after_tc)
                        for k_ in range(2):
                            P.cc("AllGather", GROUPS, kt_loc[k_], kt_gat[k_], r=["kt_o%d" % k_], w=["kt_gat%d" % k_])
                            P.cc("AllGather", GROUPS, v_loc[k_], v_gat[k_],
                                 r=["v_o%d.%d" % (tc, k_) for tc in range(5)], w=["v_gat%d" % k_])
                        P.barrier(keep=["kt_gat0", "kt_gat1", "v_gat0", "v_gat1"])
                    with ExitStack() as ph:
                        K.es = ph
                        alloc_epost(K, with_qt=False)
                        epost_segment(K, L,
                                      [k_.rearrange("(r p) t -> r p t", p=128) for k_ in kt_gat],
                                      [v.rearrange("(r p) (t s e) -> r p t s e", p=128, t=18, s=2) for v in v_gat],
                                      [a.rearrange("(r p) (t n) -> r p t n", p=128, t=6) for a in ab_gat],
                                      dft, dftc, wout[j])
                        P.barrier()
            else:
                e_loc = internal("e_loc%d" % L, [128, 240], BF16)
                e_gat = internal("e_gat%d" % L, [512, 240], BF16)
                with ExitStack() as ul:
                    K.es = ul
                    K.U = K.sb("U", [128, 8, UW], BF16)
                    with ExitStack() as ph:
                        K.es = ph
                        alloc_norm(K)
                        alloc_opre(K, with_u=False)
                        opre_segment(K, L, wpw1[j], bpw1[j], ntc)
                        halo_segment(K, e_loc, e_gat)
                        P.barrier()
                    with ExitStack() as ph:
                        K.es = ph
                        alloc_opost(K, with_u=False)
                        opost_segment(K, L, wdw[j], pv4[j], wpw2[j], ntc, ukeys=True)
                        P.barrier()
            with ExitStack() as ph:
                K.es = ph
                alloc_norm(K)
                alloc_mlp(K)
                def store_tc(tc):
                    t0_, W_ = TCS[tc]
                    P.dma("sp", xo[:, :, t0_:t0_ + W_], K.X[:, :, t0_:t0_ + W_], r=xkeys(tc), w=["xo%d" % tc], final=True)

                mlp_segment(K, L, w1[L], w2[L], ntc, after_last=store_tc if L == 3 else None)
                P.barrier()
        K.es = es
        P.emit()
    return nc


def kernel(x, c, ctx, c_ctx, ada_w, ada_b, norm1_g, norm2_g, mlp_w1, mlp_w2,
           attn_w_in, q_norm_g, k_norm_g, attn_w_out,
           conv_w_pw1, conv_b_pw1, conv_w_dw, conv_b_dw, conv_ln_g, conv_ln_b,
           conv_w_pw2, conv_b_pw2):
    f32 = lambda a: np.ascontiguousarray(np.asarray(a, dtype=np.float32))
    x, c, ctx, c_ctx = f32(x), f32(c), f32(ctx), f32(c_ctx)
    ada_w, ada_b, norm1_g, norm2_g = f32(ada_w), f32(ada_b), f32(norm1_g), f32(norm2_g)
    mlp_w1, mlp_w2, attn_w_in, attn_w_out = f32(mlp_w1), f32(mlp_w2), f32(attn_w_in), f32(attn_w_out)
    conv_w_pw1, conv_w_pw2, conv_w_dw = f32(conv_w_pw1), f32(conv_w_pw2), f32(conv_w_dw)
    C = consts()
    shared = {}
    for L in range(4):
        shared["w1_%d" % L] = mlp_w1[L]
        shared["w2_%d" % L] = mlp_w2[L]
    for j in range(2):
        g = np.zeros((128, 8), np.float32)
        for fc in range(8):
            src = np.asarray(q_norm_g[j] if fc < 6 else k_norm_g[j], np.float32)
            g[:64, fc] = src
            g[64:, fc] = src
        shared["win_%d" % j] = perm_win(attn_w_in[j])
        shared["qkg_%d" % j] = g
        shared["wout_%d" % j] = attn_w_out[j]
        shared["wpw1_%d" % j] = conv_w_pw1[j]
        shared["bpw1_%d" % j] = fm_vec(conv_b_pw1[j])
        shared["wdw_%d" % j] = np.ascontiguousarray(conv_w_dw[j].reshape(31, 8, 128).transpose(2, 1, 0))
        shared["pv4_%d" % j] = np.ascontiguousarray(np.stack(
            [fm_vec(conv_b_dw[j]), fm_vec(conv_ln_g[j]), fm_vec(conv_ln_b[j]), fm_vec(conv_b_pw2[j])], axis=1))
        shared["wpw2_%d" % j] = conv_w_pw2[j]
    shared.update(blk=C["blk"], perm=C["perm"], cs64=C["cs64"], dftc=C["dftc"],
                  ident=_bf(np.eye(128, dtype=np.float32)))
    maps = []
    for i in range(NCORES):
        b, r = i // 4, i % 4
        xt = np.concatenate([x[b, r * NLAT:(r + 1) * NLAT], ctx[b]], axis=0)
        adab = fm_vec(ada_b[r])
        hm = np.zeros((128, 8), np.float32)
        if r > 0:
            hm[:, r - 1] = 1.0
        if r < 3:
            hm[:, 4 + r + 1] = 1.0
        m = dict(shared)
        m.update(x_in=fm_tokens(xt), adaw=ada_w[r],
                 adab=np.ascontiguousarray(np.repeat(adab[:, :, None], 2, axis=2)),
                 cvec=np.ascontiguousarray(np.stack([fm_vec(c[b]), fm_vec(c_ctx)], axis=-1)),
                 ng=np.ascontiguousarray(np.stack([fm_vec(norm1_g[r]), fm_vec(norm2_g[r])], axis=1)),
                 hmask=hm, rope=C["rope"][r], dft=C["dft"][r])
        maps.append(m)
    if "F" not in _PROGS:
        _PROGS["F"] = build_fused()
    res = run_bass_kernel_spmd(_PROGS["F"], maps, core_ids=list(range(NCORES))).results
    out = np.zeros((2, 4 * NLAT, D), np.float32)
    for i in range(NCORES):
        b, r = i // 4, i % 4
        xo = np.asarray(res[i]["x_o"])
        out[b, r * NLAT:(r + 1) * NLAT] = xo.transpose(2, 1, 0).reshape(NLAT, D)
    return out
```

```python
import math
from contextlib import ExitStack

import numpy as np
import ml_dtypes
import concourse.bass as bass
import concourse.mybir as mybir
from concourse.bass_utils import run_bass_kernel_spmd

F32 = mybir.dt.float32
BF16 = mybir.dt.bfloat16
AF = mybir.ActivationFunctionType
ALU = mybir.AluOpType
NPBF = ml_dtypes.bfloat16

D = 1024
NLAT = 2048
NCTX = 256
NT = NLAT + NCTX
TCS = [(0, 512), (512, 512), (1024, 512), (1536, 512), (2048, 256)]
EPS = 1e-6
NCORES = 8


class _Op:
    __slots__ = ("eng", "fn", "deps", "dma", "sem", "semval", "signal", "count", "idx", "final", "inc")


class Prog:
    ENGS = ("pe", "act", "dve", "pool", "sp")

    def __init__(self, nc):
        self.nc = nc
        self.ops = []
        self.state = {}
        self.dma_sem_of = {}
        self.dma_sem_cnt = []
        self.finals = []

    def _add(self, eng, fn, r, w, dma=False, final=False, inc=16):
        op = _Op()
        op.inc = inc
        op.eng, op.fn, op.dma, op.final = eng, fn, dma, final
        op.signal = False
        op.count = None
        op.idx = len(self.ops)
        deps = {}
        for k in r:
            st = self.state.setdefault(k, [None, []])
            if st[0] is not None:
                deps[st[0]] = "raw"
        for k in w:
            st = self.state.setdefault(k, [None, []])
            if st[0] is not None:
                deps[st[0]] = "waw"
            for ri in st[1]:
                if ri not in deps:
                    deps[ri] = "war"
        for k in r:
            rl = self.state[k][1]
            if not dma:
                rl[:] = [ri for ri in rl if self.ops[ri].dma or self.ops[ri].eng != eng]
            rl.append(op.idx)
        for k in w:
            self.state[k] = [op.idx, []]
        op.deps = []
        latest = {}
        for di, kind in deps.items():
            dop = self.ops[di]
            if dop.dma:
                op.deps.append(di)
            elif dop.eng == eng:
                if eng == "pe":
                    continue
                latest[dop.eng] = max(latest.get(dop.eng, -1), di)
            else:
                latest[dop.eng] = max(latest.get(dop.eng, -1), di)
        for di in latest.values():
            self.ops[di].signal = True
            op.deps.append(di)
        if dma:
            key = w[0]
            if key not in self.dma_sem_of:
                self.dma_sem_of[key] = len(self.dma_sem_cnt)
                self.dma_sem_cnt.append(0)
            si = self.dma_sem_of[key]
            self.dma_sem_cnt[si] += inc
            op.sem = si
            op.semval = self.dma_sem_cnt[si]
            if final:
                self.finals.append(op.idx)
        self.ops.append(op)
        return op

    def barrier(self, keep=()):
        last = {}
        lastdma = {}
        kept = {}
        for k in keep:
            st = self.state.get(k)
            if st is not None and st[0] is not None and self.ops[st[0]].dma:
                kept[k] = st[0]
        skip_sems = set(self.ops[i].sem for i in kept.values())
        seen = getattr(self, "_bar_seen", {})
        for op in self.ops:
            if op.dma:
                if op.sem not in skip_sems and op.semval > seen.get(op.sem, 0):
                    lastdma[op.sem] = op.idx
            elif op.fn is not None:
                last[op.eng] = op.idx
        for si, li in lastdma.items():
            seen[si] = self.ops[li].semval
        self._bar_seen = seen
        for e in self.ENGS:
            op = _Op()
            op.eng, op.fn, op.dma, op.final = e, None, False, False
            op.signal = False
            op.count = None
            op.idx = len(self.ops)
            op.deps = []
            for e2, li in last.items():
                if e2 != e:
                    self.ops[li].signal = True
                    op.deps.append(li)
            for si, li in lastdma.items():
                op.deps.append(li)
            self.ops.append(op)
        self.state = {k: [i, []] for k, i in kept.items()}

    def op(self, eng, fn, r=(), w=()):
        return self._add(eng, fn, tuple(r), tuple(w))

    def dma(self, q, out, in_, r=(), w=(), final=False):
        assert q in ("sp", "pool")
        return self._add(q, lambda e: e.dma_start(out=out, in_=in_), tuple(r), tuple(w),
                         dma=True, final=final)

    def cc(self, kind, groups, in_ap, out_ap, r=(), w=()):
        return self._add("pool", lambda e: e.collective_compute(kind, ALU.bypass, replica_groups=groups,
                                                                ins=[in_ap], outs=[out_ap]),
                         tuple(r), tuple(w), dma=True, inc=1)

    def emit(self):
        nc = self.nc
        cnt = {e: 0 for e in self.ENGS}
        for op in self.ops:
            if op.dma or op.fn is None:
                continue
            if op.signal:
                cnt[op.eng] += 1
                op.count = cnt[op.eng]
        for e in self.ENGS:
            assert cnt[e] < 60000, (e, cnt[e])
        for v in self.dma_sem_cnt:
            assert v < 60000, v
        with ExitStack() as es:
            esem = {e: es.enter_context(nc.semaphore("s_" + e)) for e in ("pe", "act", "dve", "pool")}
            dsem = [es.enter_context(nc.semaphore("d%d" % i)) for i in range(len(self.dma_sem_cnt))]
            block = es.enter_context(nc.Block())
            ops = self.ops
            finals = self.finals

            def run(ename, e):
                waited = {}
                for op in ops:
                    if op.eng != ename:
                        continue
                    for di in op.deps:
                        dop = ops[di]
                        if dop.dma:
                            sem, val, key = dsem[dop.sem], dop.semval, ("d", dop.sem)
                        else:
                            sem, val, key = esem[dop.eng], dop.count, ("e", dop.eng)
                        if waited.get(key, 0) >= val:
                            continue
                        waited[key] = val
                        e.wait_ge(sem, val)
                    if op.fn is None:
                        continue
                    ins = op.fn(e)
                    if op.dma:
                        ins.then_inc(dsem[op.sem], op.inc)
                    elif op.signal:
                        ins.then_inc(esem[ename], 1)
                if ename == "sp":
                    for fi in finals:
                        fop = ops[fi]
                        e.wait_ge(dsem[fop.sem], fop.semval)

            @block.tensor
            def _(e):
                run("pe", e)

            @block.scalar
            def _(e):
                run("act", e)

            @block.vector
            def _(e):
                run("dve", e)

            @block.gpsimd
            def _(e):
                run("pool", e)

            @block.sync
            def _(e):
                run("sp", e)


class Ctx:
    def __init__(self, nc, es):
        self.nc = nc
        self.es = es
        self.P = Prog(nc)
        self.din = {}
        self.dout = {}

    def dram_in(self, name, shape, dt):
        t = self.nc.dram_tensor(name, list(shape), dt, kind="ExternalInput").ap()
        self.din[name] = t
        return t

    def dram_out(self, name, shape, dt):
        t = self.nc.dram_tensor(name, list(shape), dt, kind="ExternalOutput").ap()
        self.dout[name] = t
        return t

    def sb(self, name, shape, dt):
        self.nsb = getattr(self, "nsb", 0) + 1
        return self.es.enter_context(self.nc.sbuf_tensor("%s_%d" % (name, self.nsb), list(shape), dt))

    def mm(self, out, lhsT, rhs, start, stop, r, w):
        self.P.op("pe", lambda e: e.matmul(out, lhsT, rhs, start=start, stop=stop), r, w)

    def act(self, out, in_, func, r, w, bias=None, scale=None):
        kw = {}
        if bias is not None:
            kw["bias"] = bias
        if scale is not None:
            kw["scale"] = scale
        self.P.op("act", lambda e: e.activation(out, in_, func, **kw), r, w)

    def tt(self, eng, out, in0, in1, op, r, w):
        self.P.op(eng, lambda e: e.tensor_tensor(out, in0, in1, op), r, w)

    def ts(self, eng, out, in0, s1, s2, op0, op1, r, w):
        if op1 is None:
            self.P.op(eng, lambda e: e.tensor_scalar(out, in0, s1, None, op0), r, w)
        else:
            self.P.op(eng, lambda e: e.tensor_scalar(out, in0, s1, s2, op0, op1), r, w)

    def stt(self, out, in0, scalar, in1, op0, op1, r, w):
        self.P.op("dve", lambda e: e.scalar_tensor_tensor(out, in0, scalar, in1, op0, op1), r, w)

    def cp(self, eng, out, in_, r, w):
        self.P.op(eng, lambda e: e.tensor_copy(out, in_), r, w)

    def recip(self, out, in_, r, w):
        self.P.op("dve", lambda e: e.reciprocal(out, in_), r, w)

    def memset(self, eng, ap, val, w):
        self.P.op(eng, lambda e: e.memset(ap, val), (), w)


def _bf(a):
    return np.ascontiguousarray(a).astype(NPBF)


def build_mod():
    nc = bass.Bass("TRN2", target_bir_lowering=False)
    with ExitStack() as es:
        K = Ctx(nc, es)
        P = K.P
        adaw = K.dram_in("adaw", [D, 6 * D], F32)
        adab = K.dram_in("adab", [128, 48, 2], F32)
        cvec = K.dram_in("cvec", [128, 8, 2], F32)
        ng = K.dram_in("ng", [128, 2, 8], F32)
        modo = K.dram_out("modo", [128, 48, 2], F32)
        CV = K.sb("CV", [128, 8, 2], F32)
        ABs = K.sb("ABs", [128, 48, 2], F32)
        NG = K.sb("NG", [128, 2, 8], F32)
        E1 = K.sb("E1", [128, 8, 2], F32)
        S = K.sb("S", [128, 8, 2], BF16)
        MOD = K.sb("MOD", [128, 48, 2], F32)
        WM = [K.sb("WM%d" % i, [128, 8, 1024], BF16) for i in range(2)]
        PS = es.enter_context(nc.psum_tensor("PS", [128, 8, 512], F32))
        P.dma("sp", CV[:], cvec, w=["CV"])
        P.dma("sp", ABs[:], adab, w=["ABs"])
        P.dma("sp", NG[:], ng, w=["NG"])
        K.act(E1[:], CV[:], AF.Exp, ["CV"], ["E1"], scale=-1.0)
        K.ts("dve", E1[:], E1[:], 1.0, None, ALU.add, None, ["E1"], ["E1"])
        K.recip(E1[:], E1[:], ["E1"], ["E1"])
        K.tt("dve", S[:], CV[:], E1[:], ALU.mult, ["CV", "E1"], ["S"])
        adaw_v = adaw.rearrange("(kc p) n -> p kc n", p=128)
        for j in range(6):
            wm = WM[j % 2]
            wk = "WM%d" % (j % 2)
            P.dma("pool", wm[:], adaw_v[:, :, j * 1024:(j + 1) * 1024], w=[wk])
            bank = j % 2
            for c in range(8):
                for kc in range(8):
                    K.mm(PS[:, bank, c * 2:c * 2 + 2], wm[:, kc, c * 128:(c + 1) * 128], S[:, kc, :],
                         kc == 0, kc == 7, [wk, "S"], ["ps%d" % bank])
            K.tt("dve", MOD[:, j * 8:(j + 1) * 8, :],
                 PS[:, bank, 0:16].rearrange("p (c v) -> p c v", v=2),
                 ABs[:, j * 8:(j + 1) * 8, :], ALU.add, ["ps%d" % bank, "ABs"], ["MOD%d" % j])
        for j, gi in ((1, 0), (4, 1)):
            for v in range(2):
                K.stt(MOD[:, j * 8:(j + 1) * 8, v], MOD[:, j * 8:(j + 1) * 8, v], 1.0, NG[:, gi, :],
                      ALU.add, ALU.mult, ["MOD%d" % j, "NG"], ["MOD%d" % j])
        P.dma("sp", modo, MOD[:], r=["MOD%d" % j for j in range(6)], w=["modo"], final=True)
        P.emit()
    return nc


def alloc_common(K):
    nc, es = K.nc, K.es
    K.X = K.sb("X", [128, 8, NT], F32)
    K.PS = es.enter_context(nc.psum_tensor("PS", [128, 8, 512], F32))
    K.ONES = K.sb("ONES", [128, 128], BF16)
    K.MOD = {}
    K.memset("pool", K.ONES[:], 1.0, ["ONES"])


def alloc_norm(K):
    K.SQ = K.sb("SQ", [128, 8, 512], BF16)
    K.LNV = K.sb("LNV", [128, 512], F32)
    K.RSTD = K.sb("RSTD", [128, 512], F32)
    K.TMP = [K.sb("TMP%d" % i, [128, 512], F32) for i in range(2)]


def load_mod(K, L, name=None):
    t = K.dram_in(name or ("mod%d" % L), [128, 48, 2], F32)
    K.MOD[L] = K.sb("MODL%d" % L, [128, 48, 2], F32)
    K.P.dma("sp", K.MOD[L][:], t, w=["MOD%d" % L])


def xkeys(tc):
    return ["X%d.%d" % (tc, c) for c in range(8)]


def norm_sq(K, tc):
    t0, W = TCS[tc]
    X = K.X
    for c in range(8):
        eng = "dve" if c % 2 == 0 else "pool"
        K.tt(eng, K.SQ[:, c, :W], X[:, c, t0:t0 + W], X[:, c, t0:t0 + W], ALU.mult,
             ["X%d.%d" % (tc, c)], ["SQ%d" % c])


def norm_rest(K, L, which, tc, out_fn, out_keys):
    t0, W = TCS[tc]
    v = 1 if tc == 4 else 0
    MOD = K.MOD[L]
    ja, jb = (1, 0) if which == 0 else (4, 3)
    mk = "MOD%d" % L
    X, PS = K.X, K.PS
    for c in range(8):
        K.mm(PS[:, 7, :W], K.ONES[:], K.SQ[:, c, :W], c == 0, c == 7, ["ONES", "SQ%d" % c], ["ps7"])
    K.act(K.LNV[:, :W], PS[:, 7, :W], AF.Ln, ["ps7"], ["LNV"], bias=K.EPSB[:], scale=1.0 / D)
    K.act(K.RSTD[:, :W], K.LNV[:, :W], AF.Exp, ["LNV"], ["RSTD"], scale=-0.5)
    for c in range(8):
        tb = c % 2
        K.stt(K.TMP[tb][:, :W], X[:, c, t0:t0 + W], MOD[:, ja * 8 + c, v:v + 1], K.RSTD[:, :W],
              ALU.mult, ALU.mult, ["X%d.%d" % (tc, c), mk, "RSTD"], ["TMP%d" % tb])
        K.act(out_fn(c), K.TMP[tb][:, :W], AF.Identity, ["TMP%d" % tb, mk], out_keys(c),
              bias=MOD[:, jb * 8 + c, v:v + 1], scale=1.0)


def norm_mod(K, L, which, tc, out_fn, out_keys):
    norm_sq(K, tc)
    norm_rest(K, L, which, tc, out_fn, out_keys)


def alloc_eps(K):
    K.EPSB = K.sb("EPSB", [128, 1], F32)
    K.memset("pool", K.EPSB[:], EPS, ["EPSB"])
    K.ONEB = K.sb("ONEB", [128, 1], F32)
    K.memset("pool", K.ONEB[:], 1.0, ["ONEB"])


def mlp_segment(K, L, w1, w2, ntc=5, after_last=None):
    X, PS = K.X, K.PS
    H = K.H
    MOD = K.MOD[L]
    mk = "MOD%d" % L
    def norm2(tc):
        t0, W = TCS[tc]
        norm_mod(K, L, 1, tc, lambda c, t0=t0, W=W: H[:, c, t0:t0 + W], lambda c, tc=tc: ["H%d" % tc])

    norm2(0)
    w1v = w1.rearrange("(kc p) n -> p kc n", p=128)
    w2v = w2.rearrange("(fc p) n -> p fc n", p=128)
    items = [(e, tc) for e in range(8) for tc in range(ntc)]

    def load(e):
        K.P.dma("pool", K.W1E[e % 2][:], w1v[:, :, e * 512:(e + 1) * 512], w=["W1E%d" % (e % 2)])
        K.P.dma("pool", K.W2E[e % 2][:], w2v[:, e * 4:(e + 1) * 4, :], w=["W2E%d" % (e % 2)])

    def part1(i):
        e, tc = items[i]
        t0, W = TCS[tc]
        ab = i % 2
        for f in range(4):
            bank = f % 2
            for kc in range(8):
                K.mm(PS[:, bank, :W], K.W1E[e % 2][:, kc, f * 128:(f + 1) * 128], H[:, kc, t0:t0 + W],
                     kc == 0, kc == 7, ["W1E%d" % (e % 2), "H%d" % tc], ["ps%d" % bank])
            K.act(K.RL[f % 2][:, :W], PS[:, bank, :W], AF.Relu, ["ps%d" % bank], ["RL%d" % (f % 2)])
            eng = "pool" if f % 2 == 0 else "dve"
            K.tt(eng, K.AH[ab][:, f, :W], K.RL[f % 2][:, :W], K.RL[f % 2][:, :W], ALU.mult,
                 ["RL%d" % (f % 2)], ["AH%d.%d" % (ab, f)])

    def part2(i):
        e, tc = items[i]
        t0, W = TCS[tc]
        v = 1 if tc == 4 else 0
        ab = i % 2
        for d in range(8):
            bank = 2 + d % 2
            for f in range(4):
                K.mm(PS[:, bank, :W], K.W2E[e % 2][:, f, d * 128:(d + 1) * 128], K.AH[ab][:, f, :W],
                     f == 0, f == 3, ["W2E%d" % (e % 2), "AH%d.%d" % (ab, f)], ["ps%d" % bank])
            K.stt(X[:, d, t0:t0 + W], PS[:, bank, :W], MOD[:, 40 + d, v:v + 1], X[:, d, t0:t0 + W],
                  ALU.mult, ALU.add, ["ps%d" % bank, mk, "X%d.%d" % (tc, d)], ["X%d.%d" % (tc, d)])

    load(0)
    part1(0)
    for i in range(len(items)):
        e0, tc0 = items[i]
        if tc0 == 1 and e0 + 1 < 8:
            load(e0 + 1)
        if i + 1 < len(items):
            if items[i + 1][0] == 0:
                norm2(items[i + 1][1])
            part1(i + 1)
        part2(i)
        if e0 == 7 and after_last is not None:
            after_last(tc0)


def alloc_mlp(K):
    K.H = K.sb("H", [128, 8, NT], BF16)
    K.W1E = [K.sb("W1E%d" % i, [128, 8, 512], BF16) for i in range(2)]
    K.W2E = [K.sb("W2E%d" % i, [128, 4, 1024], BF16) for i in range(2)]
    K.RL = [K.sb("RL%d" % i, [128, 512], F32) for i in range(2)]
    K.AH = [K.sb("AH%d" % i, [128, 4, 512], BF16) for i in range(2)]


def alloc_epre_noqt(K):
    alloc_epre(K, with_qt=False)


def alloc_epre(K, with_qt=True):
    K.WIN = K.sb("WIN", [128, 8, 1536], BF16)
    K.HC = [K.sb("HC%d" % i, [128, 8, 512], BF16) for i in range(2)]
    if with_qt:
        K.QT = K.sb("QT", [128, 6, NT], BF16)
    K.KTL = K.sb("KTL", [128, 2, NT], BF16)
    K.VTC = [K.sb("VTC%d" % i, [128, 4, 2, 2, 65], BF16) for i in range(2)]
    K.ABC = [K.sb("ABC%d" % i, [128, 4, 512], BF16) for i in range(2)]
    K.FTC = K.sb("FTC", [128, 2, 512], BF16)
    K.ROPE = [K.sb("ROPE0", [128, 2, 512], F32)] * 2
    K.PSB = [K.sb("PSB%d" % i, [128, 512], F32) for i in range(2)]
    K.SQ1 = [K.sb("SQ1%d" % i, [128, 512], BF16) for i in range(2)]
    K.QN = [K.sb("QN%d" % i, [128, 512], F32) for i in range(2)]
    K.T1 = [K.sb("T10", [128, 512], F32)] * 2
    K.T2 = [K.sb("T20", [128, 512], F32)] * 2
    K.LN2 = [K.sb("LN20", [128, 512], F32)] * 2
    K.RS2 = [K.sb("RS20", [128, 512], F32)] * 2
    if not hasattr(K, "BLK"):
        K.BLK = K.sb("BLK", [128, 128], BF16)
        K.PERM = K.sb("PERM", [128, 128], F32)
        K.CS64 = K.sb("CS64", [128, 256], BF16)
    K.QKG = K.sb("QKG", [128, 8], F32)


def epre_segment(K, L, win, rope, qkg, blk, perm, cs64, kt_o, v_o, ab_o, final=True, load_consts=True, after_tc=None):
    X, PS, P = K.X, K.PS, K.P
    P.dma("pool", K.WIN[:], win.rearrange("(kc p) n -> p kc n", p=128), w=["WIN"])
    if load_consts:
        P.dma("sp", K.BLK[:], blk, w=["BLK"])
        P.dma("sp", K.PERM[:], perm, w=["PERM"])
        P.dma("sp", K.CS64[:], cs64, w=["CS64"])
    P.dma("sp", K.QKG[:], qkg, w=["QKG"])
    K.memset("pool", K.VTC[0][:], 1.0, ["VTC0"])
    K.memset("pool", K.VTC[1][:], 1.0, ["VTC1"])
    def nrm(tc_, part):
        W_ = TCS[tc_][1]
        hcb = K.HC[tc_ % 2]
        hkb = "HC%d" % (tc_ % 2)
        if part == 0:
            norm_sq(K, tc_)
        else:
            norm_rest(K, L, 0, tc_, lambda c, hcb=hcb, W_=W_: hcb[:, c, :W_], lambda c, hkb=hkb: [hkb])

    nrm(0, 0)
    nrm(0, 1)
    for tc in range(5):
        t0, W = TCS[tc]
        hb = tc % 2
        hc = K.HC[hb]
        hk = "HC%d" % hb
        if tc < 4:
            P.dma("sp", K.ROPE[0][:], rope[:, :, t0:t0 + W], w=["ROPE0"])
        def stA(fc):
            pb = fc % 2
            b0 = fc % 2
            for kc in range(8):
                K.mm(PS[:, b0, :W], K.WIN[:, kc, fc * 128:(fc + 1) * 128], hc[:, kc, :W],
                     kc == 0, kc == 7, ["WIN", hk], ["ps%d" % b0])
            K.act(K.PSB[pb][:, :W], PS[:, b0, :W], AF.Identity, ["ps%d" % b0], ["PSB%d" % pb])
            K.tt("pool", K.SQ1[pb][:, :W], K.PSB[pb][:, :W], K.PSB[pb][:, :W], ALU.mult,
                 ["PSB%d" % pb], ["SQ1%d" % pb])

        def stB(fc):
            pb = fc % 2
            b1 = 2 + fc % 2
            K.mm(PS[:, b1, :W], K.BLK[:], K.SQ1[pb][:, :W], True, True, ["BLK", "SQ1%d" % pb], ["ps%d" % b1])
            K.act(K.LN2[pb][:, :W], PS[:, b1, :W], AF.Ln, ["ps%d" % b1], ["LN20"],
                  bias=K.EPSB[:], scale=1.0 / 64)
            K.act(K.RS2[pb][:, :W], K.LN2[pb][:, :W], AF.Exp, ["LN20"], ["RS20"], scale=-0.5)
            K.stt(K.QN[pb][:, :W], K.PSB[pb][:, :W], K.QKG[:, fc:fc + 1], K.RS2[pb][:, :W],
                  ALU.mult, ALU.mult, ["PSB%d" % pb, "QKG", "RS20"], ["QN%d" % pb])

        def stC(fc):
            pb = fc % 2
            b2 = 4 + fc % 2
            if fc < 6:
                dest, dk = K.QT[:, fc, t0:t0 + W], "QT%d.%d" % (tc, fc)
            else:
                dest, dk = K.KTL[:, fc - 6, t0:t0 + W], "KTL%d" % (fc - 6)
            if tc < 4:
                rp = K.ROPE[0]
                rk = "ROPE0"
                K.mm(PS[:, b2, :W], K.PERM[:], K.QN[pb][:, :W], True, True, ["PERM", "QN%d" % pb], ["ps%d" % b2])
                K.tt("dve", K.T1[pb][:, :W], K.QN[pb][:, :W], rp[:, 0, :W], ALU.mult,
                     ["QN%d" % pb, rk], ["T10"])
                K.tt("dve", K.T2[pb][:, :W], PS[:, b2, :W], rp[:, 1, :W], ALU.mult,
                     ["ps%d" % b2, rk], ["T20"])
                K.tt("pool", dest, K.T1[pb][:, :W], K.T2[pb][:, :W], ALU.add,
                     ["T10", "T20"], [dk])
            else:
                K.cp("pool", dest, K.QN[pb][:, :W], ["QN%d" % pb], [dk])

        stA(0)
        if tc + 1 < 5:
            nrm(tc + 1, 0)
        for fc in range(8):
            if fc + 1 < 8:
                stA(fc + 1)
            stB(fc)
            if fc >= 1:
                stC(fc - 1)
            if fc == 1 and tc + 1 < 5:
                nrm(tc + 1, 1)
        stC(7)
        rb = [6, 4, 5]
        ri = [0]

        def nb():
            b = rb[ri[0] % 3]
            ri[0] += 1
            return b

        for tt_ in range(W // 128):
            gt = t0 // 128 + tt_
            b = nb()
            for kc in range(8):
                K.mm(PS[:, b, 0:256], hc[:, kc, tt_ * 128:(tt_ + 1) * 128], K.WIN[:, kc, 1024:1280],
                     kc == 0, kc == 7, ["WIN", hk], ["ps%d" % b])
            K.act(K.VTC[tc % 2][:, tt_, :, :, 0:64], PS[:, b, 0:256].rearrange("p (a s e) -> p a s e", a=2, s=2),
                  AF.Identity, ["ps%d" % b], ["VTC%d" % (tc % 2)])
        for half in range(2):
            b = nb()
            for kc in range(8):
                K.mm(PS[:, b, :W], K.WIN[:, kc, 1280 + half * 128:1280 + (half + 1) * 128], hc[:, kc, :W],
                     kc == 0, kc == 7, ["WIN", hk], ["ps%d" % b])
            K.act(K.FTC[:, half, :W], PS[:, b, :W], AF.Identity, ["ps%d" % b], ["FTC%d" % half])
        for tt_ in range(W // 128):
            gt = t0 // 128 + tt_
            for half in range(2):
                b = nb()
                K.mm(PS[:, b, 0:256], K.FTC[:, half, tt_ * 128:(tt_ + 1) * 128], K.CS64[:],
                     True, True, ["FTC%d" % half, "CS64"], ["ps%d" % b])
                K.cp("dve", K.ABC[tc % 2][:, tt_, half * 256:(half + 1) * 256], PS[:, b, 0:256],
                     ["ps%d" % b], ["ABC%d" % (tc % 2)])
        nt_ = W // 128
        g0 = t0 // 128
        for a_ in range(2):
            P.dma("sp", v_o[a_][:, g0:g0 + nt_], K.VTC[tc % 2][:, :nt_, a_], r=["VTC%d" % (tc % 2)],
                  w=["v_o%d.%d" % (tc, a_)], final=final)
        for t_ in range(0, nt_, 2):
            ch, off = (g0 + t_) // 6, (g0 + t_) % 6
            P.dma("sp", ab_o[ch][:, off:off + 2, :], K.ABC[tc % 2][:, t_:t_ + 2, :], r=["ABC%d" % (tc % 2)],
                  w=["ab_o%d.%d" % (tc, t_)], final=final)
        if after_tc is not None:
            after_tc(tc)
    for k_ in range(2):
        P.dma("sp", kt_o[k_], K.KTL[:, k_, :], r=["KTL%d" % k_], w=["kt_o%d" % k_], final=final)


def fm_vec(v):
    v = np.asarray(v, np.float32)
    return np.ascontiguousarray(v.reshape(-1, 128).T)


def fm_tokens(a):
    T = a.shape[0]
    return np.ascontiguousarray(a.reshape(T, 8, 128).transpose(2, 1, 0))


_CONST = {}


def consts():
    if _CONST:
        return _CONST
    blk = np.zeros((128, 128), np.float32)
    blk[:64, :64] = 1.0
    blk[64:, 64:] = 1.0
    perm = np.zeros((128, 128), np.float32)
    for j in range(64):
        perm[2 * j + 1, 2 * j] = -1.0
        perm[2 * j, 2 * j + 1] = 1.0
    n = np.arange(64)
    ang = 2 * np.pi * np.outer(n, n) / 64.0
    c64, s64 = np.cos(ang), np.sin(ang)
    cs = np.zeros((128, 256), np.float64)
    for g in range(2):
        cs[g * 64:(g + 1) * 64, g * 64:(g + 1) * 64] = c64
        cs[g * 64:(g + 1) * 64, 128 + g * 64:128 + (g + 1) * 64] = s64
    _CONST["blk"] = _bf(blk)
    _CONST["perm"] = perm
    _CONST["cs64"] = _bf(cs)
    freqs = 10000.0 ** (-np.arange(16, dtype=np.float32) / 16)
    ropes = []
    for r in range(4):
        t = np.arange(r * NLAT, (r + 1) * NLAT)
        row = (t // 64).astype(np.float32)
        col = (t % 64).astype(np.float32)
        ang = np.concatenate([row[:, None] * freqs, col[:, None] * freqs], axis=-1).astype(np.float32)
        cos, sin = np.cos(ang), np.sin(ang)
        tab = np.zeros((128, 2, NLAT), np.float32)
        for p in range(128):
            jj = (p % 64) // 2
            tab[p, 0] = cos[:, jj]
            tab[p, 1] = sin[:, jj]
        ropes.append(tab)
    _CONST["rope"] = ropes
    tabs = []
    l = np.arange(8192, dtype=np.int64)
    sc = 1.0 / math.sqrt(8192 * 64)
    for r in range(4):
        k = np.arange(r * NLAT, (r + 1) * NLAT, dtype=np.int64)
        m = (l[:, None] * k[None, :]) % 8192
        a = 2 * np.pi * m / 8192.0
        tab = np.stack([np.cos(a) * sc, -np.sin(a) * sc], axis=1)
        tabs.append(_bf(tab.reshape(64, 128, 2, NLAT)))
    _CONST["dft"] = tabs
    l2 = np.arange(256, dtype=np.int64)
    m = (l2[:, None] * l2[None, :]) % 256
    a = 2 * np.pi * m / 256.0
    sc2 = 1.0 / math.sqrt(256 * 64)
    _CONST["dftc"] = _bf(np.stack([np.cos(a) * sc2, -np.sin(a) * sc2], axis=1).reshape(2, 128, 2, 256))
    return _CONST


def perm_win(w):
    cols = []
    for a in range(6):
        cols += list(range(a * 64, a * 64 + 64)) + list(range((a + 6) * 64, (a + 6) * 64 + 64))
    for kv in (0, 2, 1, 3):
        cols += list(range(768 + kv * 64, 768 + kv * 64 + 64))
    for kv in (0, 2, 1, 3):
        cols += list(range(1024 + kv * 64, 1024 + kv * 64 + 64))
    cols += list(range(1280, 1536))
    return np.ascontiguousarray(w[:, cols])


def build_p0():
    nc = bass.Bass("TRN2", target_bir_lowering=False)
    with ExitStack() as es:
        K = Ctx(nc, es)
        xin = K.dram_in("x_in", [128, 8, NT], F32)
        win = K.dram_in("win", [D, 1536], F32)
        rope = K.dram_in("rope", [128, 2, NLAT], F32)
        qkg = K.dram_in("qkg", [128, 8], F32)
        blk = K.dram_in("blk", [128, 128], BF16)
        perm = K.dram_in("perm", [128, 128], F32)
        cs64 = K.dram_in("cs64", [128, 256], BF16)
        kt_o = K.dram_out("kt_o", [128, 2, NT], BF16)
        v_o = K.dram_out("v_o", [128, 18, 2, 2, 65], BF16)
        ab_o = K.dram_out("ab_o", [128, 18, 512], BF16)
        qt_o = K.dram_out("qt_o", [128, 6, NT], BF16)
        alloc_common(K)
        alloc_eps(K)
        alloc_norm(K)
        alloc_epre(K)
        load_mod(K, 0)
        for c in range(8):
            K.P.dma("sp", K.X[:, c, :], xin[:, c, :], w=["X%d.%d" % (tc, c) for tc in range(5)])
        epre_segment(K, 0, win, rope, qkg, blk, perm, cs64, kt_o, v_o, ab_o)
        K.P.dma("sp", qt_o, K.QT[:], r=["QT%d.%d" % (tc, fc) for tc in range(5) for fc in range(6)],
                w=["qt_o"], final=True)
        K.P.emit()
    return nc


def alloc_epost(K, with_qt=True):
    if with_qt:
        K.QT = K.sb("QT", [128, 6, NT], BF16)
    K.KT = K.sb("KT", [128, 8448], BF16)
    K.V = K.sb("V", [128, 66, 2, 65], BF16)
    K.WO = K.sb("WO", [64, 12, 1024], BF16)
    K.WOF = K.sb("WOF", [128, 2, 1024], BF16)
    K.CAT = K.sb("CAT", [64, 6, 512], BF16)
    K.PB = [K.sb("PB%d" % i, [128, 2, 512], BF16) for i in range(3)]
    K.FT = K.sb("FT", [128, 2, NT], BF16)
    K.ABG = [K.sb("ABG%d" % i, [128, 2, 512], BF16) for i in range(2)]
    K.TABC = K.sb("TABC", [128, 2, 2, 256], BF16)
    K.ABGC = K.sb("ABGC", [128, 2, 512], BF16)
    K.OSB = K.sb("OSB", [65, 2, 512], F32)
    K.RDL = K.sb("RDL", [65, 2, 512], F32)
    K.RD = K.sb("RD", [65, 2, 512], BF16)
    K.ONESR = K.sb("ONESR", [128, 64], BF16)


def epost_segment(K, L, kt_all, v_all, ab_all, dft, dftc, wout, gk=()):
    X, PS, P = K.X, K.PS, K.P
    MOD = K.MOD[L]
    mk = "MOD%d" % L
    K.memset("pool", K.ONESR[:], 1.0, ["ONESR"])
    P.dma("pool", K.WO[:], wout[0:768, :].rearrange("(h d) n -> d h n", d=64), w=["WO"])
    P.dma("pool", K.WOF[:], wout[768:1024, :].rearrange("(c p) n -> p c n", p=128), w=["WOF"])
    TAB = [K.KT[:, 0:8192].rearrange("p (a b c) -> p a b c", a=2, b=2),
           K.V[:].rearrange("p a b c -> p (a b c)")[:, 0:8192].rearrange("p (a b c) -> p a b c", a=2, b=2)]
    TK = ["KTa", "Va"]
    for g in range(32):
        r = g // 8
        tl = (2 * g) % 16
        tb = g % 2
        P.dma("sp", TAB[tb], dft[2 * g:2 * g + 2].rearrange("l p s k -> p l s k"), w=[TK[tb]])
        P.dma("sp", K.ABG[tb][:], ab_all[tl // 6][r, :, tl % 6:tl % 6 + 2, :], r=gk, w=["ABG%d" % tb])
        for li in range(2):
            for half in range(2):
                for s in range(2):
                    for kc in range(4):
                        bank = half * 4 + kc
                        K.mm(PS[:, bank, :], K.ABG[tb][:, li, half * 256 + s * 128:half * 256 + (s + 1) * 128],
                             TAB[tb][:, li, s, kc * 512:(kc + 1) * 512],
                             g == 0 and li == 0 and s == 0, g == 31 and li == 1 and s == 1,
                             ["ABG%d" % tb, TK[tb]], ["ps%d" % bank])
    for half in range(2):
        for kc in range(4):
            bank = half * 4 + kc
            if bank % 2 == 0:
                K.act(K.FT[:, half, kc * 512:(kc + 1) * 512], PS[:, bank, :], AF.Identity, ["ps%d" % bank], ["FT%d" % kc])
            else:
                K.cp("dve", K.FT[:, half, kc * 512:(kc + 1) * 512], PS[:, bank, :], ["ps%d" % bank], ["FT%d" % kc])
    P.dma("sp", K.TABC[:], dftc.rearrange("l p s k -> p l s k"), w=["TABC"])
    P.dma("sp", K.ABGC[:], ab_all[2][0, :, 4:6, :], r=gk, w=["ABGC"])
    for half in range(2):
        for li in range(2):
            for s in range(2):
                K.mm(PS[:, half, 0:256], K.ABGC[:, li, half * 256 + s * 128:half * 256 + (s + 1) * 128],
                     K.TABC[:, li, s, :], li == 0 and s == 0, li == 1 and s == 1, ["ABGC", "TABC"], ["ps%d" % half])
        K.act(K.FT[:, half, 2048:2304], PS[:, half, 0:256], AF.Identity, ["ps%d" % half], ["FT4"])
    P.barrier()
    jobs = [(kp, tc, a3) for kp in range(2) for tc in range(5) for a3 in range(3)]

    def load_kv(kp):
        for r in range(4):
            P.dma("sp", K.KT[:, r * 2048:(r + 1) * 2048], kt_all[kp][r, :, 0:2048], r=["kt_gat%d" % kp], w=["KT%d" % r])
            P.dma("sp", K.V[:, r * 16:(r + 1) * 16, :, :], v_all[kp][r, :, 0:16, :, :], r=["v_gat%d" % kp], w=["V%d" % r])
        P.dma("sp", K.KT[:, 8192:8448], kt_all[kp][0, :, 2048:2304], r=["kt_gat%d" % kp], w=["KT4"])
        P.dma("sp", K.V[:, 64:66, :, :], v_all[kp][0, :, 16:18, :, :], r=["v_gat%d" % kp], w=["V4"])

    def kts_of(tc):
        return list(range(66)) if tc < 4 else [64, 65]

    def S(job, i, g0):
        kp, tc, a3 = job
        t0, W = TCS[tc]
        a = 3 * kp + a3
        qk = "QT%d.%d" % (tc, a)
        kt = kts_of(tc)[i]
        sb = (g0 + i) % 3
        kk = "KT%d" % min(kt // 16, 4)
        K.mm(PS[:, 2 * sb, :W], K.KT[0:64, kt * 128:(kt + 1) * 128], K.QT[0:64, a, t0:t0 + W],
             True, True, [kk, qk], ["ps%d" % (2 * sb)])
        K.mm(PS[:, 2 * sb + 1, :W], K.KT[64:128, kt * 128:(kt + 1) * 128], K.QT[64:128, a, t0:t0 + W],
             True, True, [kk, qk], ["ps%d" % (2 * sb + 1)])

    def prologue(job, g0):
        n = len(kts_of(job[1]))
        S(job, 0, g0)
        if n > 1:
            S(job, 1, g0)

    def body(job, g0, pending=None):
        kp, tc, a3 = job
        t0, W = TCS[tc]
        kts = kts_of(tc)
        n = len(kts)
        had_pending = pending is not None
        for i in range(n):
            sb = (g0 + i) % 3
            kt = kts[i]
            vk = "V%d" % min(kt // 16, 4)
            if i + 2 < n:
                S(job, i + 2, g0)
            K.act(K.PB[sb][:, :, :W], PS[:, 2 * sb:2 * sb + 2, :W], AF.Exp,
                  ["ps%d" % (2 * sb), "ps%d" % (2 * sb + 1)], ["PB%d" % sb], scale=0.125)
            for s_ in range(2):
                K.mm(PS[0:65, 6 + s_, :W], K.V[:, kt, s_, :], K.PB[sb][:, s_, :W],
                     i == 0, i == n - 1, [vk, "PB%d" % sb], ["ps%d" % (6 + s_)])
            while pending and pending[0][0] <= i:
                pending.pop(0)[1](sb)
        while pending:
            pending.pop(0)[1]((g0 + n) % 3)
        K.cp("dve", K.OSB[0:65, :, :W], PS[0:65, 6:8, :W], ["ps6", "ps7"], ["OSB"])
        K.recip(K.RDL[64:65, :, :W], K.OSB[64:65, :, :W], ["OSB"], ["RDL"])
        K.cp("dve", K.RD[64:65, :, :W], K.RDL[64:65, :, :W], ["RDL"], ["RD"])

    def finish(job, bs):
        kp, tc, a3 = job
        t0, W = TCS[tc]
        v = 1 if tc == 4 else 0
        for s_ in range(2):
            K.mm(PS[0:64, 2 * bs + s_, :W], K.ONESR[64:65, 0:64], K.RD[64:65, s_, :W], True, True,
                 ["ONESR", "RD"], ["ps%d" % (2 * bs + s_)])
        K.tt("dve", K.CAT[0:64, 2 * a3:2 * a3 + 2, :W], K.OSB[0:64, :, :W], PS[0:64, 2 * bs:2 * bs + 2, :W],
             ALU.mult, ["OSB", "ps%d" % (2 * bs), "ps%d" % (2 * bs + 1)], ["CAT%d" % (2 * a3), "CAT%d" % (2 * a3 + 1)])
    def outproj_d(job, d, bs):
        kp, tc, a3 = job
        t0, W = TCS[tc]
        v = 1 if tc == 4 else 0
        if True:
            if True:
                bank = 2 * bs + d % 2
                for slot in range(6):
                    head = 3 * kp + slot // 2 + 6 * (slot % 2)
                    K.mm(PS[:, bank, :W], K.WO[0:64, head, d * 128:(d + 1) * 128], K.CAT[0:64, slot, :W],
                         slot == 0, slot == 5 and kp == 1, ["WO", "CAT%d" % slot], ["ps%d" % bank])
                if kp == 0:
                    for half in range(2):
                        K.mm(PS[:, bank, :W], K.WOF[:, half, d * 128:(d + 1) * 128], K.FT[:, half, t0:t0 + W],
                             False, half == 1, ["WOF", "FT%d" % tc], ["ps%d" % bank])
                K.stt(X[:, d, t0:t0 + W], PS[:, bank, :W], MOD[:, 16 + d, v:v + 1], X[:, d, t0:t0 + W],
                      ALU.mult, ALU.add, ["ps%d" % bank, mk, "X%d.%d" % (tc, d)], ["X%d.%d" % (tc, d)])

    g0 = 0
    load_kv(0)
    prologue(jobs[0], g0)
    pending = []
    for ji, job in enumerate(jobs):
        n = len(kts_of(job[1]))
        body(job, g0, pending)
        pending = [(10, (lambda bs, job=job: finish(job, bs)))]
        if job[2] == 2:
            for d in range(8):
                pending.append((12 + 7 * d, (lambda bs, job=job, d=d: outproj_d(job, d, bs))))
        g0n = g0 + n + 1
        if ji + 1 < len(jobs):
            nj = jobs[ji + 1]
            if nj[0] != job[0]:
                while pending:
                    pending.pop(0)[1]((g0 + n) % 3)
                load_kv(nj[0])
            prologue(nj, g0n)
        g0 = g0n
    while pending:
        pending.pop(0)[1](g0 % 3)


def load_x(K, xin):
    for c in range(8):
        K.P.dma("sp", K.X[:, c, :], xin[:, c, :], w=["X%d.%d" % (tc, c) for tc in range(5)])


def store_x(K, xo, n=NT):
    for c in range(8):
        K.P.dma("sp", xo[:, c, :], K.X[:, c, 0:n], r=["X%d.%d" % (tc, c) for tc in range(5)],
                w=["xo%d" % c], final=True)


UW = NLAT + 30 + NCTX + 30


def ucol(tc):
    return 15 + TCS[tc][0] if tc < 4 else NLAT + 30 + 15


def alloc_opre(K, with_u=True):
    K.WPW1 = K.sb("WPW1", [128, 8, 2048], BF16)
    K.HC = [K.sb("HC%d" % i, [128, 8, 512], BF16) for i in range(2)]
    if with_u:
        K.U = K.sb("U", [128, 8, UW], BF16)
    K.BPW1 = K.sb("BPW1", [128, 16], F32)
    K.NEGB = K.sb("NEGB", [128, 16], F32)
    K.EG = [K.sb("EG%d" % i, [128, 512], F32) for i in range(2)]
    K.SG = [K.sb("SG%d" % i, [128, 512], F32) for i in range(2)]


def opre_segment(K, L, wpw1, bpw1, ntc=5):
    X, PS, P = K.X, K.PS, K.P
    P.dma("pool", K.WPW1[:], wpw1.rearrange("(kc p) n -> p kc n", p=128), w=["WPW1"])
    P.dma("sp", K.BPW1[:], bpw1, w=["BPW1"])
    K.ts("dve", K.NEGB[:], K.BPW1[:], -1.0, None, ALU.mult, None, ["BPW1"], ["NEGB"])
    K.memset("pool", K.U[:], 0.0, ["U%d" % tc for tc in range(5)])
    def nrm(tc, part):
        W = TCS[tc][1]
        hcb = K.HC[tc % 2]
        hkb = "HC%d" % (tc % 2)
        if part == 0:
            norm_sq(K, tc)
        else:
            norm_rest(K, L, 0, tc, lambda c, hcb=hcb, W=W: hcb[:, c, :W], lambda c, hkb=hkb: [hkb])

    nrm(0, 0)
    nrm(0, 1)
    for tc in range(ntc):
        t0, W = TCS[tc]
        hc = K.HC[tc % 2]
        hk = "HC%d" % (tc % 2)
        u0 = ucol(tc)
        for c in range(8):
            if tc + 1 < ntc and c == 0:
                nrm(tc + 1, 0)
            if tc + 1 < ntc and c == 2:
                nrm(tc + 1, 1)
            pb = c % 2
            bv, bg = c % 2, 2 + c % 2
            for kc in range(8):
                K.mm(PS[:, bv, :W], K.WPW1[:, kc, c * 128:(c + 1) * 128], hc[:, kc, :W],
                     kc == 0, kc == 7, ["WPW1", hk], ["ps%d" % bv])
            for kc in range(8):
                K.mm(PS[:, bg, :W], K.WPW1[:, kc, 1024 + c * 128:1024 + (c + 1) * 128], hc[:, kc, :W],
                     kc == 0, kc == 7, ["WPW1", hk], ["ps%d" % bg])
            K.act(K.EG[pb][:, :W], PS[:, bg, :W], AF.Exp, ["ps%d" % bg, "NEGB"], ["EG%d" % pb],
                  bias=K.NEGB[:, 8 + c:9 + c], scale=-1.0)
            K.act(K.EG[pb][:, :W], K.EG[pb][:, :W], AF.Ln, ["EG%d" % pb, "ONEB"], ["EG%d" % pb], bias=K.ONEB[:], scale=1.0)
            K.act(K.SG[pb][:, :W], K.EG[pb][:, :W], AF.Exp, ["EG%d" % pb], ["SG%d" % pb], scale=-1.0)
            K.stt(K.U[:, c, u0:u0 + W], PS[:, bv, :W], K.BPW1[:, c:c + 1], K.SG[pb][:, :W],
                  ALU.add, ALU.mult, ["ps%d" % bv, "BPW1", "SG%d" % pb], ["U%d" % tc])


def alloc_opost(K, with_u=True):
    if with_u:
        K.U = K.sb("U", [128, 8, UW], BF16)
    K.WPW2 = K.sb("WPW2", [128, 8, 1024], BF16)
    K.ACC = K.sb("ACC", [128, 8, 512], F32)
    K.SQC = K.sb("SQC", [128, 8, 512], BF16)
    K.Z = K.sb("Z", [128, 8, 512], BF16)
    K.COB = K.Z
    K.DIAG = [K.sb("DIAG%d" % i, [128, 31, 128], BF16) for i in range(2)]
    if not hasattr(K, "IDENT"):
        K.IDENT = K.sb("IDENT", [128, 128], BF16)
    K.WDW = K.sb("WDW", [128, 8, 31], F32)
    K.PV5 = K.sb("PV5", [128, 5, 8], F32)
    K.GB = K.sb("GB", [128, 8, 2], F32)
    K.MEAN = K.sb("MEAN", [128, 512], F32)
    K.VAR = K.sb("VAR", [128, 512], F32)
    K.LNV2 = K.sb("LNV2", [128, 512], F32)
    K.RSTD2 = K.sb("RSTD2", [128, 512], F32)
    K.TA = [K.sb("TA0", [128, 512], F32)] * 2
    K.TB = [K.sb("TB%d" % i, [128, 512], F32) for i in range(2)]
    K.TE = [K.sb("TE%d" % i, [128, 512], F32) for i in range(2)]
    K.T3 = [K.sb("T3%d" % i, [128, 512], F32) for i in range(2)]


def opost_segment(K, L, wdw, pv4, wpw2, ntc=5, ukeys=False, ident=None):
    X, PS, P = K.X, K.PS, K.P
    if ident is not None:
        P.dma("sp", K.IDENT[:], ident, w=["IDENT"])
    MOD = K.MOD[L]
    mk = "MOD%d" % L
    P.dma("pool", K.WPW2[:], wpw2.rearrange("(kc p) n -> p kc n", p=128), w=["WPW2"])
    P.dma("sp", K.WDW[:], wdw, w=["WDW"])
    P.dma("sp", K.PV5[:, 0:4, :], pv4, w=["PV5"])
    K.ts("dve", K.PV5[:, 4, :], K.PV5[:, 2, :], -1.0, None, ALU.mult, None, ["PV5"], ["PV5n"])
    for v in range(2):
        K.tt("dve", K.GB[:, :, v], MOD[:, 16:24, v], K.PV5[:, 3, :], ALU.mult, [mk, "PV5"], ["GB"])
    def conv_mm(tc, c):
        t0, W = TCS[tc]
        s0 = ucol(tc) - 15
        db = c % 2
        dk = "DIAG%d" % db
        idv = K.IDENT[:]
        wv = K.WDW[:, c, :]
        in0 = bass.AP(idv.tensor, idv.offset, [list(idv.ap[0]), [0, 31], [1, 128]])
        in1 = bass.AP(wv.tensor, wv.offset, [list(wv.ap[0]), [1, 31], [0, 128]])
        K.tt("dve", K.DIAG[db][:], in0, in1, ALU.mult, ["IDENT", "WDW"], [dk])
        bank = 2 + c % 4
        for j in range(31):
            K.mm(PS[:, bank, :W], K.DIAG[db][:, j, :], K.U[:, c, s0 + j:s0 + j + W], j == 0, j == 30,
                 [dk, "U"], ["ps%d" % bank])

    def conv_ev(tc, c):
        W = TCS[tc][1]
        bank = 2 + c % 4
        K.act(K.ACC[:, c, :W], PS[:, bank, :W], AF.Identity, ["ps%d" % bank, "PV5"], ["ACC%d" % c],
              bias=K.PV5[:, 0, c:c + 1], scale=1.0)

    def conv_tail(tc):
        W = TCS[tc][1]
        for c in range(8):
            K.cp("dve", K.COB[:, c, :W], K.ACC[:, c, :W], ["ACC%d" % c], ["Z%d" % c])
            K.tt("pool", K.SQC[:, c, :W], K.ACC[:, c, :W], K.ACC[:, c, :W], ALU.mult, ["ACC%d" % c], ["SQC%d" % c])

    def stats(tc):
        W = TCS[tc][1]
        for c in range(8):
            K.mm(PS[:, 6, :W], K.ONES[:], K.COB[:, c, :W], c == 0, c == 7, ["ONES", "Z%d" % c], ["ps6"])
        for c in range(8):
            K.mm(PS[:, 7, :W], K.ONES[:], K.SQC[:, c, :W], c == 0, c == 7, ["ONES", "SQC%d" % c], ["ps7"])
        K.act(K.MEAN[:, :W], PS[:, 6, :W], AF.Identity, ["ps6"], ["MEAN"], scale=1.0 / D)
        K.tt("pool", K.VAR[:, :W], K.MEAN[:, :W], K.MEAN[:, :W], ALU.mult, ["MEAN"], ["VAR"])
        K.stt(K.VAR[:, :W], PS[:, 7, :W], 1.0 / D, K.VAR[:, :W], ALU.mult, ALU.subtract, ["ps7", "VAR"], ["VAR"])
        K.act(K.LNV2[:, :W], K.VAR[:, :W], AF.Ln, ["VAR"], ["LNV2"], bias=K.EPSB[:], scale=1.0)
        K.act(K.RSTD2[:, :W], K.LNV2[:, :W], AF.Exp, ["LNV2"], ["RSTD2"], scale=-0.5)

    def ln_chain(tc):
        W = TCS[tc][1]
        for c in range(8):
            pb = c % 2
            K.tt("dve", K.TA[pb][:, :W], K.ACC[:, c, :W], K.MEAN[:, :W], ALU.subtract,
                 ["ACC%d" % c, "MEAN"], ["TA0"])
            K.stt(K.TB[pb][:, :W], K.TA[pb][:, :W], K.PV5[:, 1, c:c + 1], K.RSTD2[:, :W], ALU.mult, ALU.mult,
                  ["TA0", "PV5", "RSTD2"], ["TB%d" % pb])
            K.act(K.TE[pb][:, :W], K.TB[pb][:, :W], AF.Exp, ["TB%d" % pb, "PV5n"], ["TE%d" % pb],
                  bias=K.PV5[:, 4, c:c + 1], scale=-1.0)
            K.act(K.TE[pb][:, :W], K.TE[pb][:, :W], AF.Ln, ["TE%d" % pb, "ONEB"], ["TE%d" % pb], bias=K.ONEB[:], scale=1.0)
            K.act(K.TE[pb][:, :W], K.TE[pb][:, :W], AF.Exp, ["TE%d" % pb], ["TE%d" % pb], scale=-1.0)
            K.stt(K.Z[:, c, :W], K.TB[pb][:, :W], K.PV5[:, 2, c:c + 1], K.TE[pb][:, :W], ALU.add, ALU.mult,
                  ["TB%d" % pb, "PV5", "TE%d" % pb], ["Z%d" % c])

    def pw2(tc):
        t0, W = TCS[tc]
        v = 1 if tc == 4 else 0
        for d in range(8):
            bank = d % 2
            for c in range(8):
                K.mm(PS[:, bank, :W], K.WPW2[:, c, d * 128:(d + 1) * 128], K.Z[:, c, :W], c == 0, c == 7,
                     ["WPW2", "Z%d" % c], ["ps%d" % bank])
            K.act(K.T3[bank][:, :W], PS[:, bank, :W], AF.Identity, ["ps%d" % bank, mk, "GB"], ["T3%d" % bank],
                  bias=K.GB[:, d, v:v + 1], scale=MOD[:, 16 + d, v:v + 1])
            K.tt("dve", X[:, d, t0:t0 + W], X[:, d, t0:t0 + W], K.T3[bank][:, :W], ALU.add,
                 ["X%d.%d" % (tc, d), "T3%d" % bank], ["X%d.%d" % (tc, d)])

    for c in range(8):
        conv_mm(0, c)
        conv_ev(0, c)
    conv_tail(0)
    for tc in range(ntc):
        nxt = tc + 1 < ntc
        stats(tc)
        if nxt:
            conv_mm(tc + 1, 0)
            conv_mm(tc + 1, 1)
        ln_chain(tc)
        if nxt:
            conv_ev(tc + 1, 0)
            conv_ev(tc + 1, 1)
        pw2(tc)
        if nxt:
            for c in range(2, 8):
                conv_mm(tc + 1, c)
                conv_ev(tc + 1, c)
            conv_tail(tc + 1)


def _epre_io(K):
    win = K.dram_in("win", [D, 1536], F32)
    rope = K.dram_in("rope", [128, 2, NLAT], F32)
    qkg = K.dram_in("qkg", [128, 8], F32)
    blk = K.dram_in("blk", [128, 128], BF16)
    perm = K.dram_in("perm", [128, 128], F32)
    cs64 = K.dram_in("cs64", [128, 256], BF16)
    kt_o = K.dram_out("kt_o", [128, 2, NT], BF16)
    v_o = K.dram_out("v_o", [128, 18, 2, 2, 65], BF16)
    ab_o = K.dram_out("ab_o", [128, 18, 512], BF16)
    qt_o = K.dram_out("qt_o", [128, 6, NT], BF16)
    return win, rope, qkg, blk, perm, cs64, kt_o, v_o, ab_o, qt_o


def _epre_run(K, L, io):
    win, rope, qkg, blk, perm, cs64, kt_o, v_o, ab_o, qt_o = io
    epre_segment(K, L, win, rope, qkg, blk, perm, cs64, [kt_o[:, k_, :] for k_ in range(2)],
                 [v_o[:, :, a_, :, :] for a_ in range(2)], [ab_o[:, 6 * c_:6 * c_ + 6, :] for c_ in range(3)])
    K.P.dma("sp", qt_o, K.QT[:], r=["QT%d.%d" % (tc, fc) for tc in range(5) for fc in range(6)],
            w=["qt_o"], final=True)


def build_pA():
    nc = bass.Bass("TRN2", target_bir_lowering=False)
    with ExitStack() as es:
        K = Ctx(nc, es)
        xin = K.dram_in("x_in", [128, 8, NT], F32)
        io = _epre_io(K)
        alloc_common(K)
        alloc_eps(K)
        load_mod(K, 0, "modA")
        load_x(K, xin)
        alloc_norm(K)
        alloc_epre(K)
        _epre_run(K, 0, io)
        K.P.emit()
    return nc


def build_pB():
    nc = bass.Bass("TRN2", target_bir_lowering=False)
    with ExitStack() as es:
        K = Ctx(nc, es)
        xin = K.dram_in("x_in", [128, 8, NT], F32)
        qt_in = K.dram_in("qt_in", [128, 6, NT], BF16)
        kt_all = K.dram_in("kt_all", [4, 128, 2, NT], BF16)
        v_all = K.dram_in("v_all", [4, 128, 18, 2, 2, 65], BF16)
        ab_all = K.dram_in("ab_all", [4, 128, 18, 512], BF16)
        dft = K.dram_in("dft", [64, 128, 2, NLAT], BF16)
        dftc = K.dram_in("dftc", [2, 128, 2, 256], BF16)
        wout = K.dram_in("wout", [D, D], F32)
        w1 = K.dram_in("w1", [D, 4 * D], F32)
        w2 = K.dram_in("w2", [4 * D, D], F32)
        wpw1 = K.dram_in("wpw1", [D, 2 * D], F32)
        bpw1 = K.dram_in("bpw1", [128, 16], F32)
        xo = K.dram_out("x_o", [128, 8, NT], F32)
        uo = K.dram_out("u_o", [128, 8, UW], BF16)
        alloc_common(K)
        alloc_eps(K)
        load_mod(K, 0, "modA")
        load_mod(K, 1, "modB")
        load_x(K, xin)
        with ExitStack() as ph:
            K.es = ph
            alloc_epost(K)
            K.P.dma("sp", K.QT[:], qt_in, w=["QT%d.%d" % (tc, fc) for tc in range(5) for fc in range(6)])
            epost_segment(K, 0, [kt_all[:, :, k_, :] for k_ in range(2)],
                          [v_all[:, :, :, a_, :, :] for a_ in range(2)],
                          [ab_all[:, :, 6 * c_:6 * c_ + 6, :] for c_ in range(3)], dft, dftc, wout)
            K.P.barrier()
        with ExitStack() as ph:
            K.es = ph
            alloc_norm(K)
            alloc_mlp(K)
            mlp_segment(K, 0, w1, w2)
            K.P.barrier()
        with ExitStack() as ph:
            K.es = ph
            alloc_norm(K)
            alloc_opre(K)
            opre_segment(K, 1, wpw1, bpw1)
            K.P.dma("sp", uo, K.U[:], r=["U%d" % tc for tc in range(5)], w=["u_o"], final=True)
            K.P.barrier()
        K.es = es
        store_x(K, xo)
        K.P.emit()
    return nc


def build_pC(last):
    nc = bass.Bass("TRN2", target_bir_lowering=False)
    ntc = 4 if last else 5
    with ExitStack() as es:
        K = Ctx(nc, es)
        xin = K.dram_in("x_in", [128, 8, NT], F32)
        u_in = K.dram_in("u_in", [128, 8, UW], BF16)
        wdw = K.dram_in("wdw", [128, 8, 31], F32)
        pv4 = K.dram_in("pv4", [128, 4, 8], F32)
        wpw2 = K.dram_in("wpw2", [D, D], F32)
        ident = K.dram_in("ident", [128, 128], BF16)
        w1 = K.dram_in("w1", [D, 4 * D], F32)
        w2 = K.dram_in("w2", [4 * D, D], F32)
        if not last:
            io = _epre_io(K)
            xo = K.dram_out("x_o", [128, 8, NT], F32)
        else:
            xo = K.dram_out("x_o", [128, 8, NLAT], F32)
        alloc_common(K)
        alloc_eps(K)
        load_mod(K, 0, "modA")
        if not last:
            load_mod(K, 1, "modB")
        load_x(K, xin)
        with ExitStack() as ph:
            K.es = ph
            alloc_opost(K)
            K.P.dma("sp", K.U[:], u_in, w=["U"])
            opost_segment(K, 0, wdw, pv4, wpw2, ntc, ident=ident)
            K.P.barrier()
        with ExitStack() as ph:
            K.es = ph
            alloc_norm(K)
            alloc_mlp(K)
            mlp_segment(K, 0, w1, w2, ntc)
            K.P.barrier()
        if not last:
            with ExitStack() as ph:
                K.es = ph
                alloc_norm(K)
                alloc_epre(K)
                _epre_run(K, 1, io)
                K.P.barrier()
        K.es = es
        store_x(K, xo, NLAT if last else NT)
        K.P.emit()
    return nc


_PROGS = {}


def _prog(name):
    if name not in _PROGS:
        _PROGS[name] = {"mod": build_mod, "A": build_pA, "B": build_pB,
                        "C": lambda: build_pC(False), "D": lambda: build_pC(True)}[name]()
    return _PROGS[name]


def _run(name, in_maps):
    res = run_bass_kernel_spmd(_prog(name), in_maps, core_ids=list(range(NCORES)))
    return res.results


def kernel_multi(x, c, ctx, c_ctx, ada_w, ada_b, norm1_g, norm2_g, mlp_w1, mlp_w2,
           attn_w_in, q_norm_g, k_norm_g, attn_w_out,
           conv_w_pw1, conv_b_pw1, conv_w_dw, conv_b_dw, conv_ln_g, conv_ln_b,
           conv_w_pw2, conv_b_pw2):
    f32 = lambda a: np.ascontiguousarray(np.asarray(a, dtype=np.float32))
    x, c, ctx, c_ctx = f32(x), f32(c), f32(ctx), f32(c_ctx)
    ada_w, ada_b, norm1_g, norm2_g = f32(ada_w), f32(ada_b), f32(norm1_g), f32(norm2_g)
    mlp_w1, mlp_w2, attn_w_in, attn_w_out = f32(mlp_w1), f32(mlp_w2), f32(attn_w_in), f32(attn_w_out)
    conv_w_pw1, conv_w_pw2, conv_w_dw = f32(conv_w_pw1), f32(conv_w_pw2), f32(conv_w_dw)
    C = consts()
    cores = [(i // 4, i % 4) for i in range(NCORES)]
    maps = []
    for b, r in cores:
        cvec = np.ascontiguousarray(np.stack([fm_vec(c[b]), fm_vec(c_ctx)], axis=-1))
        adab = fm_vec(ada_b[r])
        adab = np.ascontiguousarray(np.repeat(adab[:, :, None], 2, axis=2))
        ng = np.ascontiguousarray(np.stack([fm_vec(norm1_g[r]), fm_vec(norm2_g[r])], axis=1))
        maps.append(dict(adaw=ada_w[r], adab=adab, cvec=cvec, ng=ng))
    rm = _run("mod", maps)
    mod = {(b, L): np.asarray(rm[b * 4 + L]["modo"]) for b in range(2) for L in range(4)}

    def qkg_of(j):
        g = np.zeros((128, 8), np.float32)
        for fc in range(8):
            src = np.asarray(q_norm_g[j] if fc < 6 else k_norm_g[j], np.float32)
            g[:64, fc] = src
            g[64:, fc] = src
        return g

    def epre_inputs(j, r):
        return dict(win=perm_win(attn_w_in[j]), rope=C["rope"][r], qkg=qkg_of(j), blk=C["blk"],
                    perm=C["perm"], cs64=C["cs64"])

    def gather(res, key, b):
        return np.ascontiguousarray(np.stack([np.asarray(res[b * 4 + rr][key]) for rr in range(4)], 0))

    def epost_inputs(res, j, b, r, i):
        return dict(qt_in=np.asarray(res[i]["qt_o"]), kt_all=gather(res, "kt_o", b), v_all=gather(res, "v_o", b),
                    ab_all=gather(res, "ab_o", b), dft=C["dft"][r], dftc=C["dftc"], wout=attn_w_out[j])

    def u_with_halo(res, b, r):
        u = np.array(np.asarray(res[b * 4 + r]["u_o"]))
        if r > 0:
            ul = np.asarray(res[b * 4 + r - 1]["u_o"])
            u[:, :, 0:15] = ul[:, :, NLAT:NLAT + 15]
        if r < 3:
            ur = np.asarray(res[b * 4 + r + 1]["u_o"])
            u[:, :, NLAT + 15:NLAT + 30] = ur[:, :, 15:30]
        return np.ascontiguousarray(u)

    def opost_inputs(j):
        wdw = np.ascontiguousarray(conv_w_dw[j].reshape(31, 8, 128).transpose(2, 1, 0))
        pv4 = np.ascontiguousarray(np.stack([fm_vec(conv_b_dw[j]), fm_vec(conv_ln_g[j]), fm_vec(conv_ln_b[j]),
                                             fm_vec(conv_b_pw2[j])], axis=1))
        return dict(wdw=wdw, pv4=pv4, wpw2=conv_w_pw2[j], ident=_bf(np.eye(128, dtype=np.float32)))

    maps = []
    for b, r in cores:
        xt = np.concatenate([x[b, r * NLAT:(r + 1) * NLAT], ctx[b]], axis=0)
        m = dict(x_in=fm_tokens(xt), modA=mod[(b, 0)])
        m.update(epre_inputs(0, r))
        maps.append(m)
    xcur = [m["x_in"] for m in maps]
    res = _run("A", maps)
    for L in (0, 2):
        j = L // 2
        maps = []
        for i, (b, r) in enumerate(cores):
            m = dict(x_in=xcur[i], modA=mod[(b, L)], modB=mod[(b, L + 1)], w1=mlp_w1[L], w2=mlp_w2[L],
                     wpw1=conv_w_pw1[j], bpw1=fm_vec(conv_b_pw1[j]))
            m.update(epost_inputs(res, j, b, r, i))
            maps.append(m)
        res = _run("B", maps)
        xcur = [np.asarray(res[i]["x_o"]) for i in range(NCORES)]
        last = L == 2
        maps = []
        for i, (b, r) in enumerate(cores):
            m = dict(x_in=xcur[i], u_in=u_with_halo(res, b, r), modA=mod[(b, L + 1)],
                     w1=mlp_w1[L + 1], w2=mlp_w2[L + 1])
            m.update(opost_inputs(j))
            if not last:
                m["modB"] = mod[(b, L + 2)]
                m.update(epre_inputs(j + 1, r))
            maps.append(m)
        res = _run("D" if last else "C", maps)
        if not last:
            xcur = [np.asarray(res[i]["x_o"]) for i in range(NCORES)]
    out = np.zeros((2, 4 * NLAT, D), np.float32)
    for i, (b, r) in enumerate(cores):
        xo = np.asarray(res[i]["x_o"])
        out[b, r * NLAT:(r + 1) * NLAT] = xo.transpose(2, 1, 0).reshape(NLAT, D)
    return out


GROUPS = [[0, 1, 2, 3], [4, 5, 6, 7]]


def mod_segment(K, adaw, adab, cvec, ng, m_loc, m_gat):
    nc, P, PS = K.nc, K.P, K.PS
    CV = K.sb("CV", [128, 8, 2], F32)
    ABs = K.sb("ABs", [128, 48, 2], F32)
    NG = K.sb("NG", [128, 2, 8], F32)
    E1 = K.sb("E1", [128, 8, 2], F32)
    S = K.sb("S", [128, 8, 2], BF16)
    MODL = K.sb("MODL", [128, 48, 2], F32)
    WM = [K.sb("WM%d" % i, [128, 8, 1024], BF16) for i in range(2)]
    P.dma("sp", CV[:], cvec, w=["CV"])
    P.dma("sp", ABs[:], adab, w=["ABs"])
    P.dma("sp", NG[:], ng, w=["NG"])
    K.act(E1[:], CV[:], AF.Exp, ["CV"], ["E1"], scale=-1.0)
    K.ts("dve", E1[:], E1[:], 1.0, None, ALU.add, None, ["E1"], ["E1"])
    K.recip(E1[:], E1[:], ["E1"], ["E1"])
    K.tt("dve", S[:], CV[:], E1[:], ALU.mult, ["CV", "E1"], ["S"])
    adaw_v = adaw.rearrange("(kc p) n -> p kc n", p=128)
    for j in range(6):
        wm = WM[j % 2]
        wk = "WM%d" % (j % 2)
        P.dma("pool", wm[:], adaw_v[:, :, j * 1024:(j + 1) * 1024], w=[wk])
        bank = j % 2
        for c in range(8):
            for kc in range(8):
                K.mm(PS[:, bank, c * 2:c * 2 + 2], wm[:, kc, c * 128:(c + 1) * 128], S[:, kc, :],
                     kc == 0, kc == 7, [wk, "S"], ["ps%d" % bank])
        K.tt("dve", MODL[:, j * 8:(j + 1) * 8, :],
             PS[:, bank, 0:16].rearrange("p (c v) -> p c v", v=2),
             ABs[:, j * 8:(j + 1) * 8, :], ALU.add, ["ps%d" % bank, "ABs"], ["MODL%d" % j])
    for j, gi in ((1, 0), (4, 1)):
        for v in range(2):
            K.stt(MODL[:, j * 8:(j + 1) * 8, v], MODL[:, j * 8:(j + 1) * 8, v], 1.0, NG[:, gi, :],
                  ALU.add, ALU.mult, ["MODL%d" % j, "NG"], ["MODL%d" % j])
    P.dma("sp", m_loc, MODL[:].rearrange("p a v -> p (a v)"), r=["MODL%d" % j for j in range(6)], w=["m_loc"])
    P.cc("AllGather", GROUPS, m_loc, m_gat, r=["m_loc"], w=["m_gat"])
    mg = m_gat.rearrange("(r p) (a v) -> r p a v", p=128, v=2)
    for L in range(4):
        P.dma("sp", K.MOD[L][:], mg[L], r=["m_gat"], w=["MOD%d" % L])


def halo_segment(K, e_loc, e_gat):
    P = K.P
    U = K.U
    E4 = K.sb("E4", [128, 4, 8, 2, 15], BF16)
    HAL = K.sb("HAL", [128, 2, 8, 15], F32)
    ED = K.sb("ED", [128, 8, 2, 15], BF16)
    ukeys = ["U%d" % tc for tc in range(5)]
    K.cp("pool", ED[:, :, 0, :], U[:, :, 15:30], ukeys, ["ED"])
    K.cp("pool", ED[:, :, 1, :], U[:, :, NLAT:NLAT + 15], ukeys, ["ED"])
    P.dma("sp", e_loc, ED[:].rearrange("p c s e -> p (c s e)"), r=["ED"], w=["e_loc"])
    P.cc("AllGather", GROUPS, e_loc, e_gat, r=["e_loc"], w=["e_gat"])
    P.dma("sp", E4[:], e_gat.rearrange("(r p) (c s e) -> p r c s e", p=128, c=8, s=2), r=["e_gat"], w=["E4"])
    for side in range(2):
        src_s = 1 - side
        for rr in range(4):
            mcol = K.HMASK[:, side * 4 + rr:side * 4 + rr + 1]
            if rr == 0:
                K.ts("dve", HAL[:, side], E4[:, rr, :, src_s, :], mcol, None, ALU.mult, None,
                     ["E4", "HMASK"], ["HAL%d" % side])
            else:
                K.stt(HAL[:, side], E4[:, rr, :, src_s, :], mcol, HAL[:, side], ALU.mult, ALU.add,
                      ["E4", "HMASK", "HAL%d" % side], ["HAL%d" % side])
    K.cp("dve", U[:, :, 0:15], HAL[:, 0], ["HAL0"], ["U0"])
    K.cp("dve", U[:, :, NLAT + 15:NLAT + 30], HAL[:, 1], ["HAL1"], ["U3"])


def build_fused():
    nc = bass.Bass("TRN2", target_bir_lowering=False)
    with ExitStack() as es:
        K = Ctx(nc, es)
        P = K.P
        di = K.dram_in
        xin = di("x_in", [128, 8, NT], F32)
        adaw = di("adaw", [D, 6 * D], F32)
        adab = di("adab", [128, 48, 2], F32)
        cvec = di("cvec", [128, 8, 2], F32)
        ng = di("ng", [128, 2, 8], F32)
        hmask = di("hmask", [128, 8], F32)
        rope = di("rope", [128, 2, NLAT], F32)
        blk = di("blk", [128, 128], BF16)
        perm = di("perm", [128, 128], F32)
        cs64 = di("cs64", [128, 256], BF16)
        ident = di("ident", [128, 128], BF16)
        dft = di("dft", [64, 128, 2, NLAT], BF16)
        dftc = di("dftc", [2, 128, 2, 256], BF16)
        w1 = [di("w1_%d" % L, [D, 4 * D], F32) for L in range(4)]
        w2 = [di("w2_%d" % L, [4 * D, D], F32) for L in range(4)]
        win = [di("win_%d" % j, [D, 1536], F32) for j in range(2)]
        qkg = [di("qkg_%d" % j, [128, 8], F32) for j in range(2)]
        wout = [di("wout_%d" % j, [D, D], F32) for j in range(2)]
        wpw1 = [di("wpw1_%d" % j, [D, 2 * D], F32) for j in range(2)]
        bpw1 = [di("bpw1_%d" % j, [128, 16], F32) for j in range(2)]
        wdw = [di("wdw_%d" % j, [128, 8, 31], F32) for j in range(2)]
        pv4 = [di("pv4_%d" % j, [128, 4, 8], F32) for j in range(2)]
        wpw2 = [di("wpw2_%d" % j, [D, D], F32) for j in range(2)]
        xo = K.dram_out("x_o", [128, 8, NLAT], F32)

        def internal(name, shape, dt):
            return nc.dram_tensor(name, list(shape), dt, kind="Internal").ap()

        m_loc = internal("m_loc", [128, 96], F32)
        m_gat = internal("m_gat", [512, 96], F32)
        alloc_common(K)
        alloc_eps(K)
        K.HMASK = K.sb("HMASK", [128, 8], F32)
        P.dma("sp", K.HMASK[:], hmask, w=["HMASK"])
        for L in range(4):
            K.MOD[L] = K.sb("MODL%d" % L, [128, 48, 2], F32)
        load_x(K, xin)
        with ExitStack() as ph:
            K.es = ph
            mod_segment(K, adaw, adab, cvec, ng, m_loc, m_gat)
            P.barrier()
        K.es = es
        K.BLK = K.sb("BLK", [128, 128], BF16)
        K.PERM = K.sb("PERM", [128, 128], F32)
        K.CS64 = K.sb("CS64", [128, 256], BF16)
        P.dma("sp", K.BLK[:], blk, w=["BLK"])
        P.dma("sp", K.PERM[:], perm, w=["PERM"])
        P.dma("sp", K.CS64[:], cs64, w=["CS64"])
        K.IDENT = K.sb("IDENT", [128, 128], BF16)
        P.dma("sp", K.IDENT[:], ident, w=["IDENT"])
        for L in range(4):
            j = L // 2
            ntc = 4 if L == 3 else 5
            if L % 2 == 0:
                kt_loc = [internal("kt_loc%d_%d" % (L, k_), [128, NT], BF16) for k_ in range(2)]
                kt_gat = [internal("kt_gat%d_%d" % (L, k_), [512, NT], BF16) for k_ in range(2)]
                v_loc = [internal("v_loc%d_%d" % (L, k_), [128, 18 * 130], BF16) for k_ in range(2)]
                v_gat = [internal("v_gat%d_%d" % (L, k_), [512, 18 * 130], BF16) for k_ in range(2)]
                ab_loc = [internal("ab_loc%d_%d" % (L, k_), [128, 6 * 512], BF16) for k_ in range(3)]
                ab_gat = [internal("ab_gat%d_%d" % (L, k_), [512, 6 * 512], BF16) for k_ in range(3)]
                with ExitStack() as ql:
                    K.es = ql
                    K.QT = K.sb("QT", [128, 6, NT], BF16)
                    with ExitStack() as ph:
                        K.es = ph
                        alloc_norm(K)
                        alloc_epre_noqt(K)
                        abk = [[], [], []]
                        for tc in range(5):
                            t0_, W_ = TCS[tc]
                            for t_ in range(0, W_ // 128, 2):
                                abk[(t0_ // 128 + t_) // 6].append("ab_o%d.%d" % (tc, t_))

                        def after_tc(tc, ab_loc=ab_loc, ab_gat=ab_gat, abk=abk):
                            k_ = {1: 0, 2: 1, 4: 2}.get(tc)
                            if k_ is not None:
                                P.cc("AllGather", GROUPS, ab_loc[k_], ab_gat[k_], r=abk[k_], w=["ab_gat%d" % k_])

                        epre_segment(K, L, win[j], rope, qkg[j], blk, perm, cs64,
                                     kt_loc,
                                     [v.rearrange("p (t s e) -> p t s e", t=18, s=2) for v in v_loc],
                                     [a.rearrange("p (t n) -> p t n", t=6) for a in ab_loc],
                                     final=False, load_consts=False, after_tc=after_tc)
                        for k_ in range(2):
                            P.cc("AllGather", GROUPS, kt_loc[k_], kt_gat[k_], r=["kt_o%d" % k_], w=["kt_gat%d" % k_])
                            P.cc("AllGather", GROUPS, v_loc[k_], v_gat[k_],
                                 r=["v_o%d.%d" % (tc, k_) for tc in range(5)], w=["v_gat%d" % k_])
                        P.barrier(keep=["kt_gat0", "kt_gat1", "v_gat0", "v_gat1"])
                    with ExitStack() as ph:
                        K.es = ph
                        alloc_epost(K, with_qt=False)
                        epost_segment(K, L,
                                      [k_.rearrange("(r p) t -> r p t", p=128) for k_ in kt_gat],
                                      [v.rearrange("(r p) (t s e) -> r p t s e", p=128, t=18, s=2) for v in v_gat],
                                      [a.rearrange("(r p) (t n) -> r p t n", p=128, t=6) for a in ab_gat],
                                      dft, dftc, wout[j])
                        P.barrier()
            else:
                e_loc = internal("e_loc%d" % L, [128, 240], BF16)
                e_gat = internal("e_gat%d" % L, [512, 240], BF16)
                with ExitStack() as ul:
                    K.es = ul
                    K.U = K.sb("U", [128, 8, UW], BF16)
                    with ExitStack() as ph:
                        K.es = ph
                        alloc_norm(K)
                        alloc_opre(K, with_u=False)
                        opre_segment(K, L, wpw1[j], bpw1[j], ntc)
                        halo_segment(K, e_loc, e_gat)
                        P.barrier()
                    with ExitStack() as ph:
                        K.es = ph
                        alloc_opost(K, with_u=False)
                        opost_segment(K, L, wdw[j], pv4[j], wpw2[j], ntc, ukeys=True)
                        P.barrier()
            with ExitStack() as ph:
                K.es = ph
                alloc_norm(K)
                alloc_mlp(K)
                def store_tc(tc):
                    t0_, W_ = TCS[tc]
                    P.dma("sp", xo[:, :, t0_:t0_ + W_], K.X[:, :, t0_:t0_ + W_], r=xkeys(tc), w=["xo%d" % tc], final=True)

                mlp_segment(K, L, w1[L], w2[L], ntc, after_last=store_tc if L == 3 else None)
                P.barrier()
        K.es = es
        P.emit()
    return nc


def kernel(x, c, ctx, c_ctx, ada_w, ada_b, norm1_g, norm2_g, mlp_w1, mlp_w2,
           attn_w_in, q_norm_g, k_norm_g, attn_w_out,
           conv_w_pw1, conv_b_pw1, conv_w_dw, conv_b_dw, conv_ln_g, conv_ln_b,
           conv_w_pw2, conv_b_pw2):
    f32 = lambda a: np.ascontiguousarray(np.asarray(a, dtype=np.float32))
    x, c, ctx, c_ctx = f32(x), f32(c), f32(ctx), f32(c_ctx)
    ada_w, ada_b, norm1_g, norm2_g = f32(ada_w), f32(ada_b), f32(norm1_g), f32(norm2_g)
    mlp_w1, mlp_w2, attn_w_in, attn_w_out = f32(mlp_w1), f32(mlp_w2), f32(attn_w_in), f32(attn_w_out)
    conv_w_pw1, conv_w_pw2, conv_w_dw = f32(conv_w_pw1), f32(conv_w_pw2), f32(conv_w_dw)
    C = consts()
    shared = {}
    for L in range(4):
        shared["w1_%d" % L] = mlp_w1[L]
        shared["w2_%d" % L] = mlp_w2[L]
    for j in range(2):
        g = np.zeros((128, 8), np.float32)
        for fc in range(8):
            src = np.asarray(q_norm_g[j] if fc < 6 else k_norm_g[j], np.float32)
            g[:64, fc] = src
            g[64:, fc] = src
        shared["win_%d" % j] = perm_win(attn_w_in[j])
        shared["qkg_%d" % j] = g
        shared["wout_%d" % j] = attn_w_out[j]
        shared["wpw1_%d" % j] = conv_w_pw1[j]
        shared["bpw1_%d" % j] = fm_vec(conv_b_pw1[j])
        shared["wdw_%d" % j] = np.ascontiguousarray(conv_w_dw[j].reshape(31, 8, 128).transpose(2, 1, 0))
        shared["pv4_%d" % j] = np.ascontiguousarray(np.stack(
            [fm_vec(conv_b_dw[j]), fm_vec(conv_ln_g[j]), fm_vec(conv_ln_b[j]), fm_vec(conv_b_pw2[j])], axis=1))
        shared["wpw2_%d" % j] = conv_w_pw2[j]
    shared.update(blk=C["blk"], perm=C["perm"], cs64=C["cs64"], dftc=C["dftc"],
                  ident=_bf(np.eye(128, dtype=np.float32)))
    maps = []
    for i in range(NCORES):
        b, r = i // 4, i % 4
        xt = np.concatenate([x[b, r * NLAT:(r + 1) * NLAT], ctx[b]], axis=0)
        adab = fm_vec(ada_b[r])
        hm = np.zeros((128, 8), np.float32)
        if r > 0:
            hm[:, r - 1] = 1.0
        if r < 3:
            hm[:, 4 + r + 1] = 1.0
        m = dict(shared)
        m.update(x_in=fm_tokens(xt), adaw=ada_w[r],
                 adab=np.ascontiguousarray(np.repeat(adab[:, :, None], 2, axis=2)),
                 cvec=np.ascontiguousarray(np.stack([fm_vec(c[b]), fm_vec(c_ctx)], axis=-1)),
                 ng=np.ascontiguousarray(np.stack([fm_vec(norm1_g[r]), fm_vec(norm2_g[r])], axis=1)),
                 hmask=hm, rope=C["rope"][r], dft=C["dft"][r])
        maps.append(m)
    if "F" not in _PROGS:
        _PROGS["F"] = build_fused()
    res = run_bass_kernel_spmd(_PROGS["F"], maps, core_ids=list(range(NCORES))).results
    out = np.zeros((2, 4 * NLAT, D), np.float32)
    for i in range(NCORES):
        b, r = i // 4, i % 4
        xo = np.asarray(res[i]["x_o"])
        out[b, r * NLAT:(r + 1) * NLAT] = xo.transpose(2, 1, 0).reshape(NLAT, D)
    return out
```
